# Optimizing a Trainium2 kernel written in Bass

```python
import math
import jax, jax.numpy as jnp
from jax import lax
import numpy as np

D_MODEL = 2048
BATCH = 8
SEQ = 2048
DEPTH = 2

GRID_W = 64
CTX_LEN = 256
HEAD_DIM = 128
ATT_HEADS = 6
ATT_KV_HEADS = 2
RET_HEADS = 4
RET_DK = 128
RET_DV = 128
SSD_HEADS = 12
SSD_HEAD_DIM = 64
SSD_GROUPS = 2
SSD_STATE = 128
SSD_CONV = 5
ATT_W = ATT_HEADS * HEAD_DIM
KV_W = ATT_KV_HEADS * HEAD_DIM
RET_QK_W = RET_HEADS * RET_DK
RET_W = RET_HEADS * RET_DV
SSD_W = SSD_HEADS * SSD_HEAD_DIM
SSD_BC = SSD_GROUPS * SSD_STATE
CONV_CH = SSD_W + 2 * SSD_BC
D_MIX = ATT_W + RET_W + SSD_W
IN_WIDTHS = (ATT_W, KV_W, KV_W, RET_QK_W, RET_QK_W, RET_W, RET_W, SSD_W, CONV_CH, SSD_HEADS, SSD_HEADS)
D_IN = sum(IN_WIDTHS)
D_FF = -(-8 * D_MODEL // (3 * 256)) * 256
CHUNK = 128
Q_BLOCK = 128
ROPE_THETA = 10000.0
EPS = 1e-6

kernel_name = "hybrid_attn_retention_ssd_dit_block"


def rmsnorm(x, g=None):
    xf = x.astype(jnp.float32)
    y = xf * lax.rsqrt(jnp.mean(xf * xf, axis=-1, keepdims=True) + EPS)
    if g is not None:
        y = y * g.astype(jnp.float32)
    return y.astype(x.dtype)


def axial_rope(rows):
    row = jnp.repeat(jnp.arange(rows, dtype=jnp.float32), GRID_W)
    col = jnp.tile(jnp.arange(GRID_W, dtype=jnp.float32), rows)
    n_freq = HEAD_DIM // 4
    inv = ROPE_THETA ** (-jnp.arange(n_freq, dtype=jnp.float32) / n_freq)
    ang = jnp.concatenate([row[:, None] * inv, col[:, None] * inv], axis=-1)
    return jnp.cos(ang), jnp.sin(ang)


def apply_rope(x, cos, sin):
    xf = x.astype(jnp.float32).reshape(*x.shape[:-1], -1, 2)
    x1, x2 = xf[..., 0], xf[..., 1]
    cs, sn = cos[None, :, None, :], sin[None, :, None, :]
    out = jnp.stack([x1 * cs - x2 * sn, x1 * sn + x2 * cs], axis=-1).reshape(x.shape)
    return out.astype(x.dtype)


def dwconv_centred(u, w, b):
    k = w.shape[0]
    out = lax.conv_general_dilated(u, w[:, None, :].astype(u.dtype), window_strides=(1,),
                                   padding=[(k // 2, k // 2)], dimension_numbers=('NWC', 'WIO', 'NWC'),
                                   feature_group_count=u.shape[-1])
    return out + b


def chunked_scan(q, k, v, log_a, s0):
    bsz, L, H, N = q.shape
    P = v.shape[-1]
    nc = L // CHUNK
    f32 = jnp.float32
    qc = q.reshape(bsz, nc, CHUNK, H, N).astype(f32)
    kc = k.reshape(bsz, nc, CHUNK, H, N).astype(f32)
    vc = v.reshape(bsz, nc, CHUNK, H, P).astype(f32)
    acs = jnp.cumsum(log_a.astype(f32).reshape(bsz, nc, CHUNK, H), axis=2)
    tri = jnp.tril(jnp.ones((CHUNK, CHUNK), dtype=bool))
    seg = acs[:, :, :, None, :] - acs[:, :, None, :, :]
    dmat = jnp.exp(jnp.where(tri[None, None, :, :, None], seg, -jnp.inf))
    scores = jnp.einsum('bcihn,bcjhn->bcijh', qc, kc) * dmat
    y_intra = jnp.einsum('bcijh,bcjhp->bcihp', scores, vc)
    decay_end = jnp.exp(acs[:, :, -1:, :] - acs)
    states = jnp.einsum('bcjhn,bcjh,bcjhp->cbhnp', kc, decay_end, vc)
    chunk_decay = jnp.exp(acs[:, :, -1, :]).transpose(1, 0, 2)

    def step(s, inp):
        st, dec = inp
        return s * dec[..., None, None] + st, s

    s_final, s_enter = lax.scan(step, s0.astype(f32), (states, chunk_decay))
    y_inter = jnp.einsum('bcihn,cbhnp,bcih->bcihp', qc, s_enter, jnp.exp(acs))
    return (y_intra + y_inter).reshape(bsz, L, H, P).astype(v.dtype), s_final


def bidir_scan(q, k_f, k_b, v, la_f, la_b, s0_f, s0_b):
    y_f, s_f = chunked_scan(q, k_f, v, la_f, s0_f)
    fl = lambda t: jnp.flip(t, axis=1)
    y_b, s_b = chunked_scan(fl(q), fl(k_b), fl(v), fl(la_b), s0_b)
    return y_f + fl(y_b), s_f, s_b


def block_attention(q, k, v):
    bsz, L, hq, dh = q.shape
    rep = hq // ATT_KV_HEADS
    nb = L // Q_BLOCK
    qb = q.reshape(bsz, nb, Q_BLOCK, ATT_KV_HEADS, rep, dh).transpose(1, 0, 2, 3, 4, 5)
    scale = HEAD_DIM ** -0.5

    def one_block(qi):
        s = jnp.einsum('bqgrd,bkgd->bgrqk', qi, k).astype(jnp.float32) * scale
        pr = jax.nn.softmax(s, axis=-1).astype(v.dtype)
        return jnp.einsum('bgrqk,bkgd->bqgrd', pr, v)

    out = lax.map(one_block, qb)
    return out.transpose(1, 0, 2, 3, 4, 5).reshape(bsz, L, hq * dh)


def token_tensors(h, p, rope):
    bsz, L, _ = h.shape
    splits = [int(s) for s in np.cumsum(IN_WIDTHS)[:-1]]
    aq, ak, av, rq, rk, rv, rg, z, xbc, dtf, dtb = jnp.split(h @ p['w_in'], splits, axis=-1)
    aq = rmsnorm(aq.reshape(bsz, L, ATT_HEADS, HEAD_DIM), p['q_norm_g'])
    ak = rmsnorm(ak.reshape(bsz, L, ATT_KV_HEADS, HEAD_DIM), p['k_norm_g'])
    av = av.reshape(bsz, L, ATT_KV_HEADS, HEAD_DIM)
    rq = rq.reshape(bsz, L, RET_HEADS, RET_DK)
    rk = rk.reshape(bsz, L, RET_HEADS, RET_DK) * (RET_DK ** -0.5)
    rv = rv.reshape(bsz, L, RET_HEADS, RET_DV)
    if rope is not None:
        cos, sin = rope
        aq, ak, rq, rk = (apply_rope(t, cos, sin) for t in (aq, ak, rq, rk))
    xbc = jax.nn.silu(dwconv_centred(xbc, p['conv_w'], p['conv_b']))
    xs, bs, cs = jnp.split(xbc, [SSD_W, SSD_W + SSD_BC], axis=-1)
    hpg = SSD_HEADS // SSD_GROUPS
    xs = xs.reshape(bsz, L, SSD_HEADS, SSD_HEAD_DIM)
    bs = jnp.repeat(bs.reshape(bsz, L, SSD_GROUPS, SSD_STATE), hpg, axis=2)
    cs = jnp.repeat(cs.reshape(bsz, L, SSD_GROUPS, SSD_STATE), hpg, axis=2)
    dt_f = jax.nn.softplus(dtf.astype(jnp.float32) + p['dt_bias_f'].astype(jnp.float32))
    dt_b = jax.nn.softplus(dtb.astype(jnp.float32) + p['dt_bias_b'].astype(jnp.float32))
    la_f = dt_f * -jnp.exp(p['a_log_f'].astype(jnp.float32))
    la_b = dt_b * -jnp.exp(p['a_log_b'].astype(jnp.float32))
    ret_lf = jnp.broadcast_to(jnp.log1p(-jnp.exp2(p['ret_decay_f'].astype(jnp.float32))), (bsz, L, RET_HEADS))
    ret_lb = jnp.broadcast_to(jnp.log1p(-jnp.exp2(p['ret_decay_b'].astype(jnp.float32))), (bsz, L, RET_HEADS))
    return dict(aq=aq, ak=ak, av=av, rq=rq, rk=rk, rv=rv, rg=rg, ret_lf=ret_lf, ret_lb=ret_lb,
                z=z, xs=xs, cs=cs, k_f=bs * dt_f[..., None].astype(bs.dtype),
                k_b=bs * dt_b[..., None].astype(bs.dtype), la_f=la_f, la_b=la_b)


def mixer_output(att, ret, ssd, t, p):
    bsz, L = att.shape[:2]
    ret = rmsnorm(ret).reshape(bsz, L, RET_W) * jax.nn.silu(t['rg'])
    ssd = (ssd + p['d_skip'][:, None] * t['xs']).reshape(bsz, L, SSD_W)
    ssd = rmsnorm(ssd * jax.nn.silu(t['z']), p['ssd_norm_g'])
    return jnp.concatenate([att, ret, ssd], axis=-1) @ p['w_out']


def swiglu(h, p):
    return (jax.nn.silu(h @ p['w_gate']) * (h @ p['w_up'])) @ p['w_down']


def hybrid_layer(x, xc, c, c_ctx, p, rope, last):
    bsz = x.shape[0]
    mod = jax.nn.silu(c) @ p['w_mod'] + p['b_mod']
    mod_c = jax.nn.silu(c_ctx) @ p['w_mod'] + p['b_mod']
    sh1, sc1, g1, sh2, sc2, g2 = [m[:, None, :] for m in jnp.split(mod, 6, axis=-1)]
    shc1, scc1, gc1, shc2, scc2, gc2 = jnp.split(mod_c, 6, axis=-1)

    h = rmsnorm(x, p['pre_mix_g']) * (1 + sc1) + sh1
    hc = rmsnorm(xc, p['pre_mix_g']) * (1 + scc1) + shc1
    t = token_tensors(h, p, rope)
    tc = token_tensors(hc, p, None)

    zr = jnp.zeros((bsz, RET_HEADS, RET_DK, RET_DV), jnp.float32)
    zs = jnp.zeros((bsz, SSD_HEADS, SSD_STATE, SSD_HEAD_DIM), jnp.float32)
    ret_c, rs_f, rs_b = bidir_scan(tc['rq'], tc['rk'], tc['rk'], tc['rv'], tc['ret_lf'], tc['ret_lb'], zr, zr)
    ssd_c, ss_f, ss_b = bidir_scan(tc['cs'], tc['k_f'], tc['k_b'], tc['xs'], tc['la_f'], tc['la_b'], zs, zs)

    k_all = jnp.concatenate([tc['ak'], t['ak']], axis=1)
    v_all = jnp.concatenate([tc['av'], t['av']], axis=1)
    att = block_attention(t['aq'], k_all, v_all)
    ret, _, _ = bidir_scan(t['rq'], t['rk'], t['rk'], t['rv'], t['ret_lf'], t['ret_lb'], rs_f, rs_b)
    ssd, _, _ = bidir_scan(t['cs'], t['k_f'], t['k_b'], t['xs'], t['la_f'], t['la_b'], ss_f, ss_b)
    m = mixer_output(att, ret, ssd, t, p)
    x = x + g1 * rmsnorm(m, p['post_mix_g'])
    f = swiglu(rmsnorm(x, p['pre_ffn_g']) * (1 + sc2) + sh2, p)
    x = x + g2 * rmsnorm(f, p['post_ffn_g'])

    if not last:
        att_c = block_attention(tc['aq'], tc['ak'], tc['av'])
        mc = mixer_output(att_c, ret_c, ssd_c, tc, p)
        xc = xc + gc1 * rmsnorm(mc, p['post_mix_g'])
        fc = swiglu(rmsnorm(xc, p['pre_ffn_g']) * (1 + scc2) + shc2, p)
        xc = xc + gc2 * rmsnorm(fc, p['post_ffn_g'])
    return x, xc


def setup_inputs(seed: int = 0) -> dict:
    key = jax.random.key(seed)
    ks = jax.random.split(key, 32)
    f32 = jnp.float32
    nrm = lambda k, shape, s: jax.random.normal(k, shape, f32) * s
    L = DEPTH
    lo, hi = math.log(1e-3), math.log(1e-1)
    dt_f = jnp.exp(jax.random.uniform(ks[17], (L, SSD_HEADS), f32) * (hi - lo) + lo)
    dt_b = jnp.exp(jax.random.uniform(ks[18], (L, SSD_HEADS), f32) * (hi - lo) + lo)
    base_decay = -5.0 - jnp.arange(RET_HEADS, dtype=f32)
    return {
        "x": nrm(ks[0], (BATCH, SEQ, D_MODEL), 1.0),
        "c": nrm(ks[1], (BATCH, D_MODEL), 1.0),
        "ctx": nrm(ks[2], (BATCH, CTX_LEN, D_MODEL), 1.0),
        "c_ctx": nrm(ks[3], (D_MODEL,), 1.0),
        "w_mod": nrm(ks[4], (L, D_MODEL, 6 * D_MODEL), 0.5 * D_MODEL ** -0.5),
        "b_mod": nrm(ks[5], (L, 6 * D_MODEL), 0.02),
        "pre_mix_g": 1.0 + nrm(ks[6], (L, D_MODEL), 0.02),
        "post_mix_g": 1.0 + nrm(ks[7], (L, D_MODEL), 0.02),
        "pre_ffn_g": 1.0 + nrm(ks[8], (L, D_MODEL), 0.02),
        "post_ffn_g": 1.0 + nrm(ks[9], (L, D_MODEL), 0.02),
        "w_in": nrm(ks[10], (L, D_MODEL, D_IN), D_MODEL ** -0.5),
        "q_norm_g": 1.0 + nrm(ks[11], (L, HEAD_DIM), 0.02),
        "k_norm_g": 1.0 + nrm(ks[12], (L, HEAD_DIM), 0.02),
        "ret_decay_f": base_decay + nrm(ks[13], (L, RET_HEADS), 0.1),
        "ret_decay_b": base_decay + nrm(ks[14], (L, RET_HEADS), 0.1),
        "conv_w": nrm(ks[15], (L, SSD_CONV, CONV_CH), SSD_CONV ** -0.5),
        "conv_b": nrm(ks[16], (L, CONV_CH), 0.02),
        "dt_bias_f": dt_f + jnp.log(-jnp.expm1(-dt_f)),
        "dt_bias_b": dt_b + jnp.log(-jnp.expm1(-dt_b)),
        "a_log_f": jnp.log(jax.random.uniform(ks[19], (L, SSD_HEADS), f32, 1.0, 16.0)),
        "a_log_b": jnp.log(jax.random.uniform(ks[20], (L, SSD_HEADS), f32, 1.0, 16.0)),
        "d_skip": 1.0 + nrm(ks[21], (L, SSD_HEADS), 0.02),
        "ssd_norm_g": 1.0 + nrm(ks[22], (L, SSD_W), 0.02),
        "w_out": nrm(ks[23], (L, D_MIX, D_MODEL), D_MIX ** -0.5),
        "w_gate": nrm(ks[24], (L, D_MODEL, D_FF), D_MODEL ** -0.5),
        "w_up": nrm(ks[25], (L, D_MODEL, D_FF), D_MODEL ** -0.5),
        "w_down": nrm(ks[26], (L, D_FF, D_MODEL), D_FF ** -0.5),
    }


def reference(x, c, ctx, c_ctx, w_mod, b_mod, pre_mix_g, post_mix_g, pre_ffn_g, post_ffn_g, w_in,
              q_norm_g, k_norm_g, ret_decay_f, ret_decay_b, conv_w, conv_b, dt_bias_f, dt_bias_b,
              a_log_f, a_log_b, d_skip, ssd_norm_g, w_out, w_gate, w_up, w_down):
    n_lat = x.shape[1]
    ROWS = n_lat // GRID_W
    rope = axial_rope(ROWS)
    xc = ctx
    for i in range(DEPTH):
        p = dict(w_mod=w_mod[i], b_mod=b_mod[i], pre_mix_g=pre_mix_g[i], post_mix_g=post_mix_g[i],
                 pre_ffn_g=pre_ffn_g[i], post_ffn_g=post_ffn_g[i], w_in=w_in[i], q_norm_g=q_norm_g[i],
                 k_norm_g=k_norm_g[i], ret_decay_f=ret_decay_f[i], ret_decay_b=ret_decay_b[i],
                 conv_w=conv_w[i], conv_b=conv_b[i], dt_bias_f=dt_bias_f[i], dt_bias_b=dt_bias_b[i],
                 a_log_f=a_log_f[i], a_log_b=a_log_b[i], d_skip=d_skip[i], ssd_norm_g=ssd_norm_g[i],
                 w_out=w_out[i], w_gate=w_gate[i], w_up=w_up[i], w_down=w_down[i])
        x, xc = hybrid_layer(x, xc, c, c_ctx, p, rope, last=(i == DEPTH - 1))
    return x
```

```python
import math
import numpy as np
from contextlib import ExitStack
import concourse.bass as bass
import concourse.mybir as mybir
from concourse.bass_utils import run_bass_kernel_spmd

F32 = mybir.dt.float32
BF16 = mybir.dt.bfloat16
AF = mybir.ActivationFunctionType
ALU = mybir.AluOpType
AX = mybir.AxisListType

D = 2048
T = 2304
NT = 18
DIN = 5400
DFF = 5632
EPS = 1e-6
NCONST = 10 * 128 + 2
PW = 4120


class _Op:
    __slots__ = ("eng", "fn", "deps", "ch", "pos", "is_dma", "vc", "waits", "signal", "rank")


class Sched:
    ENGS = ("pe", "act", "dve", "pool", "sp")

    def __init__(self, nc, stack, n_dma_sems=12):
        self.nc = nc
        self.h = {"pe": nc.tensor, "act": nc.scalar, "dve": nc.vector, "pool": nc.gpsimd, "sp": nc.sync}
        self.ops = []
        self.n_emitted = 0
        self.eng_pos = {e: 0 for e in self.ENGS}
        self.last_w = {}
        self.readers = {}
        self.esem = {e: stack.enter_context(nc.semaphore("s_" + e)) for e in self.ENGS}
        self.dsems = {}
        self.dcount = {}
        self.drr = {}
        for q in ("sp", "pool"):
            self.dsems[q] = [stack.enter_context(nc.semaphore("d_%s%d" % (q, i))) for i in range(n_dma_sems)]
            self.drr[q] = 0
            for i in range(n_dma_sems):
                self.dcount[(q, i)] = 0
        self.by_chpos = {}
        self.last_on_ch = {}
        self.clock = {e: {} for e in self.ENGS}
        self.rk = {e: 0 for e in self.ENGS}
        self.nw = 0

    def _deps(self, r, w):
        deps = []
        for k in r:
            o = self.last_w.get(k)
            if o is not None:
                deps.append(o)
        for k in w:
            o = self.last_w.get(k)
            if o is not None:
                deps.append(o)
            deps.extend(self.readers.get(k, ()))
        return deps

    def _commit(self, op, r, w):
        for k in r:
            self.readers.setdefault(k, []).append(op)
        for k in w:
            self.last_w[k] = op
            self.readers[k] = []
        self.ops.append(op)
        self.by_chpos[(op.ch, op.pos)] = op
        self.last_on_ch[op.ch] = op

    def add(self, eng, fn, r=(), w=()):
        op = _Op()
        op.eng = eng
        op.fn = fn
        op.is_dma = False
        op.deps = self._deps(r, w)
        self.eng_pos[eng] += 1
        op.ch = eng
        op.pos = self.eng_pos[eng]
        op.signal = False
        self._commit(op, r, w)
        return op

    def dma(self, q, fn, r=(), w=()):
        op = _Op()
        op.eng = q
        op.fn = fn
        op.is_dma = True
        op.deps = self._deps(r, w)
        i = self.drr[q]
        self.drr[q] = (i + 1) % len(self.dsems[q])
        self.dcount[(q, i)] += 1
        op.ch = ("d", q, i)
        op.pos = self.dcount[(q, i)]
        op.signal = True
        self._commit(op, r, w)
        return op

    def barrier(self):
        lasts = [o for o in self.last_on_ch.values() if o.fn is not None or o.is_dma]
        for e in self.ENGS:
            op = _Op()
            op.eng = e
            op.fn = None
            op.is_dma = False
            op.deps = list(lasts)
            self.eng_pos[e] += 1
            op.ch = e
            op.pos = self.eng_pos[e]
            op.signal = False
            self.ops.append(op)
            self.by_chpos[(op.ch, op.pos)] = op
        self.last_w = {}
        self.readers = {}

    def emit(self):
        ops = self.ops[self.n_emitted:]
        clock = self.clock
        for op in ops:
            E = op.eng
            ck = clock[E]
            need = {}
            for d in op.deps:
                if (not d.is_dma) and d.eng == "pe" and E == "pe" and (not op.is_dma) and op.fn is not None:
                    continue
                if ck.get(d.ch, 0) < d.pos and need.get(d.ch, 0) < d.pos:
                    need[d.ch] = d.pos
            if op.is_dma and op.pos > 1:
                if ck.get(op.ch, 0) < op.pos - 1 and need.get(op.ch, 0) < op.pos - 1:
                    need[op.ch] = op.pos - 1
            op.waits = []
            for ch, pos in need.items():
                if ck.get(ch, 0) >= pos:
                    continue
                p = self.by_chpos[(ch, pos)]
                p.signal = True
                op.waits.append(p)
                for c, v in p.vc.items():
                    if ck.get(c, 0) < v:
                        ck[c] = v
            vc = dict(ck)
            vc[op.ch] = op.pos
            op.vc = vc
        for op in ops:
            if op.is_dma:
                op.rank = 16 * op.pos
            elif op.signal:
                self.rk[op.eng] += 1
                op.rank = self.rk[op.eng]
        for op in ops:
            h = self.h[op.eng]
            for p in op.waits:
                if p.is_dma:
                    sem = self.dsems[p.ch[1]][p.ch[2]]
                else:
                    sem = self.esem[p.eng]
                h.wait_ge(sem, p.rank)
                self.nw += 1
            if op.fn is None:
                continue
            inst = op.fn(h)
            if op.is_dma:
                inst.then_inc(self.dsems[op.ch[1]][op.ch[2]], 16)
            elif op.signal:
                inst.then_inc(self.esem[op.eng], 1)
            op.fn = None
        self.n_emitted = len(self.ops)
        return dict(n_ops=len(self.ops), n_waits=self.nw, ranks=dict(self.rk))

    def phase_end(self):
        self.barrier()
        return self.emit()


def make_consts():
    i = np.arange(128)
    J, I = np.meshgrid(i, i, indexing="ij")
    c = np.zeros((128, NCONST), np.float32)
    c[:, 0:128] = (J == I)
    c[:, 128:256] = (J <= I)
    c[:, 256:384] = (J >= I)
    c[:, 384:512] = 1.0
    c[:, 512:640] = np.where(I >= J, 0.0, -30000.0)
    c[:, 640:768] = np.where(I <= J, 0.0, -30000.0)
    c[:, 768:896] = np.maximum(I - J, 0)
    c[:, 896:1024] = np.maximum(J - I, 0)
    c[:, 1024:1152] = I + 1
    c[:, 1152:1280] = 128 - I
    c[:, 1280] = 127 - i
    c[:, 1281] = i
    return c


def make_rope():
    rows = 2048 // 64
    row = np.repeat(np.arange(rows, dtype=np.float32), 64)
    col = np.tile(np.arange(64, dtype=np.float32), rows)
    n_freq = 32
    inv = (np.float32(10000.0) ** (-np.arange(n_freq, dtype=np.float32) / n_freq)).astype(np.float32)
    ang = np.concatenate([row[:, None] * inv, col[:, None] * inv], axis=-1).astype(np.float32)
    return np.concatenate([np.cos(ang), np.sin(ang)], axis=-1).astype(np.float32)


SMALL = [("b_mod", [2, 12288]), ("pre_mix_g", [2, 2048]), ("post_mix_g", [2, 2048]), ("pre_ffn_g", [2, 2048]),
         ("post_ffn_g", [2, 2048]), ("q_norm_g", [2, 128]), ("k_norm_g", [2, 128]), ("ret_decay_f", [2, 4]),
         ("ret_decay_b", [2, 4]), ("conv_w", [2, 5, 1280]), ("conv_b", [2, 1280]), ("dt_bias_f", [2, 12]),
         ("dt_bias_b", [2, 12]), ("a_log_f", [2, 12]), ("a_log_b", [2, 12]), ("d_skip", [2, 12]),
         ("ssd_norm_g", [2, 768])]
BIG = [("w_mod", [2, 2048, 12288]), ("w_in", [2, 2048, DIN]), ("w_out", [2, 2048, 2048]),
       ("w_gate", [2, 2048, DFF]), ("w_up", [2, 2048, DFF]), ("w_down", [2, DFF, 2048])]


def build(nlayers=2, debug=False, stop=None, phases=None, feed=(), layers=None):
    nc = bass.Bass("TRN2", target_bir_lowering=False)
    SHAPES = dict(SMALL + BIG)
    SHAPES.update({"x": [2048, 2048], "ctx": [256, 2048], "cvec": [2, 2048], "consts": [128, NCONST], "rope": [2048, 128]})

    class LazyIn(dict):
        def __missing__(self, name):
            ap = nc.dram_tensor(name, SHAPES[name], F32, kind="ExternalInput").ap()
            self[name] = ap
            return ap

    I = LazyIn()
    if not debug:
        for n in ["x", "ctx", "cvec", "consts", "rope"] + [n for n, _ in SMALL + BIG]:
            I[n]
    out = nc.dram_tensor("out", [2048, 2048], F32, kind="ExternalOutput").ap()
    skind = "ExternalOutput" if debug else "Internal"

    def scr(name, shape, dt=F32):
        k = "ExternalInput" if name in feed else skind
        return nc.dram_tensor(name, shape, dt, kind=k).ap()

    modv = scr("modv", [2, 2, 12288])
    proj = scr("proj", [T, PW])
    xbcT = scr("xbcT", [1280, T])
    mixT = scr("mixT", [2048, T], BF16)
    x1s = scr("x1s", [T, D])
    h2T = scr("h2T", [2048, T], BF16)
    fsc = scr("fsc", [T, D])
    xnext = scr("xnext", [T, D])

    class Stop(Exception):
        pass

    with ExitStack() as gst:
        S = Sched(nc, gst)

        uid = [0]

        def mk_alloc(st):
            def Tl(name, shape, dt=F32):
                uid[0] += 1
                return st.enter_context(nc.sbuf_tensor("%s_%d" % (name, uid[0]), shape, dt))

            def Pl(name, shape, dt=F32):
                uid[0] += 1
                return st.enter_context(nc.psum_tensor("%s_%d" % (name, uid[0]), shape, dt))
            return Tl, Pl

        GT, GP = mk_alloc(gst)
        cst = GT("cst", [128, NCONST])
        ident_b = GT("ident_b", [128, 128], BF16)
        ones_b = GT("ones_b", [128, 128], BF16)
        S.dma("sp", lambda h: h.dma_start(out=cst[:], in_=I["consts"][:, :]), w=["cst"])
        S.add("dve", lambda h: h.tensor_copy(out=ident_b[:], in_=cst[:, 0:128]), r=["cst"], w=["ident_b"])
        S.add("dve", lambda h: h.tensor_copy(out=ones_b[:], in_=cst[:, 384:512]), r=["cst"], w=["ones_b"])
        ident_f = cst[:, 0:128]
        tri_f = cst[:, 128:256]
        tri_b = cst[:, 256:384]
        ones_f = cst[:, 384:512]
        mneg = [cst[:, 512:640], cst[:, 640:768]]
        D1 = cst[:, 768:896]
        D2 = cst[:, 896:1024]
        rowidx = [cst[:, 1024:1152], cst[:, 1152:1280]]
        colexp = [cst[:, 1280:1281], cst[:, 1281:1282]]
        S.phase_end()

        def check_stop(tag):
            if stop == tag:
                raise Stop()

        def rstd_ops(ssq_ap, out_ap, inv_n, rk, wk):
            S.add("act", lambda h: h.activation(out=out_ap, in_=ssq_ap, func=AF.Ln, scale=inv_n, bias=EPS), r=rk, w=wk)
            S.add("act", lambda h: h.activation(out=out_ap, in_=out_ap, func=AF.Exp, scale=-0.5), r=wk, w=wk)

        def xsrc(l, tt):
            if l == 0:
                if tt < 2:
                    return I["ctx"][tt * 128:(tt + 1) * 128, :]
                return I["x"][(tt - 2) * 128:(tt - 1) * 128, :]
            return xnext[tt * 128:(tt + 1) * 128, :]

        def bcast_row(ap_row, n):
            return ap_row.to_broadcast([128, n])

        def phase_mod():
            with ExitStack() as st:
                Tl, Pl = mk_alloc(st)
                cT = Tl("cT", [128, 16, 2])
                ce = Tl("ce", [128, 16, 2])
                sT = Tl("sT", [128, 16, 2], BF16)
                for r_ in range(2):
                    S.dma("sp", lambda h, r_=r_: h.dma_start(out=cT[:, :, r_], in_=I["cvec"][r_].rearrange("(k p) -> p k", p=128),
                                                             allow_slow_non_contiguous=True), w=["cT"])
                S.add("act", lambda h: h.activation(out=ce[:], in_=cT[:], func=AF.Exp, scale=-1.0), r=["cT"], w=["ce"])
                S.add("dve", lambda h: h.tensor_scalar(out=ce[:], in0=ce[:], scalar1=1.0, scalar2=None, op0=ALU.add), r=["ce"], w=["ce"])
                S.add("dve", lambda h: h.reciprocal(out=ce[:], in_=ce[:]), r=["ce"], w=["ce"])
                S.add("dve", lambda h: h.tensor_tensor(out=sT[:], in0=cT[:], in1=ce[:], op=ALU.mult), r=["ce", "cT"], w=["sT"])
                wb = [Tl("wmb%d" % i, [128, 16, 512], BF16) for i in range(2)]
                pm = [Pl("pm%d" % i, [128, 512]) for i in range(2)]
                bsb = Tl("bsb", [2, 12288])
                msb = Tl("msb", [2, 12288])
                for l in range(nlayers):
                    S.dma("sp", lambda h, l=l: h.dma_start(out=bsb[:], in_=I["b_mod"][l:l + 1, :].to_broadcast([2, 12288])), w=["bsb"])
                    wv = I["w_mod"][l].rearrange("(k p) n -> p k n", p=128)
                    for nb in range(24):
                        b = nb % 2
                        for k4 in range(4):
                            S.dma("pool", lambda h, b=b, k4=k4, nb=nb, wv=wv: h.dma_start(
                                out=wb[b][:, 4 * k4:4 * k4 + 4, :], in_=wv[:, 4 * k4:4 * k4 + 4, nb * 512:(nb + 1) * 512]),
                                w=[("wmb", b, k4)])
                        for k in range(16):
                            S.add("pe", lambda h, b=b, k=k: h.matmul(pm[b][0:2, :], lhsT=sT[:, k, :], rhs=wb[b][:, k, :],
                                                                     start=(k == 0), stop=(k == 15)),
                                  r=["sT", ("wmb", b, k // 4)], w=[("pm", b)])
                        S.add("dve", lambda h, b=b, nb=nb: h.tensor_tensor(out=msb[:, nb * 512:(nb + 1) * 512], in0=pm[b][0:2, :],
                                                                         in1=bsb[:, nb * 512:(nb + 1) * 512], op=ALU.add),
                              r=[("pm", b), "bsb"], w=["msb"])
                    S.dma("sp", lambda h, l=l: h.dma_start(out=modv[l], in_=msb[:]), r=["msb"], w=[("modv", l)])
                S.phase_end()

        def norm_mod_tiles(l, Tl, tagp, gname, sc_off, sh_off, r):
            gm = Tl(tagp + "gm", [128, D])
            sh = Tl(tagp + "sh", [128, D])
            gp = Tl(tagp + "gp", [128, D])
            return gm, sh, gp

        def load_norm_mod(l, gm, sh, gp, key, gname, sc_off, sh_off, r):
            S.dma("sp", lambda h: h.dma_start(out=gp[:], in_=bcast_row(I[gname][l:l + 1, :], D)), w=[key + "gp"])
            S.dma("sp", lambda h: h.dma_start(out=gm[:], in_=bcast_row(modv[l, r:r + 1, sc_off:sc_off + D], D)), w=[key + "gm"])
            S.dma("sp", lambda h: h.dma_start(out=sh[:], in_=bcast_row(modv[l, r:r + 1, sh_off:sh_off + D], D)), w=[key + "sh"])
            S.add("dve", lambda h: h.scalar_tensor_tensor(out=gm[:], in0=gm[:], scalar=1.0, in1=gp[:], op0=ALU.add, op1=ALU.mult),
                  r=[key + "gp", key + "gm"], w=[key + "gm"])

        def load_gate_mod(l, G, gp, key, gname, g_off, r):
            S.dma("sp", lambda h: h.dma_start(out=gp[:], in_=bcast_row(I[gname][l:l + 1, :], D)), w=[key + "gp"])
            S.dma("sp", lambda h: h.dma_start(out=G[:], in_=bcast_row(modv[l, r:r + 1, g_off:g_off + D], D)), w=[key + "G"])
            S.add("pool", lambda h: h.tensor_tensor(out=G[:], in0=G[:], in1=gp[:], op=ALU.mult), r=[key + "gp", key + "G"], w=[key + "G"])

        def norm_transpose_tile(l, tt, xt_ap, xkey, gm, sh, mkey, tmp, hb, junk, ssq, pt, dst_fn, wkeys, i):
            S.add("act", lambda h: h.activation(out=junk[:], in_=xt_ap, func=AF.Square, accum_out=ssq[:, 0:1]),
                  r=[xkey], w=["junk", "ssq"])
            rstd_ops(ssq[:, 0:1], ssq[:, 1:2], 1.0 / D, ["ssq"], ["rstd"])
            S.add("dve", lambda h: h.scalar_tensor_tensor(out=tmp[:], in0=xt_ap, scalar=ssq[:, 1:2], in1=gm[:], op0=ALU.mult, op1=ALU.mult),
                  r=[xkey, "rstd", mkey + "gm"], w=["tmp"])
            S.add("dve", lambda h: h.tensor_tensor(out=hb[:], in0=tmp[:], in1=sh[:], op=ALU.add), r=["tmp", mkey + "sh"], w=[("hb", i % 2)])
            for k in range(16):
                S.add("pe", lambda h, k=k: h.transpose(out=pt[:, k, :], in_=hb[:, k * 128:(k + 1) * 128], identity=ident_b[:]),
                      r=[("hb", i % 2)], w=[("pt", i % 2)])
            dst_fn(pt, ("pt", i % 2), wkeys)

        def phase_in(l):
            with ExitStack() as st_o:
                To, Po = mk_alloc(st_o)
                hT = To("hT", [128, 16, T], BF16)
                with ExitStack() as st:
                    Tl, Pl = mk_alloc(st)
                    mods = {}
                    for r, nm in ((1, "c"), (0, "l")):
                        gm = Tl("gm" + nm, [128, D])
                        sh = Tl("sh" + nm, [128, D])
                        gp = Tl("gp" + nm, [128, D])
                        load_norm_mod(l, gm, sh, gp, nm, "pre_mix_g", 2048, 0, r)
                        mods[r] = (gm, sh, nm)
                    xt = [Tl("xt%d" % i, [128, D]) for i in range(2)]
                    tmp = Tl("tmp", [128, D])
                    hb = [Tl("hb%d" % i, [128, D], BF16) for i in range(2)]
                    junk = Tl("junk", [128, D], BF16)
                    ssq = Tl("ssq", [128, 2])
                    pt = [Pl("pt%d" % i, [128, 16, 128], BF16) for i in range(2)]
                    for tt in range(NT):
                        i = tt
                        gm, sh, nm = mods[1 if tt < 2 else 0]
                        S.dma("sp", lambda h, tt=tt, i=i: h.dma_start(out=xt[i % 2][:], in_=xsrc(l, tt)), w=[("xt", i % 2)])

                        def dst(ptile, pkey, wkeys, tt=tt):
                            S.add("act", lambda h: h.activation(out=hT[:, :, tt * 128:(tt + 1) * 128], in_=ptile[:], func=AF.Copy),
                                  r=[pkey], w=wkeys)
                        norm_transpose_tile(l, tt, xt[i % 2][:], ("xt", i % 2), gm, sh, nm, tmp, hb[i % 2], junk, ssq, pt[i % 2], dst,
                                            [("hT", tt)], i)
                    S.phase_end()
                with ExitStack() as st:
                    Tl, Pl = mk_alloc(st)
                    wb = [Tl("wib%d" % i, [128, 16, 512], BF16) for i in range(2)]
                    stg = [Tl("stg%d" % i, [128, 512]) for i in range(4)]
                    pp = [Pl("pp%d" % i, [128, 512]) for i in range(4)]
                    wv = I["w_in"][l].rearrange("(k p) n -> p k n", p=128)
                    cnt = [0]

                    def evac(ps_ap, n, dram_ap, pkey):
                        i = cnt[0]
                        cnt[0] += 1
                        s = stg[i % 4]
                        if i % 2 == 0:
                            S.add("act", lambda h: h.activation(out=s[:, 0:n], in_=ps_ap, func=AF.Copy), r=[pkey], w=[("stg", i % 4)])
                        else:
                            S.add("dve", lambda h: h.tensor_copy(out=s[:, 0:n], in_=ps_ap), r=[pkey], w=[("stg", i % 4)])
                        S.dma("sp", lambda h: h.dma_start(out=dram_ap, in_=s[:, 0:n]), r=[("stg", i % 4)], w=[("dram", i)])

                    blocks = [(c0, 512) for c0 in range(0, 5120, 512)] + [(5120, 280)]
                    pi = [0]
                    for bi, (c0, ncol) in enumerate(blocks):
                        b = bi % 2
                        for k4 in range(4):
                            S.dma("pool", lambda h, b=b, k4=k4, c0=c0, ncol=ncol: h.dma_start(
                                out=wb[b][:, 4 * k4:4 * k4 + 4, 0:ncol], in_=wv[:, 4 * k4:4 * k4 + 4, c0:c0 + ncol]), w=[("wib", b, k4)])
                        wkeys = [("wib", b, k4) for k4 in range(4)]
                        if c0 < 4096:
                            tm = (0, ncol, c0)
                            fm_chunks = []
                        elif c0 < 5120:
                            tm = None
                            fm_chunks = [(j, (c0 - 4096) // 128 + j) for j in range(4)]
                        else:
                            tm = (256, 24, 4096)
                            fm_chunks = [(0, 8), (1, 9)]
                        if tm is not None:
                            co, n, dc = tm
                            for tt in range(NT):
                                p = pi[0] % 4
                                pi[0] += 1
                                for k in range(16):
                                    S.add("pe", lambda h, p=p, k=k, tt=tt, co=co, n=n, b=b: h.matmul(
                                        pp[p][:, 0:n], lhsT=hT[:, k, tt * 128:(tt + 1) * 128], rhs=wb[b][:, k, co:co + n],
                                        start=(k == 0), stop=(k == 15)), r=wkeys, w=[("pp", p)])
                                evac(pp[p][:, 0:n], n, proj[tt * 128:(tt + 1) * 128, dc:dc + n], ("pp", p))
                        for (j, ch) in fm_chunks:
                            for t0 in range(0, T, 512):
                                n = min(512, T - t0)
                                p = pi[0] % 4
                                pi[0] += 1
                                for k in range(16):
                                    S.add("pe", lambda h, p=p, k=k, t0=t0, n=n, j=j, b=b: h.matmul(
                                        pp[p][:, 0:n], lhsT=wb[b][:, k, j * 128:(j + 1) * 128], rhs=hT[:, k, t0:t0 + n],
                                        start=(k == 0), stop=(k == 15)), r=wkeys, w=[("pp", p)])
                                evac(pp[p][:, 0:n], n, xbcT[ch * 128:(ch + 1) * 128, t0:t0 + n], ("pp", p))
                    S.phase_end()

        def rope_ops(src, dst, cs, nh, lat, skey, dkey, cskey, tmps):
            if not lat:
                S.add("pool", lambda h: h.tensor_copy(out=dst, in_=src), r=[skey], w=[dkey, (dkey, "b")])
                return
            t1, t2, t3, t4 = tmps
            s4 = src.rearrange("p (h i two) -> p h i two", h=nh, two=2)
            d4 = dst.rearrange("p (h i two) -> p h i two", h=nh, two=2)
            x1 = s4[:, :, :, 0]
            x2 = s4[:, :, :, 1]
            cosb = cs[:, 0:64].unsqueeze(1).to_broadcast([128, nh, 64])
            sinb = cs[:, 64:128].unsqueeze(1).to_broadcast([128, nh, 64])
            v = lambda t: t[:, 0:nh * 64].rearrange("p (h i) -> p h i", h=nh)
            S.add("dve", lambda h: h.tensor_tensor(out=v(t1), in0=x1, in1=cosb, op=ALU.mult), r=[skey, cskey], w=["rt1"])
            S.add("dve", lambda h: h.tensor_tensor(out=v(t2), in0=x2, in1=sinb, op=ALU.mult), r=[skey, cskey], w=["rt2"])
            S.add("dve", lambda h: h.tensor_tensor(out=d4[:, :, :, 0], in0=v(t1), in1=v(t2), op=ALU.subtract), r=["rt1", "rt2"], w=[dkey])
            S.add("pool", lambda h: h.tensor_tensor(out=v(t3), in0=x1, in1=sinb, op=ALU.mult), r=[skey, cskey], w=["rt3"])
            S.add("pool", lambda h: h.tensor_tensor(out=v(t4), in0=x2, in1=cosb, op=ALU.mult), r=[skey, cskey], w=["rt4"])
            S.add("pool", lambda h: h.tensor_tensor(out=d4[:, :, :, 1], in0=v(t3), in1=v(t4), op=ALU.add), r=["rt3", "rt4"], w=[(dkey, "b")])

        def phase_att(l):
            last = (l == nlayers - 1)
            with ExitStack() as st:
                Tl, Pl = mk_alloc(st)
                QKT = Tl("QKT", [128, 8, T], BF16)
                Vtm = Tl("Vtm", [128, NT, 256], BF16)
                gqk = Tl("gqk", [128, 8, 128])
                pr = [Tl("pr%d" % i, [128, 1280]) for i in range(2)]
                sq = Tl("sq", [128, 1024])
                qn = Tl("qn", [128, 1024])
                tmps = [Tl("rt%d" % i, [128, 512]) for i in range(4)]
                qr = [Tl("qr%d" % i, [128, 1024], BF16) for i in range(2)]
                cs = [Tl("cs%d" % i, [128, 128]) for i in range(2)]
                st8 = Tl("st8", [128, 16])
                ptq = Pl("ptq", [128, 8, 128], BF16)
                for hh in range(8):
                    src = I["q_norm_g"] if hh < 6 else I["k_norm_g"]
                    S.dma("sp", lambda h, hh=hh, src=src: h.dma_start(out=gqk[:, hh, :], in_=src[l:l + 1, :].to_broadcast([128, 128])), w=["gqk"])
                S.add("dve", lambda h: h.tensor_scalar(out=gqk[:, 0:6, :], in0=gqk[:, 0:6, :], scalar1=float(128 ** -0.5), scalar2=None, op0=ALU.mult),
                      r=["gqk"], w=["gqk"])
                def prep_tile(tt):
                    i = tt
                    lat = tt >= 2
                    p_ = pr[i % 2]
                    S.dma("sp", lambda h, tt=tt, p_=p_: h.dma_start(out=p_[:], in_=proj[tt * 128:(tt + 1) * 128, 0:1280]), w=[("pr", i % 2)])
                    if lat:
                        S.dma("sp", lambda h, tt=tt, i=i: h.dma_start(out=cs[i % 2][:], in_=I["rope"][(tt - 2) * 128:(tt - 1) * 128, :]), w=[("cs", i % 2)])
                    S.add("act", lambda h, p_=p_: h.activation(out=sq[:], in_=p_[:, 0:1024], func=AF.Square), r=[("pr", i % 2)], w=["sq"])
                    S.add("dve", lambda h: h.tensor_reduce(out=st8[:, 0:8], in_=sq[:].rearrange("p (h d) -> p h d", h=8), axis=AX.X, op=ALU.add),
                          r=["sq"], w=["ssq8"])
                    rstd_ops(st8[:, 0:8], st8[:, 8:16], 1.0 / 128, ["ssq8"], ["rstd8"])
                    S.add("dve", lambda h, p_=p_: h.tensor_tensor(out=qn[:].rearrange("p (h d) -> p h d", h=8),
                                                                 in0=p_[:, 0:1024].rearrange("p (h d) -> p h d", h=8),
                                                                 in1=st8[:, 8:16].unsqueeze(2).to_broadcast([128, 8, 128]), op=ALU.mult),
                          r=[("pr", i % 2), "rstd8"], w=["qn"])
                    S.add("pool", lambda h: h.tensor_tensor(out=qn[:], in0=qn[:], in1=gqk[:].rearrange("p h d -> p (h d)"), op=ALU.mult),
                          r=["qn", "gqk"], w=["qn"])
                    q_ = qr[i % 2]
                    rope_ops(qn[:], q_[:], cs[i % 2], 8, lat, "qn", ("qr", i % 2), ("cs", i % 2), tmps)
                    for hh in range(8):
                        S.add("pe", lambda h, hh=hh, q_=q_: h.transpose(out=ptq[:, hh, :], in_=q_[:, hh * 128:(hh + 1) * 128], identity=ident_b[:]),
                              r=[("qr", i % 2), (("qr", i % 2), "b")], w=["ptq"])
                    S.add("act", lambda h, tt=tt: h.activation(out=QKT[:, :, tt * 128:(tt + 1) * 128], in_=ptq[:], func=AF.Copy), r=["ptq"], w=[("QKT", tt)])
                    S.add("act", lambda h, tt=tt, p_=p_: h.activation(out=Vtm[:, tt, :], in_=p_[:, 1024:1280], func=AF.Copy), r=[("pr", i % 2)], w=[("Vtm", tt)])
                for tt in range(NT):
                    prep_tile(tt)
                ps_s = [Pl("ps_s%d" % i, [128, 512]) for i in range(2)]
                ps_o = [Pl("ps_o%d" % i, [128, 512]) for i in range(2)]
                ps_d = [Pl("ps_d%d" % i, [128, 512]) for i in range(2)]
                pT = [Tl("pT%d" % i, [128, 512], BF16) for i in range(3)]
                rden = Tl("rden", [128, 512])
                oT = [Tl("oT%d" % i, [128, 512], BF16) for i in range(2)]
                qblocks = [(256 + qb * 512, 512, list(range(NT))) for qb in range(4)]
                if not last:
                    qblocks = [(0, 256, [0, 1])] + qblocks
                gi = 0
                si = 0
                def att_head(q0, n, kts, hh, gi):
                        nonlocal si
                        g = hh // 3
                        o = gi % 2
                        nk = len(kts)
                        slots = []

                        def do_s(j):
                            nonlocal si
                            sidx = si
                            si += 1
                            kt = kts[j]
                            S.add("pe", lambda h, sidx=sidx, kt=kt: h.matmul(ps_s[sidx % 2][:, 0:n], lhsT=QKT[:, 6 + g, kt * 128:(kt + 1) * 128],
                                                                           rhs=QKT[:, hh, q0:q0 + n], start=True, stop=True),
                                  r=[("QKT", kt)] + [("QKT", q0 // 128 + a) for a in range(n // 128)], w=[("ps_s", sidx % 2)])
                            S.add("act", lambda h, sidx=sidx: h.activation(out=pT[sidx % 3][:, 0:n], in_=ps_s[sidx % 2][:, 0:n], func=AF.Exp),
                                  r=[("ps_s", sidx % 2)], w=[("pT", sidx % 3)])
                            slots.append(sidx)

                        def do_o(j):
                            sidx = slots[j]
                            kt = kts[j]
                            S.add("pe", lambda h, sidx=sidx, kt=kt: h.matmul(ps_o[o][:, 0:n], lhsT=Vtm[:, kt, g * 128:(g + 1) * 128], rhs=pT[sidx % 3][:, 0:n],
                                                                           start=(j == 0), stop=(j == nk - 1)),
                                  r=[("pT", sidx % 3), ("Vtm", kt)], w=[("ps_o", o)])
                            S.add("pe", lambda h, sidx=sidx: h.matmul(ps_d[o][:, 0:n], lhsT=ones_b[:], rhs=pT[sidx % 3][:, 0:n],
                                                                    start=(j == 0), stop=(j == nk - 1)),
                                  r=[("pT", sidx % 3)], w=[("ps_d", o)])
                        do_s(0)
                        for j in range(nk):
                            if j + 1 < nk:
                                do_s(j + 1)
                            do_o(j)
                        S.add("dve", lambda h, o=o: h.reciprocal(out=rden[:, 0:n], in_=ps_d[o][:, 0:n]), r=[("ps_d", o)], w=["rden"])
                        S.add("dve", lambda h, o=o: h.tensor_tensor(out=oT[o][:, 0:n], in0=ps_o[o][:, 0:n], in1=rden[:, 0:n], op=ALU.mult),
                              r=[("ps_o", o), "rden"], w=[("oT", o)])
                        S.dma("sp", lambda h, o=o, hh=hh, q0=q0, n=n: h.dma_start(out=mixT[hh * 128:(hh + 1) * 128, q0:q0 + n], in_=oT[o][:, 0:n]),
                              r=[("oT", o)], w=[("mixT", gi)])
                for (q0, n, kts) in qblocks:
                    for hh in range(6):
                        att_head(q0, n, kts, hh, gi)
                        gi += 1
                S.phase_end()

        def silu_ops(eng2, src, e, dst, skey, ekey, dkey):
            S.add("act", lambda h: h.activation(out=e, in_=src, func=AF.Exp, scale=-1.0), r=[skey], w=[ekey])
            S.add(eng2, lambda h: h.tensor_scalar(out=e, in0=e, scalar1=1.0, scalar2=None, op0=ALU.add), r=[ekey], w=[ekey])
            S.add("dve", lambda h: h.reciprocal(out=e, in_=e), r=[ekey], w=[ekey])
            S.add(eng2, lambda h: h.tensor_tensor(out=dst, in0=src, in1=e, op=ALU.mult), r=[skey, ekey], w=[dkey])

        def phase_ret(l):
            last = (l == nlayers - 1)
            with ExitStack() as st:
                Tl, Pl = mk_alloc(st)
                RQK = Tl("RQK", [128, 8, T], BF16)
                RKtm = Tl("RKtm", [128, NT, 512], BF16)
                RVtm = Tl("RVtm", [128, NT, 512], BF16)
                rgs = Tl("rgs", [128, NT, 512], BF16)
                SAll = [Tl("SAll%d" % d_, [128, NT, 512], BF16) for d_ in range(2)]
                S32 = [Tl("S32%d" % d_, [128, 512]) for d_ in range(2)]
                MaskT = Tl("MaskT", [128, 4, 128])
                ERow = [Tl("ERow%d" % d_, [128, 4, 128], BF16) for d_ in range(2)]
                dec = Tl("dec", [128, 8])
                lg = Tl("lg", [128, 8])
                dend = Tl("dend", [128, 8])
                gam = Tl("gam", [128, 8])
                marg = Tl("marg", [128, 128])
                pr = [Tl("rpr%d" % i, [128, 2048]) for i in range(2)]
                tmps = [Tl("rrt%d" % i, [128, 512]) for i in range(4)]
                qr = [Tl("rqr%d" % i, [128, 1024], BF16) for i in range(2)]
                cs = [Tl("rcs%d" % i, [128, 128]) for i in range(2)]
                ee = Tl("ree", [128, 512])
                ptq = Pl("rptq", [128, 8, 128], BF16)
                S.dma("sp", lambda h: h.dma_start(out=dec[:, 0:4], in_=I["ret_decay_f"][l:l + 1, :].to_broadcast([128, 4])), w=["dec"])
                S.dma("sp", lambda h: h.dma_start(out=dec[:, 4:8], in_=I["ret_decay_b"][l:l + 1, :].to_broadcast([128, 4])), w=["dec"])
                S.add("act", lambda h: h.activation(out=lg[:], in_=dec[:], func=AF.Exp, scale=float(math.log(2.0))), r=["dec"], w=["lg"])
                S.add("act", lambda h: h.activation(out=lg[:], in_=lg[:], func=AF.Ln, scale=-1.0, bias=1.0), r=["lg"], w=["lg"])
                S.add("act", lambda h: h.activation(out=gam[:], in_=lg[:], func=AF.Exp, scale=128.0), r=["lg"], w=["gam"])

                def mk_head(hh):
                    S.add("dve", lambda h: h.tensor_scalar(out=marg[:], in0=D1, scalar1=lg[:, hh:hh + 1], scalar2=None, op0=ALU.mult), r=["lg"], w=["marg"])
                    S.add("dve", lambda h: h.scalar_tensor_tensor(out=marg[:], in0=D2, scalar=lg[:, 4 + hh:5 + hh], in1=marg[:], op0=ALU.mult, op1=ALU.add),
                          r=["lg", "marg"], w=["marg"])
                    S.add("act", lambda h: h.activation(out=marg[:], in_=marg[:], func=AF.Exp), r=["marg"], w=["marg"])
                    S.add("dve", lambda h: h.tensor_tensor(out=MaskT[:, hh, :], in0=marg[:], in1=ident_f, op=ALU.add), r=["marg"], w=["MaskT"])
                    for d_ in range(2):
                        S.add("act", lambda h, d_=d_: h.activation(out=ERow[d_][:, hh, :], in_=rowidx[d_], func=AF.Exp, scale=lg[:, 4 * d_ + hh:4 * d_ + hh + 1]),
                              r=["lg"], w=["ERow"])
                        S.add("act", lambda h, d_=d_: h.activation(out=dend[:, 4 * d_ + hh:4 * d_ + hh + 1], in_=colexp[d_], func=AF.Exp,
                                                                   scale=lg[:, 4 * d_ + hh:4 * d_ + hh + 1]), r=["lg"], w=["dend"])
                for hh in range(4):
                    mk_head(hh)

                def prep_tile(tt):
                    i = tt
                    lat = tt >= 2
                    p_ = pr[i % 2]
                    S.dma("sp", lambda h: h.dma_start(out=p_[:], in_=proj[tt * 128:(tt + 1) * 128, 1280:3328]), w=[("pr", i % 2)])
                    if lat:
                        S.dma("sp", lambda h: h.dma_start(out=cs[i % 2][:], in_=I["rope"][(tt - 2) * 128:(tt - 1) * 128, :]), w=[("cs", i % 2)])
                    S.add("act", lambda h: h.activation(out=p_[:, 512:1024], in_=p_[:, 512:1024], func=AF.Copy, scale=float(128 ** -0.5)),
                          r=[("pr", i % 2)], w=[("pr", i % 2)])
                    q_ = qr[i % 2]
                    rope_ops(p_[:, 0:1024], q_[:], cs[i % 2], 8, lat, ("pr", i % 2), ("qr", i % 2), ("cs", i % 2), tmps)
                    for hh in range(8):
                        S.add("pe", lambda h, hh=hh: h.transpose(out=ptq[:, hh, :], in_=q_[:, hh * 128:(hh + 1) * 128], identity=ident_b[:]),
                              r=[("qr", i % 2), (("qr", i % 2), "b")], w=["ptq"])
                    S.add("act", lambda h: h.activation(out=RQK[:, :, tt * 128:(tt + 1) * 128], in_=ptq[:], func=AF.Copy), r=["ptq"], w=[("RQK", tt)])
                    S.add("pool", lambda h: h.tensor_copy(out=RKtm[:, tt, :], in_=q_[:, 512:1024]), r=[("qr", i % 2), (("qr", i % 2), "b")], w=[("RKtm", tt)])
                    S.add("pool", lambda h: h.tensor_copy(out=RVtm[:, tt, :], in_=p_[:, 1024:1536]), r=[("pr", i % 2)], w=[("RVtm", tt)])
                    silu_ops("pool", p_[:, 1536:2048], ee[:], rgs[:, tt, :], ("pr", i % 2), "ee", ("rgs", tt))
                for tt in range(NT):
                    prep_tile(tt)

                psA = Pl("psA", [128, 512])
                RVs = [Tl("RVs%d" % i, [128, 512], BF16) for i in range(2)]
                orders = [list(range(NT)), [1, 0] + list(range(NT - 1, 1, -1))]

                def state_step(d_, idx):
                    order = orders[d_]
                    c = order[idx]
                    if idx == 0:
                        S.add("pool", lambda h: h.memset(S32[d_][:], 0.0), w=[("S32", d_)])
                        S.add("pool", lambda h: h.memset(SAll[d_][:, c, :], 0.0), w=[("SAll", d_, c)])
                    if idx == NT - 1:
                        return
                    rv = RVs[idx % 2]
                    S.add("pool", lambda h: h.tensor_tensor(out=rv[:].rearrange("p (h d) -> p h d", h=4),
                                                            in0=RVtm[:, c, :].rearrange("p (h d) -> p h d", h=4),
                                                            in1=dend[:, 4 * d_:4 * d_ + 4].unsqueeze(2).to_broadcast([128, 4, 128]), op=ALU.mult),
                          r=[("RVtm", c), "dend"], w=[("RVs", idx % 2)])
                    for hh in range(4):
                        S.add("pe", lambda h, hh=hh: h.matmul(psA[:, hh * 128:(hh + 1) * 128], lhsT=RKtm[:, c, hh * 128:(hh + 1) * 128],
                                                             rhs=rv[:, hh * 128:(hh + 1) * 128], start=True, stop=True),
                              r=[("RKtm", c), ("RVs", idx % 2)], w=["psA"])
                    S.add("dve", lambda h: h.tensor_tensor(out=S32[d_][:].rearrange("p (h d) -> p h d", h=4),
                                                           in0=S32[d_][:].rearrange("p (h d) -> p h d", h=4),
                                                           in1=gam[:, 4 * d_:4 * d_ + 4].unsqueeze(2).to_broadcast([128, 4, 128]), op=ALU.mult),
                          r=[("S32", d_), "gam"], w=[("S32", d_)])
                    S.add("dve", lambda h: h.tensor_tensor(out=S32[d_][:], in0=S32[d_][:], in1=psA[:], op=ALU.add), r=[("S32", d_), "psA"], w=[("S32", d_)])
                    cn = order[idx + 1]
                    S.add("act", lambda h: h.activation(out=SAll[d_][:, cn, :], in_=S32[d_][:], func=AF.Copy), r=[("S32", d_)], w=[("SAll", d_, cn)])
                for d_ in range(2):
                    for idx in range(NT):
                        state_step(d_, idx)

                psS = [Pl("psS%d" % i, [128, 128]) for i in range(2)]
                psY = [Pl("psY%d" % i, [128, 512]) for i in range(2)]
                ptr = Pl("ptr", [128, 4, 128], BF16)
                SDT = [Tl("SDT%d" % i, [128, 128], BF16) for i in range(2)]
                RQs = [[Tl("RQs%d_%d" % (d_, i), [128, 128], BF16) for i in range(2)] for d_ in range(2)]
                sqy = Tl("sqy", [128, 512])
                yn = Tl("yn", [128, 512])
                yb = [Tl("yb%d" % i, [128, 512], BF16) for i in range(2)]
                stgr = [Tl("stgr%d" % i, [128, 4, 128], BF16) for i in range(2)]
                st4 = Tl("st4", [128, 8])
                cnt = [0]

                def chunk(c):
                    y = psY[c % 2]
                    for hh in range(4):
                        i = cnt[0]
                        cnt[0] += 1
                        cs_ = slice(c * 128, (c + 1) * 128)
                        S.add("pe", lambda h, hh=hh, i=i: h.matmul(psS[i % 2][:], lhsT=RQK[:, 4 + hh, cs_], rhs=RQK[:, hh, cs_], start=True, stop=True),
                              r=[("RQK", c)], w=[("psS", i % 2)])
                        S.add("dve", lambda h, hh=hh, i=i: h.tensor_tensor(out=SDT[i % 2][:], in0=psS[i % 2][:], in1=MaskT[:, hh, :], op=ALU.mult),
                              r=[("psS", i % 2), "MaskT"], w=[("SDT", i % 2)])
                        for d_ in range(2):
                            S.add("pool", lambda h, hh=hh, i=i, d_=d_: h.tensor_tensor(out=RQs[d_][i % 2][:], in0=RQK[:, hh, cs_], in1=ERow[d_][:, hh, :], op=ALU.mult),
                                  r=[("RQK", c), "ERow"], w=[("RQs", d_, i % 2)])
                        S.add("pe", lambda h, hh=hh, i=i: h.matmul(y[:, hh * 128:(hh + 1) * 128], lhsT=SDT[i % 2][:], rhs=RVtm[:, c, hh * 128:(hh + 1) * 128],
                                                                  start=True, stop=False), r=[("SDT", i % 2), ("RVtm", c)], w=[("psY", c % 2)])
                        for d_ in range(2):
                            S.add("pe", lambda h, hh=hh, i=i, d_=d_: h.matmul(y[:, hh * 128:(hh + 1) * 128], lhsT=RQs[d_][i % 2][:],
                                                                             rhs=SAll[d_][:, c, hh * 128:(hh + 1) * 128], start=False, stop=(d_ == 1)),
                                  r=[("RQs", d_, i % 2), ("SAll", d_, c)], w=[("psY", c % 2)])
                    S.add("act", lambda h: h.activation(out=sqy[:], in_=y[:], func=AF.Square), r=[("psY", c % 2)], w=["sqy"])
                    S.add("dve", lambda h: h.tensor_reduce(out=st4[:, 0:4], in_=sqy[:].rearrange("p (h d) -> p h d", h=4), axis=AX.X, op=ALU.add),
                          r=["sqy"], w=["ssq4"])
                    rstd_ops(st4[:, 0:4], st4[:, 4:8], 1.0 / 128, ["ssq4"], ["rstd4"])
                    S.add("dve", lambda h: h.tensor_tensor(out=yn[:].rearrange("p (h d) -> p h d", h=4), in0=y[:].rearrange("p (h d) -> p h d", h=4),
                                                           in1=st4[:, 4:8].unsqueeze(2).to_broadcast([128, 4, 128]), op=ALU.mult),
                          r=[("psY", c % 2), "rstd4"], w=["yn"])
                    yb_ = yb[c % 2]
                    S.add("pool", lambda h: h.tensor_tensor(out=yb_[:], in0=yn[:], in1=rgs[:, c, :], op=ALU.mult), r=["yn", ("rgs", c)], w=[("yb", c % 2)])
                    for hh in range(4):
                        S.add("pe", lambda h, hh=hh: h.transpose(out=ptr[:, hh, :], in_=yb_[:, hh * 128:(hh + 1) * 128], identity=ident_b[:]),
                              r=[("yb", c % 2)], w=["ptr"])
                    sg = stgr[c % 2]
                    S.add("act", lambda h: h.activation(out=sg[:], in_=ptr[:], func=AF.Copy), r=["ptr"], w=[("stgr", c % 2)])
                    S.dma("sp", lambda h: h.dma_start(out=mixT[768:1280, c * 128:(c + 1) * 128].rearrange("(h d) t -> d h t", h=4), in_=sg[:]),
                          r=[("stgr", c % 2)], w=[("mixTr", c)])
                for c in range(2 if last else 0, NT):
                    chunk(c)
                S.phase_end()

        def phase_ssd(l):
            last = (l == nlayers - 1)
            NH = 12
            with ExitStack() as st_o:
                To, Po = mk_alloc(st_o)
                BT = To("BT", [128, 2, T], BF16)
                CT = To("CT", [128, 2, T], BF16)
                Btm = To("Btm", [128, NT, 256], BF16)
                xs = To("xs", [128, NT, 768], BF16)
                la = To("la", [128, 2, NT, NH])
                lndt = To("lndt", [128, 2, NT, NH])
                acs = To("acs", [128, 2, NT, NH])
                tot = To("tot", [128, 2, NT, NH])
                ein = To("ein", [128, 2, NT, NH])
                cdc = To("cdc", [128, 2, NT, NH])
                wend = To("wend", [128, 2, NT, NH])
                lb = To("lb", [128, 2, NT, NH])
                dsk = To("dsk", [128, NH])
                gssd = To("gssd", [128, 768])
                with ExitStack() as st:
                    Tl, Pl = mk_alloc(st)
                    cw = Tl("cw", [128, 10, 5])
                    cb = Tl("cb", [128, 10])
                    for k in range(5):
                        S.dma("sp", lambda h, k=k: h.dma_start(out=cw[:, :, k], in_=I["conv_w"][l, k].rearrange("(c p) -> p c", p=128),
                                                               allow_slow_non_contiguous=True), w=["cw"])
                    S.dma("sp", lambda h: h.dma_start(out=cb[:], in_=I["conv_b"][l].rearrange("(c p) -> p c", p=128), allow_slow_non_contiguous=True), w=["cb"])
                    S.dma("sp", lambda h: h.dma_start(out=dsk[:], in_=I["d_skip"][l:l + 1, :].to_broadcast([128, NH])), w=["dsk"])
                    S.dma("sp", lambda h: h.dma_start(out=gssd[:], in_=I["ssd_norm_g"][l:l + 1, :].to_broadcast([128, 768])), w=["gssd"])
                    UW = 2312
                    u = [Tl("u%d" % i, [128, UW]) for i in range(2)]
                    acc = Tl("acc", [128, UW])
                    ee = Tl("cee", [128, UW])
                    ob = [Tl("ob%d" % i, [128, UW], BF16) for i in range(2)]
                    ptx = Pl("ptx", [128, 8, 128], BF16)
                    for i in range(2):
                        S.add("pool", lambda h, i=i: h.memset(u[i][:], 0.0), w=[("u", i)])

                    def conv_chunk(cc):
                        i = cc
                        u_ = u[i % 2]
                        o_ = ob[i % 2]
                        S.dma("sp", lambda h: h.dma_start(out=u_[:, 2:258], in_=xbcT[cc * 128:(cc + 1) * 128, 0:256]), w=[("u", i % 2)])
                        S.dma("sp", lambda h: h.dma_start(out=u_[:, 262:2310], in_=xbcT[cc * 128:(cc + 1) * 128, 256:T]), w=[("u", i % 2)])
                        n = 2308
                        S.add("dve", lambda h: h.tensor_scalar(out=acc[:, 2:2 + n], in0=u_[:, 0:n], scalar1=cw[:, cc, 0:1], scalar2=cb[:, cc:cc + 1],
                                                               op0=ALU.mult, op1=ALU.add), r=[("u", i % 2), "cw", "cb"], w=["acc"])
                        for k in range(1, 5):
                            S.add("dve", lambda h, k=k: h.scalar_tensor_tensor(out=acc[:, 2:2 + n], in0=u_[:, k:k + n], scalar=cw[:, cc, k:k + 1],
                                                                               in1=acc[:, 2:2 + n], op0=ALU.mult, op1=ALU.add),
                                  r=[("u", i % 2), "cw", "acc"], w=["acc"])
                        silu_ops("pool", acc[:, 2:2 + n], ee[:, 2:2 + n], o_[:, 2:2 + n], "acc", "cee", ("ob", i % 2))
                        def tok(tt):
                            return (2 + tt * 128) if tt < 2 else (262 + (tt - 2) * 128)
                        if cc < 6 or cc in (6, 7):
                            for t0 in range(0, NT, 8):
                                nt = min(8, NT - t0)
                                for a in range(nt):
                                    tt = t0 + a
                                    S.add("pe", lambda h, a=a, tt=tt: h.transpose(out=ptx[:, a, :], in_=o_[:, tok(tt):tok(tt) + 128], identity=ident_b[:]),
                                          r=[("ob", i % 2)], w=["ptx"])
                                if cc < 6:
                                    S.add("act", lambda h, t0=t0, nt=nt: h.activation(out=xs[:, t0:t0 + nt, cc * 128:(cc + 1) * 128], in_=ptx[:, 0:nt, :], func=AF.Copy),
                                          r=["ptx"], w=[("xs", cc, t0)])
                                else:
                                    g = cc - 6
                                    S.add("act", lambda h, t0=t0, nt=nt: h.activation(out=Btm[:, t0:t0 + nt, g * 128:(g + 1) * 128], in_=ptx[:, 0:nt, :], func=AF.Copy),
                                          r=["ptx"], w=[("Btm", g, t0)])
                        if cc >= 6:
                            dstT = BT if cc < 8 else CT
                            g = (cc - 6) % 2
                            S.add("pool", lambda h: h.tensor_copy(out=dstT[:, g, 0:256], in_=o_[:, 2:258]), r=[("ob", i % 2)], w=[("BCT", cc, 0)])
                            S.add("pool", lambda h: h.tensor_copy(out=dstT[:, g, 256:T], in_=o_[:, 262:2310]), r=[("ob", i % 2)], w=[("BCT", cc, 1)])
                    for cc in range(10):
                        conv_chunk(cc)
                    dtr = Tl("dtr", [128, NT, 24])
                    dtb = Tl("dtb", [128, 24])
                    alg = Tl("alg", [128, 24])
                    dtv = Tl("dtv", [128, 2, NT, NH])
                    tmpd = Tl("tmpd", [128, 2, NT, NH])
                    psc = Pl("psc", [128, 512])
                    pst = Pl("pst", [128, 512])
                    S.dma("sp", lambda h: h.dma_start(out=dtr[:], in_=proj[:, 4096:4120].rearrange("(t p) c -> p t c", p=128)), w=["dtr"])
                    S.dma("sp", lambda h: h.dma_start(out=dtb[:, 0:12], in_=I["dt_bias_f"][l:l + 1, :].to_broadcast([128, 12])), w=["dtb"])
                    S.dma("sp", lambda h: h.dma_start(out=dtb[:, 12:24], in_=I["dt_bias_b"][l:l + 1, :].to_broadcast([128, 12])), w=["dtb"])
                    S.dma("sp", lambda h: h.dma_start(out=alg[:, 0:12], in_=I["a_log_f"][l:l + 1, :].to_broadcast([128, 12])), w=["alg"])
                    S.dma("sp", lambda h: h.dma_start(out=alg[:, 12:24], in_=I["a_log_b"][l:l + 1, :].to_broadcast([128, 12])), w=["alg"])
                    S.add("act", lambda h: h.activation(out=alg[:], in_=alg[:], func=AF.Exp), r=["alg"], w=["alg"])
                    for d_ in range(2):
                        S.add("dve", lambda h, d_=d_: h.tensor_tensor(out=dtv[:, d_], in0=dtr[:, :, d_ * 12:(d_ + 1) * 12],
                                                                    in1=dtb[:, d_ * 12:(d_ + 1) * 12].unsqueeze(1).to_broadcast([128, NT, NH]), op=ALU.add),
                              r=["dtr", "dtb"], w=["dtv"])
                    F2 = lambda t: t[:].rearrange("p a b c -> p (a b c)")
                    S.add("act", lambda h: h.activation(out=F2(dtv), in_=F2(dtv), func=AF.Exp), r=["dtv"], w=["dtv"])
                    S.add("act", lambda h: h.activation(out=F2(dtv), in_=F2(dtv), func=AF.Ln, bias=1.0), r=["dtv"], w=["dtv"])
                    S.add("dve", lambda h: h.tensor_scalar(out=F2(dtv), in0=F2(dtv), scalar1=1e-30, scalar2=None, op0=ALU.max), r=["dtv"], w=["dtv"])
                    S.add("act", lambda h: h.activation(out=F2(lndt), in_=F2(dtv), func=AF.Ln), r=["dtv"], w=["lndt"])
                    for d_ in range(2):
                        S.add("dve", lambda h, d_=d_: h.tensor_tensor(out=la[:, d_], in0=dtv[:, d_],
                                                                    in1=alg[:, d_ * 12:(d_ + 1) * 12].unsqueeze(1).to_broadcast([128, NT, NH]), op=ALU.mult),
                              r=["dtv", "alg"], w=["la"])
                    S.add("dve", lambda h: h.tensor_scalar(out=F2(la), in0=F2(la), scalar1=-1.0, scalar2=None, op0=ALU.mult), r=["la"], w=["la"])
                    NQ = NT * NH
                    S.add("pe", lambda h: h.matmul(psc[:, 0:NQ], lhsT=tri_f, rhs=la[:, 0].rearrange("p a b -> p (a b)"), start=True, stop=True), r=["la"], w=["psc"])
                    S.add("pe", lambda h: h.matmul(psc[:, NQ:2 * NQ], lhsT=tri_b, rhs=la[:, 1].rearrange("p a b -> p (a b)"), start=True, stop=True), r=["la"], w=["psc"])
                    S.add("pe", lambda h: h.matmul(pst[:, 0:2 * NQ], lhsT=ones_f, rhs=F2(la), start=True, stop=True), r=["la"], w=["pst"])
                    S.add("dve", lambda h: h.tensor_copy(out=F2(acs), in_=psc[:, 0:2 * NQ]), r=["psc"], w=["acs"])
                    S.add("dve", lambda h: h.tensor_copy(out=F2(tot), in_=pst[:, 0:2 * NQ]), r=["pst"], w=["tot"])
                    S.add("act", lambda h: h.activation(out=F2(ein), in_=F2(acs), func=AF.Exp), r=["acs"], w=["ein"])
                    S.add("act", lambda h: h.activation(out=F2(cdc), in_=F2(tot), func=AF.Exp), r=["tot"], w=["cdc"])
                    S.add("dve", lambda h: h.tensor_tensor(out=F2(lb), in0=F2(lndt), in1=F2(acs), op=ALU.subtract), r=["lndt", "acs"], w=["lb"])
                    S.add("dve", lambda h: h.tensor_tensor(out=F2(tmpd), in0=F2(lb), in1=F2(tot), op=ALU.add), r=["lb", "tot"], w=["tmpd"])
                    S.add("act", lambda h: h.activation(out=F2(wend), in_=F2(tmpd), func=AF.Exp), r=["tmpd"], w=["wend"])
                    S.phase_end()
                with ExitStack() as st:
                    Tl, Pl = mk_alloc(st)
                    SAll = [Tl("sSAll%d" % d_, [128, NT, 768], BF16) for d_ in range(2)]
                    S32 = [Tl("sS32%d" % d_, [128, 768]) for d_ in range(2)]
                    xsw = [Tl("xsw%d" % i, [128, 768], BF16) for i in range(2)]
                    psA = Pl("spsA", [128, 2, 512])
                    orders = [list(range(NT)), [1, 0] + list(range(NT - 1, 1, -1))]

                    def bc12(t2d):
                        return t2d.unsqueeze(2).to_broadcast([128, NH, 64])

                    def v12(ap):
                        return ap.rearrange("p (h d) -> p h d", h=NH)

                    def state_step(d_, idx):
                        order = orders[d_]
                        c = order[idx]
                        if idx == 0:
                            S.add("pool", lambda h: h.memset(S32[d_][:], 0.0), w=[("S32", d_)])
                            S.add("pool", lambda h: h.memset(SAll[d_][:, c, :], 0.0), w=[("SAll", d_, c)])
                        if idx == NT - 1:
                            return
                        xw = xsw[idx % 2]
                        S.add("pool", lambda h: h.tensor_tensor(out=v12(xw[:]), in0=v12(xs[:, c, :]), in1=bc12(wend[:, d_, c, :]), op=ALU.mult),
                              w=[("xsw", idx % 2)])
                        for g in range(2):
                            S.add("pe", lambda h, g=g: h.matmul(psA[:, g, 0:384], lhsT=Btm[:, c, g * 128:(g + 1) * 128], rhs=xw[:, g * 384:(g + 1) * 384],
                                                               start=True, stop=True), r=[("xsw", idx % 2)], w=["psA"])
                        S.add("dve", lambda h: h.tensor_tensor(out=v12(S32[d_][:]), in0=v12(S32[d_][:]), in1=bc12(cdc[:, d_, c, :]), op=ALU.mult),
                              r=[("S32", d_)], w=[("S32", d_)])
                        S.add("dve", lambda h: h.tensor_tensor(out=S32[d_][:].rearrange("p (g x) -> p g x", g=2), in0=S32[d_][:].rearrange("p (g x) -> p g x", g=2),
                                                               in1=psA[:, :, 0:384], op=ALU.add), r=[("S32", d_), "psA"], w=[("S32", d_)])
                        cn = order[idx + 1]
                        S.add("act", lambda h: h.activation(out=SAll[d_][:, cn, :], in_=S32[d_][:], func=AF.Copy), r=[("S32", d_)], w=[("SAll", d_, cn)])
                    for d_ in range(2):
                        for idx in range(NT):
                            state_step(d_, idx)

                    Rt = [Tl("Rt%d" % d_, [128, NH, 128]) for d_ in range(2)]
                    mneg4 = [Tl("mneg4_%d" % d_, [128, 4, 128]) for d_ in range(2)]
                    Et = [Tl("Et%d" % i, [128, 128], BF16) for i in range(3)]
                    Wt = [Tl("Wt%d" % i, [128, 128], BF16) for i in range(16)]
                    psE = [Pl("psE%d" % i, [128, 4, 128]) for i in range(2)]
                    psG = Pl("psG", [128, 2, 128])
                    psY = Pl("spsY", [128, 1024])
                    psFB = psA
                    ptr = Pl("sptr", [128, 6, 128], BF16)
                    y1 = Tl("y1", [128, 768])
                    y2 = Tl("y2", [128, 768])
                    zt = [Tl("zt%d" % i, [128, 768]) for i in range(2)]
                    ze = Tl("ze", [128, 768])
                    zs = Tl("zs", [128, 768])
                    junk = Tl("sjunk", [128, 768], BF16)
                    yo = Tl("yo", [128, 768], BF16)
                    stg = [Tl("sstg%d" % i, [128, 6, 128], BF16) for i in range(2)]
                    st2 = Tl("st2", [128, 2])
                    for d_ in range(2):
                        S.add("pool", lambda h, d_=d_: h.tensor_copy(out=mneg4[d_][:], in_=mneg[d_].unsqueeze(1).to_broadcast([128, 4, 128])), w=["mneg4"])
                    tris = [tri_f, tri_b]
                    ecnt = [0]
                    wcnt = [0]

                    def chunk(c):
                        cs_ = slice(c * 128, (c + 1) * 128)
                        S.dma("sp", lambda h: h.dma_start(out=zt[c % 2][:], in_=proj[c * 128:(c + 1) * 128, 3328:4096]), w=[("zt", c % 2)])
                        for g in range(2):
                            S.add("pe", lambda h, g=g: h.matmul(psG[:, g, :], lhsT=BT[:, g, cs_], rhs=CT[:, g, cs_], start=True, stop=True), w=["psG"])
                        for d_ in range(2):
                            S.add("dve", lambda h, d_=d_: h.tensor_tensor(out=Rt[d_][:], in0=la[:, d_, c, :].unsqueeze(2).to_broadcast([128, NH, 128]),
                                                                        in1=tris[d_].unsqueeze(1).to_broadcast([128, NH, 128]), op=ALU.mult), w=[("Rt", d_)])
                        for q in range(3):
                            wids = {}
                            for d_ in range(2):
                                e = ecnt[0]
                                ecnt[0] += 1
                                pe_ = psE[e % 2]
                                S.add("pe", lambda h, d_=d_, q=q, pe_=pe_: h.matmul(pe_[:].rearrange("p a b -> p (a b)"), lhsT=ones_f,
                                                                                 rhs=Rt[d_][:, 4 * q:4 * q + 4, :].rearrange("p a b -> p (a b)"), start=True, stop=False),
                                      r=[("Rt", d_)], w=[("psE", e % 2)])
                                S.add("pe", lambda h, d_=d_, pe_=pe_: h.matmul(pe_[:].rearrange("p a b -> p (a b)"), lhsT=ident_f,
                                                                            rhs=mneg4[d_][:].rearrange("p a b -> p (a b)"), start=False, stop=True),
                                      r=["mneg4"], w=[("psE", e % 2)])
                                for a in range(4):
                                    hh = 4 * q + a
                                    w_ = wcnt[0]
                                    wcnt[0] += 1
                                    wids[(a, d_)] = w_
                                    S.add("act", lambda h, a=a, hh=hh, d_=d_, pe_=pe_, w_=w_: h.activation(out=Et[w_ % 3][:], in_=pe_[:, a, :], func=AF.Exp,
                                                                                                    bias=lb[:, d_, c, hh:hh + 1]),
                                          r=[("psE", e % 2)], w=[("Et", w_ % 3)])
                                    S.add("dve", lambda h, hh=hh, w_=w_: h.tensor_tensor(out=Wt[w_ % 16][:], in0=Et[w_ % 3][:], in1=psG[:, hh // 6, :], op=ALU.mult),
                                          r=[("Et", w_ % 3), "psG"], w=[("Wt", w_ % 16)])
                            for a in range(4):
                                hh = 4 * q + a
                                for d_ in range(2):
                                    w_ = wids[(a, d_)]
                                    S.add("pe", lambda h, hh=hh, d_=d_, w_=w_: h.matmul(psY[:, hh * 64:(hh + 1) * 64], lhsT=Wt[w_ % 16][:], rhs=xs[:, c, hh * 64:(hh + 1) * 64],
                                                                                   start=(d_ == 0), stop=(d_ == 1)),
                                          r=[("Wt", w_ % 16)], w=["psY"])
                        for d_ in range(2):
                            for g in range(2):
                                S.add("pe", lambda h, d_=d_, g=g: h.matmul(psFB[:, g, 0:384], lhsT=CT[:, g, cs_], rhs=SAll[d_][:, c, g * 384:(g + 1) * 384],
                                                                         start=True, stop=True), r=[("SAll", d_, c)], w=["psA"])
                            yd = y1 if d_ == 0 else y2
                            for g in range(2):
                                S.add("dve", lambda h, d_=d_, g=g, yd=yd: h.tensor_tensor(
                                    out=yd[:, g * 384:(g + 1) * 384].rearrange("p (h d) -> p h d", h=6),
                                    in0=psFB[:, g, 0:384].rearrange("p (h d) -> p h d", h=6),
                                    in1=ein[:, d_, c, g * 6:(g + 1) * 6].unsqueeze(2).to_broadcast([128, 6, 64]), op=ALU.mult),
                                    r=["psA"], w=[("y", d_)])
                        S.add("pool", lambda h: h.tensor_tensor(out=y1[:], in0=y1[:], in1=y2[:], op=ALU.add), r=[("y", 0), ("y", 1)], w=[("y", 0)])
                        S.add("dve", lambda h: h.tensor_tensor(out=y1[:], in0=y1[:], in1=psY[:, 0:768], op=ALU.add), r=[("y", 0), "psY"], w=[("y", 0)])
                        S.add("pool", lambda h: h.tensor_tensor(out=v12(y2[:]), in0=v12(xs[:, c, :]), in1=bc12(dsk[:]), op=ALU.mult), r=[("y", 0)], w=[("y", 1)])
                        S.add("pool", lambda h: h.tensor_tensor(out=y1[:], in0=y1[:], in1=y2[:], op=ALU.add), r=[("y", 0), ("y", 1)], w=[("y", 0)])
                        silu_ops("pool", zt[c % 2][:], ze[:], zs[:], ("zt", c % 2), "ze", "zs")
                        S.add("pool", lambda h: h.tensor_tensor(out=y1[:], in0=y1[:], in1=zs[:], op=ALU.mult), r=[("y", 0), "zs"], w=[("y", 0)])
                        S.add("act", lambda h: h.activation(out=junk[:], in_=y1[:], func=AF.Square, accum_out=st2[:, 0:1]), r=[("y", 0)], w=["sjunk", "ssq"])
                        rstd_ops(st2[:, 0:1], st2[:, 1:2], 1.0 / 768, ["ssq"], ["rstd"])
                        S.add("dve", lambda h: h.scalar_tensor_tensor(out=yo[:], in0=y1[:], scalar=st2[:, 1:2], in1=gssd[:], op0=ALU.mult, op1=ALU.mult),
                              r=[("y", 0), "rstd"], w=["yo"])
                        for a in range(6):
                            S.add("pe", lambda h, a=a: h.transpose(out=ptr[:, a, :], in_=yo[:, a * 128:(a + 1) * 128], identity=ident_b[:]), r=["yo"], w=["ptr"])
                        sg = stg[c % 2]
                        S.add("act", lambda h: h.activation(out=sg[:], in_=ptr[:], func=AF.Copy), r=["ptr"], w=[("stg", c % 2)])
                        S.dma("sp", lambda h: h.dma_start(out=mixT[1280:2048, cs_].rearrange("(a d) t -> d a t", a=6), in_=sg[:]),
                              r=[("stg", c % 2)], w=[("mixTs", c)])
                    for c in range(2 if last else 0, NT):
                        chunk(c)
                    S.phase_end()

        def phase_out(l):
            last = (l == nlayers - 1)
            with ExitStack() as st:
                Tl, Pl = mk_alloc(st)
                wo = Tl("wo", [128, 16, D], BF16)
                wv = I["w_out"][l].rearrange("(k p) n -> p k n", p=128)
                for k4 in range(4):
                    for nb in range(2):
                        S.dma("pool", lambda h, k4=k4, nb=nb: h.dma_start(out=wo[:, 4 * k4:4 * k4 + 4, nb * 1024:(nb + 1) * 1024],
                                                                        in_=wv[:, 4 * k4:4 * k4 + 4, nb * 1024:(nb + 1) * 1024]), w=[("wo", k4, nb)])
                G1 = Tl("G1", [128, D])
                gm = Tl("ogm", [128, D])
                sh = Tl("osh", [128, D])
                gp = Tl("ogp", [128, D])
                mT = [Tl("mT%d" % i, [128, 16, 128], BF16) for i in range(2)]
                xt = [Tl("oxt%d" % i, [128, D]) for i in range(2)]
                x1 = [Tl("ox1%d" % i, [128, D]) for i in range(2)]
                tmp = Tl("otmp", [128, D])
                hb = [Tl("ohb%d" % i, [128, D], BF16) for i in range(2)]
                junk = Tl("ojunk", [128, D], BF16)
                ssq = Tl("ossq", [128, 2])
                sq4 = Tl("osq4", [128, 8])
                stg = [Tl("ostg%d" % i, [128, 16, 128], BF16) for i in range(2)]
                psM = [Pl("psM%d" % i, [128, 512]) for i in range(4)]
                pt = [Pl("opt%d" % i, [128, 16, 128], BF16) for i in range(2)]

                def load_mods(r):
                    load_gate_mod(l, G1, gp, "o", "post_mix_g", 4096, r)
                    load_norm_mod(l, gm, sh, gp, "o", "pre_ffn_g", 4 * 2048, 3 * 2048, r)

                def tile(tt, i):
                    S.dma("sp", lambda h: h.dma_start(out=mT[i % 2][:], in_=mixT[:, tt * 128:(tt + 1) * 128].rearrange("(k p) t -> p k t", p=128)),
                          w=[("mT", i % 2)])
                    S.dma("sp", lambda h: h.dma_start(out=xt[i % 2][:], in_=xsrc(l, tt)), w=[("xt", i % 2)])
                    for cb in range(4):
                        for k in range(16):
                            S.add("pe", lambda h, cb=cb, k=k: h.matmul(psM[cb][:], lhsT=mT[i % 2][:, k, :], rhs=wo[:, k, cb * 512:(cb + 1) * 512],
                                                                      start=(k == 0), stop=(k == 15)),
                                  r=[("mT", i % 2), ("wo", k // 4, cb // 2)], w=[("psM", cb)])
                    for cb in range(4):
                        S.add("act", lambda h, cb=cb: h.activation(out=junk[:, cb * 512:(cb + 1) * 512], in_=psM[cb][:], func=AF.Square,
                                                                   accum_out=sq4[:, cb:cb + 1]), r=[("psM", cb)], w=["ojunk", ("sq4", cb)])
                    S.add("dve", lambda h: h.tensor_reduce(out=sq4[:, 4:5], in_=sq4[:, 0:4], axis=AX.X, op=ALU.add), r=[("sq4", cb) for cb in range(4)], w=["mss"])
                    rstd_ops(sq4[:, 4:5], sq4[:, 5:6], 1.0 / D, ["mss"], ["mrstd"])
                    x1_ = x1[i % 2]
                    for cb in range(4):
                        S.add("dve", lambda h, cb=cb: h.scalar_tensor_tensor(out=x1_[:, cb * 512:(cb + 1) * 512], in0=psM[cb][:], scalar=sq4[:, 5:6],
                                                                            in1=G1[:, cb * 512:(cb + 1) * 512], op0=ALU.mult, op1=ALU.mult),
                              r=[("psM", cb), "mrstd", "oG"], w=[("x1", i % 2)])
                    S.add("pool", lambda h: h.tensor_tensor(out=x1_[:], in0=x1_[:], in1=xt[i % 2][:], op=ALU.add),
                          r=[("x1", i % 2), ("xt", i % 2)], w=[("x1", i % 2)])
                    S.dma("sp", lambda h: h.dma_start(out=x1s[tt * 128:(tt + 1) * 128, :], in_=x1_[:]), r=[("x1", i % 2)], w=[("x1s", tt)])

                    def dst(ptile, pkey, wkeys):
                        sg = stg[i % 2]
                        S.add("act", lambda h: h.activation(out=sg[:], in_=ptile[:], func=AF.Copy), r=[pkey], w=[("ostg", i % 2)])
                        S.dma("sp", lambda h: h.dma_start(out=h2T[:, tt * 128:(tt + 1) * 128].rearrange("(k p) t -> p k t", p=128), in_=sg[:]),
                              r=[("ostg", i % 2)], w=wkeys)
                    norm_transpose_tile(l, tt, x1_[:], ("x1", i % 2), gm, sh, "o", tmp, hb[i % 2], junk, ssq, pt[i % 2], dst, [("h2T", tt)], i)
                i = 0
                if not last:
                    load_mods(1)
                    for tt in range(2):
                        tile(tt, i)
                        i += 1
                load_mods(0)
                for tt in range(2, NT):
                    tile(tt, i)
                    i += 1
                S.phase_end()

        def phase_ffn(l):
            last = (l == nlayers - 1)
            if last:
                halves = [(256, 1024), (1280, 1024)]
            else:
                halves = [(0, 1152), (1152, 1152)]
            wg_v = I["w_gate"][l].rearrange("(k p) n -> p k n", p=128)
            wu_v = I["w_up"][l].rearrange("(k p) n -> p k n", p=128)
            wd_v = I["w_down"][l].rearrange("(k p) n -> p k n", p=128)
            NJ = DFF // 128
            for (t0, nt) in halves:
                with ExitStack() as st_o:
                    To, Po = mk_alloc(st_o)
                    aT = To("aT", [128, NJ, nt], BF16)
                    with ExitStack() as st:
                        Tl, Pl = mk_alloc(st)
                        hh_ = Tl("h2h", [128, 16, nt], BF16)
                        for k4 in range(4):
                            S.dma("sp", lambda h, k4=k4: h.dma_start(out=hh_[:, 4 * k4:4 * k4 + 4, :],
                                                                     in_=h2T[:, t0:t0 + nt].rearrange("(k p) t -> p k t", p=128)[:, 4 * k4:4 * k4 + 4, :]),
                                  w=[("h2h", k4)])
                        wg = [Tl("wg%d" % i, [128, 16, 256], BF16) for i in range(2)]
                        wu = [Tl("wu%d" % i, [128, 16, 256], BF16) for i in range(2)]
                        psg = [Pl("psg%d" % i, [128, 512]) for i in range(3)]
                        psu = [Pl("psu%d" % i, [128, 512]) for i in range(3)]
                        ee = [Tl("fe%d" % i, [128, 512]) for i in range(2)]
                        tg = [Tl("ftg%d" % i, [128, 512]) for i in range(2)]
                        tbs = [(a, min(512, nt - a)) for a in range(0, nt, 512)]
                        cnt = [0]

                        def wblock(jb):
                            b = jb % 2
                            for k4 in range(4):
                                S.dma("pool", lambda h, k4=k4: h.dma_start(out=wg[b][:, 4 * k4:4 * k4 + 4, :], in_=wg_v[:, 4 * k4:4 * k4 + 4, jb * 256:(jb + 1) * 256]),
                                      w=[("wg", b, k4)])
                                S.dma("pool", lambda h, k4=k4: h.dma_start(out=wu[b][:, 4 * k4:4 * k4 + 4, :], in_=wu_v[:, 4 * k4:4 * k4 + 4, jb * 256:(jb + 1) * 256]),
                                      w=[("wu", b, k4)])
                            for jj in range(2):
                                j = jb * 2 + jj
                                for (a, n) in tbs:
                                    i = cnt[0]
                                    cnt[0] += 1
                                    pg = psg[i % 3]
                                    pu = psu[i % 3]
                                    for k in range(16):
                                        S.add("pe", lambda h, k=k, pg=pg, a=a, n=n, jj=jj: h.matmul(pg[:, 0:n], lhsT=wg[b][:, k, jj * 128:(jj + 1) * 128],
                                                                                             rhs=hh_[:, k, a:a + n], start=(k == 0), stop=(k == 15)),
                                              r=[("wg", b, k // 4), ("h2h", k // 4)], w=[("psg", i % 3)])
                                    for k in range(16):
                                        S.add("pe", lambda h, k=k, pu=pu, a=a, n=n, jj=jj: h.matmul(pu[:, 0:n], lhsT=wu[b][:, k, jj * 128:(jj + 1) * 128],
                                                                                             rhs=hh_[:, k, a:a + n], start=(k == 0), stop=(k == 15)),
                                              r=[("wu", b, k // 4), ("h2h", k // 4)], w=[("psu", i % 3)])
                                    e_ = ee[i % 2]
                                    t_ = tg[i % 2]
                                    S.add("act", lambda h, pg=pg, e_=e_, n=n: h.activation(out=e_[:, 0:n], in_=pg[:, 0:n], func=AF.Exp, scale=-1.0),
                                          r=[("psg", i % 3)], w=[("fe", i % 2)])
                                    S.add("pool", lambda h, e_=e_, n=n: h.tensor_scalar(out=e_[:, 0:n], in0=e_[:, 0:n], scalar1=1.0, scalar2=None, op0=ALU.add),
                                          r=[("fe", i % 2)], w=[("fe", i % 2)])
                                    S.add("dve", lambda h, e_=e_, n=n: h.reciprocal(out=e_[:, 0:n], in_=e_[:, 0:n]), r=[("fe", i % 2)], w=[("fe", i % 2)])
                                    S.add("dve", lambda h, e_=e_, t_=t_, pg=pg, n=n: h.tensor_tensor(out=t_[:, 0:n], in0=pg[:, 0:n], in1=e_[:, 0:n], op=ALU.mult),
                                          r=[("fe", i % 2), ("psg", i % 3)], w=[("ftg", i % 2)])
                                    S.add("dve", lambda h, t_=t_, pu=pu, n=n, a=a, j=j: h.tensor_tensor(out=aT[:, j, a:a + n], in0=t_[:, 0:n], in1=pu[:, 0:n], op=ALU.mult),
                                          r=[("ftg", i % 2), ("psu", i % 3)], w=[("aT", j, a)])
                        for jb in range(NJ // 2):
                            wblock(jb)
                        S.phase_end()
                    with ExitStack() as st:
                        Tl, Pl = mk_alloc(st)
                        wd = [Tl("wd%d" % i, [128, NJ, 256], BF16) for i in range(2)]
                        psd = [Pl("psd%d" % i, [128, 256]) for i in range(4)]
                        stg = [Tl("fstg%d" % i, [128, 256]) for i in range(4)]
                        cnt = [0]

                        def dblock(cb):
                            b = cb % 2
                            for k4 in range(4):
                                S.dma("pool", lambda h, k4=k4: h.dma_start(out=wd[b][:, 11 * k4:11 * k4 + 11, :], in_=wd_v[:, 11 * k4:11 * k4 + 11, cb * 256:(cb + 1) * 256]),
                                      w=[("wd", b, k4)])
                            for a in range(0, nt, 128):
                                i = cnt[0]
                                cnt[0] += 1
                                p_ = psd[i % 4]
                                for k in range(NJ):
                                    S.add("pe", lambda h, k=k, p_=p_, a=a: h.matmul(p_[:], lhsT=aT[:, k, a:a + 128], rhs=wd[b][:, k, :], start=(k == 0), stop=(k == NJ - 1)),
                                          r=[("wd", b, k // 11)], w=[("psd", i % 4)])
                                s_ = stg[i % 4]
                                if i % 2 == 0:
                                    S.add("act", lambda h, p_=p_, s_=s_: h.activation(out=s_[:], in_=p_[:], func=AF.Copy), r=[("psd", i % 4)], w=[("fstg", i % 4)])
                                else:
                                    S.add("dve", lambda h, p_=p_, s_=s_: h.tensor_copy(out=s_[:], in_=p_[:]), r=[("psd", i % 4)], w=[("fstg", i % 4)])
                                S.dma("sp", lambda h, s_=s_, a=a: h.dma_start(out=fsc[t0 + a:t0 + a + 128, cb * 256:(cb + 1) * 256], in_=s_[:]),
                                      r=[("fstg", i % 4)], w=[("fsc", i)])
                        for cb in range(8):
                            dblock(cb)
                        S.phase_end()

        def phase_fin(l):
            last = (l == nlayers - 1)
            with ExitStack() as st:
                Tl, Pl = mk_alloc(st)
                G2 = Tl("G2", [128, D])
                gp = Tl("fgp", [128, D])
                xa = [Tl("fxa%d" % i, [128, D]) for i in range(2)]
                fa = [Tl("ffa%d" % i, [128, D]) for i in range(2)]
                xo = [Tl("fxo%d" % i, [128, D]) for i in range(2)]
                junk = Tl("fjunk", [128, D], BF16)
                ssq = Tl("fssq", [128, 2])

                def tile(tt, i):
                    S.dma("sp", lambda h: h.dma_start(out=xa[i % 2][:], in_=x1s[tt * 128:(tt + 1) * 128, :]), w=[("xa", i % 2)])
                    S.dma("sp", lambda h: h.dma_start(out=fa[i % 2][:], in_=fsc[tt * 128:(tt + 1) * 128, :]), w=[("fa", i % 2)])
                    S.add("act", lambda h: h.activation(out=junk[:], in_=fa[i % 2][:], func=AF.Square, accum_out=ssq[:, 0:1]), r=[("fa", i % 2)], w=["fjunk", "fss"])
                    rstd_ops(ssq[:, 0:1], ssq[:, 1:2], 1.0 / D, ["fss"], ["frstd"])
                    S.add("dve", lambda h: h.scalar_tensor_tensor(out=xo[i % 2][:], in0=fa[i % 2][:], scalar=ssq[:, 1:2], in1=G2[:], op0=ALU.mult, op1=ALU.mult),
                          r=[("fa", i % 2), "frstd", "fG"], w=[("xo", i % 2)])
                    S.add("pool", lambda h: h.tensor_tensor(out=xo[i % 2][:], in0=xo[i % 2][:], in1=xa[i % 2][:], op=ALU.add), r=[("xo", i % 2), ("xa", i % 2)], w=[("xo", i % 2)])
                    if last:
                        dst = out[(tt - 2) * 128:(tt - 1) * 128, :]
                    else:
                        dst = xnext[tt * 128:(tt + 1) * 128, :]
                    S.dma("sp", lambda h: h.dma_start(out=dst, in_=xo[i % 2][:]), r=[("xo", i % 2)], w=[("xout", tt)])
                i = 0
                if not last:
                    load_gate_mod(l, G2, gp, "f", "post_ffn_g", 5 * 2048, 1)
                    for tt in range(2):
                        tile(tt, i)
                        i += 1
                load_gate_mod(l, G2, gp, "f", "post_ffn_g", 5 * 2048, 0)
                for tt in range(2, NT):
                    tile(tt, i)
                    i += 1
                S.phase_end()

        PH = {}
        PH["out"] = phase_out
        PH["ffn"] = phase_ffn
        PH["fin"] = phase_fin
        PH["ssd"] = phase_ssd
        PH["ret"] = phase_ret
        PH["att"] = phase_att

        PH["in"] = phase_in
        try:
            if phases is None or "mod" in phases:
                phase_mod()
            check_stop("mod")
            for l in (layers if layers is not None else range(nlayers)):
                for nm in ("in", "att", "ret", "ssd", "out", "ffn", "fin"):
                    if nm in PH and (phases is None or nm in phases):
                        PH[nm](l)
                        check_stop("%s%d" % (nm, l))
        except Stop:
            pass
        S.phase_end()
        stats = S.emit()
    return nc, stats, list(I.keys())


def make_in_maps(inputs):
    consts = make_consts()
    rope = make_rope()
    maps = []
    shared = {n: np.ascontiguousarray(inputs[n], dtype=np.float32) for n, _ in SMALL + BIG}
    for b in range(8):
        m = dict(shared)
        m["x"] = np.ascontiguousarray(inputs["x"][b])
        m["ctx"] = np.ascontiguousarray(inputs["ctx"][b])
        m["cvec"] = np.ascontiguousarray(np.stack([inputs["c"][b], inputs["c_ctx"]]))
        m["consts"] = consts
        m["rope"] = rope
        maps.append(m)
    return maps


def kernel(**inputs):
    nc, _, _ = build(nlayers=2, debug=False)
    maps = make_in_maps(inputs)
    res = run_bass_kernel_spmd(nc, maps, core_ids=list(range(8)))
    return np.stack([np.asarray(r["out"], dtype=np.float32) for r in res.results], axis=0)
```

```python
import math
import numpy as np
from contextlib import ExitStack
import concourse.bass as bass
import concourse.mybir as mybir
from concourse.bass_utils import run_bass_kernel_spmd

F32 = mybir.dt.float32
BF16 = mybir.dt.bfloat16
AF = mybir.ActivationFunctionType
ALU = mybir.AluOpType
AX = mybir.AxisListType

D = 2048
T = 2304
NT = 18
DIN = 5400
DFF = 5632
EPS = 1e-6
NCONST = 10 * 128 + 2
PW = 4120


class _Op:
    __slots__ = ("eng", "fn", "deps", "ch", "pos", "is_dma", "vc", "waits", "signal", "rank")


class Sched:
    ENGS = ("pe", "act", "dve", "pool", "sp")

    def __init__(self, nc, stack, n_dma_sems=12):
        self.nc = nc
        self.h = {"pe": nc.tensor, "act": nc.scalar, "dve": nc.vector, "pool": nc.gpsimd, "sp": nc.sync}
        self.ops = []
        self.n_emitted = 0
        self.eng_pos = {e: 0 for e in self.ENGS}
        self.last_w = {}
        self.readers = {}
        self.esem = {e: stack.enter_context(nc.semaphore("s_" + e)) for e in self.ENGS}
        self.dsems = {}
        self.dcount = {}
        self.drr = {}
        for q in ("sp", "pool"):
            self.dsems[q] = [stack.enter_context(nc.semaphore("d_%s%d" % (q, i))) for i in range(n_dma_sems)]
            self.drr[q] = 0
            for i in range(n_dma_sems):
                self.dcount[(q, i)] = 0
        self.by_chpos = {}
        self.last_on_ch = {}
        self.clock = {e: {} for e in self.ENGS}
        self.rk = {e: 0 for e in self.ENGS}
        self.nw = 0

    def _deps(self, r, w):
        deps = []
        for k in r:
            o = self.last_w.get(k)
            if o is not None:
                deps.append(o)
        for k in w:
            o = self.last_w.get(k)
            if o is not None:
                deps.append(o)
            deps.extend(self.readers.get(k, ()))
        return deps

    def _commit(self, op, r, w):
        for k in r:
            self.readers.setdefault(k, []).append(op)
        for k in w:
            self.last_w[k] = op
            self.readers[k] = []
        self.ops.append(op)
        self.by_chpos[(op.ch, op.pos)] = op
        self.last_on_ch[op.ch] = op

    def add(self, eng, fn, r=(), w=()):
        op = _Op()
        op.eng = eng
        op.fn = fn
        op.is_dma = False
        op.deps = self._deps(r, w)
        self.eng_pos[eng] += 1
        op.ch = eng
        op.pos = self.eng_pos[eng]
        op.signal = False
        self._commit(op, r, w)
        return op

    def dma(self, q, fn, r=(), w=()):
        op = _Op()
        op.eng = q
        op.fn = fn
        op.is_dma = True
        op.deps = self._deps(r, w)
        i = self.drr[q]
        self.drr[q] = (i + 1) % len(self.dsems[q])
        self.dcount[(q, i)] += 1
        op.ch = ("d", q, i)
        op.pos = self.dcount[(q, i)]
        op.signal = True
        self._commit(op, r, w)
        return op

    def barrier(self):
        lasts = [o for o in self.last_on_ch.values() if o.fn is not None or o.is_dma]
        for e in self.ENGS:
            op = _Op()
            op.eng = e
            op.fn = None
            op.is_dma = False
            op.deps = list(lasts)
            self.eng_pos[e] += 1
            op.ch = e
            op.pos = self.eng_pos[e]
            op.signal = False
            self.ops.append(op)
            self.by_chpos[(op.ch, op.pos)] = op
        self.last_w = {}
        self.readers = {}

    def emit(self):
        ops = self.ops[self.n_emitted:]
        clock = self.clock
        for op in ops:
            E = op.eng
            ck = clock[E]
            need = {}
            for d in op.deps:
                if (not d.is_dma) and d.eng == "pe" and E == "pe" and (not op.is_dma) and op.fn is not None:
                    continue
                if ck.get(d.ch, 0) < d.pos and need.get(d.ch, 0) < d.pos:
                    need[d.ch] = d.pos
            if op.is_dma and op.pos > 1:
                if ck.get(op.ch, 0) < op.pos - 1 and need.get(op.ch, 0) < op.pos - 1:
                    need[op.ch] = op.pos - 1
            op.waits = []
            for ch, pos in need.items():
                if ck.get(ch, 0) >= pos:
                    continue
                p = self.by_chpos[(ch, pos)]
                p.signal = True
                op.waits.append(p)
                for c, v in p.vc.items():
                    if ck.get(c, 0) < v:
                        ck[c] = v
            vc = dict(ck)
            vc[op.ch] = op.pos
            op.vc = vc
        for op in ops:
            if op.is_dma:
                op.rank = 16 * op.pos
            elif op.signal:
                self.rk[op.eng] += 1
                op.rank = self.rk[op.eng]
        for op in ops:
            h = self.h[op.eng]
            for p in op.waits:
                if p.is_dma:
                    sem = self.dsems[p.ch[1]][p.ch[2]]
                else:
                    sem = self.esem[p.eng]
                h.wait_ge(sem, p.rank)
                self.nw += 1
            if op.fn is None:
                continue
            inst = op.fn(h)
            if op.is_dma:
                inst.then_inc(self.dsems[op.ch[1]][op.ch[2]], 16)
            elif op.signal:
                inst.then_inc(self.esem[op.eng], 1)
            op.fn = None
        self.n_emitted = len(self.ops)
        return dict(n_ops=len(self.ops), n_waits=self.nw, ranks=dict(self.rk))

    def phase_end(self):
        self.barrier()
        return self.emit()


def make_consts():
    i = np.arange(128)
    J, I = np.meshgrid(i, i, indexing="ij")
    c = np.zeros((128, NCONST), np.float32)
    c[:, 0:128] = (J == I)
    c[:, 128:256] = (J <= I)
    c[:, 256:384] = (J >= I)
    c[:, 384:512] = 1.0
    c[:, 512:640] = np.where(I >= J, 0.0, -30000.0)
    c[:, 640:768] = np.where(I <= J, 0.0, -30000.0)
    c[:, 768:896] = np.maximum(I - J, 0)
    c[:, 896:1024] = np.maximum(J - I, 0)
    c[:, 1024:1152] = I + 1
    c[:, 1152:1280] = 128 - I
    c[:, 1280] = 127 - i
    c[:, 1281] = i
    return c


def make_rope():
    rows = 2048 // 64
    row = np.repeat(np.arange(rows, dtype=np.float32), 64)
    col = np.tile(np.arange(64, dtype=np.float32), rows)
    n_freq = 32
    inv = (np.float32(10000.0) ** (-np.arange(n_freq, dtype=np.float32) / n_freq)).astype(np.float32)
    ang = np.concatenate([row[:, None] * inv, col[:, None] * inv], axis=-1).astype(np.float32)
    return np.concatenate([np.cos(ang), np.sin(ang)], axis=-1).astype(np.float32)


SMALL = [("b_mod", [2, 12288]), ("pre_mix_g", [2, 2048]), ("post_mix_g", [2, 2048]), ("pre_ffn_g", [2, 2048]),
         ("post_ffn_g", [2, 2048]), ("q_norm_g", [2, 128]), ("k_norm_g", [2, 128]), ("ret_decay_f", [2, 4]),
         ("ret_decay_b", [2, 4]), ("conv_w", [2, 5, 1280]), ("conv_b", [2, 1280]), ("dt_bias_f", [2, 12]),
         ("dt_bias_b", [2, 12]), ("a_log_f", [2, 12]), ("a_log_b", [2, 12]), ("d_skip", [2, 12]),
         ("ssd_norm_g", [2, 768])]
BIG = [("w_mod", [2, 2048, 12288]), ("w_in", [2, 2048, DIN]), ("w_out", [2, 2048, 2048]),
       ("w_gate", [2, 2048, DFF]), ("w_up", [2, 2048, DFF]), ("w_down", [2, DFF, 2048])]


def build(nlayers=2, debug=False, stop=None, phases=None, feed=(), layers=None):
    nc = bass.Bass("TRN2", target_bir_lowering=False)
    SHAPES = dict(SMALL + BIG)
    SHAPES.update({"x": [2048, 2048], "ctx": [256, 2048], "cvec": [2, 2048], "consts": [128, NCONST], "rope": [2048, 128]})

    class LazyIn(dict):
        def __missing__(self, name):
            ap = nc.dram_tensor(name, SHAPES[name], F32, kind="ExternalInput").ap()
            self[name] = ap
            return ap

    I = LazyIn()
    if not debug:
        for n in ["x", "ctx", "cvec", "consts", "rope"] + [n for n, _ in SMALL + BIG]:
            I[n]
    out = nc.dram_tensor("out", [2048, 2048], F32, kind="ExternalOutput").ap()
    skind = "ExternalOutput" if debug else "Internal"

    def scr(name, shape, dt=F32):
        k = "ExternalInput" if name in feed else skind
        return nc.dram_tensor(name, shape, dt, kind=k).ap()

    modv = scr("modv", [2, 2, 12288])
    proj = scr("proj", [T, PW])
    xbcT = scr("xbcT", [1280, T])
    mixT = scr("mixT", [2048, T], BF16)
    x1s = scr("x1s", [T, D])
    h2T = scr("h2T", [2048, T], BF16)
    fsc = scr("fsc", [T, D])
    xnext = scr("xnext", [T, D])
    zsd = scr("zsd", [T, 768], BF16)

    class Stop(Exception):
        pass

    with ExitStack() as gst:
        S = Sched(nc, gst)

        uid = [0]

        def mk_alloc(st):
            def Tl(name, shape, dt=F32):
                uid[0] += 1
                return st.enter_context(nc.sbuf_tensor("%s_%d" % (name, uid[0]), shape, dt))

            def Pl(name, shape, dt=F32):
                uid[0] += 1
                return st.enter_context(nc.psum_tensor("%s_%d" % (name, uid[0]), shape, dt))
            return Tl, Pl

        GT, GP = mk_alloc(gst)
        cst = GT("cst", [128, NCONST])
        ident_b = GT("ident_b", [128, 128], BF16)
        ones_b = GT("ones_b", [128, 128], BF16)
        S.dma("sp", lambda h: h.dma_start(out=cst[:], in_=I["consts"][:, :]), w=["cst"])
        S.add("dve", lambda h: h.tensor_copy(out=ident_b[:], in_=cst[:, 0:128]), r=["cst"], w=["ident_b"])
        S.add("dve", lambda h: h.tensor_copy(out=ones_b[:], in_=cst[:, 384:512]), r=["cst"], w=["ones_b"])
        ident_f = cst[:, 0:128]
        tri_f = cst[:, 128:256]
        tri_b = cst[:, 256:384]
        ones_f = cst[:, 384:512]
        mneg = [cst[:, 512:640], cst[:, 640:768]]
        D1 = cst[:, 768:896]
        D2 = cst[:, 896:1024]
        rowidx = [cst[:, 1024:1152], cst[:, 1152:1280]]
        colexp = [cst[:, 1280:1281], cst[:, 1281:1282]]
        S.phase_end()

        def check_stop(tag):
            if stop == tag:
                raise Stop()

        def rstd_ops(ssq_ap, out_ap, inv_n, rk, wk):
            S.add("act", lambda h: h.activation(out=out_ap, in_=ssq_ap, func=AF.Ln, scale=inv_n, bias=EPS), r=rk, w=wk)
            S.add("act", lambda h: h.activation(out=out_ap, in_=out_ap, func=AF.Exp, scale=-0.5), r=wk, w=wk)

        def xsrc(l, tt):
            if l == 0:
                if tt < 2:
                    return I["ctx"][tt * 128:(tt + 1) * 128, :]
                return I["x"][(tt - 2) * 128:(tt - 1) * 128, :]
            return xnext[tt * 128:(tt + 1) * 128, :]

        def bcast_row(ap_row, n):
            return ap_row.to_broadcast([128, n])

        def phase_mod():
            with ExitStack() as st:
                Tl, Pl = mk_alloc(st)
                cT = Tl("cT", [128, 16, 2])
                ce = Tl("ce", [128, 16, 2])
                sT = Tl("sT", [128, 16, 2], BF16)
                for r_ in range(2):
                    S.dma("sp", lambda h, r_=r_: h.dma_start(out=cT[:, :, r_], in_=I["cvec"][r_].rearrange("(k p) -> p k", p=128),
                                                             allow_slow_non_contiguous=True), w=["cT"])
                S.add("act", lambda h: h.activation(out=ce[:], in_=cT[:], func=AF.Exp, scale=-1.0), r=["cT"], w=["ce"])
                S.add("dve", lambda h: h.tensor_scalar(out=ce[:], in0=ce[:], scalar1=1.0, scalar2=None, op0=ALU.add), r=["ce"], w=["ce"])
                S.add("dve", lambda h: h.reciprocal(out=ce[:], in_=ce[:]), r=["ce"], w=["ce"])
                S.add("dve", lambda h: h.tensor_tensor(out=sT[:], in0=cT[:], in1=ce[:], op=ALU.mult), r=["ce", "cT"], w=["sT"])
                wb = [Tl("wmb%d" % i, [128, 16, 512], BF16) for i in range(2)]
                pm = [Pl("pm%d" % i, [128, 512]) for i in range(2)]
                bsb = Tl("bsb", [2, 12288])
                msb = Tl("msb", [2, 12288])
                for l in range(nlayers):
                    S.dma("sp", lambda h, l=l: h.dma_start(out=bsb[:], in_=I["b_mod"][l:l + 1, :].to_broadcast([2, 12288])), w=["bsb"])
                    wv = I["w_mod"][l].rearrange("(k p) n -> p k n", p=128)
                    for nb in range(24):
                        b = nb % 2
                        for k4 in range(4):
                            S.dma("pool", lambda h, b=b, k4=k4, nb=nb, wv=wv: h.dma_start(
                                out=wb[b][:, 4 * k4:4 * k4 + 4, :], in_=wv[:, 4 * k4:4 * k4 + 4, nb * 512:(nb + 1) * 512]),
                                w=[("wmb", b, k4)])
                        for k in range(16):
                            S.add("pe", lambda h, b=b, k=k: h.matmul(pm[b][0:2, :], lhsT=sT[:, k, :], rhs=wb[b][:, k, :],
                                                                     start=(k == 0), stop=(k == 15)),
                                  r=["sT", ("wmb", b, k // 4)], w=[("pm", b)])
                        S.add("dve", lambda h, b=b, nb=nb: h.tensor_tensor(out=msb[:, nb * 512:(nb + 1) * 512], in0=pm[b][0:2, :],
                                                                         in1=bsb[:, nb * 512:(nb + 1) * 512], op=ALU.add),
                              r=[("pm", b), "bsb"], w=["msb"])
                    S.dma("sp", lambda h, l=l: h.dma_start(out=modv[l], in_=msb[:]), r=["msb"], w=[("modv", l)])
                S.phase_end()

        def norm_mod_tiles(l, Tl, tagp, gname, sc_off, sh_off, r):
            gm = Tl(tagp + "gm", [128, D])
            sh = Tl(tagp + "sh", [128, D])
            gp = Tl(tagp + "gp", [128, D])
            return gm, sh, gp

        def load_norm_mod(l, gm, sh, gp, key, gname, sc_off, sh_off, r):
            S.dma("sp", lambda h: h.dma_start(out=gp[:], in_=bcast_row(I[gname][l:l + 1, :], D)), w=[key + "gp"])
            S.dma("sp", lambda h: h.dma_start(out=gm[:], in_=bcast_row(modv[l, r:r + 1, sc_off:sc_off + D], D)), w=[key + "gm"])
            S.dma("sp", lambda h: h.dma_start(out=sh[:], in_=bcast_row(modv[l, r:r + 1, sh_off:sh_off + D], D)), w=[key + "sh"])
            S.add("dve", lambda h: h.scalar_tensor_tensor(out=gm[:], in0=gm[:], scalar=1.0, in1=gp[:], op0=ALU.add, op1=ALU.mult),
                  r=[key + "gp", key + "gm"], w=[key + "gm"])

        def load_gate_mod(l, G, gp, key, gname, g_off, r):
            S.dma("sp", lambda h: h.dma_start(out=gp[:], in_=bcast_row(I[gname][l:l + 1, :], D)), w=[key + "gp"])
            S.dma("sp", lambda h: h.dma_start(out=G[:], in_=bcast_row(modv[l, r:r + 1, g_off:g_off + D], D)), w=[key + "G"])
            S.add("dve", lambda h: h.tensor_tensor(out=G[:], in0=G[:], in1=gp[:], op=ALU.mult), r=[key + "gp", key + "G"], w=[key + "G"])

        def norm_transpose_tile(l, tt, xt_ap, xkey, gm, sh, mkey, tmp, hb, junk, ssq, pt, dst_fn, wkeys, i):
            S.add("act", lambda h: h.activation(out=junk[:], in_=xt_ap, func=AF.Square, accum_out=ssq[:, 0:1]),
                  r=[xkey], w=["junk", "ssq"])
            rstd_ops(ssq[:, 0:1], ssq[:, 1:2], 1.0 / D, ["ssq"], ["rstd"])
            S.add("dve", lambda h: h.scalar_tensor_tensor(out=tmp[:], in0=xt_ap, scalar=ssq[:, 1:2], in1=gm[:], op0=ALU.mult, op1=ALU.mult),
                  r=[xkey, "rstd", mkey + "gm"], w=["tmp"])
            S.add("dve", lambda h: h.tensor_tensor(out=hb[:], in0=tmp[:], in1=sh[:], op=ALU.add), r=["tmp", mkey + "sh"], w=[("hb", i % 2)])
            for k in range(16):
                S.add("pe", lambda h, k=k: h.transpose(out=pt[:, k, :], in_=hb[:, k * 128:(k + 1) * 128], identity=ident_b[:]),
                      r=[("hb", i % 2)], w=[("pt", i % 2)])
            dst_fn(pt, ("pt", i % 2), wkeys)

        def phase_in(l):
            with ExitStack() as st_o:
                To, Po = mk_alloc(st_o)
                hT = To("hT", [128, 16, T], BF16)
                with ExitStack() as st:
                    Tl, Pl = mk_alloc(st)
                    mods = {}
                    for r, nm in ((1, "c"), (0, "l")):
                        gm = Tl("gm" + nm, [128, D])
                        sh = Tl("sh" + nm, [128, D])
                        gp = Tl("gp" + nm, [128, D])
                        load_norm_mod(l, gm, sh, gp, nm, "pre_mix_g", 2048, 0, r)
                        mods[r] = (gm, sh, nm)
                    xt = [Tl("xt%d" % i, [128, D]) for i in range(2)]
                    tmp = Tl("tmp", [128, D])
                    hb = [Tl("hb%d" % i, [128, D], BF16) for i in range(2)]
                    junk = Tl("junk", [128, D], BF16)
                    ssq = Tl("ssq", [128, 2])
                    pt = [Pl("pt%d" % i, [128, 16, 128], BF16) for i in range(2)]
                    for tt in range(NT):
                        i = tt
                        gm, sh, nm = mods[1 if tt < 2 else 0]
                        S.dma("sp", lambda h, tt=tt, i=i: h.dma_start(out=xt[i % 2][:], in_=xsrc(l, tt)), w=[("xt", i % 2)])

                        def dst(ptile, pkey, wkeys, tt=tt):
                            S.add("act", lambda h: h.activation(out=hT[:, :, tt * 128:(tt + 1) * 128], in_=ptile[:], func=AF.Copy),
                                  r=[pkey], w=wkeys)
                        norm_transpose_tile(l, tt, xt[i % 2][:], ("xt", i % 2), gm, sh, nm, tmp, hb[i % 2], junk, ssq, pt[i % 2], dst,
                                            [("hT", tt)], i)
                    S.phase_end()
                with ExitStack() as st:
                    Tl, Pl = mk_alloc(st)
                    wb = [Tl("wib%d" % i, [128, 16, 512], BF16) for i in range(2)]
                    stg = [Tl("stg%d" % i, [128, 512]) for i in range(4)]
                    pp = [Pl("pp%d" % i, [128, 512]) for i in range(4)]
                    wv = I["w_in"][l].rearrange("(k p) n -> p k n", p=128)
                    cnt = [0]

                    def evac(ps_ap, n, dram_ap, pkey):
                        i = cnt[0]
                        cnt[0] += 1
                        s = stg[i % 4]
                        if i % 2 == 0:
                            S.add("act", lambda h: h.activation(out=s[:, 0:n], in_=ps_ap, func=AF.Copy), r=[pkey], w=[("stg", i % 4)])
                        else:
                            S.add("dve", lambda h: h.tensor_copy(out=s[:, 0:n], in_=ps_ap), r=[pkey], w=[("stg", i % 4)])
                        S.dma("sp", lambda h: h.dma_start(out=dram_ap, in_=s[:, 0:n]), r=[("stg", i % 4)], w=[("dram", i)])

                    blocks = [(c0, 512) for c0 in range(0, 5120, 512)] + [(5120, 280)]
                    pi = [0]
                    for bi, (c0, ncol) in enumerate(blocks):
                        b = bi % 2
                        for k4 in range(4):
                            S.dma("pool", lambda h, b=b, k4=k4, c0=c0, ncol=ncol: h.dma_start(
                                out=wb[b][:, 4 * k4:4 * k4 + 4, 0:ncol], in_=wv[:, 4 * k4:4 * k4 + 4, c0:c0 + ncol]), w=[("wib", b, k4)])
                        wkeys = [("wib", b, k4) for k4 in range(4)]
                        if c0 < 4096:
                            tm = (0, ncol, c0)
                            fm_chunks = []
                        elif c0 < 5120:
                            tm = None
                            fm_chunks = [(j, (c0 - 4096) // 128 + j) for j in range(4)]
                        else:
                            tm = (256, 24, 4096)
                            fm_chunks = [(0, 8), (1, 9)]
                        if tm is not None:
                            co, n, dc = tm
                            for tt in range(NT):
                                p = pi[0] % 4
                                pi[0] += 1
                                for k in range(16):
                                    S.add("pe", lambda h, p=p, k=k, tt=tt, co=co, n=n, b=b: h.matmul(
                                        pp[p][:, 0:n], lhsT=hT[:, k, tt * 128:(tt + 1) * 128], rhs=wb[b][:, k, co:co + n],
                                        start=(k == 0), stop=(k == 15)), r=wkeys, w=[("pp", p)])
                                evac(pp[p][:, 0:n], n, proj[tt * 128:(tt + 1) * 128, dc:dc + n], ("pp", p))
                        for (j, ch) in fm_chunks:
                            for t0 in range(0, T, 512):
                                n = min(512, T - t0)
                                p = pi[0] % 4
                                pi[0] += 1
                                for k in range(16):
                                    S.add("pe", lambda h, p=p, k=k, t0=t0, n=n, j=j, b=b: h.matmul(
                                        pp[p][:, 0:n], lhsT=wb[b][:, k, j * 128:(j + 1) * 128], rhs=hT[:, k, t0:t0 + n],
                                        start=(k == 0), stop=(k == 15)), r=wkeys, w=[("pp", p)])
                                evac(pp[p][:, 0:n], n, xbcT[ch * 128:(ch + 1) * 128, t0:t0 + n], ("pp", p))
                    S.phase_end()

        def rope_ops(src, dst, cs, nh, lat, skey, dkey, cskey, tmps):
            if not lat:
                S.add("dve", lambda h: h.tensor_copy(out=dst, in_=src), r=[skey], w=[dkey, (dkey, "b")])
                return
            t1, t2, t3, t4 = tmps
            s4 = src.rearrange("p (h i two) -> p h i two", h=nh, two=2)
            d4 = dst.rearrange("p (h i two) -> p h i two", h=nh, two=2)
            x1 = s4[:, :, :, 0]
            x2 = s4[:, :, :, 1]
            cosb = cs[:, 0:64].unsqueeze(1).to_broadcast([128, nh, 64])
            sinb = cs[:, 64:128].unsqueeze(1).to_broadcast([128, nh, 64])
            v = lambda t: t[:, 0:nh * 64].rearrange("p (h i) -> p h i", h=nh)
            S.add("dve", lambda h: h.tensor_tensor(out=v(t1), in0=x1, in1=cosb, op=ALU.mult), r=[skey, cskey], w=["rt1"])
            S.add("dve", lambda h: h.tensor_tensor(out=v(t2), in0=x2, in1=sinb, op=ALU.mult), r=[skey, cskey], w=["rt2"])
            S.add("dve", lambda h: h.tensor_tensor(out=d4[:, :, :, 0], in0=v(t1), in1=v(t2), op=ALU.subtract), r=["rt1", "rt2"], w=[dkey])
            S.add("dve", lambda h: h.tensor_tensor(out=v(t3), in0=x1, in1=sinb, op=ALU.mult), r=[skey, cskey], w=["rt3"])
            S.add("dve", lambda h: h.tensor_tensor(out=v(t4), in0=x2, in1=cosb, op=ALU.mult), r=[skey, cskey], w=["rt4"])
            S.add("dve", lambda h: h.tensor_tensor(out=d4[:, :, :, 1], in0=v(t3), in1=v(t4), op=ALU.add), r=["rt3", "rt4"], w=[(dkey, "b")])

        def phase_att(l):
            last = (l == nlayers - 1)
            with ExitStack() as st:
                Tl, Pl = mk_alloc(st)
                QKT = Tl("QKT", [128, 8, T], BF16)
                Vtm = Tl("Vtm", [128, NT, 256], BF16)
                gqk = Tl("gqk", [128, 8, 128])
                pr = [Tl("pr%d" % i, [128, 1280]) for i in range(2)]
                sq = Tl("sq", [128, 1024])
                qn = Tl("qn", [128, 1024])
                tmps = [Tl("rt%d" % i, [128, 512]) for i in range(4)]
                qr = [Tl("qr%d" % i, [128, 1024], BF16) for i in range(2)]
                cs = [Tl("cs%d" % i, [128, 128]) for i in range(2)]
                st8 = Tl("st8", [128, 16])
                ptq = Pl("ptq", [128, 8, 128], BF16)
                for hh in range(8):
                    src = I["q_norm_g"] if hh < 6 else I["k_norm_g"]
                    S.dma("sp", lambda h, hh=hh, src=src: h.dma_start(out=gqk[:, hh, :], in_=src[l:l + 1, :].to_broadcast([128, 128])), w=["gqk"])
                S.add("dve", lambda h: h.tensor_scalar(out=gqk[:, 0:6, :], in0=gqk[:, 0:6, :], scalar1=float(128 ** -0.5), scalar2=None, op0=ALU.mult),
                      r=["gqk"], w=["gqk"])
                def prep_tile(tt):
                    i = tt
                    lat = tt >= 2
                    p_ = pr[i % 2]
                    S.dma("sp", lambda h, tt=tt, p_=p_: h.dma_start(out=p_[:], in_=proj[tt * 128:(tt + 1) * 128, 0:1280]), w=[("pr", i % 2)])
                    if lat:
                        S.dma("sp", lambda h, tt=tt, i=i: h.dma_start(out=cs[i % 2][:], in_=I["rope"][(tt - 2) * 128:(tt - 1) * 128, :]), w=[("cs", i % 2)])
                    S.add("act", lambda h, p_=p_: h.activation(out=sq[:], in_=p_[:, 0:1024], func=AF.Square), r=[("pr", i % 2)], w=["sq"])
                    S.add("dve", lambda h: h.tensor_reduce(out=st8[:, 0:8], in_=sq[:].rearrange("p (h d) -> p h d", h=8), axis=AX.X, op=ALU.add),
                          r=["sq"], w=["ssq8"])
                    rstd_ops(st8[:, 0:8], st8[:, 8:16], 1.0 / 128, ["ssq8"], ["rstd8"])
                    S.add("dve", lambda h, p_=p_: h.tensor_tensor(out=qn[:].rearrange("p (h d) -> p h d", h=8),
                                                                 in0=p_[:, 0:1024].rearrange("p (h d) -> p h d", h=8),
                                                                 in1=st8[:, 8:16].unsqueeze(2).to_broadcast([128, 8, 128]), op=ALU.mult),
                          r=[("pr", i % 2), "rstd8"], w=["qn"])
                    S.add("dve", lambda h: h.tensor_tensor(out=qn[:], in0=qn[:], in1=gqk[:].rearrange("p h d -> p (h d)"), op=ALU.mult),
                          r=["qn", "gqk"], w=["qn"])
                    q_ = qr[i % 2]
                    rope_ops(qn[:], q_[:], cs[i % 2], 8, lat, "qn", ("qr", i % 2), ("cs", i % 2), tmps)
                    for hh in range(8):
                        S.add("pe", lambda h, hh=hh, q_=q_: h.transpose(out=ptq[:, hh, :], in_=q_[:, hh * 128:(hh + 1) * 128], identity=ident_b[:]),
                              r=[("qr", i % 2), (("qr", i % 2), "b")], w=["ptq"])
                    S.add("act", lambda h, tt=tt: h.activation(out=QKT[:, :, tt * 128:(tt + 1) * 128], in_=ptq[:], func=AF.Copy), r=["ptq"], w=[("QKT", tt)])
                    S.add("act", lambda h, tt=tt, p_=p_: h.activation(out=Vtm[:, tt, :], in_=p_[:, 1024:1280], func=AF.Copy), r=[("pr", i % 2)], w=[("Vtm", tt)])
                for tt in range(NT):
                    prep_tile(tt)
                ps_s = [Pl("ps_s%d" % i, [128, 512]) for i in range(2)]
                ps_o = [Pl("ps_o%d" % i, [128, 512]) for i in range(2)]
                ps_d = [Pl("ps_d%d" % i, [128, 512]) for i in range(2)]
                pT = [Tl("pT%d" % i, [128, 512], BF16) for i in range(3)]
                rden = Tl("rden", [128, 512])
                oT = [Tl("oT%d" % i, [128, 512], BF16) for i in range(2)]
                qblocks = [(256 + qb * 512, 512, list(range(NT))) for qb in range(4)]
                if not last:
                    qblocks = [(0, 256, [0, 1])] + qblocks
                gi = 0
                si = 0
                def att_head(q0, n, kts, hh, gi):
                        nonlocal si
                        g = hh // 3
                        o = gi % 2
                        nk = len(kts)
                        slots = []

                        def do_s(j):
                            nonlocal si
                            sidx = si
                            si += 1
                            kt = kts[j]
                            S.add("pe", lambda h, sidx=sidx, kt=kt: h.matmul(ps_s[sidx % 2][:, 0:n], lhsT=QKT[:, 6 + g, kt * 128:(kt + 1) * 128],
                                                                           rhs=QKT[:, hh, q0:q0 + n], start=True, stop=True),
                                  r=[("QKT", kt)] + [("QKT", q0 // 128 + a) for a in range(n // 128)], w=[("ps_s", sidx % 2)])
                            S.add("act", lambda h, sidx=sidx: h.activation(out=pT[sidx % 3][:, 0:n], in_=ps_s[sidx % 2][:, 0:n], func=AF.Exp),
                                  r=[("ps_s", sidx % 2)], w=[("pT", sidx % 3)])
                            slots.append(sidx)

                        def do_o(j):
                            sidx = slots[j]
                            kt = kts[j]
                            S.add("pe", lambda h, sidx=sidx, kt=kt: h.matmul(ps_o[o][:, 0:n], lhsT=Vtm[:, kt, g * 128:(g + 1) * 128], rhs=pT[sidx % 3][:, 0:n],
                                                                           start=(j == 0), stop=(j == nk - 1)),
                                  r=[("pT", sidx % 3), ("Vtm", kt)], w=[("ps_o", o)])
                            S.add("pe", lambda h, sidx=sidx: h.matmul(ps_d[o][:, 0:n], lhsT=ones_b[:], rhs=pT[sidx % 3][:, 0:n],
                                                                    start=(j == 0), stop=(j == nk - 1)),
                                  r=[("pT", sidx % 3)], w=[("ps_d", o)])
                        do_s(0)
                        for j in range(nk):
                            if j + 1 < nk:
                                do_s(j + 1)
                            do_o(j)
                        S.add("dve", lambda h, o=o: h.reciprocal(out=rden[:, 0:n], in_=ps_d[o][:, 0:n]), r=[("ps_d", o)], w=["rden"])
                        S.add("dve", lambda h, o=o: h.tensor_tensor(out=oT[o][:, 0:n], in0=ps_o[o][:, 0:n], in1=rden[:, 0:n], op=ALU.mult),
                              r=[("ps_o", o), "rden"], w=[("oT", o)])
                        S.dma("sp", lambda h, o=o, hh=hh, q0=q0, n=n: h.dma_start(out=mixT[hh * 128:(hh + 1) * 128, q0:q0 + n], in_=oT[o][:, 0:n]),
                              r=[("oT", o)], w=[("mixT", gi)])
                for (q0, n, kts) in qblocks:
                    for hh in range(6):
                        att_head(q0, n, kts, hh, gi)
                        gi += 1
                S.phase_end()

        def silu2_ops(src, e, dst, skey, ekey, dkey, in_scale=0.5):
            S.add("act", lambda h: h.activation(out=e, in_=src, func=AF.Tanh, scale=in_scale), r=[skey], w=[ekey])
            S.add("dve", lambda h: h.scalar_tensor_tensor(out=dst, in0=e, scalar=1.0, in1=src, op0=ALU.add, op1=ALU.mult), r=[skey, ekey], w=[dkey])

        def phase_ret(l):
            last = (l == nlayers - 1)
            with ExitStack() as st:
                Tl, Pl = mk_alloc(st)
                RQK = Tl("RQK", [128, 8, T], BF16)
                RKtm = Tl("RKtm", [128, NT, 512], BF16)
                RVtm = Tl("RVtm", [128, NT, 512], BF16)
                rgs = Tl("rgs", [128, NT, 512], BF16)
                SAll = [Tl("SAll%d" % d_, [128, NT, 512], BF16) for d_ in range(2)]
                S32 = [Tl("S32%d" % d_, [128, 512]) for d_ in range(2)]
                MaskT = Tl("MaskT", [128, 4, 128])
                ERow = [Tl("ERow%d" % d_, [128, 4, 128], BF16) for d_ in range(2)]
                dec = Tl("dec", [128, 8])
                lg = Tl("lg", [128, 8])
                dend = Tl("dend", [128, 8])
                gam = Tl("gam", [128, 8])
                marg = Tl("marg", [128, 128])
                pr = [Tl("rpr%d" % i, [128, 2048]) for i in range(2)]
                tmps = [Tl("rrt%d" % i, [128, 512]) for i in range(4)]
                qr = [Tl("rqr%d" % i, [128, 1024], BF16) for i in range(2)]
                cs = [Tl("rcs%d" % i, [128, 128]) for i in range(2)]
                ee = Tl("ree", [128, 512])
                ptq = Pl("rptq", [128, 8, 128], BF16)
                S.dma("sp", lambda h: h.dma_start(out=dec[:, 0:4], in_=I["ret_decay_f"][l:l + 1, :].to_broadcast([128, 4])), w=["dec"])
                S.dma("sp", lambda h: h.dma_start(out=dec[:, 4:8], in_=I["ret_decay_b"][l:l + 1, :].to_broadcast([128, 4])), w=["dec"])
                S.add("act", lambda h: h.activation(out=lg[:], in_=dec[:], func=AF.Exp, scale=float(math.log(2.0))), r=["dec"], w=["lg"])
                S.add("act", lambda h: h.activation(out=lg[:], in_=lg[:], func=AF.Ln, scale=-1.0, bias=1.0), r=["lg"], w=["lg"])
                S.add("act", lambda h: h.activation(out=gam[:], in_=lg[:], func=AF.Exp, scale=128.0), r=["lg"], w=["gam"])

                def mk_head(hh):
                    S.add("dve", lambda h: h.tensor_scalar(out=marg[:], in0=D1, scalar1=lg[:, hh:hh + 1], scalar2=None, op0=ALU.mult), r=["lg"], w=["marg"])
                    S.add("dve", lambda h: h.scalar_tensor_tensor(out=marg[:], in0=D2, scalar=lg[:, 4 + hh:5 + hh], in1=marg[:], op0=ALU.mult, op1=ALU.add),
                          r=["lg", "marg"], w=["marg"])
                    S.add("act", lambda h: h.activation(out=marg[:], in_=marg[:], func=AF.Exp), r=["marg"], w=["marg"])
                    S.add("dve", lambda h: h.tensor_tensor(out=MaskT[:, hh, :], in0=marg[:], in1=ident_f, op=ALU.add), r=["marg"], w=["MaskT"])
                    for d_ in range(2):
                        S.add("act", lambda h, d_=d_: h.activation(out=ERow[d_][:, hh, :], in_=rowidx[d_], func=AF.Exp, scale=lg[:, 4 * d_ + hh:4 * d_ + hh + 1]),
                              r=["lg"], w=["ERow"])
                        S.add("act", lambda h, d_=d_: h.activation(out=dend[:, 4 * d_ + hh:4 * d_ + hh + 1], in_=colexp[d_], func=AF.Exp,
                                                                   scale=lg[:, 4 * d_ + hh:4 * d_ + hh + 1]), r=["lg"], w=["dend"])
                for hh in range(4):
                    mk_head(hh)

                def prep_tile(tt):
                    i = tt
                    lat = tt >= 2
                    p_ = pr[i % 2]
                    S.dma("sp", lambda h: h.dma_start(out=p_[:], in_=proj[tt * 128:(tt + 1) * 128, 1280:3328]), w=[("pr", i % 2)])
                    if lat:
                        S.dma("sp", lambda h: h.dma_start(out=cs[i % 2][:], in_=I["rope"][(tt - 2) * 128:(tt - 1) * 128, :]), w=[("cs", i % 2)])
                    S.add("act", lambda h: h.activation(out=p_[:, 512:1024], in_=p_[:, 512:1024], func=AF.Copy, scale=float(128 ** -0.5)),
                          r=[("pr", i % 2)], w=[("pr", i % 2)])
                    q_ = qr[i % 2]
                    rope_ops(p_[:, 0:1024], q_[:], cs[i % 2], 8, lat, ("pr", i % 2), ("qr", i % 2), ("cs", i % 2), tmps)
                    for hh in range(8):
                        S.add("pe", lambda h, hh=hh: h.transpose(out=ptq[:, hh, :], in_=q_[:, hh * 128:(hh + 1) * 128], identity=ident_b[:]),
                              r=[("qr", i % 2), (("qr", i % 2), "b")], w=["ptq"])
                    S.add("act", lambda h: h.activation(out=RQK[:, :, tt * 128:(tt + 1) * 128], in_=ptq[:], func=AF.Copy), r=["ptq"], w=[("RQK", tt)])
                    S.add("dve", lambda h: h.tensor_copy(out=RKtm[:, tt, :], in_=q_[:, 512:1024]), r=[("qr", i % 2), (("qr", i % 2), "b")], w=[("RKtm", tt)])
                    S.add("act", lambda h: h.activation(out=RVtm[:, tt, :], in_=p_[:, 1024:1536], func=AF.Copy), r=[("pr", i % 2)], w=[("RVtm", tt)])
                    silu2_ops(p_[:, 1536:2048], ee[:], rgs[:, tt, :], ("pr", i % 2), "ee", ("rgs", tt))
                for tt in range(NT):
                    prep_tile(tt)

                psA = Pl("psA", [128, 512])
                RVs = [Tl("RVs%d" % i, [128, 512], BF16) for i in range(2)]
                orders = [list(range(NT)), [1, 0] + list(range(NT - 1, 1, -1))]

                def state_step(d_, idx):
                    order = orders[d_]
                    c = order[idx]
                    if idx == 0:
                        S.add("pool", lambda h: h.memset(S32[d_][:], 0.0), w=[("S32", d_)])
                        S.add("pool", lambda h: h.memset(SAll[d_][:, c, :], 0.0), w=[("SAll", d_, c)])
                    if idx == NT - 1:
                        return
                    rv = RVs[idx % 2]
                    S.add("dve", lambda h: h.tensor_tensor(out=rv[:].rearrange("p (h d) -> p h d", h=4),
                                                            in0=RVtm[:, c, :].rearrange("p (h d) -> p h d", h=4),
                                                            in1=dend[:, 4 * d_:4 * d_ + 4].unsqueeze(2).to_broadcast([128, 4, 128]), op=ALU.mult),
                          r=[("RVtm", c), "dend"], w=[("RVs", idx % 2)])
                    for hh in range(4):
                        S.add("pe", lambda h, hh=hh: h.matmul(psA[:, hh * 128:(hh + 1) * 128], lhsT=RKtm[:, c, hh * 128:(hh + 1) * 128],
                                                             rhs=rv[:, hh * 128:(hh + 1) * 128], start=True, stop=True),
                              r=[("RKtm", c), ("RVs", idx % 2)], w=["psA"])
                    S.add("dve", lambda h: h.tensor_tensor(out=S32[d_][:].rearrange("p (h d) -> p h d", h=4),
                                                           in0=S32[d_][:].rearrange("p (h d) -> p h d", h=4),
                                                           in1=gam[:, 4 * d_:4 * d_ + 4].unsqueeze(2).to_broadcast([128, 4, 128]), op=ALU.mult),
                          r=[("S32", d_), "gam"], w=[("S32", d_)])
                    S.add("dve", lambda h: h.tensor_tensor(out=S32[d_][:], in0=S32[d_][:], in1=psA[:], op=ALU.add), r=[("S32", d_), "psA"], w=[("S32", d_)])
                    cn = order[idx + 1]
                    S.add("act", lambda h: h.activation(out=SAll[d_][:, cn, :], in_=S32[d_][:], func=AF.Copy), r=[("S32", d_)], w=[("SAll", d_, cn)])
                for d_ in range(2):
                    for idx in range(NT):
                        state_step(d_, idx)

                psS = [Pl("psS%d" % i, [128, 4, 128]) for i in range(2)]
                psY = [Pl("psY%d" % i, [128, 512]) for i in range(2)]
                ptr = Pl("ptr", [128, 8, 128], BF16)
                SDT = [Tl("SDT%d" % i, [128, 4, 128], BF16) for i in range(2)]
                RQs = [[Tl("RQs%d_%d" % (d_, i), [128, 4, 128], BF16) for i in range(2)] for d_ in range(2)]
                sqy = Tl("sqy", [128, 512])
                yn = Tl("yn", [128, 512])
                yb = [Tl("yb%d" % i, [128, 512], BF16) for i in range(2)]
                stgr = [Tl("stgr%d" % i, [128, 4, 128], BF16) for i in range(2)]
                st4 = Tl("st4", [128, 8])

                def chunk(c, i):
                    y = psY[i % 2]
                    cs_ = slice(c * 128, (c + 1) * 128)
                    for hh in range(4):
                        S.add("pe", lambda h, hh=hh: h.matmul(psS[i % 2][:, hh, :], lhsT=RQK[:, 4 + hh, cs_], rhs=RQK[:, hh, cs_], start=(hh == 0), stop=(hh == 3)),
                              r=[("RQK", c)], w=[("psS", i % 2)])
                    S.add("dve", lambda h: h.tensor_tensor(out=SDT[i % 2][:], in0=psS[i % 2][:], in1=MaskT[:], op=ALU.mult),
                          r=[("psS", i % 2), "MaskT"], w=[("SDT", i % 2)])
                    for d_ in range(2):
                        S.add("dve", lambda h, d_=d_: h.tensor_tensor(out=RQs[d_][i % 2][:], in0=RQK[:, 0:4, cs_], in1=ERow[d_][:], op=ALU.mult),
                              r=[("RQK", c), "ERow"], w=[("RQs", d_, i % 2)])
                    for hh in range(4):
                        S.add("pe", lambda h, hh=hh: h.matmul(y[:, hh * 128:(hh + 1) * 128], lhsT=SDT[i % 2][:, hh, :], rhs=RVtm[:, c, hh * 128:(hh + 1) * 128],
                                                             start=(hh == 0), stop=False), r=[("SDT", i % 2), ("RVtm", c)], w=[("psY", i % 2)])
                        for d_ in range(2):
                            S.add("pe", lambda h, hh=hh, d_=d_: h.matmul(y[:, hh * 128:(hh + 1) * 128], lhsT=RQs[d_][i % 2][:, hh, :],
                                                                        rhs=SAll[d_][:, c, hh * 128:(hh + 1) * 128], start=False, stop=(d_ == 1 and hh == 3)),
                                  r=[("RQs", d_, i % 2), ("SAll", d_, c)], w=[("psY", i % 2)])
                    S.add("act", lambda h: h.activation(out=sqy[:], in_=y[:], func=AF.Square), r=[("psY", i % 2)], w=["sqy"])
                    S.add("dve", lambda h: h.tensor_reduce(out=st4[:, 0:4], in_=sqy[:].rearrange("p (h d) -> p h d", h=4), axis=AX.X, op=ALU.add),
                          r=["sqy"], w=["ssq4"])
                    rstd_ops(st4[:, 0:4], st4[:, 4:8], 1.0 / 128, ["ssq4"], ["rstd4"])
                    S.add("dve", lambda h: h.tensor_tensor(out=yn[:].rearrange("p (h d) -> p h d", h=4), in0=y[:].rearrange("p (h d) -> p h d", h=4),
                                                           in1=st4[:, 4:8].unsqueeze(2).to_broadcast([128, 4, 128]), op=ALU.mult),
                          r=[("psY", i % 2), "rstd4"], w=["yn"])
                    yb_ = yb[i % 2]
                    S.add("dve", lambda h: h.scalar_tensor_tensor(out=yb_[:], in0=yn[:], scalar=0.5, in1=rgs[:, c, :], op0=ALU.mult, op1=ALU.mult),
                          r=["yn", ("rgs", c)], w=[("yb", i % 2)])
                    for hh in range(4):
                        S.add("pe", lambda h, hh=hh: h.transpose(out=ptr[:, hh, :], in_=yb_[:, hh * 128:(hh + 1) * 128], identity=ident_b[:]),
                              r=[("yb", i % 2)], w=["ptr"])
                    sg = stgr[i % 2]
                    S.add("act", lambda h: h.activation(out=sg[:], in_=ptr[:, 0:4, :], func=AF.Copy), r=["ptr"], w=[("stgr", i % 2)])
                    S.dma("sp", lambda h: h.dma_start(out=mixT[768:1280, c * 128:(c + 1) * 128].rearrange("(h d) t -> d h t", h=4), in_=sg[:]),
                          r=[("stgr", i % 2)], w=[("mixTr", c)])
                for i_, c in enumerate(range(2 if last else 0, NT)):
                    chunk(c, i_)
                S.phase_end()

        def phase_ssd(l):
            last = (l == nlayers - 1)
            NH = 12
            with ExitStack() as st_o:
                To, Po = mk_alloc(st_o)
                BT = To("BT", [128, 2, T], BF16)
                CT = To("CT", [128, 2, T], BF16)
                Btm = To("Btm", [128, NT, 256], BF16)
                xs = To("xs", [128, NT, 768], BF16)
                la = To("la", [128, 2, NT, NH])
                lndt = To("lndt", [128, 2, NT, NH])
                acs = To("acs", [128, 2, NT, NH])
                tot = To("tot", [128, 2, NT, NH])
                ein = To("ein", [128, 2, NT, NH])
                cdc = To("cdc", [128, 2, NT, NH])
                wend = To("wend", [128, 2, NT, NH])
                lb = To("lb", [128, 2, NT, NH])
                dsk = To("dsk", [128, NH])
                gssd = To("gssd", [128, 768])
                with ExitStack() as st:
                    Tl, Pl = mk_alloc(st)
                    cw = Tl("cw", [128, 10, 5])
                    cb = Tl("cb", [128, 10])
                    for k in range(5):
                        S.dma("sp", lambda h, k=k: h.dma_start(out=cw[:, :, k], in_=I["conv_w"][l, k].rearrange("(c p) -> p c", p=128),
                                                               allow_slow_non_contiguous=True), w=["cw"])
                    S.dma("sp", lambda h: h.dma_start(out=cb[:], in_=I["conv_b"][l].rearrange("(c p) -> p c", p=128), allow_slow_non_contiguous=True), w=["cb"])
                    S.dma("sp", lambda h: h.dma_start(out=dsk[:], in_=I["d_skip"][l:l + 1, :].to_broadcast([128, NH])), w=["dsk"])
                    S.dma("sp", lambda h: h.dma_start(out=gssd[:], in_=I["ssd_norm_g"][l:l + 1, :].to_broadcast([128, 768])), w=["gssd"])
                    UW = 2312
                    u = [Tl("u%d" % i, [128, UW]) for i in range(2)]
                    acc = Tl("acc", [128, UW])
                    ee = Tl("cee", [128, UW])
                    ob = [Tl("ob%d" % i, [128, UW], BF16) for i in range(2)]
                    ptx = Pl("ptx", [128, 8, 128], BF16)
                    for i in range(2):
                        S.add("pool", lambda h, i=i: h.memset(u[i][:], 0.0), w=[("u", i)])
                    S.add("dve", lambda h: h.tensor_scalar(out=cw[:], in0=cw[:], scalar1=0.5, scalar2=None, op0=ALU.mult), r=["cw"], w=["cw"])
                    S.add("dve", lambda h: h.tensor_scalar(out=cb[:], in0=cb[:], scalar1=0.5, scalar2=None, op0=ALU.mult), r=["cb"], w=["cb"])

                    def conv_chunk(cc):
                        i = cc
                        u_ = u[i % 2]
                        o_ = ob[i % 2]
                        S.dma("sp", lambda h: h.dma_start(out=u_[:, 2:258], in_=xbcT[cc * 128:(cc + 1) * 128, 0:256]), w=[("u", i % 2)])
                        S.dma("sp", lambda h: h.dma_start(out=u_[:, 262:2310], in_=xbcT[cc * 128:(cc + 1) * 128, 256:T]), w=[("u", i % 2)])
                        n = 2308
                        S.add("dve", lambda h: h.tensor_scalar(out=acc[:, 2:2 + n], in0=u_[:, 0:n], scalar1=cw[:, cc, 0:1], scalar2=cb[:, cc:cc + 1],
                                                               op0=ALU.mult, op1=ALU.add), r=[("u", i % 2), "cw", "cb"], w=["acc"])
                        for k in range(1, 5):
                            S.add("dve", lambda h, k=k: h.scalar_tensor_tensor(out=acc[:, 2:2 + n], in0=u_[:, k:k + n], scalar=cw[:, cc, k:k + 1],
                                                                               in1=acc[:, 2:2 + n], op0=ALU.mult, op1=ALU.add),
                                  r=[("u", i % 2), "cw", "acc"], w=["acc"])
                        silu2_ops(acc[:, 2:2 + n], ee[:, 2:2 + n], o_[:, 2:2 + n], "acc", "cee", ("ob", i % 2), in_scale=1.0)
                        def tok(tt):
                            return (2 + tt * 128) if tt < 2 else (262 + (tt - 2) * 128)
                        if cc < 6 or cc in (6, 7):
                            for t0 in range(0, NT, 8):
                                nt = min(8, NT - t0)
                                for a in range(nt):
                                    tt = t0 + a
                                    S.add("pe", lambda h, a=a, tt=tt: h.transpose(out=ptx[:, a, :], in_=o_[:, tok(tt):tok(tt) + 128], identity=ident_b[:]),
                                          r=[("ob", i % 2)], w=["ptx"])
                                if cc < 6:
                                    S.add("act", lambda h, t0=t0, nt=nt: h.activation(out=xs[:, t0:t0 + nt, cc * 128:(cc + 1) * 128], in_=ptx[:, 0:nt, :], func=AF.Copy),
                                          r=["ptx"], w=[("xs", cc, t0)])
                                else:
                                    g = cc - 6
                                    S.add("act", lambda h, t0=t0, nt=nt: h.activation(out=Btm[:, t0:t0 + nt, g * 128:(g + 1) * 128], in_=ptx[:, 0:nt, :], func=AF.Copy),
                                          r=["ptx"], w=[("Btm", g, t0)])
                        if cc >= 6:
                            dstT = BT if cc < 8 else CT
                            g = (cc - 6) % 2
                            S.add("act", lambda h: h.activation(out=dstT[:, g, 0:256], in_=o_[:, 2:258], func=AF.Copy), r=[("ob", i % 2)], w=[("BCT", cc, 0)])
                            S.add("act", lambda h: h.activation(out=dstT[:, g, 256:T], in_=o_[:, 262:2310], func=AF.Copy), r=[("ob", i % 2)], w=[("BCT", cc, 1)])
                    for cc in range(10):
                        conv_chunk(cc)
                    ztl = [Tl("ztl%d" % i, [128, 768]) for i in range(2)]
                    zth = [Tl("zth%d" % i, [128, 768]) for i in range(2)]
                    zso = [Tl("zso%d" % i, [128, 768], BF16) for i in range(2)]

                    def ztile(tt):
                        b = tt % 2
                        S.dma("sp", lambda h: h.dma_start(out=ztl[b][:], in_=proj[tt * 128:(tt + 1) * 128, 3328:4096]), w=[("ztl", b)])
                        S.add("act", lambda h: h.activation(out=zth[b][:], in_=ztl[b][:], func=AF.Tanh, scale=0.5), r=[("ztl", b)], w=[("zth", b)])
                        S.add("dve", lambda h: h.scalar_tensor_tensor(out=zso[b][:], in0=zth[b][:], scalar=1.0, in1=ztl[b][:], op0=ALU.add, op1=ALU.mult),
                              r=[("ztl", b), ("zth", b)], w=[("zso", b)])
                        S.dma("sp", lambda h: h.dma_start(out=zsd[tt * 128:(tt + 1) * 128, :], in_=zso[b][:]), r=[("zso", b)], w=[("zsd", tt)])
                    for tt in range(2 if last else 0, NT):
                        ztile(tt)
                    dtr = Tl("dtr", [128, NT, 24])
                    dtb = Tl("dtb", [128, 24])
                    alg = Tl("alg", [128, 24])
                    dtv = Tl("dtv", [128, 2, NT, NH])
                    tmpd = Tl("tmpd", [128, 2, NT, NH])
                    psc = Pl("psc", [128, 512])
                    pst = Pl("pst", [128, 512])
                    S.dma("sp", lambda h: h.dma_start(out=dtr[:], in_=proj[:, 4096:4120].rearrange("(t p) c -> p t c", p=128)), w=["dtr"])
                    S.dma("sp", lambda h: h.dma_start(out=dtb[:, 0:12], in_=I["dt_bias_f"][l:l + 1, :].to_broadcast([128, 12])), w=["dtb"])
                    S.dma("sp", lambda h: h.dma_start(out=dtb[:, 12:24], in_=I["dt_bias_b"][l:l + 1, :].to_broadcast([128, 12])), w=["dtb"])
                    S.dma("sp", lambda h: h.dma_start(out=alg[:, 0:12], in_=I["a_log_f"][l:l + 1, :].to_broadcast([128, 12])), w=["alg"])
                    S.dma("sp", lambda h: h.dma_start(out=alg[:, 12:24], in_=I["a_log_b"][l:l + 1, :].to_broadcast([128, 12])), w=["alg"])
                    S.add("act", lambda h: h.activation(out=alg[:], in_=alg[:], func=AF.Exp), r=["alg"], w=["alg"])
                    for d_ in range(2):
                        S.add("dve", lambda h, d_=d_: h.tensor_tensor(out=dtv[:, d_], in0=dtr[:, :, d_ * 12:(d_ + 1) * 12],
                                                                    in1=dtb[:, d_ * 12:(d_ + 1) * 12].unsqueeze(1).to_broadcast([128, NT, NH]), op=ALU.add),
                              r=["dtr", "dtb"], w=["dtv"])
                    F2 = lambda t: t[:].rearrange("p a b c -> p (a b c)")
                    S.add("act", lambda h: h.activation(out=F2(dtv), in_=F2(dtv), func=AF.Exp), r=["dtv"], w=["dtv"])
                    S.add("act", lambda h: h.activation(out=F2(dtv), in_=F2(dtv), func=AF.Ln, bias=1.0), r=["dtv"], w=["dtv"])
                    S.add("dve", lambda h: h.tensor_scalar(out=F2(dtv), in0=F2(dtv), scalar1=1e-30, scalar2=None, op0=ALU.max), r=["dtv"], w=["dtv"])
                    S.add("act", lambda h: h.activation(out=F2(lndt), in_=F2(dtv), func=AF.Ln), r=["dtv"], w=["lndt"])
                    for d_ in range(2):
                        S.add("dve", lambda h, d_=d_: h.tensor_tensor(out=la[:, d_], in0=dtv[:, d_],
                                                                    in1=alg[:, d_ * 12:(d_ + 1) * 12].unsqueeze(1).to_broadcast([128, NT, NH]), op=ALU.mult),
                              r=["dtv", "alg"], w=["la"])
                    S.add("dve", lambda h: h.tensor_scalar(out=F2(la), in0=F2(la), scalar1=-1.0, scalar2=None, op0=ALU.mult), r=["la"], w=["la"])
                    NQ = NT * NH
                    S.add("pe", lambda h: h.matmul(psc[:, 0:NQ], lhsT=tri_f, rhs=la[:, 0].rearrange("p a b -> p (a b)"), start=True, stop=True), r=["la"], w=["psc"])
                    S.add("pe", lambda h: h.matmul(psc[:, NQ:2 * NQ], lhsT=tri_b, rhs=la[:, 1].rearrange("p a b -> p (a b)"), start=True, stop=True), r=["la"], w=["psc"])
                    S.add("pe", lambda h: h.matmul(pst[:, 0:2 * NQ], lhsT=ones_f, rhs=F2(la), start=True, stop=True), r=["la"], w=["pst"])
                    S.add("dve", lambda h: h.tensor_copy(out=F2(acs), in_=psc[:, 0:2 * NQ]), r=["psc"], w=["acs"])
                    S.add("dve", lambda h: h.tensor_copy(out=F2(tot), in_=pst[:, 0:2 * NQ]), r=["pst"], w=["tot"])
                    S.add("act", lambda h: h.activation(out=F2(ein), in_=F2(acs), func=AF.Exp), r=["acs"], w=["ein"])
                    S.add("act", lambda h: h.activation(out=F2(cdc), in_=F2(tot), func=AF.Exp), r=["tot"], w=["cdc"])
                    S.add("dve", lambda h: h.tensor_tensor(out=F2(lb), in0=F2(lndt), in1=F2(acs), op=ALU.subtract), r=["lndt", "acs"], w=["lb"])
                    S.add("dve", lambda h: h.tensor_tensor(out=F2(tmpd), in0=F2(lb), in1=F2(tot), op=ALU.add), r=["lb", "tot"], w=["tmpd"])
                    S.add("act", lambda h: h.activation(out=F2(wend), in_=F2(tmpd), func=AF.Exp), r=["tmpd"], w=["wend"])
                    S.phase_end()
                with ExitStack() as st:
                    Tl, Pl = mk_alloc(st)
                    SAll = [Tl("sSAll%d" % d_, [128, NT, 768], BF16) for d_ in range(2)]
                    S32 = [Tl("sS32%d" % d_, [128, 768]) for d_ in range(2)]
                    xsw = [Tl("xsw%d" % i, [128, 768], BF16) for i in range(2)]
                    psA = Pl("spsA", [128, 2, 512])
                    orders = [list(range(NT)), [1, 0] + list(range(NT - 1, 1, -1))]

                    def bc12(t2d):
                        return t2d.unsqueeze(2).to_broadcast([128, NH, 64])

                    def v12(ap):
                        return ap.rearrange("p (h d) -> p h d", h=NH)

                    def state_step(d_, idx):
                        order = orders[d_]
                        c = order[idx]
                        if idx == 0:
                            S.add("pool", lambda h: h.memset(S32[d_][:], 0.0), w=[("S32", d_)])
                            S.add("pool", lambda h: h.memset(SAll[d_][:, c, :], 0.0), w=[("SAll", d_, c)])
                        if idx == NT - 1:
                            return
                        xw = xsw[idx % 2]
                        S.add("dve", lambda h: h.tensor_tensor(out=v12(xw[:]), in0=v12(xs[:, c, :]), in1=bc12(wend[:, d_, c, :]), op=ALU.mult),
                              w=[("xsw", idx % 2)])
                        for g in range(2):
                            S.add("pe", lambda h, g=g: h.matmul(psA[:, g, 0:384], lhsT=Btm[:, c, g * 128:(g + 1) * 128], rhs=xw[:, g * 384:(g + 1) * 384],
                                                               start=True, stop=True), r=[("xsw", idx % 2)], w=["psA"])
                        S.add("dve", lambda h: h.tensor_tensor(out=v12(S32[d_][:]), in0=v12(S32[d_][:]), in1=bc12(cdc[:, d_, c, :]), op=ALU.mult),
                              r=[("S32", d_)], w=[("S32", d_)])
                        S.add("dve", lambda h: h.tensor_tensor(out=S32[d_][:].rearrange("p (g x) -> p g x", g=2), in0=S32[d_][:].rearrange("p (g x) -> p g x", g=2),
                                                               in1=psA[:, :, 0:384], op=ALU.add), r=[("S32", d_), "psA"], w=[("S32", d_)])
                        cn = order[idx + 1]
                        S.add("act", lambda h: h.activation(out=SAll[d_][:, cn, :], in_=S32[d_][:], func=AF.Copy), r=[("S32", d_)], w=[("SAll", d_, cn)])
                    for d_ in range(2):
                        for idx in range(NT):
                            state_step(d_, idx)

                    Rt = [Tl("Rt%d" % d_, [128, NH, 128]) for d_ in range(2)]
                    mneg4 = [Tl("mneg4_%d" % d_, [128, 4, 128]) for d_ in range(2)]
                    Et = [Tl("Et%d" % i, [128, 4, 128], BF16) for i in range(2)]
                    Wt = [Tl("Wt%d" % i, [128, 4, 128], BF16) for i in range(4)]
                    psE = [Pl("psE%d" % i, [128, 4, 128]) for i in range(2)]
                    psG = Pl("psG", [128, 4, 128])
                    psY = Pl("spsY", [128, 1024])
                    psFB = psA
                    ptr = Pl("sptr", [128, 8, 128], BF16)
                    y1 = [Tl("y1_%d" % i, [128, 768]) for i in range(2)]
                    y2 = [Tl("y2_%d" % i, [128, 768]) for i in range(2)]
                    zt = [Tl("zt%d" % i, [128, 768], BF16) for i in range(2)]
                    xsd = [Tl("xsd%d" % i, [128, 768], BF16) for i in range(2)]
                    junk = Tl("sjunk", [128, 768], BF16)
                    yo = [Tl("yo%d" % i, [128, 768], BF16) for i in range(2)]
                    stg = [Tl("sstg%d" % i, [128, 6, 128], BF16) for i in range(2)]
                    st2 = Tl("st2", [128, 2])
                    for d_ in range(2):
                        S.add("dve", lambda h, d_=d_: h.tensor_copy(out=mneg4[d_][:], in_=mneg[d_].unsqueeze(1).to_broadcast([128, 4, 128])), w=["mneg4"])
                    tris = [tri_f, tri_b]
                    ecnt = [0]
                    wcnt = [0]

                    def chunk(c, ci):
                        cs_ = slice(c * 128, (c + 1) * 128)
                        b = ci % 2
                        S.dma("sp", lambda h: h.dma_start(out=zt[b][:], in_=zsd[c * 128:(c + 1) * 128, :]), w=[("zt", b)])
                        for g in range(2):
                            S.add("pe", lambda h, g=g: h.matmul(psG[:, g, :], lhsT=BT[:, g, cs_], rhs=CT[:, g, cs_], start=(g == 0), stop=(g == 1)), w=["psG"])
                        for d_ in range(2):
                            S.add("dve", lambda h, d_=d_: h.tensor_tensor(out=Rt[d_][:], in0=la[:, d_, c, :].unsqueeze(2).to_broadcast([128, NH, 128]),
                                                                        in1=tris[d_].unsqueeze(1).to_broadcast([128, NH, 128]), op=ALU.mult), w=[("Rt", d_)])
                        S.add("dve", lambda h: h.tensor_tensor(out=v12(xsd[b][:]), in0=v12(xs[:, c, :]), in1=bc12(dsk[:]), op=ALU.mult), w=[("xsd", b)])
                        for q in range(3):
                            wl = {}
                            for d_ in range(2):
                                e = ecnt[0]
                                ecnt[0] += 1
                                pe_ = psE[e % 2]
                                et_ = Et[e % 2]
                                S.add("pe", lambda h, d_=d_, q=q, pe_=pe_: h.matmul(pe_[:].rearrange("p a b -> p (a b)"), lhsT=ones_f,
                                                                                 rhs=Rt[d_][:, 4 * q:4 * q + 4, :].rearrange("p a b -> p (a b)"), start=True, stop=False),
                                      r=[("Rt", d_)], w=[("psE", e % 2)])
                                S.add("pe", lambda h, d_=d_, pe_=pe_: h.matmul(pe_[:].rearrange("p a b -> p (a b)"), lhsT=ident_f,
                                                                            rhs=mneg4[d_][:].rearrange("p a b -> p (a b)"), start=False, stop=True),
                                      r=["mneg4"], w=[("psE", e % 2)])
                                for a in range(4):
                                    hh = 4 * q + a
                                    S.add("act", lambda h, a=a, hh=hh, d_=d_, pe_=pe_, et_=et_: h.activation(out=et_[:, a, :], in_=pe_[:, a, :], func=AF.Exp,
                                                                                                      bias=lb[:, d_, c, hh:hh + 1]),
                                          r=[("psE", e % 2)], w=[("Et", e % 2)])
                                w_ = wcnt[0]
                                wcnt[0] += 1
                                wt_ = Wt[w_ % 4]
                                wl[d_] = (w_, wt_)
                                if q == 1:
                                    for half in range(2):
                                        S.add("dve", lambda h, half=half, et_=et_, wt_=wt_: h.tensor_tensor(
                                            out=wt_[:, 2 * half:2 * half + 2, :], in0=et_[:, 2 * half:2 * half + 2, :],
                                            in1=psG[:, half, :].unsqueeze(1).to_broadcast([128, 2, 128]), op=ALU.mult),
                                            r=[("Et", e % 2), "psG"], w=[("Wt", w_ % 4)])
                                else:
                                    g = 0 if q == 0 else 1
                                    S.add("dve", lambda h, g=g, et_=et_, wt_=wt_: h.tensor_tensor(
                                        out=wt_[:], in0=et_[:], in1=psG[:, g, :].unsqueeze(1).to_broadcast([128, 4, 128]), op=ALU.mult),
                                        r=[("Et", e % 2), "psG"], w=[("Wt", w_ % 4)])
                            for a in range(4):
                                hh = 4 * q + a
                                for d_ in range(2):
                                    w_, wt_ = wl[d_]
                                    S.add("pe", lambda h, hh=hh, a=a, d_=d_, wt_=wt_: h.matmul(psY[:, hh * 64:(hh + 1) * 64], lhsT=wt_[:, a, :], rhs=xs[:, c, hh * 64:(hh + 1) * 64],
                                                                                          start=(d_ == 0 and hh in (0, 8)), stop=False),
                                          r=[("Wt", w_ % 4)], w=["psY"])
                        S.add("pe", lambda h: h.matmul(psY[:, 0:512], lhsT=ident_b[:], rhs=xsd[b][:, 0:512], start=False, stop=True), r=[("xsd", b)], w=["psY"])
                        S.add("pe", lambda h: h.matmul(psY[:, 512:768], lhsT=ident_b[:], rhs=xsd[b][:, 512:768], start=False, stop=True), r=[("xsd", b)], w=["psY"])
                        for d_ in range(2):
                            for g in range(2):
                                S.add("pe", lambda h, d_=d_, g=g: h.matmul(psFB[:, g, 0:384], lhsT=CT[:, g, cs_], rhs=SAll[d_][:, c, g * 384:(g + 1) * 384],
                                                                         start=True, stop=True), r=[("SAll", d_, c)], w=["psA"])
                            yd = y1[b] if d_ == 0 else y2[b]
                            S.add("dve", lambda h, d_=d_, yd=yd: h.tensor_tensor(
                                out=yd[:].rearrange("p (g h d) -> p g h d", g=2, h=6),
                                in0=psFB[:, :, 0:384].rearrange("p g (h d) -> p g h d", h=6),
                                in1=ein[:, d_, c, :].rearrange("p (g h) -> p g h", g=2).unsqueeze(3).to_broadcast([128, 2, 6, 64]), op=ALU.mult),
                                r=["psA"], w=[("y", d_, b)])
                        S.add("dve", lambda h: h.tensor_tensor(out=y1[b][:], in0=y1[b][:], in1=y2[b][:], op=ALU.add), r=[("y", 0, b), ("y", 1, b)], w=[("y", 0, b)])
                        S.add("dve", lambda h: h.tensor_tensor(out=y1[b][:], in0=y1[b][:], in1=psY[:, 0:768], op=ALU.add), r=[("y", 0, b), "psY"], w=[("y", 0, b)])
                        S.add("dve", lambda h: h.scalar_tensor_tensor(out=y1[b][:], in0=y1[b][:], scalar=0.5, in1=zt[b][:], op0=ALU.mult, op1=ALU.mult),
                              r=[("y", 0, b), ("zt", b)], w=[("y", 0, b)])
                        S.add("act", lambda h: h.activation(out=junk[:], in_=y1[b][:], func=AF.Square, accum_out=st2[:, 0:1]), r=[("y", 0, b)], w=["sjunk", "ssq"])
                        rstd_ops(st2[:, 0:1], st2[:, 1:2], 1.0 / 768, ["ssq"], ["rstd"])
                        S.add("dve", lambda h: h.scalar_tensor_tensor(out=yo[b][:], in0=y1[b][:], scalar=st2[:, 1:2], in1=gssd[:], op0=ALU.mult, op1=ALU.mult),
                              r=[("y", 0, b), "rstd"], w=[("yo", b)])
                        for a in range(6):
                            S.add("pe", lambda h, a=a: h.transpose(out=ptr[:, a, :], in_=yo[b][:, a * 128:(a + 1) * 128], identity=ident_b[:]), r=[("yo", b)], w=["ptr"])
                        sg = stg[b]
                        S.add("act", lambda h: h.activation(out=sg[:], in_=ptr[:, 0:6, :], func=AF.Copy), r=["ptr"], w=[("stg", b)])
                        S.dma("sp", lambda h: h.dma_start(out=mixT[1280:2048, cs_].rearrange("(a d) t -> d a t", a=6), in_=sg[:]),
                              r=[("stg", b)], w=[("mixTs", c)])
                    for ci, c in enumerate(range(2 if last else 0, NT)):
                        chunk(c, ci)
                    S.phase_end()

        def phase_out(l):
            last = (l == nlayers - 1)
            with ExitStack() as st:
                Tl, Pl = mk_alloc(st)
                wo = Tl("wo", [128, 16, D], BF16)
                wv = I["w_out"][l].rearrange("(k p) n -> p k n", p=128)
                for k4 in range(4):
                    for nb in range(2):
                        S.dma("pool", lambda h, k4=k4, nb=nb: h.dma_start(out=wo[:, 4 * k4:4 * k4 + 4, nb * 1024:(nb + 1) * 1024],
                                                                        in_=wv[:, 4 * k4:4 * k4 + 4, nb * 1024:(nb + 1) * 1024]), w=[("wo", k4, nb)])
                G1 = Tl("G1", [128, D])
                gm = Tl("ogm", [128, D])
                sh = Tl("osh", [128, D])
                gp = Tl("ogp", [128, D])
                mT = [Tl("mT%d" % i, [128, 16, 128], BF16) for i in range(2)]
                xt = [Tl("oxt%d" % i, [128, D]) for i in range(2)]
                x1 = [Tl("ox1%d" % i, [128, D]) for i in range(2)]
                tmp = Tl("otmp", [128, D])
                hb = [Tl("ohb%d" % i, [128, D], BF16) for i in range(2)]
                junk = Tl("ojunk", [128, D], BF16)
                ssq = Tl("ossq", [128, 2])
                sq4 = Tl("osq4", [128, 8])
                stg = [Tl("ostg%d" % i, [128, 16, 128], BF16) for i in range(2)]
                psM = [Pl("psM%d" % i, [128, 512]) for i in range(4)]
                pt = [Pl("opt%d" % i, [128, 16, 128], BF16) for i in range(2)]

                def load_mods(r):
                    load_gate_mod(l, G1, gp, "o", "post_mix_g", 4096, r)
                    load_norm_mod(l, gm, sh, gp, "o", "pre_ffn_g", 4 * 2048, 3 * 2048, r)

                def tile(tt, i):
                    S.dma("sp", lambda h: h.dma_start(out=mT[i % 2][:], in_=mixT[:, tt * 128:(tt + 1) * 128].rearrange("(k p) t -> p k t", p=128)),
                          w=[("mT", i % 2)])
                    S.dma("sp", lambda h: h.dma_start(out=xt[i % 2][:], in_=xsrc(l, tt)), w=[("xt", i % 2)])
                    for cb in range(4):
                        for k in range(16):
                            S.add("pe", lambda h, cb=cb, k=k: h.matmul(psM[cb][:], lhsT=mT[i % 2][:, k, :], rhs=wo[:, k, cb * 512:(cb + 1) * 512],
                                                                      start=(k == 0), stop=(k == 15)),
                                  r=[("mT", i % 2), ("wo", k // 4, cb // 2)], w=[("psM", cb)])
                    for cb in range(4):
                        S.add("act", lambda h, cb=cb: h.activation(out=junk[:, cb * 512:(cb + 1) * 512], in_=psM[cb][:], func=AF.Square,
                                                                   accum_out=sq4[:, cb:cb + 1]), r=[("psM", cb)], w=["ojunk", ("sq4", cb)])
                    S.add("dve", lambda h: h.tensor_reduce(out=sq4[:, 4:5], in_=sq4[:, 0:4], axis=AX.X, op=ALU.add), r=[("sq4", cb) for cb in range(4)], w=["mss"])
                    rstd_ops(sq4[:, 4:5], sq4[:, 5:6], 1.0 / D, ["mss"], ["mrstd"])
                    x1_ = x1[i % 2]
                    for cb in range(4):
                        S.add("dve", lambda h, cb=cb: h.scalar_tensor_tensor(out=x1_[:, cb * 512:(cb + 1) * 512], in0=psM[cb][:], scalar=sq4[:, 5:6],
                                                                            in1=G1[:, cb * 512:(cb + 1) * 512], op0=ALU.mult, op1=ALU.mult),
                              r=[("psM", cb), "mrstd", "oG"], w=[("x1", i % 2)])
                    S.add("dve", lambda h: h.tensor_tensor(out=x1_[:], in0=x1_[:], in1=xt[i % 2][:], op=ALU.add),
                          r=[("x1", i % 2), ("xt", i % 2)], w=[("x1", i % 2)])
                    S.dma("sp", lambda h: h.dma_start(out=x1s[tt * 128:(tt + 1) * 128, :], in_=x1_[:]), r=[("x1", i % 2)], w=[("x1s", tt)])

                    def dst(ptile, pkey, wkeys):
                        sg = stg[i % 2]
                        S.add("act", lambda h: h.activation(out=sg[:], in_=ptile[:], func=AF.Copy), r=[pkey], w=[("ostg", i % 2)])
                        S.dma("sp", lambda h: h.dma_start(out=h2T[:, tt * 128:(tt + 1) * 128].rearrange("(k p) t -> p k t", p=128), in_=sg[:]),
                              r=[("ostg", i % 2)], w=wkeys)
                    norm_transpose_tile(l, tt, x1_[:], ("x1", i % 2), gm, sh, "o", tmp, hb[i % 2], junk, ssq, pt[i % 2], dst, [("h2T", tt)], i)
                i = 0
                if not last:
                    load_mods(1)
                    for tt in range(2):
                        tile(tt, i)
                        i += 1
                load_mods(0)
                for tt in range(2, NT):
                    tile(tt, i)
                    i += 1
                S.phase_end()

        def phase_ffn(l):
            last = (l == nlayers - 1)
            if last:
                halves = [(256, 1024), (1280, 1024)]
            else:
                halves = [(0, 1152), (1152, 1152)]
            wg_v = I["w_gate"][l].rearrange("(k p) n -> p k n", p=128)
            wu_v = I["w_up"][l].rearrange("(k p) n -> p k n", p=128)
            wd_v = I["w_down"][l].rearrange("(k p) n -> p k n", p=128)
            NJ = DFF // 128
            for (t0, nt) in halves:
                with ExitStack() as st_o:
                    To, Po = mk_alloc(st_o)
                    aT = To("aT", [128, NJ, nt], BF16)
                    with ExitStack() as st:
                        Tl, Pl = mk_alloc(st)
                        hh_ = Tl("h2h", [128, 16, nt], BF16)
                        for k4 in range(4):
                            S.dma("sp", lambda h, k4=k4: h.dma_start(out=hh_[:, 4 * k4:4 * k4 + 4, :],
                                                                     in_=h2T[:, t0:t0 + nt].rearrange("(k p) t -> p k t", p=128)[:, 4 * k4:4 * k4 + 4, :]),
                                  w=[("h2h", k4)])
                        wg = [Tl("wg%d" % i, [128, 16, 256], BF16) for i in range(2)]
                        wu = [Tl("wu%d" % i, [128, 16, 256], BF16) for i in range(2)]
                        psg = [Pl("psg%d" % i, [128, 512]) for i in range(3)]
                        psu = [Pl("psu%d" % i, [128, 512]) for i in range(3)]
                        ee = [Tl("fe%d" % i, [128, 512]) for i in range(2)]
                        tg = [Tl("ftg%d" % i, [128, 512]) for i in range(2)]
                        tbs = [(a, min(512, nt - a)) for a in range(0, nt, 512)]
                        cnt = [0]

                        def wblock(jb):
                            b = jb % 2
                            for k4 in range(4):
                                S.dma("pool", lambda h, k4=k4: h.dma_start(out=wg[b][:, 4 * k4:4 * k4 + 4, :], in_=wg_v[:, 4 * k4:4 * k4 + 4, jb * 256:(jb + 1) * 256]),
                                      w=[("wg", b, k4)])
                                S.dma("pool", lambda h, k4=k4: h.dma_start(out=wu[b][:, 4 * k4:4 * k4 + 4, :], in_=wu_v[:, 4 * k4:4 * k4 + 4, jb * 256:(jb + 1) * 256]),
                                      w=[("wu", b, k4)])
                            for jj in range(2):
                                j = jb * 2 + jj
                                for (a, n) in tbs:
                                    i = cnt[0]
                                    cnt[0] += 1
                                    pg = psg[i % 3]
                                    pu = psu[i % 3]
                                    for k in range(16):
                                        S.add("pe", lambda h, k=k, pg=pg, a=a, n=n, jj=jj: h.matmul(pg[:, 0:n], lhsT=wg[b][:, k, jj * 128:(jj + 1) * 128],
                                                                                             rhs=hh_[:, k, a:a + n], start=(k == 0), stop=(k == 15)),
                                              r=[("wg", b, k // 4), ("h2h", k // 4)], w=[("psg", i % 3)])
                                    for k in range(16):
                                        S.add("pe", lambda h, k=k, pu=pu, a=a, n=n, jj=jj: h.matmul(pu[:, 0:n], lhsT=wu[b][:, k, jj * 128:(jj + 1) * 128],
                                                                                             rhs=hh_[:, k, a:a + n], start=(k == 0), stop=(k == 15)),
                                              r=[("wu", b, k // 4), ("h2h", k // 4)], w=[("psu", i % 3)])
                                    e_ = ee[i % 2]
                                    t_ = tg[i % 2]
                                    S.add("act", lambda h, pg=pg, e_=e_, n=n: h.activation(out=e_[:, 0:n], in_=pg[:, 0:n], func=AF.Tanh, scale=0.5),
                                          r=[("psg", i % 3)], w=[("fe", i % 2)])
                                    S.add("dve", lambda h, e_=e_, t_=t_, pg=pg, n=n: h.scalar_tensor_tensor(out=t_[:, 0:n], in0=e_[:, 0:n], scalar=1.0, in1=pg[:, 0:n],
                                                                                                     op0=ALU.add, op1=ALU.mult),
                                          r=[("fe", i % 2), ("psg", i % 3)], w=[("ftg", i % 2)])
                                    S.add("dve", lambda h, t_=t_, pu=pu, n=n, a=a, j=j: h.scalar_tensor_tensor(out=aT[:, j, a:a + n], in0=t_[:, 0:n], scalar=0.5, in1=pu[:, 0:n],
                                                                                                        op0=ALU.mult, op1=ALU.mult),
                                          r=[("ftg", i % 2), ("psu", i % 3)], w=[("aT", j, a)])
                        for jb in range(NJ // 2):
                            wblock(jb)
                        S.phase_end()
                    with ExitStack() as st:
                        Tl, Pl = mk_alloc(st)
                        wd = [Tl("wd%d" % i, [128, NJ, 256], BF16) for i in range(2)]
                        psd = [Pl("psd%d" % i, [128, 256]) for i in range(4)]
                        stg = [Tl("fstg%d" % i, [128, 256]) for i in range(4)]
                        cnt = [0]

                        def dblock(cb):
                            b = cb % 2
                            for k4 in range(4):
                                S.dma("pool", lambda h, k4=k4: h.dma_start(out=wd[b][:, 11 * k4:11 * k4 + 11, :], in_=wd_v[:, 11 * k4:11 * k4 + 11, cb * 256:(cb + 1) * 256]),
                                      w=[("wd", b, k4)])
                            for a in range(0, nt, 128):
                                i = cnt[0]
                                cnt[0] += 1
                                p_ = psd[i % 4]
                                for k in range(NJ):
                                    S.add("pe", lambda h, k=k, p_=p_, a=a: h.matmul(p_[:], lhsT=aT[:, k, a:a + 128], rhs=wd[b][:, k, :], start=(k == 0), stop=(k == NJ - 1)),
                                          r=[("wd", b, k // 11)], w=[("psd", i % 4)])
                                s_ = stg[i % 4]
                                if i % 2 == 0:
                                    S.add("act", lambda h, p_=p_, s_=s_: h.activation(out=s_[:], in_=p_[:], func=AF.Copy), r=[("psd", i % 4)], w=[("fstg", i % 4)])
                                else:
                                    S.add("dve", lambda h, p_=p_, s_=s_: h.tensor_copy(out=s_[:], in_=p_[:]), r=[("psd", i % 4)], w=[("fstg", i % 4)])
                                S.dma("sp", lambda h, s_=s_, a=a: h.dma_start(out=fsc[t0 + a:t0 + a + 128, cb * 256:(cb + 1) * 256], in_=s_[:]),
                                      r=[("fstg", i % 4)], w=[("fsc", i)])
                        for cb in range(8):
                            dblock(cb)
                        S.phase_end()

        def phase_fin(l):
            last = (l == nlayers - 1)
            with ExitStack() as st:
                Tl, Pl = mk_alloc(st)
                G2 = Tl("G2", [128, D])
                gp = Tl("fgp", [128, D])
                xa = [Tl("fxa%d" % i, [128, D]) for i in range(2)]
                fa = [Tl("ffa%d" % i, [128, D]) for i in range(2)]
                xo = [Tl("fxo%d" % i, [128, D]) for i in range(2)]
                junk = Tl("fjunk", [128, D], BF16)
                ssq = Tl("fssq", [128, 2])

                def tile(tt, i):
                    S.dma("sp", lambda h: h.dma_start(out=xa[i % 2][:], in_=x1s[tt * 128:(tt + 1) * 128, :]), w=[("xa", i % 2)])
                    S.dma("sp", lambda h: h.dma_start(out=fa[i % 2][:], in_=fsc[tt * 128:(tt + 1) * 128, :]), w=[("fa", i % 2)])
                    S.add("act", lambda h: h.activation(out=junk[:], in_=fa[i % 2][:], func=AF.Square, accum_out=ssq[:, 0:1]), r=[("fa", i % 2)], w=["fjunk", "fss"])
                    rstd_ops(ssq[:, 0:1], ssq[:, 1:2], 1.0 / D, ["fss"], ["frstd"])
                    S.add("dve", lambda h: h.scalar_tensor_tensor(out=xo[i % 2][:], in0=fa[i % 2][:], scalar=ssq[:, 1:2], in1=G2[:], op0=ALU.mult, op1=ALU.mult),
                          r=[("fa", i % 2), "frstd", "fG"], w=[("xo", i % 2)])
                    S.add("dve", lambda h: h.tensor_tensor(out=xo[i % 2][:], in0=xo[i % 2][:], in1=xa[i % 2][:], op=ALU.add), r=[("xo", i % 2), ("xa", i % 2)], w=[("xo", i % 2)])
                    if last:
                        dst = out[(tt - 2) * 128:(tt - 1) * 128, :]
                    else:
                        dst = xnext[tt * 128:(tt + 1) * 128, :]
                    S.dma("sp", lambda h: h.dma_start(out=dst, in_=xo[i % 2][:]), r=[("xo", i % 2)], w=[("xout", tt)])
                i = 0
                if not last:
                    load_gate_mod(l, G2, gp, "f", "post_ffn_g", 5 * 2048, 1)
                    for tt in range(2):
                        tile(tt, i)
                        i += 1
                load_gate_mod(l, G2, gp, "f", "post_ffn_g", 5 * 2048, 0)
                for tt in range(2, NT):
                    tile(tt, i)
                    i += 1
                S.phase_end()

        PH = {}
        PH["out"] = phase_out
        PH["ffn"] = phase_ffn
        PH["fin"] = phase_fin
        PH["ssd"] = phase_ssd
        PH["ret"] = phase_ret
        PH["att"] = phase_att

        PH["in"] = phase_in
        try:
            if phases is None or "mod" in phases:
                phase_mod()
            check_stop("mod")
            for l in (layers if layers is not None else range(nlayers)):
                for nm in ("in", "att", "ret", "ssd", "out", "ffn", "fin"):
                    if nm in PH and (phases is None or nm in phases):
                        PH[nm](l)
                        check_stop("%s%d" % (nm, l))
        except Stop:
            pass
        S.phase_end()
        stats = S.emit()
    return nc, stats, list(I.keys())


def make_in_maps(inputs):
    consts = make_consts()
    rope = make_rope()
    maps = []
    shared = {n: np.ascontiguousarray(inputs[n], dtype=np.float32) for n, _ in SMALL + BIG}
    for b in range(8):
        m = dict(shared)
        m["x"] = np.ascontiguousarray(inputs["x"][b])
        m["ctx"] = np.ascontiguousarray(inputs["ctx"][b])
        m["cvec"] = np.ascontiguousarray(np.stack([inputs["c"][b], inputs["c_ctx"]]))
        m["consts"] = consts
        m["rope"] = rope
        maps.append(m)
    return maps


def kernel(**inputs):
    nc, _, _ = build(nlayers=2, debug=False)
    maps = make_in_maps(inputs)
    res = run_bass_kernel_spmd(nc, maps, core_ids=list(range(8)))
    return np.stack([np.asarray(r["out"], dtype=np.float32) for r in res.results], axis=0)
```

```python
import math
import numpy as np
from contextlib import ExitStack
import concourse.bass as bass
import concourse.mybir as mybir
from concourse.bass_utils import run_bass_kernel_spmd

F32 = mybir.dt.float32
BF16 = mybir.dt.bfloat16
AF = mybir.ActivationFunctionType
ALU = mybir.AluOpType
AX = mybir.AxisListType

D = 2048
T = 2304
NT = 18
DIN = 5400
DFF = 5632
EPS = 1e-6
NCONST = 10 * 128 + 2
PW = 4120


class _Op:
    __slots__ = ("eng", "fn", "deps", "ch", "pos", "is_dma", "vc", "waits", "signal", "rank")


class Sched:
    ENGS = ("pe", "act", "dve", "pool", "sp")

    def __init__(self, nc, stack, n_dma_sems=12):
        self.nc = nc
        self.h = {"pe": nc.tensor, "act": nc.scalar, "dve": nc.vector, "pool": nc.gpsimd, "sp": nc.sync}
        self.ops = []
        self.n_emitted = 0
        self.eng_pos = {e: 0 for e in self.ENGS}
        self.last_w = {}
        self.readers = {}
        self.esem = {e: stack.enter_context(nc.semaphore("s_" + e)) for e in self.ENGS}
        self.dsems = {}
        self.dcount = {}
        self.drr = {}
        for q in ("sp", "pool"):
            self.dsems[q] = [stack.enter_context(nc.semaphore("d_%s%d" % (q, i))) for i in range(n_dma_sems)]
            self.drr[q] = 0
            for i in range(n_dma_sems):
                self.dcount[(q, i)] = 0
        self.by_chpos = {}
        self.last_on_ch = {}
        self.clock = {e: {} for e in self.ENGS}
        self.rk = {e: 0 for e in self.ENGS}
        self.nw = 0

    def _deps(self, r, w):
        deps = []
        for k in r:
            o = self.last_w.get(k)
            if o is not None:
                deps.append(o)
        for k in w:
            o = self.last_w.get(k)
            if o is not None:
                deps.append(o)
            deps.extend(self.readers.get(k, ()))
        return deps

    def _commit(self, op, r, w):
        for k in r:
            self.readers.setdefault(k, []).append(op)
        for k in w:
            self.last_w[k] = op
            self.readers[k] = []
        self.ops.append(op)
        self.by_chpos[(op.ch, op.pos)] = op
        self.last_on_ch[op.ch] = op

    def add(self, eng, fn, r=(), w=()):
        op = _Op()
        op.eng = eng
        op.fn = fn
        op.is_dma = False
        op.deps = self._deps(r, w)
        self.eng_pos[eng] += 1
        op.ch = eng
        op.pos = self.eng_pos[eng]
        op.signal = False
        self._commit(op, r, w)
        return op

    def dma(self, q, fn, r=(), w=()):
        op = _Op()
        op.eng = q
        op.fn = fn
        op.is_dma = True
        op.deps = self._deps(r, w)
        i = self.drr[q]
        self.drr[q] = (i + 1) % len(self.dsems[q])
        self.dcount[(q, i)] += 1
        op.ch = ("d", q, i)
        op.pos = self.dcount[(q, i)]
        op.signal = True
        self._commit(op, r, w)
        return op

    def barrier(self):
        lasts = [o for o in self.last_on_ch.values() if o.fn is not None or o.is_dma]
        for e in self.ENGS:
            op = _Op()
            op.eng = e
            op.fn = None
            op.is_dma = False
            op.deps = list(lasts)
            self.eng_pos[e] += 1
            op.ch = e
            op.pos = self.eng_pos[e]
            op.signal = False
            self.ops.append(op)
            self.by_chpos[(op.ch, op.pos)] = op
        self.last_w = {}
        self.readers = {}

    def emit(self):
        ops = self.ops[self.n_emitted:]
        clock = self.clock
        for op in ops:
            E = op.eng
            ck = clock[E]
            need = {}
            for d in op.deps:
                if (not d.is_dma) and d.eng == "pe" and E == "pe" and (not op.is_dma) and op.fn is not None:
                    continue
                if ck.get(d.ch, 0) < d.pos and need.get(d.ch, 0) < d.pos:
                    need[d.ch] = d.pos
            if op.is_dma and op.pos > 1:
                if ck.get(op.ch, 0) < op.pos - 1 and need.get(op.ch, 0) < op.pos - 1:
                    need[op.ch] = op.pos - 1
            op.waits = []
            for ch, pos in need.items():
                if ck.get(ch, 0) >= pos:
                    continue
                p = self.by_chpos[(ch, pos)]
                p.signal = True
                op.waits.append(p)
                for c, v in p.vc.items():
                    if ck.get(c, 0) < v:
                        ck[c] = v
            vc = dict(ck)
            vc[op.ch] = op.pos
            op.vc = vc
        for op in ops:
            if op.is_dma:
                op.rank = 16 * op.pos
            elif op.signal:
                self.rk[op.eng] += 1
                op.rank = self.rk[op.eng]
        for op in ops:
            h = self.h[op.eng]
            for p in op.waits:
                if p.is_dma:
                    sem = self.dsems[p.ch[1]][p.ch[2]]
                else:
                    sem = self.esem[p.eng]
                h.wait_ge(sem, p.rank)
                self.nw += 1
            if op.fn is None:
                continue
            inst = op.fn(h)
            if op.is_dma:
                inst.then_inc(self.dsems[op.ch[1]][op.ch[2]], 16)
            elif op.signal:
                inst.then_inc(self.esem[op.eng], 1)
            op.fn = None
        self.n_emitted = len(self.ops)
        return dict(n_ops=len(self.ops), n_waits=self.nw, ranks=dict(self.rk))

    def phase_end(self, name=""):
        npe = sum(1 for o in self.ops if o.eng == "pe" and not o.is_dma and (o.fn is not None))
        self.phase_log = getattr(self, "phase_log", [])
        self.pe_total = getattr(self, "pe_total", 0) + npe
        self.phase_log.append((name, self.pe_total))
        self.barrier()
        r = self.emit()
        self.ops = []
        self.n_emitted = 0
        return r


def make_consts():
    i = np.arange(128)
    J, I = np.meshgrid(i, i, indexing="ij")
    c = np.zeros((128, NCONST), np.float32)
    c[:, 0:128] = (J == I)
    c[:, 128:256] = (J <= I)
    c[:, 256:384] = (J >= I)
    c[:, 384:512] = 1.0
    c[:, 512:640] = np.where(I >= J, 0.0, -30000.0)
    c[:, 640:768] = np.where(I <= J, 0.0, -30000.0)
    c[:, 768:896] = np.maximum(I - J, 0)
    c[:, 896:1024] = np.maximum(J - I, 0)
    c[:, 1024:1152] = I + 1
    c[:, 1152:1280] = 128 - I
    c[:, 1280] = 127 - i
    c[:, 1281] = i
    return c


def make_rope():
    rows = 2048 // 64
    row = np.repeat(np.arange(rows, dtype=np.float32), 64)
    col = np.tile(np.arange(64, dtype=np.float32), rows)
    n_freq = 32
    inv = (np.float32(10000.0) ** (-np.arange(n_freq, dtype=np.float32) / n_freq)).astype(np.float32)
    ang = np.concatenate([row[:, None] * inv, col[:, None] * inv], axis=-1).astype(np.float32)
    return np.concatenate([np.cos(ang), np.sin(ang)], axis=-1).astype(np.float32)


SMALL = [("b_mod", [2, 12288]), ("pre_mix_g", [2, 2048]), ("post_mix_g", [2, 2048]), ("pre_ffn_g", [2, 2048]),
         ("post_ffn_g", [2, 2048]), ("q_norm_g", [2, 128]), ("k_norm_g", [2, 128]), ("ret_decay_f", [2, 4]),
         ("ret_decay_b", [2, 4]), ("conv_w", [2, 5, 1280]), ("conv_b", [2, 1280]), ("dt_bias_f", [2, 12]),
         ("dt_bias_b", [2, 12]), ("a_log_f", [2, 12]), ("a_log_b", [2, 12]), ("d_skip", [2, 12]),
         ("ssd_norm_g", [2, 768])]
BIG = [("w_mod", [2, 2048, 12288]), ("w_in", [2, 2048, DIN]), ("w_out", [2, 2048, 2048]),
       ("w_gate", [2, 2048, DFF]), ("w_up", [2, 2048, DFF]), ("w_down", [2, DFF, 2048])]


def build(nlayers=2, debug=False, stop=None, phases=None, feed=(), layers=None):
    nc = bass.Bass("TRN2", target_bir_lowering=False)
    SHAPES = dict(SMALL + BIG)
    SHAPES.update({"x": [2048, 2048], "ctx": [256, 2048], "cvec": [2, 2048], "consts": [128, NCONST], "rope": [2048, 128]})

    class LazyIn(dict):
        def __missing__(self, name):
            ap = nc.dram_tensor(name, SHAPES[name], F32, kind="ExternalInput").ap()
            self[name] = ap
            return ap

    I = LazyIn()
    if not debug:
        for n in ["x", "ctx", "cvec", "consts", "rope"] + [n for n, _ in SMALL + BIG]:
            I[n]
    out = nc.dram_tensor("out", [2048, 2048], F32, kind="ExternalOutput").ap()
    skind = "ExternalOutput" if debug else "Internal"

    def scr(name, shape, dt=F32):
        k = "ExternalInput" if name in feed else skind
        return nc.dram_tensor(name, shape, dt, kind=k).ap()

    modv = scr("modv", [2, 2, 12288])
    proj = scr("proj", [T, PW])
    xbcT = scr("xbcT", [1280, T])
    mixT = scr("mixT", [2048, T], BF16)
    x1s = scr("x1s", [T, D])
    h2T = scr("h2T", [2048, T], BF16)
    fsc = scr("fsc", [T, D])
    xnext = scr("xnext", [T, D])
    zsd = scr("zsd", [T, 768], BF16)

    class Stop(Exception):
        pass

    with ExitStack() as gst:
        S = Sched(nc, gst)

        uid = [0]

        def mk_alloc(st):
            def Tl(name, shape, dt=F32):
                uid[0] += 1
                return st.enter_context(nc.sbuf_tensor("%s_%d" % (name, uid[0]), shape, dt))

            def Pl(name, shape, dt=F32):
                uid[0] += 1
                return st.enter_context(nc.psum_tensor("%s_%d" % (name, uid[0]), shape, dt))
            return Tl, Pl

        GT, GP = mk_alloc(gst)
        cst = GT("cst", [128, NCONST])
        ident_b = GT("ident_b", [128, 128], BF16)
        ones_b = GT("ones_b", [128, 128], BF16)
        S.dma("sp", lambda h: h.dma_start(out=cst[:], in_=I["consts"][:, :]), w=["cst"])
        S.add("dve", lambda h: h.tensor_copy(out=ident_b[:], in_=cst[:, 0:128]), r=["cst"], w=["ident_b"])
        S.add("dve", lambda h: h.tensor_copy(out=ones_b[:], in_=cst[:, 384:512]), r=["cst"], w=["ones_b"])
        ident_f = cst[:, 0:128]
        tri_f = cst[:, 128:256]
        tri_b = cst[:, 256:384]
        ones_f = cst[:, 384:512]
        mneg = [cst[:, 512:640], cst[:, 640:768]]
        D1 = cst[:, 768:896]
        D2 = cst[:, 896:1024]
        rowidx = [cst[:, 1024:1152], cst[:, 1152:1280]]
        colexp = [cst[:, 1280:1281], cst[:, 1281:1282]]
        S.phase_end("init")

        def check_stop(tag):
            if stop == tag:
                raise Stop()

        def rstd_ops(ssq_ap, out_ap, inv_n, rk, wk):
            S.add("act", lambda h: h.activation(out=out_ap, in_=ssq_ap, func=AF.Ln, scale=inv_n, bias=EPS), r=rk, w=wk)
            S.add("act", lambda h: h.activation(out=out_ap, in_=out_ap, func=AF.Exp, scale=-0.5), r=wk, w=wk)

        def xsrc(l, tt):
            if l == 0:
                if tt < 2:
                    return I["ctx"][tt * 128:(tt + 1) * 128, :]
                return I["x"][(tt - 2) * 128:(tt - 1) * 128, :]
            return xnext[tt * 128:(tt + 1) * 128, :]

        def bcast_row(ap_row, n):
            return ap_row.to_broadcast([128, n])

        def mod_task(l, Tl, Pl):
            cT = Tl("cT", [128, 16, 2])
            ce = Tl("ce", [128, 16, 2])
            sT = Tl("sT", [128, 16, 2], BF16)
            for r_ in range(2):
                S.dma("sp", lambda h, r_=r_: h.dma_start(out=cT[:, :, r_], in_=I["cvec"][r_].rearrange("(k p) -> p k", p=128),
                                                         allow_slow_non_contiguous=True), w=["mcT"])
            S.add("act", lambda h: h.activation(out=ce[:], in_=cT[:], func=AF.Exp, scale=-1.0), r=["mcT"], w=["mce"])
            S.add("dve", lambda h: h.tensor_scalar(out=ce[:], in0=ce[:], scalar1=1.0, scalar2=None, op0=ALU.add), r=["mce"], w=["mce"])
            S.add("dve", lambda h: h.reciprocal(out=ce[:], in_=ce[:]), r=["mce"], w=["mce"])
            S.add("dve", lambda h: h.tensor_tensor(out=sT[:], in0=cT[:], in1=ce[:], op=ALU.mult), r=["mce", "mcT"], w=["msT"])
            wb = [Tl("wmb%d" % i, [128, 16, 512], BF16) for i in range(2)]
            pm = Pl("pm", [128, 512])
            bsb = [Tl("bsb%d" % i, [2, 512]) for i in range(2)]
            msb = [Tl("msb%d" % i, [2, 512]) for i in range(2)]
            wv = I["w_mod"][l].rearrange("(k p) n -> p k n", p=128)
            yield

            def block(nb):
                b = nb % 2
                S.dma("sp", lambda h: h.dma_start(out=bsb[b][:], in_=I["b_mod"][l:l + 1, nb * 512:(nb + 1) * 512].to_broadcast([2, 512])), w=[("mbsb", b)])
                for k4 in range(4):
                    S.dma("pool", lambda h, k4=k4: h.dma_start(out=wb[b][:, 4 * k4:4 * k4 + 4, :], in_=wv[:, 4 * k4:4 * k4 + 4, nb * 512:(nb + 1) * 512]),
                          w=[("wmb", b, k4)])
                for k in range(16):
                    S.add("pe", lambda h, k=k: h.matmul(pm[0:2, :], lhsT=sT[:, k, :], rhs=wb[b][:, k, :], start=(k == 0), stop=(k == 15)),
                          r=["msT", ("wmb", b, k // 4)], w=["mpm"])
                S.add("dve", lambda h: h.tensor_tensor(out=msb[b][:], in0=pm[0:2, :], in1=bsb[b][:], op=ALU.add), r=["mpm", ("mbsb", b)], w=[("mmsb", b)])
                S.dma("sp", lambda h: h.dma_start(out=modv[l, :, nb * 512:(nb + 1) * 512], in_=msb[b][:]), r=[("mmsb", b)], w=[("modv", l, nb)])
            for nb in range(24):
                block(nb)
                yield

        def phase_mod(l):
            with ExitStack() as st:
                Tl, Pl = mk_alloc(st)
                for _ in mod_task(l, Tl, Pl):
                    pass
                S.phase_end("mod")

        def norm_mod_tiles(l, Tl, tagp, gname, sc_off, sh_off, r):
            gm = Tl(tagp + "gm", [128, D])
            sh = Tl(tagp + "sh", [128, D])
            gp = Tl(tagp + "gp", [128, D])
            return gm, sh, gp

        def load_norm_mod(l, gm, sh, gp, key, gname, sc_off, sh_off, r):
            S.dma("sp", lambda h: h.dma_start(out=gp[:], in_=bcast_row(I[gname][l:l + 1, :], D)), w=[key + "gp"])
            S.dma("sp", lambda h: h.dma_start(out=gm[:], in_=bcast_row(modv[l, r:r + 1, sc_off:sc_off + D], D)), w=[key + "gm"])
            S.dma("sp", lambda h: h.dma_start(out=sh[:], in_=bcast_row(modv[l, r:r + 1, sh_off:sh_off + D], D)), w=[key + "sh"])
            S.add("dve", lambda h: h.scalar_tensor_tensor(out=gm[:], in0=gm[:], scalar=1.0, in1=gp[:], op0=ALU.add, op1=ALU.mult),
                  r=[key + "gp", key + "gm"], w=[key + "gm"])

        def load_gate_mod(l, G, gp, key, gname, g_off, r):
            S.dma("sp", lambda h: h.dma_start(out=gp[:], in_=bcast_row(I[gname][l:l + 1, :], D)), w=[key + "gp"])
            S.dma("sp", lambda h: h.dma_start(out=G[:], in_=bcast_row(modv[l, r:r + 1, g_off:g_off + D], D)), w=[key + "G"])
            S.add("dve", lambda h: h.tensor_tensor(out=G[:], in0=G[:], in1=gp[:], op=ALU.mult), r=[key + "gp", key + "G"], w=[key + "G"])

        def norm_transpose_tile(l, tt, xt_ap, xkey, gm, sh, mkey, tmp, hb, junk, ssq, pt, dst_fn, wkeys, i):
            sq = ssq[:, 2 * (i % 2):2 * (i % 2) + 1]
            rs = ssq[:, 2 * (i % 2) + 1:2 * (i % 2) + 2]
            S.add("act", lambda h: h.activation(out=junk[:], in_=xt_ap, func=AF.Square, accum_out=sq),
                  r=[xkey], w=["junk", ("ssq", i % 2)])
            rstd_ops(sq, rs, 1.0 / D, [("ssq", i % 2)], [("rstd", i % 2)])
            S.add("dve", lambda h: h.scalar_tensor_tensor(out=tmp[:], in0=xt_ap, scalar=rs, in1=gm[:], op0=ALU.mult, op1=ALU.mult),
                  r=[xkey, ("rstd", i % 2), mkey + "gm"], w=["tmp"])
            S.add("dve", lambda h: h.tensor_tensor(out=hb[:], in0=tmp[:], in1=sh[:], op=ALU.add), r=["tmp", mkey + "sh"], w=[("hb", i % 2)])
            for k in range(16):
                S.add("pe", lambda h, k=k: h.transpose(out=pt[:, k, :], in_=hb[:, k * 128:(k + 1) * 128], identity=ident_b[:]),
                      r=[("hb", i % 2)], w=[("pt", i % 2)])
            return lambda: dst_fn(pt, ("pt", i % 2), wkeys)

        def phase_in(l):
            with ExitStack() as st_o:
                To, Po = mk_alloc(st_o)
                hT = To("hT", [128, 16, T], BF16)
                with ExitStack() as st:
                    Tl, Pl = mk_alloc(st)
                    mods = {}
                    for r, nm in ((1, "c"), (0, "l")):
                        gm = Tl("gm" + nm, [128, D])
                        sh = Tl("sh" + nm, [128, D])
                        gp = Tl("gp" + nm, [128, D])
                        load_norm_mod(l, gm, sh, gp, nm, "pre_mix_g", 2048, 0, r)
                        mods[r] = (gm, sh, nm)
                    xt = [Tl("xt%d" % i, [128, D]) for i in range(3)]
                    tmp = Tl("tmp", [128, D])
                    hb = [Tl("hb%d" % i, [128, D], BF16) for i in range(2)]
                    junk = Tl("junk", [128, D], BF16)
                    ssq = Tl("ssq", [128, 4])
                    pt = [Pl("pt%d" % i, [128, 16, 128], BF16) for i in range(2)]
                    pend = None
                    for tt in range(NT):
                        i = tt
                        gm, sh, nm = mods[1 if tt < 2 else 0]
                        S.dma("sp", lambda h, tt=tt, i=i: h.dma_start(out=xt[i % 3][:], in_=xsrc(l, tt)), w=[("xt", i % 3)])

                        def dst(ptile, pkey, wkeys, tt=tt):
                            S.add("act", lambda h: h.activation(out=hT[:, :, tt * 128:(tt + 1) * 128], in_=ptile[:], func=AF.Copy),
                                  r=[pkey], w=wkeys)
                        th = norm_transpose_tile(l, tt, xt[i % 3][:], ("xt", i % 3), gm, sh, nm, tmp, hb[i % 2], junk, ssq, pt[i % 2], dst,
                                                 [("hT", tt)], i)
                        if pend is not None:
                            pend()
                        pend = th
                    pend()
                    S.phase_end("P1")
                with ExitStack() as st:
                    Tl, Pl = mk_alloc(st)
                    wb = [Tl("wib%d" % i, [128, 16, 512], BF16) for i in range(2)]
                    stg = [Tl("stg%d" % i, [128, 512]) for i in range(4)]
                    pp = [Pl("pp%d" % i, [128, 512]) for i in range(4)]
                    wv = I["w_in"][l].rearrange("(k p) n -> p k n", p=128)
                    cnt = [0]

                    def evac(ps_ap, n, dram_ap, pkey):
                        i = cnt[0]
                        cnt[0] += 1
                        s = stg[i % 4]
                        if i % 2 == 0:
                            S.add("act", lambda h: h.activation(out=s[:, 0:n], in_=ps_ap, func=AF.Copy), r=[pkey], w=[("stg", i % 4)])
                        else:
                            S.add("dve", lambda h: h.tensor_copy(out=s[:, 0:n], in_=ps_ap), r=[pkey], w=[("stg", i % 4)])
                        S.dma("sp", lambda h: h.dma_start(out=dram_ap, in_=s[:, 0:n]), r=[("stg", i % 4)], w=[("dram", i)])

                    blocks = [(c0, 512) for c0 in range(0, 5120, 512)] + [(5120, 280)]
                    pi = [0]
                    for bi, (c0, ncol) in enumerate(blocks):
                        b = bi % 2
                        for k4 in range(4):
                            S.dma("pool", lambda h, b=b, k4=k4, c0=c0, ncol=ncol: h.dma_start(
                                out=wb[b][:, 4 * k4:4 * k4 + 4, 0:ncol], in_=wv[:, 4 * k4:4 * k4 + 4, c0:c0 + ncol]), w=[("wib", b, k4)])
                        wkeys = [("wib", b, k4) for k4 in range(4)]
                        if c0 < 4096:
                            tm = (0, ncol, c0)
                            fm_chunks = []
                        elif c0 < 5120:
                            tm = None
                            fm_chunks = [(j, (c0 - 4096) // 128 + j) for j in range(4)]
                        else:
                            tm = (256, 24, 4096)
                            fm_chunks = [(0, 8), (1, 9)]
                        if tm is not None:
                            co, n, dc = tm
                            for tt in range(NT):
                                p = pi[0] % 4
                                pi[0] += 1
                                for k in range(16):
                                    S.add("pe", lambda h, p=p, k=k, tt=tt, co=co, n=n, b=b: h.matmul(
                                        pp[p][:, 0:n], lhsT=hT[:, k, tt * 128:(tt + 1) * 128], rhs=wb[b][:, k, co:co + n],
                                        start=(k == 0), stop=(k == 15)), r=wkeys, w=[("pp", p)])
                                evac(pp[p][:, 0:n], n, proj[tt * 128:(tt + 1) * 128, dc:dc + n], ("pp", p))
                        for (j, ch) in fm_chunks:
                            for t0 in range(0, T, 512):
                                n = min(512, T - t0)
                                p = pi[0] % 4
                                pi[0] += 1
                                for k in range(16):
                                    S.add("pe", lambda h, p=p, k=k, t0=t0, n=n, j=j, b=b: h.matmul(
                                        pp[p][:, 0:n], lhsT=wb[b][:, k, j * 128:(j + 1) * 128], rhs=hT[:, k, t0:t0 + n],
                                        start=(k == 0), stop=(k == 15)), r=wkeys, w=[("pp", p)])
                                evac(pp[p][:, 0:n], n, xbcT[ch * 128:(ch + 1) * 128, t0:t0 + n], ("pp", p))
                    S.phase_end("P2")

        def rope_ops(src, dst, cs, nh, lat, skey, dkey, cskey, tmps):
            if not lat:
                S.add("dve", lambda h: h.tensor_copy(out=dst, in_=src), r=[skey], w=[dkey, (dkey, "b")])
                return
            t1, t2, t3, t4 = tmps
            s4 = src.rearrange("p (h i two) -> p h i two", h=nh, two=2)
            d4 = dst.rearrange("p (h i two) -> p h i two", h=nh, two=2)
            x1 = s4[:, :, :, 0]
            x2 = s4[:, :, :, 1]
            cosb = cs[:, 0:64].unsqueeze(1).to_broadcast([128, nh, 64])
            sinb = cs[:, 64:128].unsqueeze(1).to_broadcast([128, nh, 64])
            v = lambda t: t[:, 0:nh * 64].rearrange("p (h i) -> p h i", h=nh)
            S.add("dve", lambda h: h.tensor_tensor(out=v(t1), in0=x1, in1=cosb, op=ALU.mult), r=[skey, cskey], w=["rt1"])
            S.add("dve", lambda h: h.tensor_tensor(out=v(t2), in0=x2, in1=sinb, op=ALU.mult), r=[skey, cskey], w=["rt2"])
            S.add("dve", lambda h: h.tensor_tensor(out=d4[:, :, :, 0], in0=v(t1), in1=v(t2), op=ALU.subtract), r=["rt1", "rt2"], w=[dkey])
            S.add("dve", lambda h: h.tensor_tensor(out=v(t3), in0=x1, in1=sinb, op=ALU.mult), r=[skey, cskey], w=["rt3"])
            S.add("dve", lambda h: h.tensor_tensor(out=v(t4), in0=x2, in1=cosb, op=ALU.mult), r=[skey, cskey], w=["rt4"])
            S.add("dve", lambda h: h.tensor_tensor(out=d4[:, :, :, 1], in0=v(t3), in1=v(t4), op=ALU.add), r=["rt3", "rt4"], w=[(dkey, "b")])

        def phase_att(l):
            last = (l == nlayers - 1)
            with ExitStack() as st:
                Tl, Pl = mk_alloc(st)
                QKT = Tl("QKT", [128, 8, T], BF16)
                Vtm = Tl("Vtm", [128, NT, 256], BF16)
                gqk = Tl("gqk", [128, 8, 128])
                pr = [Tl("pr%d" % i, [128, 1280]) for i in range(2)]
                sq = Tl("sq", [128, 1024])
                qn = Tl("qn", [128, 1024])
                tmps = [Tl("rt%d" % i, [128, 512]) for i in range(4)]
                qr = [Tl("qr%d" % i, [128, 1024], BF16) for i in range(2)]
                cs = [Tl("cs%d" % i, [128, 128]) for i in range(2)]
                st8 = Tl("st8", [128, 16])
                ptq = Pl("ptq", [128, 8, 128], BF16)
                for hh in range(8):
                    src = I["q_norm_g"] if hh < 6 else I["k_norm_g"]
                    S.dma("sp", lambda h, hh=hh, src=src: h.dma_start(out=gqk[:, hh, :], in_=src[l:l + 1, :].to_broadcast([128, 128])), w=["gqk"])
                S.add("dve", lambda h: h.tensor_scalar(out=gqk[:, 0:6, :], in0=gqk[:, 0:6, :], scalar1=float(128 ** -0.5), scalar2=None, op0=ALU.mult),
                      r=["gqk"], w=["gqk"])
                def prep_tile(tt):
                    i = tt
                    lat = tt >= 2
                    p_ = pr[i % 2]
                    S.dma("sp", lambda h, tt=tt, p_=p_: h.dma_start(out=p_[:], in_=proj[tt * 128:(tt + 1) * 128, 0:1280]), w=[("pr", i % 2)])
                    if lat:
                        S.dma("sp", lambda h, tt=tt, i=i: h.dma_start(out=cs[i % 2][:], in_=I["rope"][(tt - 2) * 128:(tt - 1) * 128, :]), w=[("cs", i % 2)])
                    S.add("act", lambda h, p_=p_: h.activation(out=sq[:], in_=p_[:, 0:1024], func=AF.Square), r=[("pr", i % 2)], w=["sq"])
                    S.add("dve", lambda h: h.tensor_reduce(out=st8[:, 0:8], in_=sq[:].rearrange("p (h d) -> p h d", h=8), axis=AX.X, op=ALU.add),
                          r=["sq"], w=["ssq8"])
                    rstd_ops(st8[:, 0:8], st8[:, 8:16], 1.0 / 128, ["ssq8"], ["rstd8"])
                    S.add("dve", lambda h, p_=p_: h.tensor_tensor(out=qn[:].rearrange("p (h d) -> p h d", h=8),
                                                                 in0=p_[:, 0:1024].rearrange("p (h d) -> p h d", h=8),
                                                                 in1=st8[:, 8:16].unsqueeze(2).to_broadcast([128, 8, 128]), op=ALU.mult),
                          r=[("pr", i % 2), "rstd8"], w=["qn"])
                    S.add("dve", lambda h: h.tensor_tensor(out=qn[:], in0=qn[:], in1=gqk[:].rearrange("p h d -> p (h d)"), op=ALU.mult),
                          r=["qn", "gqk"], w=["qn"])
                    q_ = qr[i % 2]
                    rope_ops(qn[:], q_[:], cs[i % 2], 8, lat, "qn", ("qr", i % 2), ("cs", i % 2), tmps)
                    for hh in range(8):
                        S.add("pe", lambda h, hh=hh, q_=q_: h.transpose(out=ptq[:, hh, :], in_=q_[:, hh * 128:(hh + 1) * 128], identity=ident_b[:]),
                              r=[("qr", i % 2), (("qr", i % 2), "b")], w=["ptq"])
                    S.add("act", lambda h, tt=tt: h.activation(out=QKT[:, :, tt * 128:(tt + 1) * 128], in_=ptq[:], func=AF.Copy), r=["ptq"], w=[("QKT", tt)])
                    S.add("act", lambda h, tt=tt, p_=p_: h.activation(out=Vtm[:, tt, :], in_=p_[:, 1024:1280], func=AF.Copy), r=[("pr", i % 2)], w=[("Vtm", tt)])
                for tt in range(NT):
                    prep_tile(tt)
                ps_s = [Pl("ps_s%d" % i, [128, 512]) for i in range(2)]
                ps_o = [Pl("ps_o%d" % i, [128, 512]) for i in range(2)]
                ps_d = [Pl("ps_d%d" % i, [128, 512]) for i in range(2)]
                pT = [Tl("pT%d" % i, [128, 512], BF16) for i in range(3)]
                rden = Tl("rden", [128, 512])
                oT = [Tl("oT%d" % i, [128, 512], BF16) for i in range(2)]
                qblocks = [(256 + qb * 512, 512, list(range(NT))) for qb in range(4)]
                if not last:
                    qblocks = [(0, 256, [0, 1])] + qblocks
                gi = 0
                si = 0
                def att_head(q0, n, kts, hh, gi):
                        nonlocal si
                        g = hh // 3
                        o = gi % 2
                        nk = len(kts)
                        slots = []

                        def do_s(j):
                            nonlocal si
                            sidx = si
                            si += 1
                            kt = kts[j]
                            S.add("pe", lambda h, sidx=sidx, kt=kt: h.matmul(ps_s[sidx % 2][:, 0:n], lhsT=QKT[:, 6 + g, kt * 128:(kt + 1) * 128],
                                                                           rhs=QKT[:, hh, q0:q0 + n], start=True, stop=True),
                                  r=[("QKT", kt)] + [("QKT", q0 // 128 + a) for a in range(n // 128)], w=[("ps_s", sidx % 2)])
                            S.add("act", lambda h, sidx=sidx: h.activation(out=pT[sidx % 3][:, 0:n], in_=ps_s[sidx % 2][:, 0:n], func=AF.Exp),
                                  r=[("ps_s", sidx % 2)], w=[("pT", sidx % 3)])
                            slots.append(sidx)

                        def do_o(j):
                            sidx = slots[j]
                            kt = kts[j]
                            S.add("pe", lambda h, sidx=sidx, kt=kt: h.matmul(ps_o[o][:, 0:n], lhsT=Vtm[:, kt, g * 128:(g + 1) * 128], rhs=pT[sidx % 3][:, 0:n],
                                                                           start=(j == 0), stop=(j == nk - 1)),
                                  r=[("pT", sidx % 3), ("Vtm", kt)], w=[("ps_o", o)])
                            S.add("pe", lambda h, sidx=sidx: h.matmul(ps_d[o][:, 0:n], lhsT=ones_b[:], rhs=pT[sidx % 3][:, 0:n],
                                                                    start=(j == 0), stop=(j == nk - 1)),
                                  r=[("pT", sidx % 3)], w=[("ps_d", o)])
                        do_s(0)
                        for j in range(nk):
                            if j + 1 < nk:
                                do_s(j + 1)
                            do_o(j)
                        S.add("dve", lambda h, o=o: h.reciprocal(out=rden[:, 0:n], in_=ps_d[o][:, 0:n]), r=[("ps_d", o)], w=["rden"])
                        S.add("dve", lambda h, o=o: h.tensor_tensor(out=oT[o][:, 0:n], in0=ps_o[o][:, 0:n], in1=rden[:, 0:n], op=ALU.mult),
                              r=[("ps_o", o), "rden"], w=[("oT", o)])
                        S.dma("sp", lambda h, o=o, hh=hh, q0=q0, n=n: h.dma_start(out=mixT[hh * 128:(hh + 1) * 128, q0:q0 + n], in_=oT[o][:, 0:n]),
                              r=[("oT", o)], w=[("mixT", gi)])
                bg = mod_task(l + 1, Tl, Pl) if (l + 1 < nlayers and (phases is None or "mod" in phases)) else iter(())
                next(bg, None)
                for (q0, n, kts) in qblocks:
                    for hh in range(6):
                        att_head(q0, n, kts, hh, gi)
                        gi += 1
                        next(bg, None)
                for _ in bg:
                    pass
                S.phase_end("att")

        def silu2_ops(src, e, dst, skey, ekey, dkey, in_scale=0.5):
            S.add("act", lambda h: h.activation(out=e, in_=src, func=AF.Tanh, scale=in_scale), r=[skey], w=[ekey])
            S.add("dve", lambda h: h.scalar_tensor_tensor(out=dst, in0=e, scalar=1.0, in1=src, op0=ALU.add, op1=ALU.mult), r=[skey, ekey], w=[dkey])

        def phase_ret(l):
            last = (l == nlayers - 1)
            with ExitStack() as st:
                Tl, Pl = mk_alloc(st)
                RQK = Tl("RQK", [128, 8, T], BF16)
                RKtm = Tl("RKtm", [128, NT, 512], BF16)
                RVtm = Tl("RVtm", [128, NT, 512], BF16)
                rgs = Tl("rgs", [128, NT, 512], BF16)
                SAll = [Tl("SAll%d" % d_, [128, NT, 512], BF16) for d_ in range(2)]
                S32 = [Tl("S32%d" % d_, [128, 512]) for d_ in range(2)]
                MaskT = Tl("MaskT", [128, 4, 128])
                ERow = [Tl("ERow%d" % d_, [128, 4, 128], BF16) for d_ in range(2)]
                dec = Tl("dec", [128, 8])
                lg = Tl("lg", [128, 8])
                dend = Tl("dend", [128, 8])
                gam = Tl("gam", [128, 8])
                marg = Tl("marg", [128, 128])
                pr = [Tl("rpr%d" % i, [128, 2048]) for i in range(2)]
                tmps = [Tl("rrt%d" % i, [128, 512]) for i in range(4)]
                qr = [Tl("rqr%d" % i, [128, 1024], BF16) for i in range(2)]
                cs = [Tl("rcs%d" % i, [128, 128]) for i in range(2)]
                ee = Tl("ree", [128, 512])
                ptq = Pl("rptq", [128, 8, 128], BF16)
                S.dma("sp", lambda h: h.dma_start(out=dec[:, 0:4], in_=I["ret_decay_f"][l:l + 1, :].to_broadcast([128, 4])), w=["dec"])
                S.dma("sp", lambda h: h.dma_start(out=dec[:, 4:8], in_=I["ret_decay_b"][l:l + 1, :].to_broadcast([128, 4])), w=["dec"])
                S.add("act", lambda h: h.activation(out=lg[:], in_=dec[:], func=AF.Exp, scale=float(math.log(2.0))), r=["dec"], w=["lg"])
                S.add("act", lambda h: h.activation(out=lg[:], in_=lg[:], func=AF.Ln, scale=-1.0, bias=1.0), r=["lg"], w=["lg"])
                S.add("act", lambda h: h.activation(out=gam[:], in_=lg[:], func=AF.Exp, scale=128.0), r=["lg"], w=["gam"])

                def mk_head(hh):
                    S.add("dve", lambda h: h.tensor_scalar(out=marg[:], in0=D1, scalar1=lg[:, hh:hh + 1], scalar2=None, op0=ALU.mult), r=["lg"], w=["marg"])
                    S.add("dve", lambda h: h.scalar_tensor_tensor(out=marg[:], in0=D2, scalar=lg[:, 4 + hh:5 + hh], in1=marg[:], op0=ALU.mult, op1=ALU.add),
                          r=["lg", "marg"], w=["marg"])
                    S.add("act", lambda h: h.activation(out=marg[:], in_=marg[:], func=AF.Exp), r=["marg"], w=["marg"])
                    S.add("dve", lambda h: h.tensor_tensor(out=MaskT[:, hh, :], in0=marg[:], in1=ident_f, op=ALU.add), r=["marg"], w=["MaskT"])
                    for d_ in range(2):
                        S.add("act", lambda h, d_=d_: h.activation(out=ERow[d_][:, hh, :], in_=rowidx[d_], func=AF.Exp, scale=lg[:, 4 * d_ + hh:4 * d_ + hh + 1]),
                              r=["lg"], w=["ERow"])
                        S.add("act", lambda h, d_=d_: h.activation(out=dend[:, 4 * d_ + hh:4 * d_ + hh + 1], in_=colexp[d_], func=AF.Exp,
                                                                   scale=lg[:, 4 * d_ + hh:4 * d_ + hh + 1]), r=["lg"], w=["dend"])
                for hh in range(4):
                    mk_head(hh)

                def prep_tile(tt):
                    i = tt
                    lat = tt >= 2
                    p_ = pr[i % 2]
                    S.dma("sp", lambda h: h.dma_start(out=p_[:], in_=proj[tt * 128:(tt + 1) * 128, 1280:3328]), w=[("pr", i % 2)])
                    if lat:
                        S.dma("sp", lambda h: h.dma_start(out=cs[i % 2][:], in_=I["rope"][(tt - 2) * 128:(tt - 1) * 128, :]), w=[("cs", i % 2)])
                    S.add("act", lambda h: h.activation(out=p_[:, 512:1024], in_=p_[:, 512:1024], func=AF.Copy, scale=float(128 ** -0.5)),
                          r=[("pr", i % 2)], w=[("pr", i % 2)])
                    q_ = qr[i % 2]
                    rope_ops(p_[:, 0:1024], q_[:], cs[i % 2], 8, lat, ("pr", i % 2), ("qr", i % 2), ("cs", i % 2), tmps)
                    for hh in range(8):
                        S.add("pe", lambda h, hh=hh: h.transpose(out=ptq[:, hh, :], in_=q_[:, hh * 128:(hh + 1) * 128], identity=ident_b[:]),
                              r=[("qr", i % 2), (("qr", i % 2), "b")], w=["ptq"])
                    S.add("act", lambda h: h.activation(out=RQK[:, :, tt * 128:(tt + 1) * 128], in_=ptq[:], func=AF.Copy), r=["ptq"], w=[("RQK", tt)])
                    S.add("dve", lambda h: h.tensor_copy(out=RKtm[:, tt, :], in_=q_[:, 512:1024]), r=[("qr", i % 2), (("qr", i % 2), "b")], w=[("RKtm", tt)])
                    S.add("act", lambda h: h.activation(out=RVtm[:, tt, :], in_=p_[:, 1024:1536], func=AF.Copy), r=[("pr", i % 2)], w=[("RVtm", tt)])
                    silu2_ops(p_[:, 1536:2048], ee[:], rgs[:, tt, :], ("pr", i % 2), "ee", ("rgs", tt))
                for tt in range(NT):
                    prep_tile(tt)

                psA = Pl("psA", [128, 512])
                RVs = [Tl("RVs%d" % i, [128, 512], BF16) for i in range(2)]
                orders = [list(range(NT)), [1, 0] + list(range(NT - 1, 1, -1))]

                def state_step(d_, idx):
                    order = orders[d_]
                    c = order[idx]
                    if idx == 0:
                        S.add("pool", lambda h: h.memset(S32[d_][:], 0.0), w=[("S32", d_)])
                        S.add("pool", lambda h: h.memset(SAll[d_][:, c, :], 0.0), w=[("SAll", d_, c)])
                    if idx == NT - 1:
                        return
                    rv = RVs[idx % 2]
                    S.add("dve", lambda h: h.tensor_tensor(out=rv[:].rearrange("p (h d) -> p h d", h=4),
                                                            in0=RVtm[:, c, :].rearrange("p (h d) -> p h d", h=4),
                                                            in1=dend[:, 4 * d_:4 * d_ + 4].unsqueeze(2).to_broadcast([128, 4, 128]), op=ALU.mult),
                          r=[("RVtm", c), "dend"], w=[("RVs", idx % 2)])
                    for hh in range(4):
                        S.add("pe", lambda h, hh=hh: h.matmul(psA[:, hh * 128:(hh + 1) * 128], lhsT=RKtm[:, c, hh * 128:(hh + 1) * 128],
                                                             rhs=rv[:, hh * 128:(hh + 1) * 128], start=True, stop=True),
                              r=[("RKtm", c), ("RVs", idx % 2)], w=["psA"])
                    S.add("dve", lambda h: h.tensor_tensor(out=S32[d_][:].rearrange("p (h d) -> p h d", h=4),
                                                           in0=S32[d_][:].rearrange("p (h d) -> p h d", h=4),
                                                           in1=gam[:, 4 * d_:4 * d_ + 4].unsqueeze(2).to_broadcast([128, 4, 128]), op=ALU.mult),
                          r=[("S32", d_), "gam"], w=[("S32", d_)])
                    S.add("dve", lambda h: h.tensor_tensor(out=S32[d_][:], in0=S32[d_][:], in1=psA[:], op=ALU.add), r=[("S32", d_), "psA"], w=[("S32", d_)])
                    cn = order[idx + 1]
                    S.add("act", lambda h: h.activation(out=SAll[d_][:, cn, :], in_=S32[d_][:], func=AF.Copy), r=[("S32", d_)], w=[("SAll", d_, cn)])
                for d_ in range(2):
                    for idx in range(NT):
                        state_step(d_, idx)

                psS = [Pl("psS%d" % i, [128, 4, 128]) for i in range(2)]
                psY = [Pl("psY%d" % i, [128, 512]) for i in range(2)]
                ptr = Pl("ptr", [128, 8, 128], BF16)
                SDT = [Tl("SDT%d" % i, [128, 4, 128], BF16) for i in range(2)]
                RQs = [[Tl("RQs%d_%d" % (d_, i), [128, 4, 128], BF16) for i in range(2)] for d_ in range(2)]
                sqy = Tl("sqy", [128, 512])
                yn = Tl("yn", [128, 512])
                yb = [Tl("yb%d" % i, [128, 512], BF16) for i in range(2)]
                stgr = [Tl("stgr%d" % i, [128, 4, 128], BF16) for i in range(2)]
                st4 = Tl("st4", [128, 8])

                def chunk(c, i):
                    y = psY[i % 2]
                    cs_ = slice(c * 128, (c + 1) * 128)
                    for hh in range(4):
                        S.add("pe", lambda h, hh=hh: h.matmul(psS[i % 2][:, hh, :], lhsT=RQK[:, 4 + hh, cs_], rhs=RQK[:, hh, cs_], start=(hh == 0), stop=(hh == 3)),
                              r=[("RQK", c)], w=[("psS", i % 2)])
                    S.add("dve", lambda h: h.tensor_tensor(out=SDT[i % 2][:], in0=psS[i % 2][:], in1=MaskT[:], op=ALU.mult),
                          r=[("psS", i % 2), "MaskT"], w=[("SDT", i % 2)])
                    for d_ in range(2):
                        S.add("dve", lambda h, d_=d_: h.tensor_tensor(out=RQs[d_][i % 2][:], in0=RQK[:, 0:4, cs_], in1=ERow[d_][:], op=ALU.mult),
                              r=[("RQK", c), "ERow"], w=[("RQs", d_, i % 2)])
                    for hh in range(4):
                        S.add("pe", lambda h, hh=hh: h.matmul(y[:, hh * 128:(hh + 1) * 128], lhsT=SDT[i % 2][:, hh, :], rhs=RVtm[:, c, hh * 128:(hh + 1) * 128],
                                                             start=(hh == 0), stop=False), r=[("SDT", i % 2), ("RVtm", c)], w=[("psY", i % 2)])
                        for d_ in range(2):
                            S.add("pe", lambda h, hh=hh, d_=d_: h.matmul(y[:, hh * 128:(hh + 1) * 128], lhsT=RQs[d_][i % 2][:, hh, :],
                                                                        rhs=SAll[d_][:, c, hh * 128:(hh + 1) * 128], start=False, stop=(d_ == 1 and hh == 3)),
                                  r=[("RQs", d_, i % 2), ("SAll", d_, c)], w=[("psY", i % 2)])
                    S.add("act", lambda h: h.activation(out=sqy[:], in_=y[:], func=AF.Square), r=[("psY", i % 2)], w=["sqy"])
                    S.add("dve", lambda h: h.tensor_reduce(out=st4[:, 0:4], in_=sqy[:].rearrange("p (h d) -> p h d", h=4), axis=AX.X, op=ALU.add),
                          r=["sqy"], w=["ssq4"])
                    rstd_ops(st4[:, 0:4], st4[:, 4:8], 1.0 / 128, ["ssq4"], ["rstd4"])
                    S.add("dve", lambda h: h.tensor_tensor(out=yn[:].rearrange("p (h d) -> p h d", h=4), in0=y[:].rearrange("p (h d) -> p h d", h=4),
                                                           in1=st4[:, 4:8].unsqueeze(2).to_broadcast([128, 4, 128]), op=ALU.mult),
                          r=[("psY", i % 2), "rstd4"], w=["yn"])
                    yb_ = yb[i % 2]
                    S.add("dve", lambda h: h.scalar_tensor_tensor(out=yb_[:], in0=yn[:], scalar=0.5, in1=rgs[:, c, :], op0=ALU.mult, op1=ALU.mult),
                          r=["yn", ("rgs", c)], w=[("yb", i % 2)])
                    for hh in range(4):
                        S.add("pe", lambda h, hh=hh: h.transpose(out=ptr[:, hh, :], in_=yb_[:, hh * 128:(hh + 1) * 128], identity=ident_b[:]),
                              r=[("yb", i % 2)], w=["ptr"])
                    sg = stgr[i % 2]
                    S.add("act", lambda h: h.activation(out=sg[:], in_=ptr[:, 0:4, :], func=AF.Copy), r=["ptr"], w=[("stgr", i % 2)])
                    S.dma("sp", lambda h: h.dma_start(out=mixT[768:1280, c * 128:(c + 1) * 128].rearrange("(h d) t -> d h t", h=4), in_=sg[:]),
                          r=[("stgr", i % 2)], w=[("mixTr", c)])
                for i_, c in enumerate(range(2 if last else 0, NT)):
                    chunk(c, i_)
                S.phase_end("ret")

        def phase_ssd(l):
            last = (l == nlayers - 1)
            NH = 12
            with ExitStack() as st_o:
                To, Po = mk_alloc(st_o)
                BT = To("BT", [128, 2, T], BF16)
                CT = To("CT", [128, 2, T], BF16)
                Btm = To("Btm", [128, NT, 256], BF16)
                xs = To("xs", [128, NT, 768], BF16)
                la = To("la", [128, 2, NT, NH])
                lndt = To("lndt", [128, 2, NT, NH])
                acs = To("acs", [128, 2, NT, NH])
                tot = To("tot", [128, 2, NT, NH])
                ein = To("ein", [128, 2, NT, NH])
                cdc = To("cdc", [128, 2, NT, NH])
                wend = To("wend", [128, 2, NT, NH])
                lb = To("lb", [128, 2, NT, NH])
                dsk = To("dsk", [128, NH])
                gssd = To("gssd", [128, 768])
                with ExitStack() as st:
                    Tl, Pl = mk_alloc(st)
                    cw = Tl("cw", [128, 10, 5])
                    cb = Tl("cb", [128, 10])
                    for k in range(5):
                        S.dma("sp", lambda h, k=k: h.dma_start(out=cw[:, :, k], in_=I["conv_w"][l, k].rearrange("(c p) -> p c", p=128),
                                                               allow_slow_non_contiguous=True), w=["cw"])
                    S.dma("sp", lambda h: h.dma_start(out=cb[:], in_=I["conv_b"][l].rearrange("(c p) -> p c", p=128), allow_slow_non_contiguous=True), w=["cb"])
                    S.dma("sp", lambda h: h.dma_start(out=dsk[:], in_=I["d_skip"][l:l + 1, :].to_broadcast([128, NH])), w=["dsk"])
                    S.dma("sp", lambda h: h.dma_start(out=gssd[:], in_=I["ssd_norm_g"][l:l + 1, :].to_broadcast([128, 768])), w=["gssd"])
                    UW = 2312
                    u = [Tl("u%d" % i, [128, UW]) for i in range(2)]
                    acc = Tl("acc", [128, UW])
                    ee = Tl("cee", [128, UW])
                    ob = [Tl("ob%d" % i, [128, UW], BF16) for i in range(2)]
                    ptx = Pl("ptx", [128, 8, 128], BF16)
                    for i in range(2):
                        S.add("pool", lambda h, i=i: h.memset(u[i][:], 0.0), w=[("u", i)])
                    S.add("dve", lambda h: h.tensor_scalar(out=cw[:], in0=cw[:], scalar1=0.5, scalar2=None, op0=ALU.mult), r=["cw"], w=["cw"])
                    S.add("dve", lambda h: h.tensor_scalar(out=cb[:], in0=cb[:], scalar1=0.5, scalar2=None, op0=ALU.mult), r=["cb"], w=["cb"])

                    def conv_chunk(cc):
                        i = cc
                        u_ = u[i % 2]
                        o_ = ob[i % 2]
                        S.dma("sp", lambda h: h.dma_start(out=u_[:, 2:258], in_=xbcT[cc * 128:(cc + 1) * 128, 0:256]), w=[("u", i % 2)])
                        S.dma("sp", lambda h: h.dma_start(out=u_[:, 262:2310], in_=xbcT[cc * 128:(cc + 1) * 128, 256:T]), w=[("u", i % 2)])
                        n = 2308
                        S.add("dve", lambda h: h.tensor_scalar(out=acc[:, 2:2 + n], in0=u_[:, 0:n], scalar1=cw[:, cc, 0:1], scalar2=cb[:, cc:cc + 1],
                                                               op0=ALU.mult, op1=ALU.add), r=[("u", i % 2), "cw", "cb"], w=["acc"])
                        for k in range(1, 5):
                            S.add("dve", lambda h, k=k: h.scalar_tensor_tensor(out=acc[:, 2:2 + n], in0=u_[:, k:k + n], scalar=cw[:, cc, k:k + 1],
                                                                               in1=acc[:, 2:2 + n], op0=ALU.mult, op1=ALU.add),
                                  r=[("u", i % 2), "cw", "acc"], w=["acc"])
                        silu2_ops(acc[:, 2:2 + n], ee[:, 2:2 + n], o_[:, 2:2 + n], "acc", "cee", ("ob", i % 2), in_scale=1.0)
                        def tok(tt):
                            return (2 + tt * 128) if tt < 2 else (262 + (tt - 2) * 128)
                        if cc < 6 or cc in (6, 7):
                            for t0 in range(0, NT, 8):
                                nt = min(8, NT - t0)
                                for a in range(nt):
                                    tt = t0 + a
                                    S.add("pe", lambda h, a=a, tt=tt: h.transpose(out=ptx[:, a, :], in_=o_[:, tok(tt):tok(tt) + 128], identity=ident_b[:]),
                                          r=[("ob", i % 2)], w=["ptx"])
                                if cc < 6:
                                    S.add("act", lambda h, t0=t0, nt=nt: h.activation(out=xs[:, t0:t0 + nt, cc * 128:(cc + 1) * 128], in_=ptx[:, 0:nt, :], func=AF.Copy),
                                          r=["ptx"], w=[("xs", cc, t0)])
                                else:
                                    g = cc - 6
                                    S.add("act", lambda h, t0=t0, nt=nt: h.activation(out=Btm[:, t0:t0 + nt, g * 128:(g + 1) * 128], in_=ptx[:, 0:nt, :], func=AF.Copy),
                                          r=["ptx"], w=[("Btm", g, t0)])
                        if cc >= 6:
                            dstT = BT if cc < 8 else CT
                            g = (cc - 6) % 2
                            S.add("act", lambda h: h.activation(out=dstT[:, g, 0:256], in_=o_[:, 2:258], func=AF.Copy), r=[("ob", i % 2)], w=[("BCT", cc, 0)])
                            S.add("act", lambda h: h.activation(out=dstT[:, g, 256:T], in_=o_[:, 262:2310], func=AF.Copy), r=[("ob", i % 2)], w=[("BCT", cc, 1)])
                    for cc in range(10):
                        conv_chunk(cc)
                    ztl = [Tl("ztl%d" % i, [128, 768]) for i in range(2)]
                    zth = [Tl("zth%d" % i, [128, 768]) for i in range(2)]
                    zso = [Tl("zso%d" % i, [128, 768], BF16) for i in range(2)]

                    def ztile(tt):
                        b = tt % 2
                        S.dma("sp", lambda h: h.dma_start(out=ztl[b][:], in_=proj[tt * 128:(tt + 1) * 128, 3328:4096]), w=[("ztl", b)])
                        S.add("act", lambda h: h.activation(out=zth[b][:], in_=ztl[b][:], func=AF.Tanh, scale=0.5), r=[("ztl", b)], w=[("zth", b)])
                        S.add("dve", lambda h: h.scalar_tensor_tensor(out=zso[b][:], in0=zth[b][:], scalar=1.0, in1=ztl[b][:], op0=ALU.add, op1=ALU.mult),
                              r=[("ztl", b), ("zth", b)], w=[("zso", b)])
                        S.dma("sp", lambda h: h.dma_start(out=zsd[tt * 128:(tt + 1) * 128, :], in_=zso[b][:]), r=[("zso", b)], w=[("zsd", tt)])
                    for tt in range(2 if last else 0, NT):
                        ztile(tt)
                    dtr = Tl("dtr", [128, NT, 24])
                    dtb = Tl("dtb", [128, 24])
                    alg = Tl("alg", [128, 24])
                    dtv = Tl("dtv", [128, 2, NT, NH])
                    tmpd = Tl("tmpd", [128, 2, NT, NH])
                    psc = Pl("psc", [128, 512])
                    pst = Pl("pst", [128, 512])
                    S.dma("sp", lambda h: h.dma_start(out=dtr[:], in_=proj[:, 4096:4120].rearrange("(t p) c -> p t c", p=128)), w=["dtr"])
                    S.dma("sp", lambda h: h.dma_start(out=dtb[:, 0:12], in_=I["dt_bias_f"][l:l + 1, :].to_broadcast([128, 12])), w=["dtb"])
                    S.dma("sp", lambda h: h.dma_start(out=dtb[:, 12:24], in_=I["dt_bias_b"][l:l + 1, :].to_broadcast([128, 12])), w=["dtb"])
                    S.dma("sp", lambda h: h.dma_start(out=alg[:, 0:12], in_=I["a_log_f"][l:l + 1, :].to_broadcast([128, 12])), w=["alg"])
                    S.dma("sp", lambda h: h.dma_start(out=alg[:, 12:24], in_=I["a_log_b"][l:l + 1, :].to_broadcast([128, 12])), w=["alg"])
                    S.add("act", lambda h: h.activation(out=alg[:], in_=alg[:], func=AF.Exp), r=["alg"], w=["alg"])
                    for d_ in range(2):
                        S.add("dve", lambda h, d_=d_: h.tensor_tensor(out=dtv[:, d_], in0=dtr[:, :, d_ * 12:(d_ + 1) * 12],
                                                                    in1=dtb[:, d_ * 12:(d_ + 1) * 12].unsqueeze(1).to_broadcast([128, NT, NH]), op=ALU.add),
                              r=["dtr", "dtb"], w=["dtv"])
                    F2 = lambda t: t[:].rearrange("p a b c -> p (a b c)")
                    S.add("act", lambda h: h.activation(out=F2(dtv), in_=F2(dtv), func=AF.Exp), r=["dtv"], w=["dtv"])
                    S.add("act", lambda h: h.activation(out=F2(dtv), in_=F2(dtv), func=AF.Ln, bias=1.0), r=["dtv"], w=["dtv"])
                    S.add("dve", lambda h: h.tensor_scalar(out=F2(dtv), in0=F2(dtv), scalar1=1e-30, scalar2=None, op0=ALU.max), r=["dtv"], w=["dtv"])
                    S.add("act", lambda h: h.activation(out=F2(lndt), in_=F2(dtv), func=AF.Ln), r=["dtv"], w=["lndt"])
                    for d_ in range(2):
                        S.add("dve", lambda h, d_=d_: h.tensor_tensor(out=la[:, d_], in0=dtv[:, d_],
                                                                    in1=alg[:, d_ * 12:(d_ + 1) * 12].unsqueeze(1).to_broadcast([128, NT, NH]), op=ALU.mult),
                              r=["dtv", "alg"], w=["la"])
                    S.add("dve", lambda h: h.tensor_scalar(out=F2(la), in0=F2(la), scalar1=-1.0, scalar2=None, op0=ALU.mult), r=["la"], w=["la"])
                    NQ = NT * NH
                    S.add("pe", lambda h: h.matmul(psc[:, 0:NQ], lhsT=tri_f, rhs=la[:, 0].rearrange("p a b -> p (a b)"), start=True, stop=True), r=["la"], w=["psc"])
                    S.add("pe", lambda h: h.matmul(psc[:, NQ:2 * NQ], lhsT=tri_b, rhs=la[:, 1].rearrange("p a b -> p (a b)"), start=True, stop=True), r=["la"], w=["psc"])
                    S.add("pe", lambda h: h.matmul(pst[:, 0:2 * NQ], lhsT=ones_f, rhs=F2(la), start=True, stop=True), r=["la"], w=["pst"])
                    S.add("dve", lambda h: h.tensor_copy(out=F2(acs), in_=psc[:, 0:2 * NQ]), r=["psc"], w=["acs"])
                    S.add("dve", lambda h: h.tensor_copy(out=F2(tot), in_=pst[:, 0:2 * NQ]), r=["pst"], w=["tot"])
                    S.add("act", lambda h: h.activation(out=F2(ein), in_=F2(acs), func=AF.Exp), r=["acs"], w=["ein"])
                    S.add("act", lambda h: h.activation(out=F2(cdc), in_=F2(tot), func=AF.Exp), r=["tot"], w=["cdc"])
                    S.add("dve", lambda h: h.tensor_tensor(out=F2(lb), in0=F2(lndt), in1=F2(acs), op=ALU.subtract), r=["lndt", "acs"], w=["lb"])
                    S.add("dve", lambda h: h.tensor_tensor(out=F2(tmpd), in0=F2(lb), in1=F2(tot), op=ALU.add), r=["lb", "tot"], w=["tmpd"])
                    S.add("act", lambda h: h.activation(out=F2(wend), in_=F2(tmpd), func=AF.Exp), r=["tmpd"], w=["wend"])
                    S.phase_end("ssdprep")
                with ExitStack() as st:
                    Tl, Pl = mk_alloc(st)
                    SAll = [Tl("sSAll%d" % d_, [128, NT, 768], BF16) for d_ in range(2)]
                    S32 = [Tl("sS32%d" % d_, [128, 768]) for d_ in range(2)]
                    xsw = [Tl("xsw%d" % i, [128, 768], BF16) for i in range(2)]
                    psA = Pl("spsA", [128, 2, 512])
                    orders = [list(range(NT)), [1, 0] + list(range(NT - 1, 1, -1))]

                    def bc12(t2d):
                        return t2d.unsqueeze(2).to_broadcast([128, NH, 64])

                    def v12(ap):
                        return ap.rearrange("p (h d) -> p h d", h=NH)

                    def state_step(d_, idx):
                        order = orders[d_]
                        c = order[idx]
                        if idx == 0:
                            S.add("pool", lambda h: h.memset(S32[d_][:], 0.0), w=[("S32", d_)])
                            S.add("pool", lambda h: h.memset(SAll[d_][:, c, :], 0.0), w=[("SAll", d_, c)])
                        if idx == NT - 1:
                            return
                        xw = xsw[idx % 2]
                        S.add("dve", lambda h: h.tensor_tensor(out=v12(xw[:]), in0=v12(xs[:, c, :]), in1=bc12(wend[:, d_, c, :]), op=ALU.mult),
                              w=[("xsw", idx % 2)])
                        for g in range(2):
                            S.add("pe", lambda h, g=g: h.matmul(psA[:, g, 0:384], lhsT=Btm[:, c, g * 128:(g + 1) * 128], rhs=xw[:, g * 384:(g + 1) * 384],
                                                               start=True, stop=True), r=[("xsw", idx % 2)], w=["psA"])
                        S.add("dve", lambda h: h.tensor_tensor(out=v12(S32[d_][:]), in0=v12(S32[d_][:]), in1=bc12(cdc[:, d_, c, :]), op=ALU.mult),
                              r=[("S32", d_)], w=[("S32", d_)])
                        S.add("dve", lambda h: h.tensor_tensor(out=S32[d_][:].rearrange("p (g x) -> p g x", g=2), in0=S32[d_][:].rearrange("p (g x) -> p g x", g=2),
                                                               in1=psA[:, :, 0:384], op=ALU.add), r=[("S32", d_), "psA"], w=[("S32", d_)])
                        cn = order[idx + 1]
                        S.add("act", lambda h: h.activation(out=SAll[d_][:, cn, :], in_=S32[d_][:], func=AF.Copy), r=[("S32", d_)], w=[("SAll", d_, cn)])
                    for d_ in range(2):
                        for idx in range(NT):
                            state_step(d_, idx)

                    Rt = [Tl("Rt%d" % d_, [128, NH, 128]) for d_ in range(2)]
                    mneg4 = [Tl("mneg4_%d" % d_, [128, 4, 128]) for d_ in range(2)]
                    Et = [Tl("Et%d" % i, [128, 4, 128], BF16) for i in range(2)]
                    Wt = [Tl("Wt%d" % i, [128, 4, 128], BF16) for i in range(4)]
                    psE = [Pl("psE%d" % i, [128, 4, 128]) for i in range(2)]
                    psG = Pl("psG", [128, 4, 128])
                    psY = Pl("spsY", [128, 1024])
                    psFB = psA
                    ptr = Pl("sptr", [128, 8, 128], BF16)
                    y1 = [Tl("y1_%d" % i, [128, 768]) for i in range(2)]
                    y2 = [Tl("y2_%d" % i, [128, 768]) for i in range(2)]
                    zt = [Tl("zt%d" % i, [128, 768], BF16) for i in range(2)]
                    xsd = [Tl("xsd%d" % i, [128, 768], BF16) for i in range(2)]
                    junk = Tl("sjunk", [128, 768], BF16)
                    yo = [Tl("yo%d" % i, [128, 768], BF16) for i in range(2)]
                    stg = [Tl("sstg%d" % i, [128, 6, 128], BF16) for i in range(2)]
                    st2 = Tl("st2", [128, 2])
                    for d_ in range(2):
                        S.add("dve", lambda h, d_=d_: h.tensor_copy(out=mneg4[d_][:], in_=mneg[d_].unsqueeze(1).to_broadcast([128, 4, 128])), w=["mneg4"])
                    tris = [tri_f, tri_b]
                    ecnt = [0]
                    wcnt = [0]

                    def chunk(c, ci):
                        cs_ = slice(c * 128, (c + 1) * 128)
                        b = ci % 2
                        S.dma("sp", lambda h: h.dma_start(out=zt[b][:], in_=zsd[c * 128:(c + 1) * 128, :]), w=[("zt", b)])
                        for g in range(2):
                            S.add("pe", lambda h, g=g: h.matmul(psG[:, g, :], lhsT=BT[:, g, cs_], rhs=CT[:, g, cs_], start=(g == 0), stop=(g == 1)), w=["psG"])
                        for d_ in range(2):
                            S.add("dve", lambda h, d_=d_: h.tensor_tensor(out=Rt[d_][:], in0=la[:, d_, c, :].unsqueeze(2).to_broadcast([128, NH, 128]),
                                                                        in1=tris[d_].unsqueeze(1).to_broadcast([128, NH, 128]), op=ALU.mult), w=[("Rt", d_)])
                        S.add("dve", lambda h: h.tensor_tensor(out=v12(xsd[b][:]), in0=v12(xs[:, c, :]), in1=bc12(dsk[:]), op=ALU.mult), w=[("xsd", b)])
                        for q in range(3):
                            wl = {}
                            for d_ in range(2):
                                e = ecnt[0]
                                ecnt[0] += 1
                                pe_ = psE[e % 2]
                                et_ = Et[e % 2]
                                S.add("pe", lambda h, d_=d_, q=q, pe_=pe_: h.matmul(pe_[:].rearrange("p a b -> p (a b)"), lhsT=ones_f,
                                                                                 rhs=Rt[d_][:, 4 * q:4 * q + 4, :].rearrange("p a b -> p (a b)"), start=True, stop=False),
                                      r=[("Rt", d_)], w=[("psE", e % 2)])
                                S.add("pe", lambda h, d_=d_, pe_=pe_: h.matmul(pe_[:].rearrange("p a b -> p (a b)"), lhsT=ident_f,
                                                                            rhs=mneg4[d_][:].rearrange("p a b -> p (a b)"), start=False, stop=True),
                                      r=["mneg4"], w=[("psE", e % 2)])
                                for a in range(4):
                                    hh = 4 * q + a
                                    S.add("act", lambda h, a=a, hh=hh, d_=d_, pe_=pe_, et_=et_: h.activation(out=et_[:, a, :], in_=pe_[:, a, :], func=AF.Exp,
                                                                                                      bias=lb[:, d_, c, hh:hh + 1]),
                                          r=[("psE", e % 2)], w=[("Et", e % 2)])
                                w_ = wcnt[0]
                                wcnt[0] += 1
                                wt_ = Wt[w_ % 4]
                                wl[d_] = (w_, wt_)
                                if q == 1:
                                    for half in range(2):
                                        S.add("dve", lambda h, half=half, et_=et_, wt_=wt_: h.tensor_tensor(
                                            out=wt_[:, 2 * half:2 * half + 2, :], in0=et_[:, 2 * half:2 * half + 2, :],
                                            in1=psG[:, half, :].unsqueeze(1).to_broadcast([128, 2, 128]), op=ALU.mult),
                                            r=[("Et", e % 2), "psG"], w=[("Wt", w_ % 4)])
                                else:
                                    g = 0 if q == 0 else 1
                                    S.add("dve", lambda h, g=g, et_=et_, wt_=wt_: h.tensor_tensor(
                                        out=wt_[:], in0=et_[:], in1=psG[:, g, :].unsqueeze(1).to_broadcast([128, 4, 128]), op=ALU.mult),
                                        r=[("Et", e % 2), "psG"], w=[("Wt", w_ % 4)])
                            for a in range(4):
                                hh = 4 * q + a
                                for d_ in range(2):
                                    w_, wt_ = wl[d_]
                                    S.add("pe", lambda h, hh=hh, a=a, d_=d_, wt_=wt_: h.matmul(psY[:, hh * 64:(hh + 1) * 64], lhsT=wt_[:, a, :], rhs=xs[:, c, hh * 64:(hh + 1) * 64],
                                                                                          start=(d_ == 0 and hh in (0, 8)), stop=False),
                                          r=[("Wt", w_ % 4)], w=["psY"])
                        S.add("pe", lambda h: h.matmul(psY[:, 0:512], lhsT=ident_b[:], rhs=xsd[b][:, 0:512], start=False, stop=True), r=[("xsd", b)], w=["psY"])
                        S.add("pe", lambda h: h.matmul(psY[:, 512:768], lhsT=ident_b[:], rhs=xsd[b][:, 512:768], start=False, stop=True), r=[("xsd", b)], w=["psY"])
                        for d_ in range(2):
                            for g in range(2):
                                S.add("pe", lambda h, d_=d_, g=g: h.matmul(psFB[:, g, 0:384], lhsT=CT[:, g, cs_], rhs=SAll[d_][:, c, g * 384:(g + 1) * 384],
                                                                         start=True, stop=True), r=[("SAll", d_, c)], w=["psA"])
                            yd = y1[b] if d_ == 0 else y2[b]
                            S.add("dve", lambda h, d_=d_, yd=yd: h.tensor_tensor(
                                out=yd[:].rearrange("p (g h d) -> p g h d", g=2, h=6),
                                in0=psFB[:, :, 0:384].rearrange("p g (h d) -> p g h d", h=6),
                                in1=ein[:, d_, c, :].rearrange("p (g h) -> p g h", g=2).unsqueeze(3).to_broadcast([128, 2, 6, 64]), op=ALU.mult),
                                r=["psA"], w=[("y", d_, b)])
                        S.add("dve", lambda h: h.tensor_tensor(out=y1[b][:], in0=y1[b][:], in1=y2[b][:], op=ALU.add), r=[("y", 0, b), ("y", 1, b)], w=[("y", 0, b)])
                        S.add("dve", lambda h: h.tensor_tensor(out=y1[b][:], in0=y1[b][:], in1=psY[:, 0:768], op=ALU.add), r=[("y", 0, b), "psY"], w=[("y", 0, b)])
                        S.add("dve", lambda h: h.scalar_tensor_tensor(out=y1[b][:], in0=y1[b][:], scalar=0.5, in1=zt[b][:], op0=ALU.mult, op1=ALU.mult),
                              r=[("y", 0, b), ("zt", b)], w=[("y", 0, b)])
                        S.add("act", lambda h: h.activation(out=junk[:], in_=y1[b][:], func=AF.Square, accum_out=st2[:, 0:1]), r=[("y", 0, b)], w=["sjunk", "ssq"])
                        rstd_ops(st2[:, 0:1], st2[:, 1:2], 1.0 / 768, ["ssq"], ["rstd"])
                        S.add("dve", lambda h: h.scalar_tensor_tensor(out=yo[b][:], in0=y1[b][:], scalar=st2[:, 1:2], in1=gssd[:], op0=ALU.mult, op1=ALU.mult),
                              r=[("y", 0, b), "rstd"], w=[("yo", b)])
                        for a in range(6):
                            S.add("pe", lambda h, a=a: h.transpose(out=ptr[:, a, :], in_=yo[b][:, a * 128:(a + 1) * 128], identity=ident_b[:]), r=[("yo", b)], w=["ptr"])
                        sg = stg[b]
                        S.add("act", lambda h: h.activation(out=sg[:], in_=ptr[:, 0:6, :], func=AF.Copy), r=["ptr"], w=[("stg", b)])
                        S.dma("sp", lambda h: h.dma_start(out=mixT[1280:2048, cs_].rearrange("(a d) t -> d a t", a=6), in_=sg[:]),
                              r=[("stg", b)], w=[("mixTs", c)])
                    for ci, c in enumerate(range(2 if last else 0, NT)):
                        chunk(c, ci)
                    S.phase_end("ssdmain")

        def phase_out(l):
            last = (l == nlayers - 1)
            with ExitStack() as st:
                Tl, Pl = mk_alloc(st)
                wo = Tl("wo", [128, 16, D], BF16)
                wv = I["w_out"][l].rearrange("(k p) n -> p k n", p=128)
                for k4 in range(4):
                    for nb in range(2):
                        S.dma("pool", lambda h, k4=k4, nb=nb: h.dma_start(out=wo[:, 4 * k4:4 * k4 + 4, nb * 1024:(nb + 1) * 1024],
                                                                        in_=wv[:, 4 * k4:4 * k4 + 4, nb * 1024:(nb + 1) * 1024]), w=[("wo", k4, nb)])
                G1 = Tl("G1", [128, D])
                gm = Tl("ogm", [128, D])
                sh = Tl("osh", [128, D])
                gp = Tl("ogp", [128, D])
                mT = [Tl("mT%d" % i, [128, 16, 128], BF16) for i in range(2)]
                xt = [Tl("oxt%d" % i, [128, D]) for i in range(2)]
                x1 = [Tl("ox1%d" % i, [128, D]) for i in range(2)]
                tmp = Tl("otmp", [128, D])
                hb = [Tl("ohb%d" % i, [128, D], BF16) for i in range(2)]
                junk = Tl("ojunk", [128, D], BF16)
                ssq = Tl("ossq", [128, 4])
                sq4 = Tl("osq4", [128, 16])
                stg = [Tl("ostg%d" % i, [128, 16, 128], BF16) for i in range(2)]
                psM = [Pl("psM%d" % i, [128, 512]) for i in range(4)]
                pt = [Pl("opt%d" % i, [128, 16, 128], BF16) for i in range(2)]

                def load_mods(r):
                    load_gate_mod(l, G1, gp, "o", "post_mix_g", 4096, r)
                    load_norm_mod(l, gm, sh, gp, "o", "pre_ffn_g", 4 * 2048, 3 * 2048, r)

                def stage1(tt, i):
                    o8 = 8 * (i % 2)
                    S.dma("sp", lambda h: h.dma_start(out=mT[i % 2][:], in_=mixT[:, tt * 128:(tt + 1) * 128].rearrange("(k p) t -> p k t", p=128)),
                          w=[("mT", i % 2)])
                    S.dma("sp", lambda h: h.dma_start(out=xt[i % 2][:], in_=xsrc(l, tt)), w=[("xt", i % 2)])
                    x1_ = x1[i % 2]
                    for cb in range(4):
                        for k in range(16):
                            S.add("pe", lambda h, cb=cb, k=k: h.matmul(psM[cb][:], lhsT=mT[i % 2][:, k, :], rhs=wo[:, k, cb * 512:(cb + 1) * 512],
                                                                      start=(k == 0), stop=(k == 15)),
                                  r=[("mT", i % 2), ("wo", k // 4, cb // 2)], w=[("psM", cb)])
                        S.add("act", lambda h, cb=cb: h.activation(out=junk[:, cb * 512:(cb + 1) * 512], in_=psM[cb][:], func=AF.Square,
                                                                   accum_out=sq4[:, o8 + cb:o8 + cb + 1]), r=[("psM", cb)], w=["ojunk", ("sq4", i % 2), ("sqd", cb)])
                        S.add("dve", lambda h, cb=cb: h.tensor_copy(out=x1_[:, cb * 512:(cb + 1) * 512], in_=psM[cb][:]), r=[("psM", cb), ("sqd", cb)], w=[("x1", i % 2)])

                def stage2(tt, i):
                    o8 = 8 * (i % 2)
                    x1_ = x1[i % 2]
                    S.add("dve", lambda h: h.tensor_reduce(out=sq4[:, o8 + 4:o8 + 5], in_=sq4[:, o8:o8 + 4], axis=AX.X, op=ALU.add), r=[("sq4", i % 2)], w=[("mss", i % 2)])
                    rstd_ops(sq4[:, o8 + 4:o8 + 5], sq4[:, o8 + 5:o8 + 6], 1.0 / D, [("mss", i % 2)], [("mrstd", i % 2)])
                    S.add("dve", lambda h: h.scalar_tensor_tensor(out=x1_[:], in0=x1_[:], scalar=sq4[:, o8 + 5:o8 + 6], in1=G1[:], op0=ALU.mult, op1=ALU.mult),
                          r=[("x1", i % 2), ("mrstd", i % 2), "oG"], w=[("x1", i % 2)])
                    S.add("dve", lambda h: h.tensor_tensor(out=x1_[:], in0=x1_[:], in1=xt[i % 2][:], op=ALU.add),
                          r=[("x1", i % 2), ("xt", i % 2)], w=[("x1", i % 2)])
                    S.dma("sp", lambda h: h.dma_start(out=x1s[tt * 128:(tt + 1) * 128, :], in_=x1_[:]), r=[("x1", i % 2)], w=[("x1s", tt)])

                    def dst(ptile, pkey, wkeys):
                        sg = stg[i % 2]
                        S.add("act", lambda h: h.activation(out=sg[:], in_=ptile[:], func=AF.Copy), r=[pkey], w=[("ostg", i % 2)])
                        S.dma("sp", lambda h: h.dma_start(out=h2T[:, tt * 128:(tt + 1) * 128].rearrange("(k p) t -> p k t", p=128), in_=sg[:]),
                              r=[("ostg", i % 2)], w=wkeys)
                    return norm_transpose_tile(l, tt, x1_[:], ("x1", i % 2), gm, sh, "o", tmp, hb[i % 2], junk, ssq, pt[i % 2], dst, [("h2T", tt)], i)

                def run_tiles(tts, i0):
                    pend = None
                    for j, tt in enumerate(tts):
                        if j == 0:
                            stage1(tt, i0)
                        if j + 1 < len(tts):
                            stage1(tts[j + 1], i0 + j + 1)
                        th = stage2(tt, i0 + j)
                        if pend is not None:
                            pend()
                        pend = th
                    pend()
                if not last:
                    load_mods(1)
                    run_tiles([0, 1], 0)
                load_mods(0)
                run_tiles(list(range(2, NT)), 2)
                S.phase_end("out")

        def phase_ffn(l):
            last = (l == nlayers - 1)
            if last:
                halves = [(256, 1024), (1280, 1024)]
            else:
                halves = [(0, 1152), (1152, 1152)]
            wg_v = I["w_gate"][l].rearrange("(k p) n -> p k n", p=128)
            wu_v = I["w_up"][l].rearrange("(k p) n -> p k n", p=128)
            wd_v = I["w_down"][l].rearrange("(k p) n -> p k n", p=128)
            NJ = DFF // 128
            for (t0, nt) in halves:
                with ExitStack() as st_o:
                    To, Po = mk_alloc(st_o)
                    aT = To("aT", [128, NJ, nt], BF16)
                    with ExitStack() as st:
                        Tl, Pl = mk_alloc(st)
                        hh_ = Tl("h2h", [128, 16, nt], BF16)
                        for k4 in range(4):
                            S.dma("sp", lambda h, k4=k4: h.dma_start(out=hh_[:, 4 * k4:4 * k4 + 4, :],
                                                                     in_=h2T[:, t0:t0 + nt].rearrange("(k p) t -> p k t", p=128)[:, 4 * k4:4 * k4 + 4, :]),
                                  w=[("h2h", k4)])
                        wg = [Tl("wg%d" % i, [128, 16, 256], BF16) for i in range(2)]
                        wu = [Tl("wu%d" % i, [128, 16, 256], BF16) for i in range(2)]
                        psg = [Pl("psg%d" % i, [128, 512]) for i in range(3)]
                        psu = [Pl("psu%d" % i, [128, 512]) for i in range(3)]
                        ee = [Tl("fe%d" % i, [128, 512]) for i in range(2)]
                        tg = [Tl("ftg%d" % i, [128, 512]) for i in range(2)]
                        tbs = [(a, min(512, nt - a)) for a in range(0, nt, 512)]
                        cnt = [0]

                        def wblock(jb):
                            b = jb % 2
                            for k4 in range(4):
                                S.dma("pool", lambda h, k4=k4: h.dma_start(out=wg[b][:, 4 * k4:4 * k4 + 4, :], in_=wg_v[:, 4 * k4:4 * k4 + 4, jb * 256:(jb + 1) * 256]),
                                      w=[("wg", b, k4)])
                                S.dma("pool", lambda h, k4=k4: h.dma_start(out=wu[b][:, 4 * k4:4 * k4 + 4, :], in_=wu_v[:, 4 * k4:4 * k4 + 4, jb * 256:(jb + 1) * 256]),
                                      w=[("wu", b, k4)])
                            for jj in range(2):
                                j = jb * 2 + jj
                                for (a, n) in tbs:
                                    i = cnt[0]
                                    cnt[0] += 1
                                    pg = psg[i % 3]
                                    pu = psu[i % 3]
                                    for k in range(16):
                                        S.add("pe", lambda h, k=k, pg=pg, a=a, n=n, jj=jj: h.matmul(pg[:, 0:n], lhsT=wg[b][:, k, jj * 128:(jj + 1) * 128],
                                                                                             rhs=hh_[:, k, a:a + n], start=(k == 0), stop=(k == 15)),
                                              r=[("wg", b, k // 4), ("h2h", k // 4)], w=[("psg", i % 3)])
                                    for k in range(16):
                                        S.add("pe", lambda h, k=k, pu=pu, a=a, n=n, jj=jj: h.matmul(pu[:, 0:n], lhsT=wu[b][:, k, jj * 128:(jj + 1) * 128],
                                                                                             rhs=hh_[:, k, a:a + n], start=(k == 0), stop=(k == 15)),
                                              r=[("wu", b, k // 4), ("h2h", k // 4)], w=[("psu", i % 3)])
                                    e_ = ee[i % 2]
                                    t_ = tg[i % 2]
                                    S.add("act", lambda h, pg=pg, e_=e_, n=n: h.activation(out=e_[:, 0:n], in_=pg[:, 0:n], func=AF.Tanh, scale=0.5),
                                          r=[("psg", i % 3)], w=[("fe", i % 2)])
                                    S.add("dve", lambda h, e_=e_, t_=t_, pg=pg, n=n: h.scalar_tensor_tensor(out=t_[:, 0:n], in0=e_[:, 0:n], scalar=1.0, in1=pg[:, 0:n],
                                                                                                     op0=ALU.add, op1=ALU.mult),
                                          r=[("fe", i % 2), ("psg", i % 3)], w=[("ftg", i % 2)])
                                    S.add("dve", lambda h, t_=t_, pu=pu, n=n, a=a, j=j: h.scalar_tensor_tensor(out=aT[:, j, a:a + n], in0=t_[:, 0:n], scalar=0.5, in1=pu[:, 0:n],
                                                                                                        op0=ALU.mult, op1=ALU.mult),
                                          r=[("ftg", i % 2), ("psu", i % 3)], w=[("aT", j, a)])
                        for jb in range(NJ // 2):
                            wblock(jb)
                        S.phase_end("ffnA")
                    with ExitStack() as st:
                        Tl, Pl = mk_alloc(st)
                        wd = [Tl("wd%d" % i, [128, NJ, 256], BF16) for i in range(2)]
                        psd = [Pl("psd%d" % i, [128, 512]) for i in range(4)]
                        stg = [Tl("fstg%d" % i, [128, 256]) for i in range(4)]
                        cnt = [0]

                        def dblock(cb):
                            b = cb % 2
                            for k4 in range(4):
                                S.dma("pool", lambda h, k4=k4: h.dma_start(out=wd[b][:, 11 * k4:11 * k4 + 11, :], in_=wd_v[:, 11 * k4:11 * k4 + 11, cb * 256:(cb + 1) * 256]),
                                      w=[("wd", b, k4)])
                            for a in range(0, nt, 128):
                                i = cnt[0]
                                cnt[0] += 1
                                p_ = psd[i % 4]
                                for k in range(NJ):
                                    S.add("pe", lambda h, k=k, p_=p_, a=a: h.matmul(p_[:, 0:256], lhsT=aT[:, k, a:a + 128], rhs=wd[b][:, k, :], start=(k == 0), stop=(k == NJ - 1)),
                                          r=[("wd", b, k // 11)], w=[("psd", i % 4)])
                                s_ = stg[i % 4]
                                if i % 2 == 0:
                                    S.add("act", lambda h, p_=p_, s_=s_: h.activation(out=s_[:], in_=p_[:, 0:256], func=AF.Copy), r=[("psd", i % 4)], w=[("fstg", i % 4)])
                                else:
                                    S.add("dve", lambda h, p_=p_, s_=s_: h.tensor_copy(out=s_[:], in_=p_[:, 0:256]), r=[("psd", i % 4)], w=[("fstg", i % 4)])
                                S.dma("sp", lambda h, s_=s_, a=a: h.dma_start(out=fsc[t0 + a:t0 + a + 128, cb * 256:(cb + 1) * 256], in_=s_[:]),
                                      r=[("fstg", i % 4)], w=[("fsc", i)])
                        for cb in range(8):
                            dblock(cb)
                        S.phase_end("ffnB")

        def phase_fin(l):
            last = (l == nlayers - 1)
            with ExitStack() as st:
                Tl, Pl = mk_alloc(st)
                G2 = Tl("G2", [128, D])
                gp = Tl("fgp", [128, D])
                xa = [Tl("fxa%d" % i, [128, D]) for i in range(3)]
                fa = [Tl("ffa%d" % i, [128, D]) for i in range(3)]
                xo = [Tl("fxo%d" % i, [128, D]) for i in range(3)]
                junk = Tl("fjunk", [128, D], BF16)
                ssq = Tl("fssq", [128, 2])

                def tile(tt, i):
                    S.dma("sp", lambda h: h.dma_start(out=xa[i % 3][:], in_=x1s[tt * 128:(tt + 1) * 128, :]), w=[("xa", i % 3)])
                    S.dma("sp", lambda h: h.dma_start(out=fa[i % 3][:], in_=fsc[tt * 128:(tt + 1) * 128, :]), w=[("fa", i % 3)])
                    S.add("act", lambda h: h.activation(out=junk[:], in_=fa[i % 3][:], func=AF.Square, accum_out=ssq[:, 0:1]), r=[("fa", i % 3)], w=["fjunk", "fss"])
                    rstd_ops(ssq[:, 0:1], ssq[:, 1:2], 1.0 / D, ["fss"], ["frstd"])
                    S.add("dve", lambda h: h.scalar_tensor_tensor(out=xo[i % 3][:], in0=fa[i % 3][:], scalar=ssq[:, 1:2], in1=G2[:], op0=ALU.mult, op1=ALU.mult),
                          r=[("fa", i % 3), "frstd", "fG"], w=[("xo", i % 3)])
                    S.add("dve", lambda h: h.tensor_tensor(out=xo[i % 3][:], in0=xo[i % 3][:], in1=xa[i % 3][:], op=ALU.add), r=[("xo", i % 3), ("xa", i % 3)], w=[("xo", i % 3)])
                    if last:
                        dst = out[(tt - 2) * 128:(tt - 1) * 128, :]
                    else:
                        dst = xnext[tt * 128:(tt + 1) * 128, :]
                    S.dma("sp", lambda h: h.dma_start(out=dst, in_=xo[i % 3][:]), r=[("xo", i % 3)], w=[("xout", tt)])
                i = 0
                if not last:
                    load_gate_mod(l, G2, gp, "f", "post_ffn_g", 5 * 2048, 1)
                    for tt in range(2):
                        tile(tt, i)
                        i += 1
                load_gate_mod(l, G2, gp, "f", "post_ffn_g", 5 * 2048, 0)
                for tt in range(2, NT):
                    tile(tt, i)
                    i += 1
                S.phase_end("fin")

        PH = {}
        PH["out"] = phase_out
        PH["ffn"] = phase_ffn
        PH["fin"] = phase_fin
        PH["ssd"] = phase_ssd
        PH["ret"] = phase_ret
        PH["att"] = phase_att

        PH["in"] = phase_in
        try:
            if phases is None or "mod" in phases:
                phase_mod(0)
            check_stop("mod")
            for l in (layers if layers is not None else range(nlayers)):
                for nm in ("in", "att", "ret", "ssd", "out", "ffn", "fin"):
                    if nm in PH and (phases is None or nm in phases):
                        PH[nm](l)
                        check_stop("%s%d" % (nm, l))
        except Stop:
            pass
        S.phase_end("final")
        stats = S.emit()
    return nc, stats, list(I.keys())


def make_in_maps(inputs):
    consts = make_consts()
    rope = make_rope()
    maps = []
    shared = {n: np.ascontiguousarray(inputs[n], dtype=np.float32) for n, _ in SMALL + BIG}
    for b in range(8):
        m = dict(shared)
        m["x"] = np.ascontiguousarray(inputs["x"][b])
        m["ctx"] = np.ascontiguousarray(inputs["ctx"][b])
        m["cvec"] = np.ascontiguousarray(np.stack([inputs["c"][b], inputs["c_ctx"]]))
        m["consts"] = consts
        m["rope"] = rope
        maps.append(m)
    return maps


def kernel(**inputs):
    nc, _, _ = build(nlayers=2, debug=False)
    maps = make_in_maps(inputs)
    res = run_bass_kernel_spmd(nc, maps, core_ids=list(range(8)))
    return np.stack([np.asarray(r["out"], dtype=np.float32) for r in res.results], axis=0)
```

```python
import math
import numpy as np
from contextlib import ExitStack
import concourse.bass as bass
import concourse.mybir as mybir
from concourse.bass_utils import run_bass_kernel_spmd

F32 = mybir.dt.float32
BF16 = mybir.dt.bfloat16
AF = mybir.ActivationFunctionType
ALU = mybir.AluOpType
AX = mybir.AxisListType

D = 2048
T = 2304
NT = 18
DIN = 5400
DFF = 5632
EPS = 1e-6
NCONST = 10 * 128 + 2
PW = 4120


class _Op:
    __slots__ = ("eng", "fn", "deps", "ch", "pos", "is_dma", "vc", "waits", "signal", "rank")


class Sched:
    ENGS = ("pe", "act", "dve", "pool", "sp")

    def __init__(self, nc, stack, n_dma_sems=12):
        self.nc = nc
        self.h = {"pe": nc.tensor, "act": nc.scalar, "dve": nc.vector, "pool": nc.gpsimd, "sp": nc.sync}
        self.ops = []
        self.n_emitted = 0
        self.eng_pos = {e: 0 for e in self.ENGS}
        self.last_w = {}
        self.readers = {}
        self.esem = {e: stack.enter_context(nc.semaphore("s_" + e)) for e in self.ENGS}
        self.dsems = {}
        self.dcount = {}
        self.drr = {}
        for q in ("sp", "pool"):
            self.dsems[q] = [stack.enter_context(nc.semaphore("d_%s%d" % (q, i))) for i in range(n_dma_sems)]
            self.drr[q] = 0
            for i in range(n_dma_sems):
                self.dcount[(q, i)] = 0
        self.by_chpos = {}
        self.last_on_ch = {}
        self.clock = {e: {} for e in self.ENGS}
        self.rk = {e: 0 for e in self.ENGS}
        self.nw = 0

    def _deps(self, r, w):
        deps = []
        for k in r:
            o = self.last_w.get(k)
            if o is not None:
                deps.append(o)
        for k in w:
            o = self.last_w.get(k)
            if o is not None:
                deps.append(o)
            deps.extend(self.readers.get(k, ()))
        return deps

    def _commit(self, op, r, w):
        for k in r:
            self.readers.setdefault(k, []).append(op)
        for k in w:
            self.last_w[k] = op
            self.readers[k] = []
        self.ops.append(op)
        self.by_chpos[(op.ch, op.pos)] = op
        self.last_on_ch[op.ch] = op

    def add(self, eng, fn, r=(), w=()):
        op = _Op()
        op.eng = eng
        op.fn = fn
        op.is_dma = False
        op.deps = self._deps(r, w)
        self.eng_pos[eng] += 1
        op.ch = eng
        op.pos = self.eng_pos[eng]
        op.signal = False
        self._commit(op, r, w)
        return op

    def dma(self, q, fn, r=(), w=()):
        op = _Op()
        op.eng = q
        op.fn = fn
        op.is_dma = True
        op.deps = self._deps(r, w)
        i = self.drr[q]
        self.drr[q] = (i + 1) % len(self.dsems[q])
        self.dcount[(q, i)] += 1
        op.ch = ("d", q, i)
        op.pos = self.dcount[(q, i)]
        op.signal = True
        self._commit(op, r, w)
        return op

    def barrier(self):
        lasts = [o for o in self.last_on_ch.values() if o.fn is not None or o.is_dma]
        for e in self.ENGS:
            op = _Op()
            op.eng = e
            op.fn = None
            op.is_dma = False
            op.deps = list(lasts)
            self.eng_pos[e] += 1
            op.ch = e
            op.pos = self.eng_pos[e]
            op.signal = False
            self.ops.append(op)
            self.by_chpos[(op.ch, op.pos)] = op
        self.last_w = {}
        self.readers = {}

    def emit(self):
        ops = self.ops[self.n_emitted:]
        clock = self.clock
        for op in ops:
            E = op.eng
            ck = clock[E]
            need = {}
            for d in op.deps:
                if (not d.is_dma) and d.eng == "pe" and E == "pe" and (not op.is_dma) and op.fn is not None:
                    continue
                if ck.get(d.ch, 0) < d.pos and need.get(d.ch, 0) < d.pos:
                    need[d.ch] = d.pos
            if op.is_dma and op.pos > 1:
                if ck.get(op.ch, 0) < op.pos - 1 and need.get(op.ch, 0) < op.pos - 1:
                    need[op.ch] = op.pos - 1
            op.waits = []
            for ch, pos in need.items():
                if ck.get(ch, 0) >= pos:
                    continue
                p = self.by_chpos[(ch, pos)]
                p.signal = True
                op.waits.append(p)
                for c, v in p.vc.items():
                    if ck.get(c, 0) < v:
                        ck[c] = v
            vc = dict(ck)
            vc[op.ch] = op.pos
            op.vc = vc
        for op in ops:
            if op.is_dma:
                op.rank = 16 * op.pos
            elif op.signal:
                self.rk[op.eng] += 1
                op.rank = self.rk[op.eng]
        for op in ops:
            h = self.h[op.eng]
            for p in op.waits:
                if p.is_dma:
                    sem = self.dsems[p.ch[1]][p.ch[2]]
                else:
                    sem = self.esem[p.eng]
                h.wait_ge(sem, p.rank)
                self.nw += 1
            if op.fn is None:
                continue
            inst = op.fn(h)
            if op.is_dma:
                inst.then_inc(self.dsems[op.ch[1]][op.ch[2]], 16)
            elif op.signal:
                inst.then_inc(self.esem[op.eng], 1)
            op.fn = None
        self.n_emitted = len(self.ops)
        return dict(n_ops=len(self.ops), n_waits=self.nw, ranks=dict(self.rk))

    def phase_end(self, name=""):
        npe = sum(1 for o in self.ops if o.eng == "pe" and not o.is_dma and (o.fn is not None))
        self.phase_log = getattr(self, "phase_log", [])
        self.pe_total = getattr(self, "pe_total", 0) + npe
        self.phase_log.append((name, self.pe_total))
        self.barrier()
        r = self.emit()
        self.ops = []
        self.n_emitted = 0
        return r


def make_consts():
    i = np.arange(128)
    J, I = np.meshgrid(i, i, indexing="ij")
    c = np.zeros((128, NCONST), np.float32)
    c[:, 0:128] = (J == I)
    c[:, 128:256] = (J <= I)
    c[:, 256:384] = (J >= I)
    c[:, 384:512] = 1.0
    c[:, 512:640] = np.where(I >= J, 0.0, -30000.0)
    c[:, 640:768] = np.where(I <= J, 0.0, -30000.0)
    c[:, 768:896] = np.maximum(I - J, 0)
    c[:, 896:1024] = np.maximum(J - I, 0)
    c[:, 1024:1152] = I + 1
    c[:, 1152:1280] = 128 - I
    c[:, 1280] = 127 - i
    c[:, 1281] = i
    return c


def make_rope():
    rows = 2048 // 64
    row = np.repeat(np.arange(rows, dtype=np.float32), 64)
    col = np.tile(np.arange(64, dtype=np.float32), rows)
    n_freq = 32
    inv = (np.float32(10000.0) ** (-np.arange(n_freq, dtype=np.float32) / n_freq)).astype(np.float32)
    ang = np.concatenate([row[:, None] * inv, col[:, None] * inv], axis=-1).astype(np.float32)
    return np.concatenate([np.cos(ang), np.sin(ang)], axis=-1).astype(np.float32)


SMALL = [("b_mod", [2, 12288]), ("pre_mix_g", [2, 2048]), ("post_mix_g", [2, 2048]), ("pre_ffn_g", [2, 2048]),
         ("post_ffn_g", [2, 2048]), ("q_norm_g", [2, 128]), ("k_norm_g", [2, 128]), ("ret_decay_f", [2, 4]),
         ("ret_decay_b", [2, 4]), ("conv_w", [2, 5, 1280]), ("conv_b", [2, 1280]), ("dt_bias_f", [2, 12]),
         ("dt_bias_b", [2, 12]), ("a_log_f", [2, 12]), ("a_log_b", [2, 12]), ("d_skip", [2, 12]),
         ("ssd_norm_g", [2, 768])]
BIG = [("w_mod", [2, 2048, 12288]), ("w_in", [2, 2048, DIN]), ("w_out", [2, 2048, 2048]),
       ("w_gate", [2, 2048, DFF]), ("w_up", [2, 2048, DFF]), ("w_down", [2, DFF, 2048])]


def build(nlayers=2, debug=False, stop=None, phases=None, feed=(), layers=None):
    nc = bass.Bass("TRN2", target_bir_lowering=False)
    SHAPES = dict(SMALL + BIG)
    SHAPES.update({"x": [2048, 2048], "ctx": [256, 2048], "cvec": [2, 2048], "consts": [128, NCONST], "rope": [2048, 128]})

    class LazyIn(dict):
        def __missing__(self, name):
            ap = nc.dram_tensor(name, SHAPES[name], F32, kind="ExternalInput").ap()
            self[name] = ap
            return ap

    I = LazyIn()
    if not debug:
        for n in ["x", "ctx", "cvec", "consts", "rope"] + [n for n, _ in SMALL + BIG]:
            I[n]
    out = nc.dram_tensor("out", [2048, 2048], F32, kind="ExternalOutput").ap()
    skind = "ExternalOutput" if debug else "Internal"

    def scr(name, shape, dt=F32):
        k = "ExternalInput" if name in feed else skind
        return nc.dram_tensor(name, shape, dt, kind=k).ap()

    modv = scr("modv", [2, 2, 12288])
    proj = scr("proj", [T, PW])
    xbcT = scr("xbcT", [1280, T])
    mixT = scr("mixT", [2048, T], BF16)
    x1s = scr("x1s", [T, D])
    h2T = scr("h2T", [2048, T], BF16)
    fsc = scr("fsc", [T, D])
    xnext = scr("xnext", [T, D])
    zsd = scr("zsd", [T, 768], BF16)

    class Stop(Exception):
        pass

    with ExitStack() as gst:
        S = Sched(nc, gst)

        uid = [0]

        def mk_alloc(st):
            def Tl(name, shape, dt=F32):
                uid[0] += 1
                return st.enter_context(nc.sbuf_tensor("%s_%d" % (name, uid[0]), shape, dt))

            def Pl(name, shape, dt=F32):
                uid[0] += 1
                return st.enter_context(nc.psum_tensor("%s_%d" % (name, uid[0]), shape, dt))
            return Tl, Pl

        GT, GP = mk_alloc(gst)
        cst = GT("cst", [128, NCONST])
        ident_b = GT("ident_b", [128, 128], BF16)
        ones_b = GT("ones_b", [128, 128], BF16)
        S.dma("sp", lambda h: h.dma_start(out=cst[:], in_=I["consts"][:, :]), w=["cst"])
        S.add("dve", lambda h: h.tensor_copy(out=ident_b[:], in_=cst[:, 0:128]), r=["cst"], w=["ident_b"])
        S.add("dve", lambda h: h.tensor_copy(out=ones_b[:], in_=cst[:, 384:512]), r=["cst"], w=["ones_b"])
        ident_f = cst[:, 0:128]
        tri_f = cst[:, 128:256]
        tri_b = cst[:, 256:384]
        ones_f = cst[:, 384:512]
        mneg = [cst[:, 512:640], cst[:, 640:768]]
        D1 = cst[:, 768:896]
        D2 = cst[:, 896:1024]
        rowidx = [cst[:, 1024:1152], cst[:, 1152:1280]]
        colexp = [cst[:, 1280:1281], cst[:, 1281:1282]]
        S.phase_end("init")

        def check_stop(tag):
            if stop == tag:
                raise Stop()

        def rstd_ops(ssq_ap, out_ap, inv_n, rk, wk):
            S.add("act", lambda h: h.activation(out=out_ap, in_=ssq_ap, func=AF.Ln, scale=inv_n, bias=EPS), r=rk, w=wk)
            S.add("act", lambda h: h.activation(out=out_ap, in_=out_ap, func=AF.Exp, scale=-0.5), r=wk, w=wk)

        def xsrc(l, tt):
            if l == 0:
                if tt < 2:
                    return I["ctx"][tt * 128:(tt + 1) * 128, :]
                return I["x"][(tt - 2) * 128:(tt - 1) * 128, :]
            return xnext[tt * 128:(tt + 1) * 128, :]

        def bcast_row(ap_row, n):
            return ap_row.to_broadcast([128, n])

        def mod_task(l, Tl, Pl):
            cT = Tl("cT", [128, 16, 2])
            ce = Tl("ce", [128, 16, 2])
            sT = Tl("sT", [128, 16, 2], BF16)
            for r_ in range(2):
                S.dma("sp", lambda h, r_=r_: h.dma_start(out=cT[:, :, r_], in_=I["cvec"][r_].rearrange("(k p) -> p k", p=128),
                                                         allow_slow_non_contiguous=True), w=["mcT"])
            S.add("act", lambda h: h.activation(out=ce[:], in_=cT[:], func=AF.Exp, scale=-1.0), r=["mcT"], w=["mce"])
            S.add("dve", lambda h: h.tensor_scalar(out=ce[:], in0=ce[:], scalar1=1.0, scalar2=None, op0=ALU.add), r=["mce"], w=["mce"])
            S.add("dve", lambda h: h.reciprocal(out=ce[:], in_=ce[:]), r=["mce"], w=["mce"])
            S.add("dve", lambda h: h.tensor_tensor(out=sT[:], in0=cT[:], in1=ce[:], op=ALU.mult), r=["mce", "mcT"], w=["msT"])
            wb = [Tl("wmb%d" % i, [128, 16, 512], BF16) for i in range(2)]
            pm = Pl("pm", [128, 512])
            bsb = [Tl("bsb%d" % i, [2, 512]) for i in range(2)]
            msb = [Tl("msb%d" % i, [2, 512]) for i in range(2)]
            wv = I["w_mod"][l].rearrange("(k p) n -> p k n", p=128)
            yield

            def block(nb):
                b = nb % 2
                S.dma("sp", lambda h: h.dma_start(out=bsb[b][:], in_=I["b_mod"][l:l + 1, nb * 512:(nb + 1) * 512].to_broadcast([2, 512])), w=[("mbsb", b)])
                for k4 in range(4):
                    S.dma("pool", lambda h, k4=k4: h.dma_start(out=wb[b][:, 4 * k4:4 * k4 + 4, :], in_=wv[:, 4 * k4:4 * k4 + 4, nb * 512:(nb + 1) * 512]),
                          w=[("wmb", b, k4)])
                for k in range(16):
                    S.add("pe", lambda h, k=k: h.matmul(pm[0:2, :], lhsT=sT[:, k, :], rhs=wb[b][:, k, :], start=(k == 0), stop=(k == 15)),
                          r=["msT", ("wmb", b, k // 4)], w=["mpm"])
                S.add("dve", lambda h: h.tensor_tensor(out=msb[b][:], in0=pm[0:2, :], in1=bsb[b][:], op=ALU.add), r=["mpm", ("mbsb", b)], w=[("mmsb", b)])
                S.dma("sp", lambda h: h.dma_start(out=modv[l, :, nb * 512:(nb + 1) * 512], in_=msb[b][:]), r=[("mmsb", b)], w=[("modv", l, nb)])
            for nb in range(24):
                block(nb)
                yield

        def phase_mod(l):
            with ExitStack() as st:
                Tl, Pl = mk_alloc(st)
                for _ in mod_task(l, Tl, Pl):
                    pass
                S.phase_end("mod")

        def norm_mod_tiles(l, Tl, tagp, gname, sc_off, sh_off, r):
            gm = Tl(tagp + "gm", [128, D])
            sh = Tl(tagp + "sh", [128, D])
            gp = Tl(tagp + "gp", [128, D])
            return gm, sh, gp

        def load_norm_mod(l, gm, sh, gp, key, gname, sc_off, sh_off, r):
            S.dma("sp", lambda h: h.dma_start(out=gp[:], in_=bcast_row(I[gname][l:l + 1, :], D)), w=[key + "gp"])
            S.dma("sp", lambda h: h.dma_start(out=gm[:], in_=bcast_row(modv[l, r:r + 1, sc_off:sc_off + D], D)), w=[key + "gm"])
            S.dma("sp", lambda h: h.dma_start(out=sh[:], in_=bcast_row(modv[l, r:r + 1, sh_off:sh_off + D], D)), w=[key + "sh"])
            S.add("dve", lambda h: h.scalar_tensor_tensor(out=gm[:], in0=gm[:], scalar=1.0, in1=gp[:], op0=ALU.add, op1=ALU.mult),
                  r=[key + "gp", key + "gm"], w=[key + "gm"])

        def load_gate_mod(l, G, gp, key, gname, g_off, r):
            S.dma("sp", lambda h: h.dma_start(out=gp[:], in_=bcast_row(I[gname][l:l + 1, :], D)), w=[key + "gp"])
            S.dma("sp", lambda h: h.dma_start(out=G[:], in_=bcast_row(modv[l, r:r + 1, g_off:g_off + D], D)), w=[key + "G"])
            S.add("dve", lambda h: h.tensor_tensor(out=G[:], in0=G[:], in1=gp[:], op=ALU.mult), r=[key + "gp", key + "G"], w=[key + "G"])

        def norm_transpose_tile(l, tt, xt_ap, xkey, gm, sh, mkey, tmp, hb, junk, ssq, pt, dst_fn, wkeys, i):
            sq = ssq[:, 2 * (i % 2):2 * (i % 2) + 1]
            rs = ssq[:, 2 * (i % 2) + 1:2 * (i % 2) + 2]
            S.add("act", lambda h: h.activation(out=junk[:], in_=xt_ap, func=AF.Square, accum_out=sq),
                  r=[xkey], w=["junk", ("ssq", i % 2)])
            rstd_ops(sq, rs, 1.0 / D, [("ssq", i % 2)], [("rstd", i % 2)])
            S.add("dve", lambda h: h.scalar_tensor_tensor(out=tmp[:], in0=xt_ap, scalar=rs, in1=gm[:], op0=ALU.mult, op1=ALU.mult),
                  r=[xkey, ("rstd", i % 2), mkey + "gm"], w=["tmp"])
            S.add("dve", lambda h: h.tensor_tensor(out=hb[:], in0=tmp[:], in1=sh[:], op=ALU.add), r=["tmp", mkey + "sh"], w=[("hb", i % 2)])
            for k in range(16):
                S.add("pe", lambda h, k=k: h.transpose(out=pt[:, k, :], in_=hb[:, k * 128:(k + 1) * 128], identity=ident_b[:]),
                      r=[("hb", i % 2)], w=[("pt", i % 2)])
            return lambda: dst_fn(pt, ("pt", i % 2), wkeys)

        def phase_in(l):
            with ExitStack() as st_o:
                To, Po = mk_alloc(st_o)
                hT = To("hT", [128, 16, T], BF16)
                with ExitStack() as st:
                    Tl, Pl = mk_alloc(st)
                    mods = {}
                    for r, nm in ((1, "c"), (0, "l")):
                        gm = Tl("gm" + nm, [128, D])
                        sh = Tl("sh" + nm, [128, D])
                        gp = Tl("gp" + nm, [128, D])
                        load_norm_mod(l, gm, sh, gp, nm, "pre_mix_g", 2048, 0, r)
                        mods[r] = (gm, sh, nm)
                    xt = [Tl("xt%d" % i, [128, D]) for i in range(3)]
                    tmp = Tl("tmp", [128, D])
                    hb = [Tl("hb%d" % i, [128, D], BF16) for i in range(2)]
                    junk = Tl("junk", [128, D], BF16)
                    ssq = Tl("ssq", [128, 4])
                    pt = [Pl("pt%d" % i, [128, 16, 128], BF16) for i in range(2)]
                    pend = None
                    for tt in range(NT):
                        i = tt
                        gm, sh, nm = mods[1 if tt < 2 else 0]
                        S.dma("sp", lambda h, tt=tt, i=i: h.dma_start(out=xt[i % 3][:], in_=xsrc(l, tt)), w=[("xt", i % 3)])

                        def dst(ptile, pkey, wkeys, tt=tt):
                            S.add("act", lambda h: h.activation(out=hT[:, :, tt * 128:(tt + 1) * 128], in_=ptile[:], func=AF.Copy),
                                  r=[pkey], w=wkeys)
                        th = norm_transpose_tile(l, tt, xt[i % 3][:], ("xt", i % 3), gm, sh, nm, tmp, hb[i % 2], junk, ssq, pt[i % 2], dst,
                                                 [("hT", tt)], i)
                        if pend is not None:
                            pend()
                        pend = th
                    pend()
                    S.phase_end("P1")
                with ExitStack() as st:
                    Tl, Pl = mk_alloc(st)
                    wb = [Tl("wib%d" % i, [128, 16, 512], BF16) for i in range(2)]
                    stg = [Tl("stg%d" % i, [128, 512]) for i in range(4)]
                    pp = [Pl("pp%d" % i, [128, 512]) for i in range(4)]
                    wv = I["w_in"][l].rearrange("(k p) n -> p k n", p=128)
                    cnt = [0]

                    def evac(ps_ap, n, dram_ap, pkey):
                        i = cnt[0]
                        cnt[0] += 1
                        s = stg[i % 4]
                        if i % 2 == 0:
                            S.add("act", lambda h: h.activation(out=s[:, 0:n], in_=ps_ap, func=AF.Copy), r=[pkey], w=[("stg", i % 4)])
                        else:
                            S.add("dve", lambda h: h.tensor_copy(out=s[:, 0:n], in_=ps_ap), r=[pkey], w=[("stg", i % 4)])
                        S.dma("sp", lambda h: h.dma_start(out=dram_ap, in_=s[:, 0:n]), r=[("stg", i % 4)], w=[("dram", i)])

                    blocks = [(c0, 512) for c0 in range(0, 5120, 512)] + [(5120, 280)]
                    pi = [0]
                    for bi, (c0, ncol) in enumerate(blocks):
                        b = bi % 2
                        for k4 in range(4):
                            S.dma("pool", lambda h, b=b, k4=k4, c0=c0, ncol=ncol: h.dma_start(
                                out=wb[b][:, 4 * k4:4 * k4 + 4, 0:ncol], in_=wv[:, 4 * k4:4 * k4 + 4, c0:c0 + ncol]), w=[("wib", b, k4)])
                        wkeys = [("wib", b, k4) for k4 in range(4)]
                        if c0 < 4096:
                            tm = (0, ncol, c0)
                            fm_chunks = []
                        elif c0 < 5120:
                            tm = None
                            fm_chunks = [(j, (c0 - 4096) // 128 + j) for j in range(4)]
                        else:
                            tm = (256, 24, 4096)
                            fm_chunks = [(0, 8), (1, 9)]
                        if tm is not None:
                            co, n, dc = tm
                            for tt in range(NT):
                                p = pi[0] % 4
                                pi[0] += 1
                                for k in range(16):
                                    S.add("pe", lambda h, p=p, k=k, tt=tt, co=co, n=n, b=b: h.matmul(
                                        pp[p][:, 0:n], lhsT=hT[:, k, tt * 128:(tt + 1) * 128], rhs=wb[b][:, k, co:co + n],
                                        start=(k == 0), stop=(k == 15)), r=wkeys, w=[("pp", p)])
                                evac(pp[p][:, 0:n], n, proj[tt * 128:(tt + 1) * 128, dc:dc + n], ("pp", p))
                        for (j, ch) in fm_chunks:
                            for t0 in range(0, T, 512):
                                n = min(512, T - t0)
                                p = pi[0] % 4
                                pi[0] += 1
                                for k in range(16):
                                    S.add("pe", lambda h, p=p, k=k, t0=t0, n=n, j=j, b=b: h.matmul(
                                        pp[p][:, 0:n], lhsT=wb[b][:, k, j * 128:(j + 1) * 128], rhs=hT[:, k, t0:t0 + n],
                                        start=(k == 0), stop=(k == 15)), r=wkeys, w=[("pp", p)])
                                evac(pp[p][:, 0:n], n, xbcT[ch * 128:(ch + 1) * 128, t0:t0 + n], ("pp", p))
                    S.phase_end("P2")

        def rope_ops(src, dst, cs, nh, lat, skey, dkey, cskey, tmps):
            if not lat:
                S.add("dve", lambda h: h.tensor_copy(out=dst, in_=src), r=[skey], w=[dkey, (dkey, "b")])
                return
            t1, t2, t3, t4 = tmps
            s4 = src.rearrange("p (h i two) -> p h i two", h=nh, two=2)
            d4 = dst.rearrange("p (h i two) -> p h i two", h=nh, two=2)
            x1 = s4[:, :, :, 0]
            x2 = s4[:, :, :, 1]
            cosb = cs[:, 0:64].unsqueeze(1).to_broadcast([128, nh, 64])
            sinb = cs[:, 64:128].unsqueeze(1).to_broadcast([128, nh, 64])
            v = lambda t: t[:, 0:nh * 64].rearrange("p (h i) -> p h i", h=nh)
            S.add("dve", lambda h: h.tensor_tensor(out=v(t1), in0=x1, in1=cosb, op=ALU.mult), r=[skey, cskey], w=["rt1"])
            S.add("dve", lambda h: h.tensor_tensor(out=v(t2), in0=x2, in1=sinb, op=ALU.mult), r=[skey, cskey], w=["rt2"])
            S.add("dve", lambda h: h.tensor_tensor(out=d4[:, :, :, 0], in0=v(t1), in1=v(t2), op=ALU.subtract), r=["rt1", "rt2"], w=[dkey])
            S.add("dve", lambda h: h.tensor_tensor(out=v(t3), in0=x1, in1=sinb, op=ALU.mult), r=[skey, cskey], w=["rt3"])
            S.add("dve", lambda h: h.tensor_tensor(out=v(t4), in0=x2, in1=cosb, op=ALU.mult), r=[skey, cskey], w=["rt4"])
            S.add("dve", lambda h: h.tensor_tensor(out=d4[:, :, :, 1], in0=v(t3), in1=v(t4), op=ALU.add), r=["rt3", "rt4"], w=[(dkey, "b")])

        def phase_att(l):
            last = (l == nlayers - 1)
            with ExitStack() as st:
                Tl, Pl = mk_alloc(st)
                QKT = Tl("QKT", [128, 8, T], BF16)
                Vtm = Tl("Vtm", [128, NT, 256], BF16)
                gqk = Tl("gqk", [128, 8, 128])
                pr = [Tl("pr%d" % i, [128, 1280]) for i in range(2)]
                sq = Tl("sq", [128, 1024])
                qn = Tl("qn", [128, 1024])
                tmps = [Tl("rt%d" % i, [128, 512]) for i in range(4)]
                qr = [Tl("qr%d" % i, [128, 1024], BF16) for i in range(2)]
                cs = [Tl("cs%d" % i, [128, 128]) for i in range(2)]
                st8 = Tl("st8", [128, 16])
                ptq = Pl("ptq", [128, 8, 128], BF16)
                for hh in range(8):
                    src = I["q_norm_g"] if hh < 6 else I["k_norm_g"]
                    S.dma("sp", lambda h, hh=hh, src=src: h.dma_start(out=gqk[:, hh, :], in_=src[l:l + 1, :].to_broadcast([128, 128])), w=["gqk"])
                S.add("dve", lambda h: h.tensor_scalar(out=gqk[:, 0:6, :], in0=gqk[:, 0:6, :], scalar1=float(128 ** -0.5), scalar2=None, op0=ALU.mult),
                      r=["gqk"], w=["gqk"])
                def prep_tile(tt):
                    i = tt
                    lat = tt >= 2
                    p_ = pr[i % 2]
                    S.dma("sp", lambda h, tt=tt, p_=p_: h.dma_start(out=p_[:], in_=proj[tt * 128:(tt + 1) * 128, 0:1280]), w=[("pr", i % 2)])
                    if lat:
                        S.dma("sp", lambda h, tt=tt, i=i: h.dma_start(out=cs[i % 2][:], in_=I["rope"][(tt - 2) * 128:(tt - 1) * 128, :]), w=[("cs", i % 2)])
                    S.add("act", lambda h, p_=p_: h.activation(out=sq[:], in_=p_[:, 0:1024], func=AF.Square), r=[("pr", i % 2)], w=["sq"])
                    S.add("dve", lambda h: h.tensor_reduce(out=st8[:, 0:8], in_=sq[:].rearrange("p (h d) -> p h d", h=8), axis=AX.X, op=ALU.add),
                          r=["sq"], w=["ssq8"])
                    rstd_ops(st8[:, 0:8], st8[:, 8:16], 1.0 / 128, ["ssq8"], ["rstd8"])
                    S.add("dve", lambda h, p_=p_: h.tensor_tensor(out=qn[:].rearrange("p (h d) -> p h d", h=8),
                                                                 in0=p_[:, 0:1024].rearrange("p (h d) -> p h d", h=8),
                                                                 in1=st8[:, 8:16].unsqueeze(2).to_broadcast([128, 8, 128]), op=ALU.mult),
                          r=[("pr", i % 2), "rstd8"], w=["qn"])
                    S.add("dve", lambda h: h.tensor_tensor(out=qn[:], in0=qn[:], in1=gqk[:].rearrange("p h d -> p (h d)"), op=ALU.mult),
                          r=["qn", "gqk"], w=["qn"])
                    q_ = qr[i % 2]
                    rope_ops(qn[:], q_[:], cs[i % 2], 8, lat, "qn", ("qr", i % 2), ("cs", i % 2), tmps)
                    for hh in range(8):
                        S.add("pe", lambda h, hh=hh, q_=q_: h.transpose(out=ptq[:, hh, :], in_=q_[:, hh * 128:(hh + 1) * 128], identity=ident_b[:]),
                              r=[("qr", i % 2), (("qr", i % 2), "b")], w=["ptq"])
                    S.add("act", lambda h, tt=tt: h.activation(out=QKT[:, :, tt * 128:(tt + 1) * 128], in_=ptq[:], func=AF.Copy), r=["ptq"], w=[("QKT", tt)])
                    S.add("act", lambda h, tt=tt, p_=p_: h.activation(out=Vtm[:, tt, :], in_=p_[:, 1024:1280], func=AF.Copy), r=[("pr", i % 2)], w=[("Vtm", tt)])
                for tt in range(NT):
                    prep_tile(tt)
                ps_s = [Pl("ps_s%d" % i, [128, 512]) for i in range(2)]
                ps_o = [Pl("ps_o%d" % i, [128, 512]) for i in range(2)]
                ps_d = [Pl("ps_d%d" % i, [128, 512]) for i in range(2)]
                pT = [Tl("pT%d" % i, [128, 512], BF16) for i in range(3)]
                rden = Tl("rden", [128, 512])
                oT = [Tl("oT%d" % i, [128, 512], BF16) for i in range(2)]
                qblocks = [(256 + qb * 512, 512, list(range(NT))) for qb in range(4)]
                if not last:
                    qblocks = [(0, 256, [0, 1])] + qblocks
                gi = 0
                si = 0
                def att_head(q0, n, kts, hh, gi):
                        nonlocal si
                        g = hh // 3
                        o = gi % 2
                        nk = len(kts)
                        slots = []

                        def do_s(j):
                            nonlocal si
                            sidx = si
                            si += 1
                            kt = kts[j]
                            S.add("pe", lambda h, sidx=sidx, kt=kt: h.matmul(ps_s[sidx % 2][:, 0:n], lhsT=QKT[:, 6 + g, kt * 128:(kt + 1) * 128],
                                                                           rhs=QKT[:, hh, q0:q0 + n], start=True, stop=True),
                                  r=[("QKT", kt)] + [("QKT", q0 // 128 + a) for a in range(n // 128)], w=[("ps_s", sidx % 2)])
                            S.add("act", lambda h, sidx=sidx: h.activation(out=pT[sidx % 3][:, 0:n], in_=ps_s[sidx % 2][:, 0:n], func=AF.Exp),
                                  r=[("ps_s", sidx % 2)], w=[("pT", sidx % 3)])
                            slots.append(sidx)

                        def do_o(j):
                            sidx = slots[j]
                            kt = kts[j]
                            S.add("pe", lambda h, sidx=sidx, kt=kt: h.matmul(ps_o[o][:, 0:n], lhsT=Vtm[:, kt, g * 128:(g + 1) * 128], rhs=pT[sidx % 3][:, 0:n],
                                                                           start=(j == 0), stop=(j == nk - 1)),
                                  r=[("pT", sidx % 3), ("Vtm", kt)], w=[("ps_o", o)])
                            S.add("pe", lambda h, sidx=sidx: h.matmul(ps_d[o][:, 0:n], lhsT=ones_b[:], rhs=pT[sidx % 3][:, 0:n],
                                                                    start=(j == 0), stop=(j == nk - 1)),
                                  r=[("pT", sidx % 3)], w=[("ps_d", o)])
                        do_s(0)
                        for j in range(nk):
                            if j + 1 < nk:
                                do_s(j + 1)
                            do_o(j)
                        S.add("dve", lambda h, o=o: h.reciprocal(out=rden[:, 0:n], in_=ps_d[o][:, 0:n]), r=[("ps_d", o)], w=["rden"])
                        S.add("dve", lambda h, o=o: h.tensor_tensor(out=oT[o][:, 0:n], in0=ps_o[o][:, 0:n], in1=rden[:, 0:n], op=ALU.mult),
                              r=[("ps_o", o), "rden"], w=[("oT", o)])
                        S.dma("sp", lambda h, o=o, hh=hh, q0=q0, n=n: h.dma_start(out=mixT[hh * 128:(hh + 1) * 128, q0:q0 + n], in_=oT[o][:, 0:n]),
                              r=[("oT", o)], w=[("mixT", gi)])
                bg = mod_task(l + 1, Tl, Pl) if (l + 1 < nlayers and (phases is None or "mod" in phases)) else iter(())
                next(bg, None)
                for (q0, n, kts) in qblocks:
                    for hh in range(6):
                        att_head(q0, n, kts, hh, gi)
                        gi += 1
                        next(bg, None)
                for _ in bg:
                    pass
                S.phase_end("att")

        def silu2_ops(src, e, dst, skey, ekey, dkey, in_scale=0.5):
            S.add("act", lambda h: h.activation(out=e, in_=src, func=AF.Tanh, scale=in_scale), r=[skey], w=[ekey])
            S.add("dve", lambda h: h.scalar_tensor_tensor(out=dst, in0=e, scalar=1.0, in1=src, op0=ALU.add, op1=ALU.mult), r=[skey, ekey], w=[dkey])

        def phase_ret(l):
            last = (l == nlayers - 1)
            with ExitStack() as st:
                Tl, Pl = mk_alloc(st)
                RQK = Tl("RQK", [128, 8, T], BF16)
                RKtm = Tl("RKtm", [128, NT, 512], BF16)
                RVtm = Tl("RVtm", [128, NT, 512], BF16)
                rgs = Tl("rgs", [128, NT, 512], BF16)
                SAll = [Tl("SAll%d" % d_, [128, NT, 512], BF16) for d_ in range(2)]
                S32 = [Tl("S32%d" % d_, [128, 512]) for d_ in range(2)]
                MaskT = Tl("MaskT", [128, 4, 128])
                ERow = [Tl("ERow%d" % d_, [128, 4, 128], BF16) for d_ in range(2)]
                dec = Tl("dec", [128, 8])
                lg = Tl("lg", [128, 8])
                dend = Tl("dend", [128, 8])
                gam = Tl("gam", [128, 8])
                marg = Tl("marg", [128, 128])
                pr = [Tl("rpr%d" % i, [128, 2048]) for i in range(2)]
                tmps = [Tl("rrt%d" % i, [128, 512]) for i in range(4)]
                qr = [Tl("rqr%d" % i, [128, 1024], BF16) for i in range(2)]
                cs = [Tl("rcs%d" % i, [128, 128]) for i in range(2)]
                ee = Tl("ree", [128, 512])
                ptq = Pl("rptq", [128, 8, 128], BF16)
                S.dma("sp", lambda h: h.dma_start(out=dec[:, 0:4], in_=I["ret_decay_f"][l:l + 1, :].to_broadcast([128, 4])), w=["dec"])
                S.dma("sp", lambda h: h.dma_start(out=dec[:, 4:8], in_=I["ret_decay_b"][l:l + 1, :].to_broadcast([128, 4])), w=["dec"])
                S.add("act", lambda h: h.activation(out=lg[:], in_=dec[:], func=AF.Exp, scale=float(math.log(2.0))), r=["dec"], w=["lg"])
                S.add("act", lambda h: h.activation(out=lg[:], in_=lg[:], func=AF.Ln, scale=-1.0, bias=1.0), r=["lg"], w=["lg"])
                S.add("act", lambda h: h.activation(out=gam[:], in_=lg[:], func=AF.Exp, scale=128.0), r=["lg"], w=["gam"])

                def mk_head(hh):
                    S.add("dve", lambda h: h.tensor_scalar(out=marg[:], in0=D1, scalar1=lg[:, hh:hh + 1], scalar2=None, op0=ALU.mult), r=["lg"], w=["marg"])
                    S.add("dve", lambda h: h.scalar_tensor_tensor(out=marg[:], in0=D2, scalar=lg[:, 4 + hh:5 + hh], in1=marg[:], op0=ALU.mult, op1=ALU.add),
                          r=["lg", "marg"], w=["marg"])
                    S.add("act", lambda h: h.activation(out=marg[:], in_=marg[:], func=AF.Exp), r=["marg"], w=["marg"])
                    S.add("dve", lambda h: h.tensor_tensor(out=MaskT[:, hh, :], in0=marg[:], in1=ident_f, op=ALU.add), r=["marg"], w=["MaskT"])
                    for d_ in range(2):
                        S.add("act", lambda h, d_=d_: h.activation(out=ERow[d_][:, hh, :], in_=rowidx[d_], func=AF.Exp, scale=lg[:, 4 * d_ + hh:4 * d_ + hh + 1]),
                              r=["lg"], w=["ERow"])
                        S.add("act", lambda h, d_=d_: h.activation(out=dend[:, 4 * d_ + hh:4 * d_ + hh + 1], in_=colexp[d_], func=AF.Exp,
                                                                   scale=lg[:, 4 * d_ + hh:4 * d_ + hh + 1]), r=["lg"], w=["dend"])
                for hh in range(4):
                    mk_head(hh)

                def prep_tile(tt):
                    i = tt
                    lat = tt >= 2
                    p_ = pr[i % 2]
                    S.dma("sp", lambda h: h.dma_start(out=p_[:], in_=proj[tt * 128:(tt + 1) * 128, 1280:3328]), w=[("pr", i % 2)])
                    if lat:
                        S.dma("sp", lambda h: h.dma_start(out=cs[i % 2][:], in_=I["rope"][(tt - 2) * 128:(tt - 1) * 128, :]), w=[("cs", i % 2)])
                    S.add("act", lambda h: h.activation(out=p_[:, 512:1024], in_=p_[:, 512:1024], func=AF.Copy, scale=float(128 ** -0.5)),
                          r=[("pr", i % 2)], w=[("pr", i % 2)])
                    q_ = qr[i % 2]
                    rope_ops(p_[:, 0:1024], q_[:], cs[i % 2], 8, lat, ("pr", i % 2), ("qr", i % 2), ("cs", i % 2), tmps)
                    for hh in range(8):
                        S.add("pe", lambda h, hh=hh: h.transpose(out=ptq[:, hh, :], in_=q_[:, hh * 128:(hh + 1) * 128], identity=ident_b[:]),
                              r=[("qr", i % 2), (("qr", i % 2), "b")], w=["ptq"])
                    S.add("act", lambda h: h.activation(out=RQK[:, :, tt * 128:(tt + 1) * 128], in_=ptq[:], func=AF.Copy), r=["ptq"], w=[("RQK", tt)])
                    S.add("dve", lambda h: h.tensor_copy(out=RKtm[:, tt, :], in_=q_[:, 512:1024]), r=[("qr", i % 2), (("qr", i % 2), "b")], w=[("RKtm", tt)])
                    S.add("act", lambda h: h.activation(out=RVtm[:, tt, :], in_=p_[:, 1024:1536], func=AF.Copy), r=[("pr", i % 2)], w=[("RVtm", tt)])
                    silu2_ops(p_[:, 1536:2048], ee[:], rgs[:, tt, :], ("pr", i % 2), "ee", ("rgs", tt))
                for tt in range(NT):
                    prep_tile(tt)

                psA2 = [Pl("psA%d" % d_, [128, 512]) for d_ in range(2)]
                RVs2 = [[Tl("RVs%d_%d" % (d_, i), [128, 512], BF16) for i in range(2)] for d_ in range(2)]
                orders = [list(range(NT)), [1, 0] + list(range(NT - 1, 1, -1))]

                def state_step(d_, idx):
                    order = orders[d_]
                    c = order[idx]
                    if idx == 0:
                        S.add("pool", lambda h: h.memset(S32[d_][:], 0.0), w=[("S32", d_)])
                        S.add("pool", lambda h: h.memset(SAll[d_][:, c, :], 0.0), w=[("SAll", d_, c)])
                    if idx == NT - 1:
                        return
                    rv = RVs2[d_][idx % 2]
                    psA = psA2[d_]
                    S.add("dve", lambda h: h.tensor_tensor(out=rv[:].rearrange("p (h d) -> p h d", h=4),
                                                            in0=RVtm[:, c, :].rearrange("p (h d) -> p h d", h=4),
                                                            in1=dend[:, 4 * d_:4 * d_ + 4].unsqueeze(2).to_broadcast([128, 4, 128]), op=ALU.mult),
                          r=[("RVtm", c), "dend"], w=[("RVs", d_, idx % 2)])
                    for hh in range(4):
                        S.add("pe", lambda h, hh=hh: h.matmul(psA[:, hh * 128:(hh + 1) * 128], lhsT=RKtm[:, c, hh * 128:(hh + 1) * 128],
                                                             rhs=rv[:, hh * 128:(hh + 1) * 128], start=True, stop=True),
                              r=[("RKtm", c), ("RVs", d_, idx % 2)], w=[("psA", d_)])
                    S.add("dve", lambda h: h.tensor_tensor(out=S32[d_][:].rearrange("p (h d) -> p h d", h=4),
                                                           in0=S32[d_][:].rearrange("p (h d) -> p h d", h=4),
                                                           in1=gam[:, 4 * d_:4 * d_ + 4].unsqueeze(2).to_broadcast([128, 4, 128]), op=ALU.mult),
                          r=[("S32", d_), "gam"], w=[("S32", d_)])
                    S.add("dve", lambda h: h.tensor_tensor(out=S32[d_][:], in0=S32[d_][:], in1=psA[:], op=ALU.add), r=[("S32", d_), ("psA", d_)], w=[("S32", d_)])
                    cn = order[idx + 1]
                    S.add("act", lambda h: h.activation(out=SAll[d_][:, cn, :], in_=S32[d_][:], func=AF.Copy), r=[("S32", d_)], w=[("SAll", d_, cn)])
                for idx in range(NT):
                    for d_ in range(2):
                        state_step(d_, idx)

                psS = [Pl("psS%d" % i, [128, 4, 128]) for i in range(2)]
                psY = [Pl("psY%d" % i, [128, 512]) for i in range(2)]
                ptr = Pl("ptr", [128, 8, 128], BF16)
                SDT = [Tl("SDT%d" % i, [128, 4, 128], BF16) for i in range(2)]
                RQs = [[Tl("RQs%d_%d" % (d_, i), [128, 4, 128], BF16) for i in range(2)] for d_ in range(2)]
                sqy = Tl("sqy", [128, 512])
                yn = Tl("yn", [128, 512])
                yb = [Tl("yb%d" % i, [128, 512], BF16) for i in range(2)]
                stgr = [Tl("stgr%d" % i, [128, 4, 128], BF16) for i in range(2)]
                st4 = Tl("st4", [128, 8])

                def chunk(c, i):
                    y = psY[i % 2]
                    cs_ = slice(c * 128, (c + 1) * 128)
                    for hh in range(4):
                        S.add("pe", lambda h, hh=hh: h.matmul(psS[i % 2][:, hh, :], lhsT=RQK[:, 4 + hh, cs_], rhs=RQK[:, hh, cs_], start=(hh == 0), stop=(hh == 3)),
                              r=[("RQK", c)], w=[("psS", i % 2)])
                    S.add("dve", lambda h: h.tensor_tensor(out=SDT[i % 2][:], in0=psS[i % 2][:], in1=MaskT[:], op=ALU.mult),
                          r=[("psS", i % 2), "MaskT"], w=[("SDT", i % 2)])
                    for d_ in range(2):
                        S.add("dve", lambda h, d_=d_: h.tensor_tensor(out=RQs[d_][i % 2][:], in0=RQK[:, 0:4, cs_], in1=ERow[d_][:], op=ALU.mult),
                              r=[("RQK", c), "ERow"], w=[("RQs", d_, i % 2)])
                    for hh in range(4):
                        S.add("pe", lambda h, hh=hh: h.matmul(y[:, hh * 128:(hh + 1) * 128], lhsT=SDT[i % 2][:, hh, :], rhs=RVtm[:, c, hh * 128:(hh + 1) * 128],
                                                             start=(hh == 0), stop=False), r=[("SDT", i % 2), ("RVtm", c)], w=[("psY", i % 2)])
                        for d_ in range(2):
                            S.add("pe", lambda h, hh=hh, d_=d_: h.matmul(y[:, hh * 128:(hh + 1) * 128], lhsT=RQs[d_][i % 2][:, hh, :],
                                                                        rhs=SAll[d_][:, c, hh * 128:(hh + 1) * 128], start=False, stop=(d_ == 1 and hh == 3)),
                                  r=[("RQs", d_, i % 2), ("SAll", d_, c)], w=[("psY", i % 2)])
                def chunk_epi(c, i):
                    y = psY[i % 2]
                    S.add("act", lambda h: h.activation(out=sqy[:], in_=y[:], func=AF.Square), r=[("psY", i % 2)], w=["sqy"])
                    S.add("dve", lambda h: h.tensor_reduce(out=st4[:, 0:4], in_=sqy[:].rearrange("p (h d) -> p h d", h=4), axis=AX.X, op=ALU.add),
                          r=["sqy"], w=["ssq4"])
                    rstd_ops(st4[:, 0:4], st4[:, 4:8], 1.0 / 128, ["ssq4"], ["rstd4"])
                    S.add("dve", lambda h: h.tensor_tensor(out=yn[:].rearrange("p (h d) -> p h d", h=4), in0=y[:].rearrange("p (h d) -> p h d", h=4),
                                                           in1=st4[:, 4:8].unsqueeze(2).to_broadcast([128, 4, 128]), op=ALU.mult),
                          r=[("psY", i % 2), "rstd4"], w=["yn"])
                    yb_ = yb[i % 2]
                    S.add("dve", lambda h: h.scalar_tensor_tensor(out=yb_[:], in0=yn[:], scalar=0.5, in1=rgs[:, c, :], op0=ALU.mult, op1=ALU.mult),
                          r=["yn", ("rgs", c)], w=[("yb", i % 2)])
                    for hh in range(4):
                        S.add("pe", lambda h, hh=hh: h.transpose(out=ptr[:, hh, :], in_=yb_[:, hh * 128:(hh + 1) * 128], identity=ident_b[:]),
                              r=[("yb", i % 2)], w=["ptr"])
                    sg = stgr[i % 2]
                    S.add("act", lambda h: h.activation(out=sg[:], in_=ptr[:, 0:4, :], func=AF.Copy), r=["ptr"], w=[("stgr", i % 2)])
                    S.dma("sp", lambda h: h.dma_start(out=mixT[768:1280, c * 128:(c + 1) * 128].rearrange("(h d) t -> d h t", h=4), in_=sg[:]),
                          r=[("stgr", i % 2)], w=[("mixTr", c)])
                cl = list(range(2 if last else 0, NT))
                chunk(cl[0], 0)
                for i_, c in enumerate(cl):
                    if i_ + 1 < len(cl):
                        chunk(cl[i_ + 1], i_ + 1)
                    chunk_epi(c, i_)
                S.phase_end("ret")

        def phase_ssd(l):
            last = (l == nlayers - 1)
            NH = 12
            with ExitStack() as st_o:
                To, Po = mk_alloc(st_o)
                BT = To("BT", [128, 2, T], BF16)
                CT = To("CT", [128, 2, T], BF16)
                Btm = To("Btm", [128, NT, 256], BF16)
                xs = To("xs", [128, NT, 768], BF16)
                la = To("la", [128, 2, NT, NH])
                lndt = To("lndt", [128, 2, NT, NH])
                acs = To("acs", [128, 2, NT, NH])
                tot = To("tot", [128, 2, NT, NH])
                ein = To("ein", [128, 2, NT, NH])
                cdc = To("cdc", [128, 2, NT, NH])
                wend = To("wend", [128, 2, NT, NH])
                lb = To("lb", [128, 2, NT, NH])
                dsk = To("dsk", [128, NH])
                gssd = To("gssd", [128, 768])
                with ExitStack() as st:
                    Tl, Pl = mk_alloc(st)
                    cw = Tl("cw", [128, 10, 5])
                    cb = Tl("cb", [128, 10])
                    for k in range(5):
                        S.dma("sp", lambda h, k=k: h.dma_start(out=cw[:, :, k], in_=I["conv_w"][l, k].rearrange("(c p) -> p c", p=128),
                                                               allow_slow_non_contiguous=True), w=["cw"])
                    S.dma("sp", lambda h: h.dma_start(out=cb[:], in_=I["conv_b"][l].rearrange("(c p) -> p c", p=128), allow_slow_non_contiguous=True), w=["cb"])
                    S.dma("sp", lambda h: h.dma_start(out=dsk[:], in_=I["d_skip"][l:l + 1, :].to_broadcast([128, NH])), w=["dsk"])
                    S.dma("sp", lambda h: h.dma_start(out=gssd[:], in_=I["ssd_norm_g"][l:l + 1, :].to_broadcast([128, 768])), w=["gssd"])
                    UW = 2312
                    u = [Tl("u%d" % i, [128, UW]) for i in range(2)]
                    acc = Tl("acc", [128, UW])
                    ee = Tl("cee", [128, UW])
                    ob = [Tl("ob%d" % i, [128, UW], BF16) for i in range(2)]
                    ptx = Pl("ptx", [128, 8, 128], BF16)
                    for i in range(2):
                        S.add("pool", lambda h, i=i: h.memset(u[i][:], 0.0), w=[("u", i)])
                    S.add("dve", lambda h: h.tensor_scalar(out=cw[:], in0=cw[:], scalar1=0.5, scalar2=None, op0=ALU.mult), r=["cw"], w=["cw"])
                    S.add("dve", lambda h: h.tensor_scalar(out=cb[:], in0=cb[:], scalar1=0.5, scalar2=None, op0=ALU.mult), r=["cb"], w=["cb"])

                    def conv_chunk(cc):
                        i = cc
                        u_ = u[i % 2]
                        o_ = ob[i % 2]
                        S.dma("sp", lambda h: h.dma_start(out=u_[:, 2:258], in_=xbcT[cc * 128:(cc + 1) * 128, 0:256]), w=[("u", i % 2)])
                        S.dma("sp", lambda h: h.dma_start(out=u_[:, 262:2310], in_=xbcT[cc * 128:(cc + 1) * 128, 256:T]), w=[("u", i % 2)])
                        n = 2308
                        S.add("dve", lambda h: h.tensor_scalar(out=acc[:, 2:2 + n], in0=u_[:, 0:n], scalar1=cw[:, cc, 0:1], scalar2=cb[:, cc:cc + 1],
                                                               op0=ALU.mult, op1=ALU.add), r=[("u", i % 2), "cw", "cb"], w=["acc"])
                        for k in range(1, 5):
                            S.add("dve", lambda h, k=k: h.scalar_tensor_tensor(out=acc[:, 2:2 + n], in0=u_[:, k:k + n], scalar=cw[:, cc, k:k + 1],
                                                                               in1=acc[:, 2:2 + n], op0=ALU.mult, op1=ALU.add),
                                  r=[("u", i % 2), "cw", "acc"], w=["acc"])
                        silu2_ops(acc[:, 2:2 + n], ee[:, 2:2 + n], o_[:, 2:2 + n], "acc", "cee", ("ob", i % 2), in_scale=1.0)
                        def tok(tt):
                            return (2 + tt * 128) if tt < 2 else (262 + (tt - 2) * 128)
                        if cc < 6 or cc in (6, 7):
                            for t0 in range(0, NT, 8):
                                nt = min(8, NT - t0)
                                for a in range(nt):
                                    tt = t0 + a
                                    S.add("pe", lambda h, a=a, tt=tt: h.transpose(out=ptx[:, a, :], in_=o_[:, tok(tt):tok(tt) + 128], identity=ident_b[:]),
                                          r=[("ob", i % 2)], w=["ptx"])
                                if cc < 6:
                                    S.add("act", lambda h, t0=t0, nt=nt: h.activation(out=xs[:, t0:t0 + nt, cc * 128:(cc + 1) * 128], in_=ptx[:, 0:nt, :], func=AF.Copy),
                                          r=["ptx"], w=[("xs", cc, t0)])
                                else:
                                    g = cc - 6
                                    S.add("act", lambda h, t0=t0, nt=nt: h.activation(out=Btm[:, t0:t0 + nt, g * 128:(g + 1) * 128], in_=ptx[:, 0:nt, :], func=AF.Copy),
                                          r=["ptx"], w=[("Btm", g, t0)])
                        if cc >= 6:
                            dstT = BT if cc < 8 else CT
                            g = (cc - 6) % 2
                            S.add("act", lambda h: h.activation(out=dstT[:, g, 0:256], in_=o_[:, 2:258], func=AF.Copy), r=[("ob", i % 2)], w=[("BCT", cc, 0)])
                            S.add("act", lambda h: h.activation(out=dstT[:, g, 256:T], in_=o_[:, 262:2310], func=AF.Copy), r=[("ob", i % 2)], w=[("BCT", cc, 1)])
                    for cc in range(10):
                        conv_chunk(cc)
                    ztl = [Tl("ztl%d" % i, [128, 768]) for i in range(2)]
                    zth = [Tl("zth%d" % i, [128, 768]) for i in range(2)]
                    zso = [Tl("zso%d" % i, [128, 768], BF16) for i in range(2)]

                    def ztile(tt):
                        b = tt % 2
                        S.dma("sp", lambda h: h.dma_start(out=ztl[b][:], in_=proj[tt * 128:(tt + 1) * 128, 3328:4096]), w=[("ztl", b)])
                        S.add("act", lambda h: h.activation(out=zth[b][:], in_=ztl[b][:], func=AF.Tanh, scale=0.5), r=[("ztl", b)], w=[("zth", b)])
                        S.add("dve", lambda h: h.scalar_tensor_tensor(out=zso[b][:], in0=zth[b][:], scalar=1.0, in1=ztl[b][:], op0=ALU.add, op1=ALU.mult),
                              r=[("ztl", b), ("zth", b)], w=[("zso", b)])
                        S.dma("sp", lambda h: h.dma_start(out=zsd[tt * 128:(tt + 1) * 128, :], in_=zso[b][:]), r=[("zso", b)], w=[("zsd", tt)])
                    for tt in range(2 if last else 0, NT):
                        ztile(tt)
                    dtr = Tl("dtr", [128, NT, 24])
                    dtb = Tl("dtb", [128, 24])
                    alg = Tl("alg", [128, 24])
                    dtv = Tl("dtv", [128, 2, NT, NH])
                    tmpd = Tl("tmpd", [128, 2, NT, NH])
                    psc = Pl("psc", [128, 512])
                    pst = Pl("pst", [128, 512])
                    S.dma("sp", lambda h: h.dma_start(out=dtr[:], in_=proj[:, 4096:4120].rearrange("(t p) c -> p t c", p=128)), w=["dtr"])
                    S.dma("sp", lambda h: h.dma_start(out=dtb[:, 0:12], in_=I["dt_bias_f"][l:l + 1, :].to_broadcast([128, 12])), w=["dtb"])
                    S.dma("sp", lambda h: h.dma_start(out=dtb[:, 12:24], in_=I["dt_bias_b"][l:l + 1, :].to_broadcast([128, 12])), w=["dtb"])
                    S.dma("sp", lambda h: h.dma_start(out=alg[:, 0:12], in_=I["a_log_f"][l:l + 1, :].to_broadcast([128, 12])), w=["alg"])
                    S.dma("sp", lambda h: h.dma_start(out=alg[:, 12:24], in_=I["a_log_b"][l:l + 1, :].to_broadcast([128, 12])), w=["alg"])
                    S.add("act", lambda h: h.activation(out=alg[:], in_=alg[:], func=AF.Exp), r=["alg"], w=["alg"])
                    for d_ in range(2):
                        S.add("dve", lambda h, d_=d_: h.tensor_tensor(out=dtv[:, d_], in0=dtr[:, :, d_ * 12:(d_ + 1) * 12],
                                                                    in1=dtb[:, d_ * 12:(d_ + 1) * 12].unsqueeze(1).to_broadcast([128, NT, NH]), op=ALU.add),
                              r=["dtr", "dtb"], w=["dtv"])
                    F2 = lambda t: t[:].rearrange("p a b c -> p (a b c)")
                    S.add("act", lambda h: h.activation(out=F2(dtv), in_=F2(dtv), func=AF.Exp), r=["dtv"], w=["dtv"])
                    S.add("act", lambda h: h.activation(out=F2(dtv), in_=F2(dtv), func=AF.Ln, bias=1.0), r=["dtv"], w=["dtv"])
                    S.add("dve", lambda h: h.tensor_scalar(out=F2(dtv), in0=F2(dtv), scalar1=1e-30, scalar2=None, op0=ALU.max), r=["dtv"], w=["dtv"])
                    S.add("act", lambda h: h.activation(out=F2(lndt), in_=F2(dtv), func=AF.Ln), r=["dtv"], w=["lndt"])
                    for d_ in range(2):
                        S.add("dve", lambda h, d_=d_: h.tensor_tensor(out=la[:, d_], in0=dtv[:, d_],
                                                                    in1=alg[:, d_ * 12:(d_ + 1) * 12].unsqueeze(1).to_broadcast([128, NT, NH]), op=ALU.mult),
                              r=["dtv", "alg"], w=["la"])
                    S.add("dve", lambda h: h.tensor_scalar(out=F2(la), in0=F2(la), scalar1=-1.0, scalar2=None, op0=ALU.mult), r=["la"], w=["la"])
                    NQ = NT * NH
                    S.add("pe", lambda h: h.matmul(psc[:, 0:NQ], lhsT=tri_f, rhs=la[:, 0].rearrange("p a b -> p (a b)"), start=True, stop=True), r=["la"], w=["psc"])
                    S.add("pe", lambda h: h.matmul(psc[:, NQ:2 * NQ], lhsT=tri_b, rhs=la[:, 1].rearrange("p a b -> p (a b)"), start=True, stop=True), r=["la"], w=["psc"])
                    S.add("pe", lambda h: h.matmul(pst[:, 0:2 * NQ], lhsT=ones_f, rhs=F2(la), start=True, stop=True), r=["la"], w=["pst"])
                    S.add("dve", lambda h: h.tensor_copy(out=F2(acs), in_=psc[:, 0:2 * NQ]), r=["psc"], w=["acs"])
                    S.add("dve", lambda h: h.tensor_copy(out=F2(tot), in_=pst[:, 0:2 * NQ]), r=["pst"], w=["tot"])
                    S.add("act", lambda h: h.activation(out=F2(ein), in_=F2(acs), func=AF.Exp), r=["acs"], w=["ein"])
                    S.add("act", lambda h: h.activation(out=F2(cdc), in_=F2(tot), func=AF.Exp), r=["tot"], w=["cdc"])
                    S.add("dve", lambda h: h.tensor_tensor(out=F2(lb), in0=F2(lndt), in1=F2(acs), op=ALU.subtract), r=["lndt", "acs"], w=["lb"])
                    S.add("dve", lambda h: h.tensor_tensor(out=F2(tmpd), in0=F2(lb), in1=F2(tot), op=ALU.add), r=["lb", "tot"], w=["tmpd"])
                    S.add("act", lambda h: h.activation(out=F2(wend), in_=F2(tmpd), func=AF.Exp), r=["tmpd"], w=["wend"])
                    S.phase_end("ssdprep")
                with ExitStack() as st:
                    Tl, Pl = mk_alloc(st)
                    SAll = [Tl("sSAll%d" % d_, [128, NT, 768], BF16) for d_ in range(2)]
                    S32 = [Tl("sS32%d" % d_, [128, 768]) for d_ in range(2)]
                    xsw2 = [[Tl("xsw%d_%d" % (d_, i), [128, 768], BF16) for i in range(2)] for d_ in range(2)]
                    psA = Pl("spsA", [128, 2, 512])
                    psY = Pl("spsY", [128, 1024])
                    psAB = [psA, psY[:, :].rearrange("p (g x) -> p g x", g=2)]
                    psABk = ["psA", "psY"]
                    orders = [list(range(NT)), [1, 0] + list(range(NT - 1, 1, -1))]

                    def bc12(t2d):
                        return t2d.unsqueeze(2).to_broadcast([128, NH, 64])

                    def v12(ap):
                        return ap.rearrange("p (h d) -> p h d", h=NH)

                    def state_step(d_, idx):
                        order = orders[d_]
                        c = order[idx]
                        if idx == 0:
                            S.add("pool", lambda h: h.memset(S32[d_][:], 0.0), w=[("S32", d_)])
                            S.add("pool", lambda h: h.memset(SAll[d_][:, c, :], 0.0), w=[("SAll", d_, c)])
                        if idx == NT - 1:
                            return
                        xw = xsw2[d_][idx % 2]
                        pA = psAB[d_]
                        pk = psABk[d_]
                        S.add("dve", lambda h: h.tensor_tensor(out=v12(xw[:]), in0=v12(xs[:, c, :]), in1=bc12(wend[:, d_, c, :]), op=ALU.mult),
                              w=[("xsw", d_, idx % 2)])
                        for g in range(2):
                            S.add("pe", lambda h, g=g: h.matmul(pA[:, g, 0:384], lhsT=Btm[:, c, g * 128:(g + 1) * 128], rhs=xw[:, g * 384:(g + 1) * 384],
                                                               start=True, stop=True), r=[("xsw", d_, idx % 2)], w=[pk])
                        S.add("dve", lambda h: h.tensor_tensor(out=v12(S32[d_][:]), in0=v12(S32[d_][:]), in1=bc12(cdc[:, d_, c, :]), op=ALU.mult),
                              r=[("S32", d_)], w=[("S32", d_)])
                        S.add("dve", lambda h: h.tensor_tensor(out=S32[d_][:].rearrange("p (g x) -> p g x", g=2), in0=S32[d_][:].rearrange("p (g x) -> p g x", g=2),
                                                               in1=pA[:, :, 0:384], op=ALU.add), r=[("S32", d_), pk], w=[("S32", d_)])
                        cn = order[idx + 1]
                        S.add("act", lambda h: h.activation(out=SAll[d_][:, cn, :], in_=S32[d_][:], func=AF.Copy), r=[("S32", d_)], w=[("SAll", d_, cn)])
                    for idx in range(NT):
                        for d_ in range(2):
                            state_step(d_, idx)

                    Rt = [Tl("Rt%d" % d_, [128, NH, 128]) for d_ in range(2)]
                    mneg4 = [Tl("mneg4_%d" % d_, [128, 4, 128]) for d_ in range(2)]
                    Et = [Tl("Et%d" % i, [128, 4, 128], BF16) for i in range(2)]
                    Wt = [Tl("Wt%d" % i, [128, 4, 128], BF16) for i in range(4)]
                    psE = [Pl("psE%d" % i, [128, 4, 128]) for i in range(2)]
                    psG = Pl("psG", [128, 4, 128])
                    psFB = psA
                    ptr = Pl("sptr", [128, 8, 128], BF16)
                    y1 = [Tl("y1_%d" % i, [128, 768]) for i in range(2)]
                    y2 = [Tl("y2_%d" % i, [128, 768]) for i in range(2)]
                    zt = [Tl("zt%d" % i, [128, 768], BF16) for i in range(2)]
                    xsd = [Tl("xsd%d" % i, [128, 768], BF16) for i in range(2)]
                    junk = Tl("sjunk", [128, 768], BF16)
                    yo = [Tl("yo%d" % i, [128, 768], BF16) for i in range(2)]
                    stg = [Tl("sstg%d" % i, [128, 6, 128], BF16) for i in range(2)]
                    st2 = Tl("st2", [128, 2])
                    for d_ in range(2):
                        S.add("dve", lambda h, d_=d_: h.tensor_copy(out=mneg4[d_][:], in_=mneg[d_].unsqueeze(1).to_broadcast([128, 4, 128])), w=["mneg4"])
                    tris = [tri_f, tri_b]
                    ecnt = [0]
                    wcnt = [0]

                    def chunk(c, ci):
                        cs_ = slice(c * 128, (c + 1) * 128)
                        b = ci % 2
                        S.dma("sp", lambda h: h.dma_start(out=zt[b][:], in_=zsd[c * 128:(c + 1) * 128, :]), w=[("zt", b)])
                        for g in range(2):
                            S.add("pe", lambda h, g=g: h.matmul(psG[:, g, :], lhsT=BT[:, g, cs_], rhs=CT[:, g, cs_], start=(g == 0), stop=(g == 1)), w=["psG"])
                        for d_ in range(2):
                            S.add("dve", lambda h, d_=d_: h.tensor_tensor(out=Rt[d_][:], in0=la[:, d_, c, :].unsqueeze(2).to_broadcast([128, NH, 128]),
                                                                        in1=tris[d_].unsqueeze(1).to_broadcast([128, NH, 128]), op=ALU.mult), w=[("Rt", d_)])
                        S.add("dve", lambda h: h.tensor_tensor(out=v12(xsd[b][:]), in0=v12(xs[:, c, :]), in1=bc12(dsk[:]), op=ALU.mult), w=[("xsd", b)])
                        for q in range(3):
                            wl = {}
                            for d_ in range(2):
                                e = ecnt[0]
                                ecnt[0] += 1
                                pe_ = psE[e % 2]
                                et_ = Et[e % 2]
                                S.add("pe", lambda h, d_=d_, q=q, pe_=pe_: h.matmul(pe_[:].rearrange("p a b -> p (a b)"), lhsT=ones_f,
                                                                                 rhs=Rt[d_][:, 4 * q:4 * q + 4, :].rearrange("p a b -> p (a b)"), start=True, stop=False),
                                      r=[("Rt", d_)], w=[("psE", e % 2)])
                                S.add("pe", lambda h, d_=d_, pe_=pe_: h.matmul(pe_[:].rearrange("p a b -> p (a b)"), lhsT=ident_f,
                                                                            rhs=mneg4[d_][:].rearrange("p a b -> p (a b)"), start=False, stop=True),
                                      r=["mneg4"], w=[("psE", e % 2)])
                                for a in range(4):
                                    hh = 4 * q + a
                                    S.add("act", lambda h, a=a, hh=hh, d_=d_, pe_=pe_, et_=et_: h.activation(out=et_[:, a, :], in_=pe_[:, a, :], func=AF.Exp,
                                                                                                      bias=lb[:, d_, c, hh:hh + 1]),
                                          r=[("psE", e % 2)], w=[("Et", e % 2)])
                                w_ = wcnt[0]
                                wcnt[0] += 1
                                wt_ = Wt[w_ % 4]
                                wl[d_] = (w_, wt_)
                                if q == 1:
                                    for half in range(2):
                                        S.add("dve", lambda h, half=half, et_=et_, wt_=wt_: h.tensor_tensor(
                                            out=wt_[:, 2 * half:2 * half + 2, :], in0=et_[:, 2 * half:2 * half + 2, :],
                                            in1=psG[:, half, :].unsqueeze(1).to_broadcast([128, 2, 128]), op=ALU.mult),
                                            r=[("Et", e % 2), "psG"], w=[("Wt", w_ % 4)])
                                else:
                                    g = 0 if q == 0 else 1
                                    S.add("dve", lambda h, g=g, et_=et_, wt_=wt_: h.tensor_tensor(
                                        out=wt_[:], in0=et_[:], in1=psG[:, g, :].unsqueeze(1).to_broadcast([128, 4, 128]), op=ALU.mult),
                                        r=[("Et", e % 2), "psG"], w=[("Wt", w_ % 4)])
                            for a in range(4):
                                hh = 4 * q + a
                                for d_ in range(2):
                                    w_, wt_ = wl[d_]
                                    S.add("pe", lambda h, hh=hh, a=a, d_=d_, wt_=wt_: h.matmul(psY[:, hh * 64:(hh + 1) * 64], lhsT=wt_[:, a, :], rhs=xs[:, c, hh * 64:(hh + 1) * 64],
                                                                                          start=(d_ == 0 and hh in (0, 8)), stop=False),
                                          r=[("Wt", w_ % 4)], w=["psY"])
                        S.add("pe", lambda h: h.matmul(psY[:, 0:512], lhsT=ident_b[:], rhs=xsd[b][:, 0:512], start=False, stop=True), r=[("xsd", b)], w=["psY"])
                        S.add("pe", lambda h: h.matmul(psY[:, 512:768], lhsT=ident_b[:], rhs=xsd[b][:, 512:768], start=False, stop=True), r=[("xsd", b)], w=["psY"])
                        for d_ in range(2):
                            for g in range(2):
                                S.add("pe", lambda h, d_=d_, g=g: h.matmul(psFB[:, g, 0:384], lhsT=CT[:, g, cs_], rhs=SAll[d_][:, c, g * 384:(g + 1) * 384],
                                                                         start=True, stop=True), r=[("SAll", d_, c)], w=["psA"])
                            yd = y1[b] if d_ == 0 else y2[b]
                            S.add("dve", lambda h, d_=d_, yd=yd: h.tensor_tensor(
                                out=yd[:].rearrange("p (g h d) -> p g h d", g=2, h=6),
                                in0=psFB[:, :, 0:384].rearrange("p g (h d) -> p g h d", h=6),
                                in1=ein[:, d_, c, :].rearrange("p (g h) -> p g h", g=2).unsqueeze(3).to_broadcast([128, 2, 6, 64]), op=ALU.mult),
                                r=["psA"], w=[("y", d_, b)])
                        S.add("dve", lambda h: h.tensor_tensor(out=y1[b][:], in0=y1[b][:], in1=y2[b][:], op=ALU.add), r=[("y", 0, b), ("y", 1, b)], w=[("y", 0, b)])
                        S.add("dve", lambda h: h.tensor_tensor(out=y1[b][:], in0=y1[b][:], in1=psY[:, 0:768], op=ALU.add), r=[("y", 0, b), "psY"], w=[("y", 0, b)])
                    def chunk_epi(c, ci):
                        cs_ = slice(c * 128, (c + 1) * 128)
                        b = ci % 2
                        S.add("dve", lambda h: h.scalar_tensor_tensor(out=y1[b][:], in0=y1[b][:], scalar=0.5, in1=zt[b][:], op0=ALU.mult, op1=ALU.mult),
                              r=[("y", 0, b), ("zt", b)], w=[("y", 0, b)])
                        S.add("act", lambda h: h.activation(out=junk[:], in_=y1[b][:], func=AF.Square, accum_out=st2[:, 0:1]), r=[("y", 0, b)], w=["sjunk", "ssq"])
                        rstd_ops(st2[:, 0:1], st2[:, 1:2], 1.0 / 768, ["ssq"], ["rstd"])
                        S.add("dve", lambda h: h.scalar_tensor_tensor(out=yo[b][:], in0=y1[b][:], scalar=st2[:, 1:2], in1=gssd[:], op0=ALU.mult, op1=ALU.mult),
                              r=[("y", 0, b), "rstd"], w=[("yo", b)])
                        for a in range(6):
                            S.add("pe", lambda h, a=a: h.transpose(out=ptr[:, a, :], in_=yo[b][:, a * 128:(a + 1) * 128], identity=ident_b[:]), r=[("yo", b)], w=["ptr"])
                        sg = stg[b]
                        S.add("act", lambda h: h.activation(out=sg[:], in_=ptr[:, 0:6, :], func=AF.Copy), r=["ptr"], w=[("stg", b)])
                        S.dma("pool", lambda h: h.dma_start(out=mixT[1280:2048, cs_].rearrange("(a d) t -> d a t", a=6), in_=sg[:]),
                              r=[("stg", b)], w=[("mixTs", c)])
                    cl = list(range(2 if last else 0, NT))
                    chunk(cl[0], 0)
                    for ci, c in enumerate(cl):
                        if ci + 1 < len(cl):
                            chunk(cl[ci + 1], ci + 1)
                        chunk_epi(c, ci)
                    S.phase_end("ssdmain")

        def phase_out(l):
            last = (l == nlayers - 1)
            with ExitStack() as st:
                Tl, Pl = mk_alloc(st)
                wo = Tl("wo", [128, 16, D], BF16)
                wv = I["w_out"][l].rearrange("(k p) n -> p k n", p=128)
                for k4 in range(4):
                    for nb in range(2):
                        S.dma("pool", lambda h, k4=k4, nb=nb: h.dma_start(out=wo[:, 4 * k4:4 * k4 + 4, nb * 1024:(nb + 1) * 1024],
                                                                        in_=wv[:, 4 * k4:4 * k4 + 4, nb * 1024:(nb + 1) * 1024]), w=[("wo", k4, nb)])
                G1 = Tl("G1", [128, D])
                gm = Tl("ogm", [128, D])
                sh = Tl("osh", [128, D])
                gp = Tl("ogp", [128, D])
                mT = [Tl("mT%d" % i, [128, 16, 128], BF16) for i in range(3)]
                xt = [Tl("oxt%d" % i, [128, D]) for i in range(3)]
                x1 = [Tl("ox1%d" % i, [128, D]) for i in range(2)]
                tmp = Tl("otmp", [128, D])
                hb = [Tl("ohb%d" % i, [128, D], BF16) for i in range(2)]
                junk = Tl("ojunk", [128, D], BF16)
                ssq = Tl("ossq", [128, 4])
                sq4 = Tl("osq4", [128, 16])
                stg = [Tl("ostg%d" % i, [128, 16, 128], BF16) for i in range(2)]
                psM = [Pl("psM%d" % i, [128, 512]) for i in range(4)]
                pt = [Pl("opt%d" % i, [128, 16, 128], BF16) for i in range(2)]

                def load_mods(r):
                    load_gate_mod(l, G1, gp, "o", "post_mix_g", 4096, r)
                    load_norm_mod(l, gm, sh, gp, "o", "pre_ffn_g", 4 * 2048, 3 * 2048, r)

                def stage1(tt, i):
                    o8 = 8 * (i % 2)
                    S.dma("sp", lambda h: h.dma_start(out=mT[i % 3][:], in_=mixT[:, tt * 128:(tt + 1) * 128].rearrange("(k p) t -> p k t", p=128)),
                          w=[("mT", i % 3)])
                    S.dma("sp", lambda h: h.dma_start(out=xt[i % 3][:], in_=xsrc(l, tt)), w=[("xt", i % 3)])
                    x1_ = x1[i % 2]
                    for cb in range(4):
                        for k in range(16):
                            S.add("pe", lambda h, cb=cb, k=k: h.matmul(psM[cb][:], lhsT=mT[i % 3][:, k, :], rhs=wo[:, k, cb * 512:(cb + 1) * 512],
                                                                      start=(k == 0), stop=(k == 15)),
                                  r=[("mT", i % 3), ("wo", k // 4, cb // 2)], w=[("psM", cb)])
                        S.add("act", lambda h, cb=cb: h.activation(out=junk[:, cb * 512:(cb + 1) * 512], in_=psM[cb][:], func=AF.Square,
                                                                   accum_out=sq4[:, o8 + cb:o8 + cb + 1]), r=[("psM", cb)], w=["ojunk", ("sq4", i % 2), ("sqd", cb)])
                        S.add("dve", lambda h, cb=cb: h.tensor_copy(out=x1_[:, cb * 512:(cb + 1) * 512], in_=psM[cb][:]), r=[("psM", cb), ("sqd", cb)], w=[("x1", i % 2)])

                def stage2(tt, i):
                    o8 = 8 * (i % 2)
                    x1_ = x1[i % 2]
                    S.add("dve", lambda h: h.tensor_reduce(out=sq4[:, o8 + 4:o8 + 5], in_=sq4[:, o8:o8 + 4], axis=AX.X, op=ALU.add), r=[("sq4", i % 2)], w=[("mss", i % 2)])
                    rstd_ops(sq4[:, o8 + 4:o8 + 5], sq4[:, o8 + 5:o8 + 6], 1.0 / D, [("mss", i % 2)], [("mrstd", i % 2)])
                    S.add("dve", lambda h: h.scalar_tensor_tensor(out=x1_[:], in0=x1_[:], scalar=sq4[:, o8 + 5:o8 + 6], in1=G1[:], op0=ALU.mult, op1=ALU.mult),
                          r=[("x1", i % 2), ("mrstd", i % 2), "oG"], w=[("x1", i % 2)])
                    S.add("dve", lambda h: h.tensor_tensor(out=x1_[:], in0=x1_[:], in1=xt[i % 3][:], op=ALU.add),
                          r=[("x1", i % 2), ("xt", i % 3)], w=[("x1", i % 2)])
                    S.dma("pool", lambda h: h.dma_start(out=x1s[tt * 128:(tt + 1) * 128, :], in_=x1_[:]), r=[("x1", i % 2)], w=[("x1s", tt)])

                    def dst(ptile, pkey, wkeys):
                        sg = stg[i % 2]
                        S.add("act", lambda h: h.activation(out=sg[:], in_=ptile[:], func=AF.Copy), r=[pkey], w=[("ostg", i % 2)])
                        S.dma("pool", lambda h: h.dma_start(out=h2T[:, tt * 128:(tt + 1) * 128].rearrange("(k p) t -> p k t", p=128), in_=sg[:]),
                              r=[("ostg", i % 2)], w=wkeys)
                    return norm_transpose_tile(l, tt, x1_[:], ("x1", i % 2), gm, sh, "o", tmp, hb[i % 2], junk, ssq, pt[i % 2], dst, [("h2T", tt)], i)

                def run_tiles(tts, i0):
                    pend = None
                    for j, tt in enumerate(tts):
                        if j == 0:
                            stage1(tt, i0)
                        if j + 1 < len(tts):
                            stage1(tts[j + 1], i0 + j + 1)
                        th = stage2(tt, i0 + j)
                        if pend is not None:
                            pend()
                        pend = th
                    pend()
                if not last:
                    load_mods(1)
                    run_tiles([0, 1], 0)
                load_mods(0)
                run_tiles(list(range(2, NT)), 2)
                S.phase_end("out")

        def phase_ffn(l):
            last = (l == nlayers - 1)
            if last:
                halves = [(256, 1024), (1280, 1024)]
            else:
                halves = [(0, 1152), (1152, 1152)]
            wg_v = I["w_gate"][l].rearrange("(k p) n -> p k n", p=128)
            wu_v = I["w_up"][l].rearrange("(k p) n -> p k n", p=128)
            wd_v = I["w_down"][l].rearrange("(k p) n -> p k n", p=128)
            NJ = DFF // 128
            for (t0, nt) in halves:
                with ExitStack() as st_o:
                    To, Po = mk_alloc(st_o)
                    aT = To("aT", [128, NJ, nt], BF16)
                    with ExitStack() as st:
                        Tl, Pl = mk_alloc(st)
                        hh_ = Tl("h2h", [128, 16, nt], BF16)
                        for k4 in range(4):
                            S.dma("sp", lambda h, k4=k4: h.dma_start(out=hh_[:, 4 * k4:4 * k4 + 4, :],
                                                                     in_=h2T[:, t0:t0 + nt].rearrange("(k p) t -> p k t", p=128)[:, 4 * k4:4 * k4 + 4, :]),
                                  w=[("h2h", k4)])
                        wg = [Tl("wg%d" % i, [128, 16, 256], BF16) for i in range(2)]
                        wu = [Tl("wu%d" % i, [128, 16, 256], BF16) for i in range(2)]
                        psg = [Pl("psg%d" % i, [128, 512]) for i in range(3)]
                        psu = [Pl("psu%d" % i, [128, 512]) for i in range(3)]
                        ee = [Tl("fe%d" % i, [128, 512]) for i in range(2)]
                        tg = [Tl("ftg%d" % i, [128, 512]) for i in range(2)]
                        tbs = [(a, min(512, nt - a)) for a in range(0, nt, 512)]
                        cnt = [0]

                        def wblock(jb):
                            b = jb % 2
                            for k4 in range(4):
                                S.dma("pool", lambda h, k4=k4: h.dma_start(out=wg[b][:, 4 * k4:4 * k4 + 4, :], in_=wg_v[:, 4 * k4:4 * k4 + 4, jb * 256:(jb + 1) * 256]),
                                      w=[("wg", b, k4)])
                                S.dma("pool", lambda h, k4=k4: h.dma_start(out=wu[b][:, 4 * k4:4 * k4 + 4, :], in_=wu_v[:, 4 * k4:4 * k4 + 4, jb * 256:(jb + 1) * 256]),
                                      w=[("wu", b, k4)])
                            for jj in range(2):
                                j = jb * 2 + jj
                                for (a, n) in tbs:
                                    i = cnt[0]
                                    cnt[0] += 1
                                    pg = psg[i % 3]
                                    pu = psu[i % 3]
                                    for k in range(16):
                                        S.add("pe", lambda h, k=k, pg=pg, a=a, n=n, jj=jj: h.matmul(pg[:, 0:n], lhsT=wg[b][:, k, jj * 128:(jj + 1) * 128],
                                                                                             rhs=hh_[:, k, a:a + n], start=(k == 0), stop=(k == 15)),
                                              r=[("wg", b, k // 4), ("h2h", k // 4)], w=[("psg", i % 3)])
                                    for k in range(16):
                                        S.add("pe", lambda h, k=k, pu=pu, a=a, n=n, jj=jj: h.matmul(pu[:, 0:n], lhsT=wu[b][:, k, jj * 128:(jj + 1) * 128],
                                                                                             rhs=hh_[:, k, a:a + n], start=(k == 0), stop=(k == 15)),
                                              r=[("wu", b, k // 4), ("h2h", k // 4)], w=[("psu", i % 3)])
                                    e_ = ee[i % 2]
                                    t_ = tg[i % 2]
                                    S.add("act", lambda h, pg=pg, e_=e_, n=n: h.activation(out=e_[:, 0:n], in_=pg[:, 0:n], func=AF.Tanh, scale=0.5),
                                          r=[("psg", i % 3)], w=[("fe", i % 2)])
                                    S.add("dve", lambda h, e_=e_, t_=t_, pg=pg, n=n: h.scalar_tensor_tensor(out=t_[:, 0:n], in0=e_[:, 0:n], scalar=1.0, in1=pg[:, 0:n],
                                                                                                     op0=ALU.add, op1=ALU.mult),
                                          r=[("fe", i % 2), ("psg", i % 3)], w=[("ftg", i % 2)])
                                    S.add("dve", lambda h, t_=t_, pu=pu, n=n, a=a, j=j: h.scalar_tensor_tensor(out=aT[:, j, a:a + n], in0=t_[:, 0:n], scalar=0.5, in1=pu[:, 0:n],
                                                                                                        op0=ALU.mult, op1=ALU.mult),
                                          r=[("ftg", i % 2), ("psu", i % 3)], w=[("aT", j, a)])
                        for jb in range(NJ // 2):
                            wblock(jb)
                        S.phase_end("ffnA")
                    with ExitStack() as st:
                        Tl, Pl = mk_alloc(st)
                        wd = [Tl("wd%d" % i, [128, NJ, 256], BF16) for i in range(2)]
                        psd = [Pl("psd%d" % i, [128, 512]) for i in range(4)]
                        stg = [Tl("fstg%d" % i, [128, 256]) for i in range(4)]
                        cnt = [0]

                        def dblock(cb):
                            b = cb % 2
                            for k4 in range(4):
                                S.dma("pool", lambda h, k4=k4: h.dma_start(out=wd[b][:, 11 * k4:11 * k4 + 11, :], in_=wd_v[:, 11 * k4:11 * k4 + 11, cb * 256:(cb + 1) * 256]),
                                      w=[("wd", b, k4)])
                            for a in range(0, nt, 128):
                                i = cnt[0]
                                cnt[0] += 1
                                p_ = psd[i % 4]
                                for k in range(NJ):
                                    S.add("pe", lambda h, k=k, p_=p_, a=a: h.matmul(p_[:, 0:256], lhsT=aT[:, k, a:a + 128], rhs=wd[b][:, k, :], start=(k == 0), stop=(k == NJ - 1)),
                                          r=[("wd", b, k // 11)], w=[("psd", i % 4)])
                                s_ = stg[i % 4]
                                if i % 2 == 0:
                                    S.add("act", lambda h, p_=p_, s_=s_: h.activation(out=s_[:], in_=p_[:, 0:256], func=AF.Copy), r=[("psd", i % 4)], w=[("fstg", i % 4)])
                                else:
                                    S.add("dve", lambda h, p_=p_, s_=s_: h.tensor_copy(out=s_[:], in_=p_[:, 0:256]), r=[("psd", i % 4)], w=[("fstg", i % 4)])
                                S.dma("sp", lambda h, s_=s_, a=a: h.dma_start(out=fsc[t0 + a:t0 + a + 128, cb * 256:(cb + 1) * 256], in_=s_[:]),
                                      r=[("fstg", i % 4)], w=[("fsc", i)])
                        for cb in range(8):
                            dblock(cb)
                        S.phase_end("ffnB")

        def phase_fin(l):
            last = (l == nlayers - 1)
            with ExitStack() as st:
                Tl, Pl = mk_alloc(st)
                G2 = Tl("G2", [128, D])
                gp = Tl("fgp", [128, D])
                xa = [Tl("fxa%d" % i, [128, D]) for i in range(3)]
                fa = [Tl("ffa%d" % i, [128, D]) for i in range(3)]
                xo = [Tl("fxo%d" % i, [128, D]) for i in range(3)]
                junk = Tl("fjunk", [128, D], BF16)
                ssq = Tl("fssq", [128, 2])

                def tile(tt, i):
                    S.dma("sp", lambda h: h.dma_start(out=xa[i % 3][:], in_=x1s[tt * 128:(tt + 1) * 128, :]), w=[("xa", i % 3)])
                    S.dma("sp", lambda h: h.dma_start(out=fa[i % 3][:], in_=fsc[tt * 128:(tt + 1) * 128, :]), w=[("fa", i % 3)])
                    S.add("act", lambda h: h.activation(out=junk[:], in_=fa[i % 3][:], func=AF.Square, accum_out=ssq[:, 0:1]), r=[("fa", i % 3)], w=["fjunk", "fss"])
                    rstd_ops(ssq[:, 0:1], ssq[:, 1:2], 1.0 / D, ["fss"], ["frstd"])
                    S.add("dve", lambda h: h.scalar_tensor_tensor(out=xo[i % 3][:], in0=fa[i % 3][:], scalar=ssq[:, 1:2], in1=G2[:], op0=ALU.mult, op1=ALU.mult),
                          r=[("fa", i % 3), "frstd", "fG"], w=[("xo", i % 3)])
                    S.add("dve", lambda h: h.tensor_tensor(out=xo[i % 3][:], in0=xo[i % 3][:], in1=xa[i % 3][:], op=ALU.add), r=[("xo", i % 3), ("xa", i % 3)], w=[("xo", i % 3)])
                    if last:
                        dst = out[(tt - 2) * 128:(tt - 1) * 128, :]
                    else:
                        dst = xnext[tt * 128:(tt + 1) * 128, :]
                    S.dma("pool", lambda h: h.dma_start(out=dst, in_=xo[i % 3][:]), r=[("xo", i % 3)], w=[("xout", tt)])
                i = 0
                if not last:
                    load_gate_mod(l, G2, gp, "f", "post_ffn_g", 5 * 2048, 1)
                    for tt in range(2):
                        tile(tt, i)
                        i += 1
                load_gate_mod(l, G2, gp, "f", "post_ffn_g", 5 * 2048, 0)
                for tt in range(2, NT):
                    tile(tt, i)
                    i += 1
                S.phase_end("fin")

        PH = {}
        PH["out"] = phase_out
        PH["ffn"] = phase_ffn
        PH["fin"] = phase_fin
        PH["ssd"] = phase_ssd
        PH["ret"] = phase_ret
        PH["att"] = phase_att

        PH["in"] = phase_in
        try:
            if phases is None or "mod" in phases:
                phase_mod(0)
            check_stop("mod")
            for l in (layers if layers is not None else range(nlayers)):
                for nm in ("in", "att", "ret", "ssd", "out", "ffn", "fin"):
                    if nm in PH and (phases is None or nm in phases):
                        PH[nm](l)
                        check_stop("%s%d" % (nm, l))
        except Stop:
            pass
        S.phase_end("final")
        stats = S.emit()
    return nc, stats, list(I.keys())


def make_in_maps(inputs):
    consts = make_consts()
    rope = make_rope()
    maps = []
    shared = {n: np.ascontiguousarray(inputs[n], dtype=np.float32) for n, _ in SMALL + BIG}
    for b in range(8):
        m = dict(shared)
        m["x"] = np.ascontiguousarray(inputs["x"][b])
        m["ctx"] = np.ascontiguousarray(inputs["ctx"][b])
        m["cvec"] = np.ascontiguousarray(np.stack([inputs["c"][b], inputs["c_ctx"]]))
        m["consts"] = consts
        m["rope"] = rope
        maps.append(m)
    return maps


def kernel(**inputs):
    nc, _, _ = build(nlayers=2, debug=False)
    maps = make_in_maps(inputs)
    res = run_bass_kernel_spmd(nc, maps, core_ids=list(range(8)))
    return np.stack([np.asarray(r["out"], dtype=np.float32) for r in res.results], axis=0)
```

```python
import math
import numpy as np
from contextlib import ExitStack
import concourse.bass as bass
import concourse.mybir as mybir
from concourse.bass_utils import run_bass_kernel_spmd

F32 = mybir.dt.float32
BF16 = mybir.dt.bfloat16
AF = mybir.ActivationFunctionType
ALU = mybir.AluOpType
AX = mybir.AxisListType

D = 2048
T = 2304
NT = 18
DIN = 5400
DFF = 5632
EPS = 1e-6
NCONST = 10 * 128 + 2
PW = 4120


class _Op:
    __slots__ = ("eng", "fn", "deps", "ch", "pos", "is_dma", "vc", "waits", "signal", "rank")


class Sched:
    ENGS = ("pe", "act", "dve", "pool", "sp")

    def __init__(self, nc, stack, n_dma_sems=12):
        self.nc = nc
        self.h = {"pe": nc.tensor, "act": nc.scalar, "dve": nc.vector, "pool": nc.gpsimd, "sp": nc.sync}
        self.ops = []
        self.n_emitted = 0
        self.eng_pos = {e: 0 for e in self.ENGS}
        self.last_w = {}
        self.readers = {}
        self.esem = {e: stack.enter_context(nc.semaphore("s_" + e)) for e in self.ENGS}
        self.dsems = {}
        self.dcount = {}
        self.drr = {}
        for q in ("sp", "pool"):
            self.dsems[q] = [stack.enter_context(nc.semaphore("d_%s%d" % (q, i))) for i in range(n_dma_sems)]
            self.drr[q] = 0
            for i in range(n_dma_sems):
                self.dcount[(q, i)] = 0
        self.by_chpos = {}
        self.last_on_ch = {}
        self.clock = {e: {} for e in self.ENGS}
        self.rk = {e: 0 for e in self.ENGS}
        self.nw = 0

    def _deps(self, r, w):
        deps = []
        for k in r:
            o = self.last_w.get(k)
            if o is not None:
                deps.append(o)
        for k in w:
            o = self.last_w.get(k)
            if o is not None:
                deps.append(o)
            deps.extend(self.readers.get(k, ()))
        return deps

    def _commit(self, op, r, w):
        for k in r:
            self.readers.setdefault(k, []).append(op)
        for k in w:
            self.last_w[k] = op
            self.readers[k] = []
        self.ops.append(op)
        self.by_chpos[(op.ch, op.pos)] = op
        self.last_on_ch[op.ch] = op

    def add(self, eng, fn, r=(), w=()):
        op = _Op()
        op.eng = eng
        op.fn = fn
        op.is_dma = False
        op.deps = self._deps(r, w)
        self.eng_pos[eng] += 1
        op.ch = eng
        op.pos = self.eng_pos[eng]
        op.signal = False
        self._commit(op, r, w)
        return op

    def dma(self, q, fn, r=(), w=()):
        op = _Op()
        op.eng = q
        op.fn = fn
        op.is_dma = True
        op.deps = self._deps(r, w)
        i = self.drr[q]
        self.drr[q] = (i + 1) % len(self.dsems[q])
        self.dcount[(q, i)] += 1
        op.ch = ("d", q, i)
        op.pos = self.dcount[(q, i)]
        op.signal = True
        self._commit(op, r, w)
        return op

    def barrier(self):
        lasts = [o for o in self.last_on_ch.values() if o.fn is not None or o.is_dma]
        for e in self.ENGS:
            op = _Op()
            op.eng = e
            op.fn = None
            op.is_dma = False
            op.deps = list(lasts)
            self.eng_pos[e] += 1
            op.ch = e
            op.pos = self.eng_pos[e]
            op.signal = False
            self.ops.append(op)
            self.by_chpos[(op.ch, op.pos)] = op
        self.last_w = {}
        self.readers = {}

    def emit(self):
        ops = self.ops[self.n_emitted:]
        clock = self.clock
        for op in ops:
            E = op.eng
            ck = clock[E]
            need = {}
            for d in op.deps:
                if (not d.is_dma) and d.eng == "pe" and E == "pe" and (not op.is_dma) and op.fn is not None:
                    continue
                if ck.get(d.ch, 0) < d.pos and need.get(d.ch, 0) < d.pos:
                    need[d.ch] = d.pos
            if op.is_dma and op.pos > 1:
                if ck.get(op.ch, 0) < op.pos - 1 and need.get(op.ch, 0) < op.pos - 1:
                    need[op.ch] = op.pos - 1
            op.waits = []
            for ch, pos in need.items():
                if ck.get(ch, 0) >= pos:
                    continue
                p = self.by_chpos[(ch, pos)]
                p.signal = True
                op.waits.append(p)
                for c, v in p.vc.items():
                    if ck.get(c, 0) < v:
                        ck[c] = v
            vc = dict(ck)
            vc[op.ch] = op.pos
            op.vc = vc
        for op in ops:
            if op.is_dma:
                op.rank = 16 * op.pos
            elif op.signal:
                self.rk[op.eng] += 1
                op.rank = self.rk[op.eng]
        for op in ops:
            h = self.h[op.eng]
            for p in op.waits:
                if p.is_dma:
                    sem = self.dsems[p.ch[1]][p.ch[2]]
                else:
                    sem = self.esem[p.eng]
                h.wait_ge(sem, p.rank)
                self.nw += 1
            if op.fn is None:
                continue
            inst = op.fn(h)
            if op.is_dma:
                inst.then_inc(self.dsems[op.ch[1]][op.ch[2]], 16)
            elif op.signal:
                inst.then_inc(self.esem[op.eng], 1)
            op.fn = None
        self.n_emitted = len(self.ops)
        return dict(n_ops=len(self.ops), n_waits=self.nw, ranks=dict(self.rk))

    def phase_end(self, name=""):
        npe = sum(1 for o in self.ops if o.eng == "pe" and not o.is_dma and (o.fn is not None))
        self.phase_log = getattr(self, "phase_log", [])
        self.pe_total = getattr(self, "pe_total", 0) + npe
        self.phase_log.append((name, self.pe_total))
        self.barrier()
        r = self.emit()
        self.ops = []
        self.n_emitted = 0
        return r


def make_consts():
    i = np.arange(128)
    J, I = np.meshgrid(i, i, indexing="ij")
    c = np.zeros((128, NCONST), np.float32)
    c[:, 0:128] = (J == I)
    c[:, 128:256] = (J <= I)
    c[:, 256:384] = (J >= I)
    c[:, 384:512] = 1.0
    c[:, 512:640] = np.where(I >= J, 0.0, -30000.0)
    c[:, 640:768] = np.where(I <= J, 0.0, -30000.0)
    c[:, 768:896] = np.maximum(I - J, 0)
    c[:, 896:1024] = np.maximum(J - I, 0)
    c[:, 1024:1152] = I + 1
    c[:, 1152:1280] = 128 - I
    c[:, 1280] = 127 - i
    c[:, 1281] = i
    return c


def make_rope():
    rows = 2048 // 64
    row = np.repeat(np.arange(rows, dtype=np.float32), 64)
    col = np.tile(np.arange(64, dtype=np.float32), rows)
    n_freq = 32
    inv = (np.float32(10000.0) ** (-np.arange(n_freq, dtype=np.float32) / n_freq)).astype(np.float32)
    ang = np.concatenate([row[:, None] * inv, col[:, None] * inv], axis=-1).astype(np.float32)
    return np.concatenate([np.cos(ang), np.sin(ang)], axis=-1).astype(np.float32)


SMALL = [("b_mod", [2, 12288]), ("pre_mix_g", [2, 2048]), ("post_mix_g", [2, 2048]), ("pre_ffn_g", [2, 2048]),
         ("post_ffn_g", [2, 2048]), ("q_norm_g", [2, 128]), ("k_norm_g", [2, 128]), ("ret_decay_f", [2, 4]),
         ("ret_decay_b", [2, 4]), ("conv_w", [2, 5, 1280]), ("conv_b", [2, 1280]), ("dt_bias_f", [2, 12]),
         ("dt_bias_b", [2, 12]), ("a_log_f", [2, 12]), ("a_log_b", [2, 12]), ("d_skip", [2, 12]),
         ("ssd_norm_g", [2, 768])]
BIG = [("w_mod", [2, 2048, 12288]), ("w_in", [2, 2048, DIN]), ("w_out", [2, 2048, 2048]),
       ("w_gate", [2, 2048, DFF]), ("w_up", [2, 2048, DFF]), ("w_down", [2, DFF, 2048])]


def build(nlayers=2, debug=False, stop=None, phases=None, feed=(), layers=None):
    nc = bass.Bass("TRN2", target_bir_lowering=False)
    SHAPES = dict(SMALL + BIG)
    SHAPES.update({"x": [2048, 2048], "ctx": [256, 2048], "cvec": [2, 2048], "consts": [128, NCONST], "rope": [2048, 128]})

    class LazyIn(dict):
        def __missing__(self, name):
            ap = nc.dram_tensor(name, SHAPES[name], F32, kind="ExternalInput").ap()
            self[name] = ap
            return ap

    I = LazyIn()
    if not debug:
        for n in ["x", "ctx", "cvec", "consts", "rope"] + [n for n, _ in SMALL + BIG]:
            I[n]
    out = nc.dram_tensor("out", [2048, 2048], F32, kind="ExternalOutput").ap()
    skind = "ExternalOutput" if debug else "Internal"

    def scr(name, shape, dt=F32):
        k = "ExternalInput" if name in feed else skind
        return nc.dram_tensor(name, shape, dt, kind=k).ap()

    modv = scr("modv", [2, 2, 12288])
    proj = scr("proj", [T, PW])
    xbcT = scr("xbcT", [1280, T])
    mixT = scr("mixT", [2048, T], BF16)
    x1s = scr("x1s", [T, D])
    h2T = scr("h2T", [2048, T], BF16)
    fsc = scr("fsc", [T, D])
    xnext = scr("xnext", [T, D])
    zsd = scr("zsd", [T, 768], BF16)

    class Stop(Exception):
        pass

    with ExitStack() as gst:
        S = Sched(nc, gst)

        uid = [0]

        def mk_alloc(st):
            def Tl(name, shape, dt=F32):
                uid[0] += 1
                return st.enter_context(nc.sbuf_tensor("%s_%d" % (name, uid[0]), shape, dt))

            def Pl(name, shape, dt=F32):
                uid[0] += 1
                return st.enter_context(nc.psum_tensor("%s_%d" % (name, uid[0]), shape, dt))
            return Tl, Pl

        GT, GP = mk_alloc(gst)
        cst = GT("cst", [128, NCONST])
        ident_b = GT("ident_b", [128, 128], BF16)
        ones_b = GT("ones_b", [128, 128], BF16)
        S.dma("sp", lambda h: h.dma_start(out=cst[:], in_=I["consts"][:, :]), w=["cst"])
        S.add("dve", lambda h: h.tensor_copy(out=ident_b[:], in_=cst[:, 0:128]), r=["cst"], w=["ident_b"])
        S.add("dve", lambda h: h.tensor_copy(out=ones_b[:], in_=cst[:, 384:512]), r=["cst"], w=["ones_b"])
        ident_f = cst[:, 0:128]
        tri_f = cst[:, 128:256]
        tri_b = cst[:, 256:384]
        ones_f = cst[:, 384:512]
        mneg = [cst[:, 512:640], cst[:, 640:768]]
        D1 = cst[:, 768:896]
        D2 = cst[:, 896:1024]
        rowidx = [cst[:, 1024:1152], cst[:, 1152:1280]]
        colexp = [cst[:, 1280:1281], cst[:, 1281:1282]]
        S.phase_end("init")

        def check_stop(tag):
            if stop == tag:
                raise Stop()

        def rstd_ops(ssq_ap, out_ap, inv_n, rk, wk):
            S.add("act", lambda h: h.activation(out=out_ap, in_=ssq_ap, func=AF.Ln, scale=inv_n, bias=EPS), r=rk, w=wk)
            S.add("act", lambda h: h.activation(out=out_ap, in_=out_ap, func=AF.Exp, scale=-0.5), r=wk, w=wk)

        def xsrc(l, tt):
            if l == 0:
                if tt < 2:
                    return I["ctx"][tt * 128:(tt + 1) * 128, :]
                return I["x"][(tt - 2) * 128:(tt - 1) * 128, :]
            return xnext[tt * 128:(tt + 1) * 128, :]

        def bcast_row(ap_row, n):
            return ap_row.to_broadcast([128, n])

        def mod_task(l, Tl, Pl):
            cT = Tl("cT", [128, 16, 2])
            ce = Tl("ce", [128, 16, 2])
            sT = Tl("sT", [128, 16, 2], BF16)
            for r_ in range(2):
                S.dma("sp", lambda h, r_=r_: h.dma_start(out=cT[:, :, r_], in_=I["cvec"][r_].rearrange("(k p) -> p k", p=128),
                                                         allow_slow_non_contiguous=True), w=["mcT"])
            S.add("act", lambda h: h.activation(out=ce[:], in_=cT[:], func=AF.Exp, scale=-1.0), r=["mcT"], w=["mce"])
            S.add("dve", lambda h: h.tensor_scalar(out=ce[:], in0=ce[:], scalar1=1.0, scalar2=None, op0=ALU.add), r=["mce"], w=["mce"])
            S.add("dve", lambda h: h.reciprocal(out=ce[:], in_=ce[:]), r=["mce"], w=["mce"])
            S.add("dve", lambda h: h.tensor_tensor(out=sT[:], in0=cT[:], in1=ce[:], op=ALU.mult), r=["mce", "mcT"], w=["msT"])
            wb = [Tl("wmb%d" % i, [128, 16, 512], BF16) for i in range(2)]
            pm = Pl("pm", [128, 512])
            bsb = [Tl("bsb%d" % i, [2, 512]) for i in range(2)]
            msb = [Tl("msb%d" % i, [2, 512]) for i in range(2)]
            wv = I["w_mod"][l].rearrange("(k p) n -> p k n", p=128)
            yield

            def block(nb):
                b = nb % 2
                S.dma("sp", lambda h: h.dma_start(out=bsb[b][:], in_=I["b_mod"][l:l + 1, nb * 512:(nb + 1) * 512].to_broadcast([2, 512])), w=[("mbsb", b)])
                for k4 in range(4):
                    S.dma("pool", lambda h, k4=k4: h.dma_start(out=wb[b][:, 4 * k4:4 * k4 + 4, :], in_=wv[:, 4 * k4:4 * k4 + 4, nb * 512:(nb + 1) * 512]),
                          w=[("wmb", b, k4)])
                for k in range(16):
                    S.add("pe", lambda h, k=k: h.matmul(pm[0:2, :], lhsT=sT[:, k, :], rhs=wb[b][:, k, :], start=(k == 0), stop=(k == 15)),
                          r=["msT", ("wmb", b, k // 4)], w=["mpm"])
                S.add("dve", lambda h: h.tensor_tensor(out=msb[b][:], in0=pm[0:2, :], in1=bsb[b][:], op=ALU.add), r=["mpm", ("mbsb", b)], w=[("mmsb", b)])
                S.dma("sp", lambda h: h.dma_start(out=modv[l, :, nb * 512:(nb + 1) * 512], in_=msb[b][:]), r=[("mmsb", b)], w=[("modv", l, nb)])
            for nb in range(24):
                block(nb)
                yield

        def phase_mod(l):
            with ExitStack() as st:
                Tl, Pl = mk_alloc(st)
                for _ in mod_task(l, Tl, Pl):
                    pass
                S.phase_end("mod")

        def norm_mod_tiles(l, Tl, tagp, gname, sc_off, sh_off, r):
            gm = Tl(tagp + "gm", [128, D])
            sh = Tl(tagp + "sh", [128, D])
            gp = Tl(tagp + "gp", [128, D])
            return gm, sh, gp

        def load_norm_mod(l, gm, sh, gp, key, gname, sc_off, sh_off, r):
            S.dma("sp", lambda h: h.dma_start(out=gp[:], in_=bcast_row(I[gname][l:l + 1, :], D)), w=[key + "gp"])
            S.dma("sp", lambda h: h.dma_start(out=gm[:], in_=bcast_row(modv[l, r:r + 1, sc_off:sc_off + D], D)), w=[key + "gm"])
            S.dma("sp", lambda h: h.dma_start(out=sh[:], in_=bcast_row(modv[l, r:r + 1, sh_off:sh_off + D], D)), w=[key + "sh"])
            S.add("dve", lambda h: h.scalar_tensor_tensor(out=gm[:], in0=gm[:], scalar=1.0, in1=gp[:], op0=ALU.add, op1=ALU.mult),
                  r=[key + "gp", key + "gm"], w=[key + "gm"])

        def load_gate_mod(l, G, gp, key, gname, g_off, r):
            S.dma("sp", lambda h: h.dma_start(out=gp[:], in_=bcast_row(I[gname][l:l + 1, :], D)), w=[key + "gp"])
            S.dma("sp", lambda h: h.dma_start(out=G[:], in_=bcast_row(modv[l, r:r + 1, g_off:g_off + D], D)), w=[key + "G"])
            S.add("dve", lambda h: h.tensor_tensor(out=G[:], in0=G[:], in1=gp[:], op=ALU.mult), r=[key + "gp", key + "G"], w=[key + "G"])

        def norm_transpose_tile(l, tt, xt_ap, xkey, gm, sh, mkey, tmp, hb, junk, ssq, pt, dst_fn, wkeys, i):
            sq = ssq[:, 2 * (i % 2):2 * (i % 2) + 1]
            rs = ssq[:, 2 * (i % 2) + 1:2 * (i % 2) + 2]
            S.add("act", lambda h: h.activation(out=junk[:], in_=xt_ap, func=AF.Square, accum_out=sq),
                  r=[xkey], w=["junk", ("ssq", i % 2)])
            rstd_ops(sq, rs, 1.0 / D, [("ssq", i % 2)], [("rstd", i % 2)])
            S.add("dve", lambda h: h.scalar_tensor_tensor(out=tmp[:], in0=xt_ap, scalar=rs, in1=gm[:], op0=ALU.mult, op1=ALU.mult),
                  r=[xkey, ("rstd", i % 2), mkey + "gm"], w=["tmp"])
            S.add("dve", lambda h: h.tensor_tensor(out=hb[:], in0=tmp[:], in1=sh[:], op=ALU.add), r=["tmp", mkey + "sh"], w=[("hb", i % 2)])
            def later():
                for k in range(16):
                    S.add("pe", lambda h, k=k: h.transpose(out=pt[:, k, :], in_=hb[:, k * 128:(k + 1) * 128], identity=ident_b[:]),
                          r=[("hb", i % 2)], w=[("pt", i % 2)])
                dst_fn(pt, ("pt", i % 2), wkeys)
            return later

        def phase_in(l):
            with ExitStack() as st_o:
                To, Po = mk_alloc(st_o)
                hT = To("hT", [128, 16, T], BF16)
                with ExitStack() as st:
                    Tl, Pl = mk_alloc(st)
                    mods = {}
                    for r, nm in ((1, "c"), (0, "l")):
                        gm = Tl("gm" + nm, [128, D])
                        sh = Tl("sh" + nm, [128, D])
                        gp = Tl("gp" + nm, [128, D])
                        load_norm_mod(l, gm, sh, gp, nm, "pre_mix_g", 2048, 0, r)
                        mods[r] = (gm, sh, nm)
                    xt = [Tl("xt%d" % i, [128, D]) for i in range(3)]
                    tmp = Tl("tmp", [128, D])
                    hb = [Tl("hb%d" % i, [128, D], BF16) for i in range(2)]
                    junk = Tl("junk", [128, D], BF16)
                    ssq = Tl("ssq", [128, 4])
                    pt = [Pl("pt%d" % i, [128, 16, 128], BF16) for i in range(2)]
                    pend = None
                    for tt in range(NT):
                        i = tt
                        gm, sh, nm = mods[1 if tt < 2 else 0]
                        S.dma("sp", lambda h, tt=tt, i=i: h.dma_start(out=xt[i % 3][:], in_=xsrc(l, tt)), w=[("xt", i % 3)])

                        def dst(ptile, pkey, wkeys, tt=tt):
                            S.add("act", lambda h: h.activation(out=hT[:, :, tt * 128:(tt + 1) * 128], in_=ptile[:], func=AF.Copy),
                                  r=[pkey], w=wkeys)
                        th = norm_transpose_tile(l, tt, xt[i % 3][:], ("xt", i % 3), gm, sh, nm, tmp, hb[i % 2], junk, ssq, pt[i % 2], dst,
                                                 [("hT", tt)], i)
                        if pend is not None:
                            pend()
                        pend = th
                    pend()
                    S.phase_end("P1")
                with ExitStack() as st:
                    Tl, Pl = mk_alloc(st)
                    wb = [Tl("wib%d" % i, [128, 16, 512], BF16) for i in range(2)]
                    stg = [Tl("stg%d" % i, [128, 512]) for i in range(4)]
                    pp = [Pl("pp%d" % i, [128, 512]) for i in range(4)]
                    wv = I["w_in"][l].rearrange("(k p) n -> p k n", p=128)
                    cnt = [0]

                    def evac(ps_ap, n, dram_ap, pkey):
                        i = cnt[0]
                        cnt[0] += 1
                        s = stg[i % 4]
                        if i % 2 == 0:
                            S.add("act", lambda h: h.activation(out=s[:, 0:n], in_=ps_ap, func=AF.Copy), r=[pkey], w=[("stg", i % 4)])
                        else:
                            S.add("dve", lambda h: h.tensor_copy(out=s[:, 0:n], in_=ps_ap), r=[pkey], w=[("stg", i % 4)])
                        S.dma("sp", lambda h: h.dma_start(out=dram_ap, in_=s[:, 0:n]), r=[("stg", i % 4)], w=[("dram", i)])

                    blocks = [(c0, 512) for c0 in range(0, 5120, 512)] + [(5120, 280)]
                    pi = [0]
                    for bi, (c0, ncol) in enumerate(blocks):
                        b = bi % 2
                        for k4 in range(4):
                            S.dma("pool", lambda h, b=b, k4=k4, c0=c0, ncol=ncol: h.dma_start(
                                out=wb[b][:, 4 * k4:4 * k4 + 4, 0:ncol], in_=wv[:, 4 * k4:4 * k4 + 4, c0:c0 + ncol]), w=[("wib", b, k4)])
                        wkeys = [("wib", b, k4) for k4 in range(4)]
                        if c0 < 4096:
                            tm = (0, ncol, c0)
                            fm_chunks = []
                        elif c0 < 5120:
                            tm = None
                            fm_chunks = [(j, (c0 - 4096) // 128 + j) for j in range(4)]
                        else:
                            tm = (256, 24, 4096)
                            fm_chunks = [(0, 8), (1, 9)]
                        if tm is not None:
                            co, n, dc = tm
                            for tt in range(NT):
                                p = pi[0] % 4
                                pi[0] += 1
                                for k in range(16):
                                    S.add("pe", lambda h, p=p, k=k, tt=tt, co=co, n=n, b=b: h.matmul(
                                        pp[p][:, 0:n], lhsT=hT[:, k, tt * 128:(tt + 1) * 128], rhs=wb[b][:, k, co:co + n],
                                        start=(k == 0), stop=(k == 15)), r=wkeys, w=[("pp", p)])
                                evac(pp[p][:, 0:n], n, proj[tt * 128:(tt + 1) * 128, dc:dc + n], ("pp", p))
                        for (j, ch) in fm_chunks:
                            for t0 in range(0, T, 512):
                                n = min(512, T - t0)
                                p = pi[0] % 4
                                pi[0] += 1
                                for k in range(16):
                                    S.add("pe", lambda h, p=p, k=k, t0=t0, n=n, j=j, b=b: h.matmul(
                                        pp[p][:, 0:n], lhsT=wb[b][:, k, j * 128:(j + 1) * 128], rhs=hT[:, k, t0:t0 + n],
                                        start=(k == 0), stop=(k == 15)), r=wkeys, w=[("pp", p)])
                                evac(pp[p][:, 0:n], n, xbcT[ch * 128:(ch + 1) * 128, t0:t0 + n], ("pp", p))
                    S.phase_end("P2")

        def rope_ops(src, dst, cs, nh, lat, skey, dkey, cskey, tmps):
            if not lat:
                S.add("dve", lambda h: h.tensor_copy(out=dst, in_=src), r=[skey], w=[dkey, (dkey, "b")])
                return
            t1, t2, t3, t4 = tmps
            s4 = src.rearrange("p (h i two) -> p h i two", h=nh, two=2)
            d4 = dst.rearrange("p (h i two) -> p h i two", h=nh, two=2)
            x1 = s4[:, :, :, 0]
            x2 = s4[:, :, :, 1]
            cosb = cs[:, 0:64].unsqueeze(1).to_broadcast([128, nh, 64])
            sinb = cs[:, 64:128].unsqueeze(1).to_broadcast([128, nh, 64])
            v = lambda t: t[:, 0:nh * 64].rearrange("p (h i) -> p h i", h=nh)
            S.add("dve", lambda h: h.tensor_tensor(out=v(t1), in0=x1, in1=cosb, op=ALU.mult), r=[skey, cskey], w=["rt1"])
            S.add("dve", lambda h: h.tensor_tensor(out=v(t2), in0=x2, in1=sinb, op=ALU.mult), r=[skey, cskey], w=["rt2"])
            S.add("dve", lambda h: h.tensor_tensor(out=d4[:, :, :, 0], in0=v(t1), in1=v(t2), op=ALU.subtract), r=["rt1", "rt2"], w=[dkey])
            S.add("dve", lambda h: h.tensor_tensor(out=v(t3), in0=x1, in1=sinb, op=ALU.mult), r=[skey, cskey], w=["rt3"])
            S.add("dve", lambda h: h.tensor_tensor(out=v(t4), in0=x2, in1=cosb, op=ALU.mult), r=[skey, cskey], w=["rt4"])
            S.add("dve", lambda h: h.tensor_tensor(out=d4[:, :, :, 1], in0=v(t3), in1=v(t4), op=ALU.add), r=["rt3", "rt4"], w=[(dkey, "b")])

        def phase_att(l):
            last = (l == nlayers - 1)
            with ExitStack() as st:
                Tl, Pl = mk_alloc(st)
                QKT = Tl("QKT", [128, 8, T], BF16)
                Vtm = Tl("Vtm", [128, NT, 256], BF16)
                gqk = Tl("gqk", [128, 8, 128])
                pr = [Tl("pr%d" % i, [128, 1280]) for i in range(2)]
                sq = Tl("sq", [128, 1024])
                qn = Tl("qn", [128, 1024])
                tmps = [Tl("rt%d" % i, [128, 512]) for i in range(4)]
                qr = [Tl("qr%d" % i, [128, 1024], BF16) for i in range(2)]
                cs = [Tl("cs%d" % i, [128, 128]) for i in range(2)]
                st8 = Tl("st8", [128, 16])
                ptq = Pl("ptq", [128, 8, 128], BF16)
                for hh in range(8):
                    src = I["q_norm_g"] if hh < 6 else I["k_norm_g"]
                    S.dma("sp", lambda h, hh=hh, src=src: h.dma_start(out=gqk[:, hh, :], in_=src[l:l + 1, :].to_broadcast([128, 128])), w=["gqk"])
                S.add("dve", lambda h: h.tensor_scalar(out=gqk[:, 0:6, :], in0=gqk[:, 0:6, :], scalar1=float(128 ** -0.5), scalar2=None, op0=ALU.mult),
                      r=["gqk"], w=["gqk"])
                def prep_tile(tt):
                    i = tt
                    lat = tt >= 2
                    p_ = pr[i % 2]
                    S.dma("sp", lambda h, tt=tt, p_=p_: h.dma_start(out=p_[:], in_=proj[tt * 128:(tt + 1) * 128, 0:1280]), w=[("pr", i % 2)])
                    if lat:
                        S.dma("sp", lambda h, tt=tt, i=i: h.dma_start(out=cs[i % 2][:], in_=I["rope"][(tt - 2) * 128:(tt - 1) * 128, :]), w=[("cs", i % 2)])
                    S.add("act", lambda h, p_=p_: h.activation(out=sq[:], in_=p_[:, 0:1024], func=AF.Square), r=[("pr", i % 2)], w=["sq"])
                    S.add("dve", lambda h: h.tensor_reduce(out=st8[:, 0:8], in_=sq[:].rearrange("p (h d) -> p h d", h=8), axis=AX.X, op=ALU.add),
                          r=["sq"], w=["ssq8"])
                    rstd_ops(st8[:, 0:8], st8[:, 8:16], 1.0 / 128, ["ssq8"], ["rstd8"])
                    S.add("dve", lambda h, p_=p_: h.tensor_tensor(out=qn[:].rearrange("p (h d) -> p h d", h=8),
                                                                 in0=p_[:, 0:1024].rearrange("p (h d) -> p h d", h=8),
                                                                 in1=st8[:, 8:16].unsqueeze(2).to_broadcast([128, 8, 128]), op=ALU.mult),
                          r=[("pr", i % 2), "rstd8"], w=["qn"])
                    S.add("dve", lambda h: h.tensor_tensor(out=qn[:], in0=qn[:], in1=gqk[:].rearrange("p h d -> p (h d)"), op=ALU.mult),
                          r=["qn", "gqk"], w=["qn"])
                    q_ = qr[i % 2]
                    rope_ops(qn[:], q_[:], cs[i % 2], 8, lat, "qn", ("qr", i % 2), ("cs", i % 2), tmps)
                    for hh in range(8):
                        S.add("pe", lambda h, hh=hh, q_=q_: h.transpose(out=ptq[:, hh, :], in_=q_[:, hh * 128:(hh + 1) * 128], identity=ident_b[:]),
                              r=[("qr", i % 2), (("qr", i % 2), "b")], w=["ptq"])
                    S.add("act", lambda h, tt=tt: h.activation(out=QKT[:, :, tt * 128:(tt + 1) * 128], in_=ptq[:], func=AF.Copy), r=["ptq"], w=[("QKT", tt)])
                    S.add("act", lambda h, tt=tt, p_=p_: h.activation(out=Vtm[:, tt, :], in_=p_[:, 1024:1280], func=AF.Copy), r=[("pr", i % 2)], w=[("Vtm", tt)])
                for tt in range(NT):
                    prep_tile(tt)
                ps_s = [Pl("ps_s%d" % i, [128, 512]) for i in range(2)]
                ps_o = [Pl("ps_o%d" % i, [128, 512]) for i in range(2)]
                ps_d = [Pl("ps_d%d" % i, [128, 512]) for i in range(2)]
                pT = [Tl("pT%d" % i, [128, 512], BF16) for i in range(3)]
                rden = Tl("rden", [128, 512])
                oT = [Tl("oT%d" % i, [128, 512], BF16) for i in range(2)]
                qblocks = [(256 + qb * 512, 512, list(range(NT))) for qb in range(4)]
                if not last:
                    qblocks = [(0, 256, [0, 1])] + qblocks
                gi = 0
                si = 0
                def att_head(q0, n, kts, hh, gi):
                        nonlocal si
                        g = hh // 3
                        o = gi % 2
                        nk = len(kts)
                        slots = []

                        def do_s(j):
                            nonlocal si
                            sidx = si
                            si += 1
                            kt = kts[j]
                            S.add("pe", lambda h, sidx=sidx, kt=kt: h.matmul(ps_s[sidx % 2][:, 0:n], lhsT=QKT[:, 6 + g, kt * 128:(kt + 1) * 128],
                                                                           rhs=QKT[:, hh, q0:q0 + n], start=True, stop=True),
                                  r=[("QKT", kt)] + [("QKT", q0 // 128 + a) for a in range(n // 128)], w=[("ps_s", sidx % 2)])
                            S.add("act", lambda h, sidx=sidx: h.activation(out=pT[sidx % 3][:, 0:n], in_=ps_s[sidx % 2][:, 0:n], func=AF.Exp),
                                  r=[("ps_s", sidx % 2)], w=[("pT", sidx % 3)])
                            slots.append(sidx)

                        def do_o(j):
                            sidx = slots[j]
                            kt = kts[j]
                            S.add("pe", lambda h, sidx=sidx, kt=kt: h.matmul(ps_o[o][:, 0:n], lhsT=Vtm[:, kt, g * 128:(g + 1) * 128], rhs=pT[sidx % 3][:, 0:n],
                                                                           start=(j == 0), stop=(j == nk - 1)),
                                  r=[("pT", sidx % 3), ("Vtm", kt)], w=[("ps_o", o)])
                            S.add("pe", lambda h, sidx=sidx: h.matmul(ps_d[o][:, 0:n], lhsT=ones_b[:], rhs=pT[sidx % 3][:, 0:n],
                                                                    start=(j == 0), stop=(j == nk - 1)),
                                  r=[("pT", sidx % 3)], w=[("ps_d", o)])
                        do_s(0)
                        for j in range(nk):
                            if j + 1 < nk:
                                do_s(j + 1)
                            do_o(j)
                        S.add("dve", lambda h, o=o: h.reciprocal(out=rden[:, 0:n], in_=ps_d[o][:, 0:n]), r=[("ps_d", o)], w=["rden"])
                        S.add("dve", lambda h, o=o: h.tensor_tensor(out=oT[o][:, 0:n], in0=ps_o[o][:, 0:n], in1=rden[:, 0:n], op=ALU.mult),
                              r=[("ps_o", o), "rden"], w=[("oT", o)])
                        S.dma("sp", lambda h, o=o, hh=hh, q0=q0, n=n: h.dma_start(out=mixT[hh * 128:(hh + 1) * 128, q0:q0 + n], in_=oT[o][:, 0:n]),
                              r=[("oT", o)], w=[("mixT", gi)])
                bg = mod_task(l + 1, Tl, Pl) if (l + 1 < nlayers and (phases is None or "mod" in phases)) else iter(())
                next(bg, None)
                for (q0, n, kts) in qblocks:
                    for hh in range(6):
                        att_head(q0, n, kts, hh, gi)
                        gi += 1
                        next(bg, None)
                for _ in bg:
                    pass
                S.phase_end("att")

        def silu2_ops(src, e, dst, skey, ekey, dkey, in_scale=0.5):
            S.add("act", lambda h: h.activation(out=e, in_=src, func=AF.Tanh, scale=in_scale), r=[skey], w=[ekey])
            S.add("dve", lambda h: h.scalar_tensor_tensor(out=dst, in0=e, scalar=1.0, in1=src, op0=ALU.add, op1=ALU.mult), r=[skey, ekey], w=[dkey])

        def phase_ret(l):
            last = (l == nlayers - 1)
            with ExitStack() as st:
                Tl, Pl = mk_alloc(st)
                RQK = Tl("RQK", [128, 8, T], BF16)
                RKtm = Tl("RKtm", [128, NT, 512], BF16)
                RVtm = Tl("RVtm", [128, NT, 512], BF16)
                rgs = Tl("rgs", [128, NT, 512], BF16)
                SAll = [Tl("SAll%d" % d_, [128, NT, 512], BF16) for d_ in range(2)]
                S32 = [Tl("S32%d" % d_, [128, 512]) for d_ in range(2)]
                MaskT = Tl("MaskT", [128, 4, 128])
                ERow = [Tl("ERow%d" % d_, [128, 4, 128], BF16) for d_ in range(2)]
                dec = Tl("dec", [128, 8])
                lg = Tl("lg", [128, 8])
                dend = Tl("dend", [128, 8])
                gam = Tl("gam", [128, 8])
                marg = Tl("marg", [128, 128])
                pr = [Tl("rpr%d" % i, [128, 2048]) for i in range(2)]
                tmps = [Tl("rrt%d" % i, [128, 512]) for i in range(4)]
                qr = [Tl("rqr%d" % i, [128, 1024], BF16) for i in range(2)]
                cs = [Tl("rcs%d" % i, [128, 128]) for i in range(2)]
                ee = Tl("ree", [128, 512])
                ptq = Pl("rptq", [128, 8, 128], BF16)
                S.dma("sp", lambda h: h.dma_start(out=dec[:, 0:4], in_=I["ret_decay_f"][l:l + 1, :].to_broadcast([128, 4])), w=["dec"])
                S.dma("sp", lambda h: h.dma_start(out=dec[:, 4:8], in_=I["ret_decay_b"][l:l + 1, :].to_broadcast([128, 4])), w=["dec"])
                S.add("act", lambda h: h.activation(out=lg[:], in_=dec[:], func=AF.Exp, scale=float(math.log(2.0))), r=["dec"], w=["lg"])
                S.add("act", lambda h: h.activation(out=lg[:], in_=lg[:], func=AF.Ln, scale=-1.0, bias=1.0), r=["lg"], w=["lg"])
                S.add("act", lambda h: h.activation(out=gam[:], in_=lg[:], func=AF.Exp, scale=128.0), r=["lg"], w=["gam"])

                def mk_head(hh):
                    S.add("dve", lambda h: h.tensor_scalar(out=marg[:], in0=D1, scalar1=lg[:, hh:hh + 1], scalar2=None, op0=ALU.mult), r=["lg"], w=["marg"])
                    S.add("dve", lambda h: h.scalar_tensor_tensor(out=marg[:], in0=D2, scalar=lg[:, 4 + hh:5 + hh], in1=marg[:], op0=ALU.mult, op1=ALU.add),
                          r=["lg", "marg"], w=["marg"])
                    S.add("act", lambda h: h.activation(out=marg[:], in_=marg[:], func=AF.Exp), r=["marg"], w=["marg"])
                    S.add("dve", lambda h: h.tensor_tensor(out=MaskT[:, hh, :], in0=marg[:], in1=ident_f, op=ALU.add), r=["marg"], w=["MaskT"])
                    for d_ in range(2):
                        S.add("act", lambda h, d_=d_: h.activation(out=ERow[d_][:, hh, :], in_=rowidx[d_], func=AF.Exp, scale=lg[:, 4 * d_ + hh:4 * d_ + hh + 1]),
                              r=["lg"], w=["ERow"])
                        S.add("act", lambda h, d_=d_: h.activation(out=dend[:, 4 * d_ + hh:4 * d_ + hh + 1], in_=colexp[d_], func=AF.Exp,
                                                                   scale=lg[:, 4 * d_ + hh:4 * d_ + hh + 1]), r=["lg"], w=["dend"])
                for hh in range(4):
                    mk_head(hh)

                def prep_tile(tt):
                    i = tt
                    lat = tt >= 2
                    p_ = pr[i % 2]
                    S.dma("sp", lambda h: h.dma_start(out=p_[:], in_=proj[tt * 128:(tt + 1) * 128, 1280:3328]), w=[("pr", i % 2)])
                    if lat:
                        S.dma("sp", lambda h: h.dma_start(out=cs[i % 2][:], in_=I["rope"][(tt - 2) * 128:(tt - 1) * 128, :]), w=[("cs", i % 2)])
                    S.add("act", lambda h: h.activation(out=p_[:, 512:1024], in_=p_[:, 512:1024], func=AF.Copy, scale=float(128 ** -0.5)),
                          r=[("pr", i % 2)], w=[("pr", i % 2)])
                    q_ = qr[i % 2]
                    rope_ops(p_[:, 0:1024], q_[:], cs[i % 2], 8, lat, ("pr", i % 2), ("qr", i % 2), ("cs", i % 2), tmps)
                    for hh in range(8):
                        S.add("pe", lambda h, hh=hh: h.transpose(out=ptq[:, hh, :], in_=q_[:, hh * 128:(hh + 1) * 128], identity=ident_b[:]),
                              r=[("qr", i % 2), (("qr", i % 2), "b")], w=["ptq"])
                    S.add("act", lambda h: h.activation(out=RQK[:, :, tt * 128:(tt + 1) * 128], in_=ptq[:], func=AF.Copy), r=["ptq"], w=[("RQK", tt)])
                    S.add("dve", lambda h: h.tensor_copy(out=RKtm[:, tt, :], in_=q_[:, 512:1024]), r=[("qr", i % 2), (("qr", i % 2), "b")], w=[("RKtm", tt)])
                    S.add("act", lambda h: h.activation(out=RVtm[:, tt, :], in_=p_[:, 1024:1536], func=AF.Copy), r=[("pr", i % 2)], w=[("RVtm", tt)])
                    silu2_ops(p_[:, 1536:2048], ee[:], rgs[:, tt, :], ("pr", i % 2), "ee", ("rgs", tt))
                for tt in range(NT):
                    prep_tile(tt)

                psA2 = [Pl("psA%d" % d_, [128, 512]) for d_ in range(2)]
                RVs2 = [[Tl("RVs%d_%d" % (d_, i), [128, 512], BF16) for i in range(2)] for d_ in range(2)]
                orders = [list(range(NT)), [1, 0] + list(range(NT - 1, 1, -1))]

                def state_step(d_, idx):
                    order = orders[d_]
                    c = order[idx]
                    if idx == 0:
                        S.add("pool", lambda h: h.memset(S32[d_][:], 0.0), w=[("S32", d_)])
                        S.add("pool", lambda h: h.memset(SAll[d_][:, c, :], 0.0), w=[("SAll", d_, c)])
                    if idx == NT - 1:
                        return
                    rv = RVs2[d_][idx % 2]
                    psA = psA2[d_]
                    S.add("dve", lambda h: h.tensor_tensor(out=rv[:].rearrange("p (h d) -> p h d", h=4),
                                                            in0=RVtm[:, c, :].rearrange("p (h d) -> p h d", h=4),
                                                            in1=dend[:, 4 * d_:4 * d_ + 4].unsqueeze(2).to_broadcast([128, 4, 128]), op=ALU.mult),
                          r=[("RVtm", c), "dend"], w=[("RVs", d_, idx % 2)])
                    for hh in range(4):
                        S.add("pe", lambda h, hh=hh: h.matmul(psA[:, hh * 128:(hh + 1) * 128], lhsT=RKtm[:, c, hh * 128:(hh + 1) * 128],
                                                             rhs=rv[:, hh * 128:(hh + 1) * 128], start=True, stop=True),
                              r=[("RKtm", c), ("RVs", d_, idx % 2)], w=[("psA", d_)])
                    S.add("dve", lambda h: h.tensor_tensor(out=S32[d_][:].rearrange("p (h d) -> p h d", h=4),
                                                           in0=S32[d_][:].rearrange("p (h d) -> p h d", h=4),
                                                           in1=gam[:, 4 * d_:4 * d_ + 4].unsqueeze(2).to_broadcast([128, 4, 128]), op=ALU.mult),
                          r=[("S32", d_), "gam"], w=[("S32", d_)])
                    S.add("dve", lambda h: h.tensor_tensor(out=S32[d_][:], in0=S32[d_][:], in1=psA[:], op=ALU.add), r=[("S32", d_), ("psA", d_)], w=[("S32", d_)])
                    cn = order[idx + 1]
                    S.add("act", lambda h: h.activation(out=SAll[d_][:, cn, :], in_=S32[d_][:], func=AF.Copy), r=[("S32", d_)], w=[("SAll", d_, cn)])
                for idx in range(NT):
                    for d_ in range(2):
                        state_step(d_, idx)

                psS = [Pl("psS%d" % i, [128, 4, 128]) for i in range(2)]
                psY = [Pl("psY%d" % i, [128, 512]) for i in range(2)]
                ptr = Pl("ptr", [128, 8, 128], BF16)
                SDT = [Tl("SDT%d" % i, [128, 4, 128], BF16) for i in range(2)]
                RQs = [[Tl("RQs%d_%d" % (d_, i), [128, 4, 128], BF16) for i in range(2)] for d_ in range(2)]
                sqy = Tl("sqy", [128, 512])
                yn = Tl("yn", [128, 512])
                yb = [Tl("yb%d" % i, [128, 512], BF16) for i in range(2)]
                stgr = [Tl("stgr%d" % i, [128, 4, 128], BF16) for i in range(2)]
                st4 = Tl("st4", [128, 8])

                def chunk(c, i):
                    y = psY[i % 2]
                    cs_ = slice(c * 128, (c + 1) * 128)
                    for hh in range(4):
                        S.add("pe", lambda h, hh=hh: h.matmul(psS[i % 2][:, hh, :], lhsT=RQK[:, 4 + hh, cs_], rhs=RQK[:, hh, cs_], start=(hh == 0), stop=(hh == 3)),
                              r=[("RQK", c)], w=[("psS", i % 2)])
                    S.add("dve", lambda h: h.tensor_tensor(out=SDT[i % 2][:], in0=psS[i % 2][:], in1=MaskT[:], op=ALU.mult),
                          r=[("psS", i % 2), "MaskT"], w=[("SDT", i % 2)])
                    for d_ in range(2):
                        S.add("dve", lambda h, d_=d_: h.tensor_tensor(out=RQs[d_][i % 2][:], in0=RQK[:, 0:4, cs_], in1=ERow[d_][:], op=ALU.mult),
                              r=[("RQK", c), "ERow"], w=[("RQs", d_, i % 2)])
                    for hh in range(4):
                        S.add("pe", lambda h, hh=hh: h.matmul(y[:, hh * 128:(hh + 1) * 128], lhsT=SDT[i % 2][:, hh, :], rhs=RVtm[:, c, hh * 128:(hh + 1) * 128],
                                                             start=(hh == 0), stop=False), r=[("SDT", i % 2), ("RVtm", c)], w=[("psY", i % 2)])
                        for d_ in range(2):
                            S.add("pe", lambda h, hh=hh, d_=d_: h.matmul(y[:, hh * 128:(hh + 1) * 128], lhsT=RQs[d_][i % 2][:, hh, :],
                                                                        rhs=SAll[d_][:, c, hh * 128:(hh + 1) * 128], start=False, stop=(d_ == 1 and hh == 3)),
                                  r=[("RQs", d_, i % 2), ("SAll", d_, c)], w=[("psY", i % 2)])
                def chunk_epi(c, i):
                    y = psY[i % 2]
                    S.add("act", lambda h: h.activation(out=sqy[:], in_=y[:], func=AF.Square), r=[("psY", i % 2)], w=["sqy"])
                    S.add("dve", lambda h: h.tensor_reduce(out=st4[:, 0:4], in_=sqy[:].rearrange("p (h d) -> p h d", h=4), axis=AX.X, op=ALU.add),
                          r=["sqy"], w=["ssq4"])
                    rstd_ops(st4[:, 0:4], st4[:, 4:8], 1.0 / 128, ["ssq4"], ["rstd4"])
                    S.add("dve", lambda h: h.tensor_tensor(out=yn[:].rearrange("p (h d) -> p h d", h=4), in0=y[:].rearrange("p (h d) -> p h d", h=4),
                                                           in1=st4[:, 4:8].unsqueeze(2).to_broadcast([128, 4, 128]), op=ALU.mult),
                          r=[("psY", i % 2), "rstd4"], w=["yn"])
                    yb_ = yb[i % 2]
                    S.add("dve", lambda h: h.scalar_tensor_tensor(out=yb_[:], in0=yn[:], scalar=0.5, in1=rgs[:, c, :], op0=ALU.mult, op1=ALU.mult),
                          r=["yn", ("rgs", c)], w=[("yb", i % 2)])
                    for hh in range(4):
                        S.add("pe", lambda h, hh=hh: h.transpose(out=ptr[:, hh, :], in_=yb_[:, hh * 128:(hh + 1) * 128], identity=ident_b[:]),
                              r=[("yb", i % 2)], w=["ptr"])
                    sg = stgr[i % 2]
                    S.add("act", lambda h: h.activation(out=sg[:], in_=ptr[:, 0:4, :], func=AF.Copy), r=["ptr"], w=[("stgr", i % 2)])
                    S.dma("sp", lambda h: h.dma_start(out=mixT[768:1280, c * 128:(c + 1) * 128].rearrange("(h d) t -> d h t", h=4), in_=sg[:]),
                          r=[("stgr", i % 2)], w=[("mixTr", c)])
                cl = list(range(2 if last else 0, NT))
                chunk(cl[0], 0)
                for i_, c in enumerate(cl):
                    if i_ + 1 < len(cl):
                        chunk(cl[i_ + 1], i_ + 1)
                    chunk_epi(c, i_)
                S.phase_end("ret")

        def phase_ssd(l):
            last = (l == nlayers - 1)
            NH = 12
            with ExitStack() as st_o:
                To, Po = mk_alloc(st_o)
                BT = To("BT", [128, 2, T], BF16)
                CT = To("CT", [128, 2, T], BF16)
                Btm = To("Btm", [128, NT, 256], BF16)
                xs = To("xs", [128, NT, 768], BF16)
                la = To("la", [128, 2, NT, NH])
                lndt = To("lndt", [128, 2, NT, NH])
                acs = To("acs", [128, 2, NT, NH])
                tot = To("tot", [128, 2, NT, NH])
                ein = To("ein", [128, 2, NT, NH])
                cdc = To("cdc", [128, 2, NT, NH])
                wend = To("wend", [128, 2, NT, NH])
                lb = To("lb", [128, 2, NT, NH])
                dsk = To("dsk", [128, NH])
                gssd = To("gssd", [128, 768])
                with ExitStack() as st:
                    Tl, Pl = mk_alloc(st)
                    cw = Tl("cw", [128, 10, 5])
                    cb = Tl("cb", [128, 10])
                    for k in range(5):
                        S.dma("sp", lambda h, k=k: h.dma_start(out=cw[:, :, k], in_=I["conv_w"][l, k].rearrange("(c p) -> p c", p=128),
                                                               allow_slow_non_contiguous=True), w=["cw"])
                    S.dma("sp", lambda h: h.dma_start(out=cb[:], in_=I["conv_b"][l].rearrange("(c p) -> p c", p=128), allow_slow_non_contiguous=True), w=["cb"])
                    S.dma("sp", lambda h: h.dma_start(out=dsk[:], in_=I["d_skip"][l:l + 1, :].to_broadcast([128, NH])), w=["dsk"])
                    S.dma("sp", lambda h: h.dma_start(out=gssd[:], in_=I["ssd_norm_g"][l:l + 1, :].to_broadcast([128, 768])), w=["gssd"])
                    UW = 2312
                    u = [Tl("u%d" % i, [128, UW]) for i in range(2)]
                    acc = Tl("acc", [128, UW])
                    ee = Tl("cee", [128, UW])
                    ob = [Tl("ob%d" % i, [128, UW], BF16) for i in range(2)]
                    ptx = Pl("ptx", [128, 8, 128], BF16)
                    for i in range(2):
                        S.add("pool", lambda h, i=i: h.memset(u[i][:], 0.0), w=[("u", i)])
                    S.add("dve", lambda h: h.tensor_scalar(out=cw[:], in0=cw[:], scalar1=0.5, scalar2=None, op0=ALU.mult), r=["cw"], w=["cw"])
                    S.add("dve", lambda h: h.tensor_scalar(out=cb[:], in0=cb[:], scalar1=0.5, scalar2=None, op0=ALU.mult), r=["cb"], w=["cb"])

                    def conv_chunk(cc):
                        i = cc
                        u_ = u[i % 2]
                        o_ = ob[i % 2]
                        S.dma("sp", lambda h: h.dma_start(out=u_[:, 2:258], in_=xbcT[cc * 128:(cc + 1) * 128, 0:256]), w=[("u", i % 2)])
                        S.dma("sp", lambda h: h.dma_start(out=u_[:, 262:2310], in_=xbcT[cc * 128:(cc + 1) * 128, 256:T]), w=[("u", i % 2)])
                        n = 2308
                        S.add("dve", lambda h: h.tensor_scalar(out=acc[:, 2:2 + n], in0=u_[:, 0:n], scalar1=cw[:, cc, 0:1], scalar2=cb[:, cc:cc + 1],
                                                               op0=ALU.mult, op1=ALU.add), r=[("u", i % 2), "cw", "cb"], w=["acc"])
                        for k in range(1, 5):
                            S.add("dve", lambda h, k=k: h.scalar_tensor_tensor(out=acc[:, 2:2 + n], in0=u_[:, k:k + n], scalar=cw[:, cc, k:k + 1],
                                                                               in1=acc[:, 2:2 + n], op0=ALU.mult, op1=ALU.add),
                                  r=[("u", i % 2), "cw", "acc"], w=["acc"])
                        silu2_ops(acc[:, 2:2 + n], ee[:, 2:2 + n], o_[:, 2:2 + n], "acc", "cee", ("ob", i % 2), in_scale=1.0)
                        def tok(tt):
                            return (2 + tt * 128) if tt < 2 else (262 + (tt - 2) * 128)
                        if cc < 6 or cc in (6, 7):
                            for t0 in range(0, NT, 8):
                                nt = min(8, NT - t0)
                                for a in range(nt):
                                    tt = t0 + a
                                    S.add("pe", lambda h, a=a, tt=tt: h.transpose(out=ptx[:, a, :], in_=o_[:, tok(tt):tok(tt) + 128], identity=ident_b[:]),
                                          r=[("ob", i % 2)], w=["ptx"])
                                if cc < 6:
                                    S.add("act", lambda h, t0=t0, nt=nt: h.activation(out=xs[:, t0:t0 + nt, cc * 128:(cc + 1) * 128], in_=ptx[:, 0:nt, :], func=AF.Copy),
                                          r=["ptx"], w=[("xs", cc, t0)])
                                else:
                                    g = cc - 6
                                    S.add("act", lambda h, t0=t0, nt=nt: h.activation(out=Btm[:, t0:t0 + nt, g * 128:(g + 1) * 128], in_=ptx[:, 0:nt, :], func=AF.Copy),
                                          r=["ptx"], w=[("Btm", g, t0)])
                        if cc >= 6:
                            dstT = BT if cc < 8 else CT
                            g = (cc - 6) % 2
                            S.add("act", lambda h: h.activation(out=dstT[:, g, 0:256], in_=o_[:, 2:258], func=AF.Copy), r=[("ob", i % 2)], w=[("BCT", cc, 0)])
                            S.add("act", lambda h: h.activation(out=dstT[:, g, 256:T], in_=o_[:, 262:2310], func=AF.Copy), r=[("ob", i % 2)], w=[("BCT", cc, 1)])
                    for cc in range(10):
                        conv_chunk(cc)
                    ztl = [Tl("ztl%d" % i, [128, 768]) for i in range(2)]
                    zth = [Tl("zth%d" % i, [128, 768]) for i in range(2)]
                    zso = [Tl("zso%d" % i, [128, 768], BF16) for i in range(2)]

                    def ztile(tt):
                        b = tt % 2
                        S.dma("sp", lambda h: h.dma_start(out=ztl[b][:], in_=proj[tt * 128:(tt + 1) * 128, 3328:4096]), w=[("ztl", b)])
                        S.add("act", lambda h: h.activation(out=zth[b][:], in_=ztl[b][:], func=AF.Tanh, scale=0.5), r=[("ztl", b)], w=[("zth", b)])
                        S.add("dve", lambda h: h.scalar_tensor_tensor(out=zso[b][:], in0=zth[b][:], scalar=1.0, in1=ztl[b][:], op0=ALU.add, op1=ALU.mult),
                              r=[("ztl", b), ("zth", b)], w=[("zso", b)])
                        S.dma("sp", lambda h: h.dma_start(out=zsd[tt * 128:(tt + 1) * 128, :], in_=zso[b][:]), r=[("zso", b)], w=[("zsd", tt)])
                    for tt in range(2 if last else 0, NT):
                        ztile(tt)
                    dtr = Tl("dtr", [128, NT, 24])
                    dtb = Tl("dtb", [128, 24])
                    alg = Tl("alg", [128, 24])
                    dtv = Tl("dtv", [128, 2, NT, NH])
                    tmpd = Tl("tmpd", [128, 2, NT, NH])
                    psc = Pl("psc", [128, 512])
                    pst = Pl("pst", [128, 512])
                    S.dma("sp", lambda h: h.dma_start(out=dtr[:], in_=proj[:, 4096:4120].rearrange("(t p) c -> p t c", p=128)), w=["dtr"])
                    S.dma("sp", lambda h: h.dma_start(out=dtb[:, 0:12], in_=I["dt_bias_f"][l:l + 1, :].to_broadcast([128, 12])), w=["dtb"])
                    S.dma("sp", lambda h: h.dma_start(out=dtb[:, 12:24], in_=I["dt_bias_b"][l:l + 1, :].to_broadcast([128, 12])), w=["dtb"])
                    S.dma("sp", lambda h: h.dma_start(out=alg[:, 0:12], in_=I["a_log_f"][l:l + 1, :].to_broadcast([128, 12])), w=["alg"])
                    S.dma("sp", lambda h: h.dma_start(out=alg[:, 12:24], in_=I["a_log_b"][l:l + 1, :].to_broadcast([128, 12])), w=["alg"])
                    S.add("act", lambda h: h.activation(out=alg[:], in_=alg[:], func=AF.Exp), r=["alg"], w=["alg"])
                    for d_ in range(2):
                        S.add("dve", lambda h, d_=d_: h.tensor_tensor(out=dtv[:, d_], in0=dtr[:, :, d_ * 12:(d_ + 1) * 12],
                                                                    in1=dtb[:, d_ * 12:(d_ + 1) * 12].unsqueeze(1).to_broadcast([128, NT, NH]), op=ALU.add),
                              r=["dtr", "dtb"], w=["dtv"])
                    F2 = lambda t: t[:].rearrange("p a b c -> p (a b c)")
                    S.add("act", lambda h: h.activation(out=F2(dtv), in_=F2(dtv), func=AF.Exp), r=["dtv"], w=["dtv"])
                    S.add("act", lambda h: h.activation(out=F2(dtv), in_=F2(dtv), func=AF.Ln, bias=1.0), r=["dtv"], w=["dtv"])
                    S.add("dve", lambda h: h.tensor_scalar(out=F2(dtv), in0=F2(dtv), scalar1=1e-30, scalar2=None, op0=ALU.max), r=["dtv"], w=["dtv"])
                    S.add("act", lambda h: h.activation(out=F2(lndt), in_=F2(dtv), func=AF.Ln), r=["dtv"], w=["lndt"])
                    for d_ in range(2):
                        S.add("dve", lambda h, d_=d_: h.tensor_tensor(out=la[:, d_], in0=dtv[:, d_],
                                                                    in1=alg[:, d_ * 12:(d_ + 1) * 12].unsqueeze(1).to_broadcast([128, NT, NH]), op=ALU.mult),
                              r=["dtv", "alg"], w=["la"])
                    S.add("dve", lambda h: h.tensor_scalar(out=F2(la), in0=F2(la), scalar1=-1.0, scalar2=None, op0=ALU.mult), r=["la"], w=["la"])
                    NQ = NT * NH
                    S.add("pe", lambda h: h.matmul(psc[:, 0:NQ], lhsT=tri_f, rhs=la[:, 0].rearrange("p a b -> p (a b)"), start=True, stop=True), r=["la"], w=["psc"])
                    S.add("pe", lambda h: h.matmul(psc[:, NQ:2 * NQ], lhsT=tri_b, rhs=la[:, 1].rearrange("p a b -> p (a b)"), start=True, stop=True), r=["la"], w=["psc"])
                    S.add("pe", lambda h: h.matmul(pst[:, 0:2 * NQ], lhsT=ones_f, rhs=F2(la), start=True, stop=True), r=["la"], w=["pst"])
                    S.add("dve", lambda h: h.tensor_copy(out=F2(acs), in_=psc[:, 0:2 * NQ]), r=["psc"], w=["acs"])
                    S.add("dve", lambda h: h.tensor_copy(out=F2(tot), in_=pst[:, 0:2 * NQ]), r=["pst"], w=["tot"])
                    S.add("act", lambda h: h.activation(out=F2(ein), in_=F2(acs), func=AF.Exp), r=["acs"], w=["ein"])
                    S.add("act", lambda h: h.activation(out=F2(cdc), in_=F2(tot), func=AF.Exp), r=["tot"], w=["cdc"])
                    S.add("dve", lambda h: h.tensor_tensor(out=F2(lb), in0=F2(lndt), in1=F2(acs), op=ALU.subtract), r=["lndt", "acs"], w=["lb"])
                    S.add("dve", lambda h: h.tensor_tensor(out=F2(tmpd), in0=F2(lb), in1=F2(tot), op=ALU.add), r=["lb", "tot"], w=["tmpd"])
                    S.add("act", lambda h: h.activation(out=F2(wend), in_=F2(tmpd), func=AF.Exp), r=["tmpd"], w=["wend"])
                    S.phase_end("ssdprep")
                with ExitStack() as st:
                    Tl, Pl = mk_alloc(st)
                    SAll = [Tl("sSAll%d" % d_, [128, NT, 768], BF16) for d_ in range(2)]
                    S32 = [Tl("sS32%d" % d_, [128, 768]) for d_ in range(2)]
                    xsw2 = [[Tl("xsw%d_%d" % (d_, i), [128, 768], BF16) for i in range(2)] for d_ in range(2)]
                    psA = Pl("spsA", [128, 2, 512])
                    psY = Pl("spsY", [128, 1024])
                    psAB = [psA, psY[:, :].rearrange("p (g x) -> p g x", g=2)]
                    psABk = ["psA", "psY"]
                    orders = [list(range(NT)), [1, 0] + list(range(NT - 1, 1, -1))]

                    def bc12(t2d):
                        return t2d.unsqueeze(2).to_broadcast([128, NH, 64])

                    def v12(ap):
                        return ap.rearrange("p (h d) -> p h d", h=NH)

                    def state_step(d_, idx):
                        order = orders[d_]
                        c = order[idx]
                        if idx == 0:
                            S.add("pool", lambda h: h.memset(S32[d_][:], 0.0), w=[("S32", d_)])
                            S.add("pool", lambda h: h.memset(SAll[d_][:, c, :], 0.0), w=[("SAll", d_, c)])
                        if idx == NT - 1:
                            return
                        xw = xsw2[d_][idx % 2]
                        pA = psAB[d_]
                        pk = psABk[d_]
                        S.add("dve", lambda h: h.tensor_tensor(out=v12(xw[:]), in0=v12(xs[:, c, :]), in1=bc12(wend[:, d_, c, :]), op=ALU.mult),
                              w=[("xsw", d_, idx % 2)])
                        for g in range(2):
                            S.add("pe", lambda h, g=g: h.matmul(pA[:, g, 0:384], lhsT=Btm[:, c, g * 128:(g + 1) * 128], rhs=xw[:, g * 384:(g + 1) * 384],
                                                               start=True, stop=True), r=[("xsw", d_, idx % 2)], w=[pk])
                        S.add("dve", lambda h: h.tensor_tensor(out=v12(S32[d_][:]), in0=v12(S32[d_][:]), in1=bc12(cdc[:, d_, c, :]), op=ALU.mult),
                              r=[("S32", d_)], w=[("S32", d_)])
                        S.add("dve", lambda h: h.tensor_tensor(out=S32[d_][:].rearrange("p (g x) -> p g x", g=2), in0=S32[d_][:].rearrange("p (g x) -> p g x", g=2),
                                                               in1=pA[:, :, 0:384], op=ALU.add), r=[("S32", d_), pk], w=[("S32", d_)])
                        cn = order[idx + 1]
                        S.add("act", lambda h: h.activation(out=SAll[d_][:, cn, :], in_=S32[d_][:], func=AF.Copy), r=[("S32", d_)], w=[("SAll", d_, cn)])
                    for idx in range(NT):
                        for d_ in range(2):
                            state_step(d_, idx)

                    Rt = [Tl("Rt%d" % d_, [128, NH, 128]) for d_ in range(2)]
                    mneg4 = [Tl("mneg4_%d" % d_, [128, 4, 128]) for d_ in range(2)]
                    Et = [Tl("Et%d" % i, [128, 4, 128], BF16) for i in range(2)]
                    Wt = [Tl("Wt%d" % i, [128, 4, 128], BF16) for i in range(4)]
                    psE = [Pl("psE%d" % i, [128, 4, 128]) for i in range(2)]
                    psG = Pl("psG", [128, 4, 128])
                    psFB = psA
                    ptr = Pl("sptr", [128, 8, 128], BF16)
                    y1 = [Tl("y1_%d" % i, [128, 768]) for i in range(2)]
                    y2 = [Tl("y2_%d" % i, [128, 768]) for i in range(2)]
                    zt = [Tl("zt%d" % i, [128, 768], BF16) for i in range(2)]
                    xsd = [Tl("xsd%d" % i, [128, 768], BF16) for i in range(2)]
                    junk = Tl("sjunk", [128, 768], BF16)
                    yo = [Tl("yo%d" % i, [128, 768], BF16) for i in range(2)]
                    stg = [Tl("sstg%d" % i, [128, 6, 128], BF16) for i in range(2)]
                    st2 = Tl("st2", [128, 2])
                    for d_ in range(2):
                        S.add("dve", lambda h, d_=d_: h.tensor_copy(out=mneg4[d_][:], in_=mneg[d_].unsqueeze(1).to_broadcast([128, 4, 128])), w=["mneg4"])
                    tris = [tri_f, tri_b]
                    ecnt = [0]
                    wcnt = [0]

                    def chunk(c, ci):
                        cs_ = slice(c * 128, (c + 1) * 128)
                        b = ci % 2
                        S.dma("sp", lambda h: h.dma_start(out=zt[b][:], in_=zsd[c * 128:(c + 1) * 128, :]), w=[("zt", b)])
                        for g in range(2):
                            S.add("pe", lambda h, g=g: h.matmul(psG[:, g, :], lhsT=BT[:, g, cs_], rhs=CT[:, g, cs_], start=(g == 0), stop=(g == 1)), w=["psG"])
                        for d_ in range(2):
                            S.add("dve", lambda h, d_=d_: h.tensor_tensor(out=Rt[d_][:], in0=la[:, d_, c, :].unsqueeze(2).to_broadcast([128, NH, 128]),
                                                                        in1=tris[d_].unsqueeze(1).to_broadcast([128, NH, 128]), op=ALU.mult), w=[("Rt", d_)])
                        S.add("dve", lambda h: h.tensor_tensor(out=v12(xsd[b][:]), in0=v12(xs[:, c, :]), in1=bc12(dsk[:]), op=ALU.mult), w=[("xsd", b)])
                        for q in range(3):
                            wl = {}
                            for d_ in range(2):
                                e = ecnt[0]
                                ecnt[0] += 1
                                pe_ = psE[e % 2]
                                et_ = Et[e % 2]
                                S.add("pe", lambda h, d_=d_, q=q, pe_=pe_: h.matmul(pe_[:].rearrange("p a b -> p (a b)"), lhsT=ones_f,
                                                                                 rhs=Rt[d_][:, 4 * q:4 * q + 4, :].rearrange("p a b -> p (a b)"), start=True, stop=False),
                                      r=[("Rt", d_)], w=[("psE", e % 2)])
                                S.add("pe", lambda h, d_=d_, pe_=pe_: h.matmul(pe_[:].rearrange("p a b -> p (a b)"), lhsT=ident_f,
                                                                            rhs=mneg4[d_][:].rearrange("p a b -> p (a b)"), start=False, stop=True),
                                      r=["mneg4"], w=[("psE", e % 2)])
                                for a in range(4):
                                    hh = 4 * q + a
                                    S.add("act", lambda h, a=a, hh=hh, d_=d_, pe_=pe_, et_=et_: h.activation(out=et_[:, a, :], in_=pe_[:, a, :], func=AF.Exp,
                                                                                                      bias=lb[:, d_, c, hh:hh + 1]),
                                          r=[("psE", e % 2)], w=[("Et", e % 2)])
                                w_ = wcnt[0]
                                wcnt[0] += 1
                                wt_ = Wt[w_ % 4]
                                wl[d_] = (w_, wt_)
                                if q == 1:
                                    for half in range(2):
                                        S.add("dve", lambda h, half=half, et_=et_, wt_=wt_: h.tensor_tensor(
                                            out=wt_[:, 2 * half:2 * half + 2, :], in0=et_[:, 2 * half:2 * half + 2, :],
                                            in1=psG[:, half, :].unsqueeze(1).to_broadcast([128, 2, 128]), op=ALU.mult),
                                            r=[("Et", e % 2), "psG"], w=[("Wt", w_ % 4)])
                                else:
                                    g = 0 if q == 0 else 1
                                    S.add("dve", lambda h, g=g, et_=et_, wt_=wt_: h.tensor_tensor(
                                        out=wt_[:], in0=et_[:], in1=psG[:, g, :].unsqueeze(1).to_broadcast([128, 4, 128]), op=ALU.mult),
                                        r=[("Et", e % 2), "psG"], w=[("Wt", w_ % 4)])
                            for a in range(4):
                                hh = 4 * q + a
                                for d_ in range(2):
                                    w_, wt_ = wl[d_]
                                    S.add("pe", lambda h, hh=hh, a=a, d_=d_, wt_=wt_: h.matmul(psY[:, hh * 64:(hh + 1) * 64], lhsT=wt_[:, a, :], rhs=xs[:, c, hh * 64:(hh + 1) * 64],
                                                                                          start=(d_ == 0 and hh in (0, 8)), stop=False),
                                          r=[("Wt", w_ % 4)], w=["psY"])
                        S.add("pe", lambda h: h.matmul(psY[:, 0:512], lhsT=ident_b[:], rhs=xsd[b][:, 0:512], start=False, stop=True), r=[("xsd", b)], w=["psY"])
                        S.add("pe", lambda h: h.matmul(psY[:, 512:768], lhsT=ident_b[:], rhs=xsd[b][:, 512:768], start=False, stop=True), r=[("xsd", b)], w=["psY"])
                        for d_ in range(2):
                            for g in range(2):
                                S.add("pe", lambda h, d_=d_, g=g: h.matmul(psFB[:, g, 0:384], lhsT=CT[:, g, cs_], rhs=SAll[d_][:, c, g * 384:(g + 1) * 384],
                                                                         start=True, stop=True), r=[("SAll", d_, c)], w=["psA"])
                            yd = y1[b] if d_ == 0 else y2[b]
                            S.add("dve", lambda h, d_=d_, yd=yd: h.tensor_tensor(
                                out=yd[:].rearrange("p (g h d) -> p g h d", g=2, h=6),
                                in0=psFB[:, :, 0:384].rearrange("p g (h d) -> p g h d", h=6),
                                in1=ein[:, d_, c, :].rearrange("p (g h) -> p g h", g=2).unsqueeze(3).to_broadcast([128, 2, 6, 64]), op=ALU.mult),
                                r=["psA"], w=[("y", d_, b)])
                        S.add("dve", lambda h: h.tensor_tensor(out=y1[b][:], in0=y1[b][:], in1=y2[b][:], op=ALU.add), r=[("y", 0, b), ("y", 1, b)], w=[("y", 0, b)])
                        S.add("dve", lambda h: h.tensor_tensor(out=y1[b][:], in0=y1[b][:], in1=psY[:, 0:768], op=ALU.add), r=[("y", 0, b), "psY"], w=[("y", 0, b)])
                    def chunk_epi(c, ci):
                        cs_ = slice(c * 128, (c + 1) * 128)
                        b = ci % 2
                        S.add("dve", lambda h: h.scalar_tensor_tensor(out=y1[b][:], in0=y1[b][:], scalar=0.5, in1=zt[b][:], op0=ALU.mult, op1=ALU.mult),
                              r=[("y", 0, b), ("zt", b)], w=[("y", 0, b)])
                        S.add("act", lambda h: h.activation(out=junk[:], in_=y1[b][:], func=AF.Square, accum_out=st2[:, 0:1]), r=[("y", 0, b)], w=["sjunk", "ssq"])
                        rstd_ops(st2[:, 0:1], st2[:, 1:2], 1.0 / 768, ["ssq"], ["rstd"])
                        S.add("dve", lambda h: h.scalar_tensor_tensor(out=yo[b][:], in0=y1[b][:], scalar=st2[:, 1:2], in1=gssd[:], op0=ALU.mult, op1=ALU.mult),
                              r=[("y", 0, b), "rstd"], w=[("yo", b)])
                        for a in range(6):
                            S.add("pe", lambda h, a=a: h.transpose(out=ptr[:, a, :], in_=yo[b][:, a * 128:(a + 1) * 128], identity=ident_b[:]), r=[("yo", b)], w=["ptr"])
                        sg = stg[b]
                        S.add("act", lambda h: h.activation(out=sg[:], in_=ptr[:, 0:6, :], func=AF.Copy), r=["ptr"], w=[("stg", b)])
                        S.dma("pool", lambda h: h.dma_start(out=mixT[1280:2048, cs_].rearrange("(a d) t -> d a t", a=6), in_=sg[:]),
                              r=[("stg", b)], w=[("mixTs", c)])
                    cl = list(range(2 if last else 0, NT))
                    chunk(cl[0], 0)
                    for ci, c in enumerate(cl):
                        if ci + 1 < len(cl):
                            chunk(cl[ci + 1], ci + 1)
                        chunk_epi(c, ci)
                    S.phase_end("ssdmain")

        def phase_out(l):
            last = (l == nlayers - 1)
            with ExitStack() as st:
                Tl, Pl = mk_alloc(st)
                wo = Tl("wo", [128, 16, D], BF16)
                wv = I["w_out"][l].rearrange("(k p) n -> p k n", p=128)
                for k4 in range(4):
                    for nb in range(2):
                        S.dma("pool", lambda h, k4=k4, nb=nb: h.dma_start(out=wo[:, 4 * k4:4 * k4 + 4, nb * 1024:(nb + 1) * 1024],
                                                                        in_=wv[:, 4 * k4:4 * k4 + 4, nb * 1024:(nb + 1) * 1024]), w=[("wo", k4, nb)])
                G1 = Tl("G1", [128, D])
                gm = Tl("ogm", [128, D])
                sh = Tl("osh", [128, D])
                gp = Tl("ogp", [128, D])
                mT = [Tl("mT%d" % i, [128, 16, 128], BF16) for i in range(3)]
                xt = [Tl("oxt%d" % i, [128, D]) for i in range(3)]
                x1 = [Tl("ox1%d" % i, [128, D]) for i in range(3)]
                tmp = Tl("otmp", [128, D])
                hb = [Tl("ohb%d" % i, [128, D], BF16) for i in range(2)]
                junk = Tl("ojunk", [128, D], BF16)
                ssq = Tl("ossq", [128, 4])
                sq4 = Tl("osq4", [128, 24])
                stg = [Tl("ostg%d" % i, [128, 16, 128], BF16) for i in range(2)]
                psM = [Pl("psM%d" % i, [128, 512]) for i in range(4)]
                pt = [Pl("opt%d" % i, [128, 16, 128], BF16) for i in range(2)]

                def load_mods(r):
                    load_gate_mod(l, G1, gp, "o", "post_mix_g", 4096, r)
                    load_norm_mod(l, gm, sh, gp, "o", "pre_ffn_g", 4 * 2048, 3 * 2048, r)

                def stage1(tt, i):
                    o8 = 8 * (i % 3)
                    S.dma("sp", lambda h: h.dma_start(out=mT[i % 3][:], in_=mixT[:, tt * 128:(tt + 1) * 128].rearrange("(k p) t -> p k t", p=128)),
                          w=[("mT", i % 3)])
                    S.dma("sp", lambda h: h.dma_start(out=xt[i % 3][:], in_=xsrc(l, tt)), w=[("xt", i % 3)])
                    x1_ = x1[i % 3]
                    for cb in range(4):
                        for k in range(16):
                            S.add("pe", lambda h, cb=cb, k=k: h.matmul(psM[cb][:], lhsT=mT[i % 3][:, k, :], rhs=wo[:, k, cb * 512:(cb + 1) * 512],
                                                                      start=(k == 0), stop=(k == 15)),
                                  r=[("mT", i % 3), ("wo", k // 4, cb // 2)], w=[("psM", cb)])
                        S.add("act", lambda h, cb=cb: h.activation(out=junk[:, cb * 512:(cb + 1) * 512], in_=psM[cb][:], func=AF.Square,
                                                                   accum_out=sq4[:, o8 + cb:o8 + cb + 1]), r=[("psM", cb)], w=["ojunk", ("sq4", i % 3), ("sqd", cb)])
                        S.add("dve", lambda h, cb=cb: h.tensor_copy(out=x1_[:, cb * 512:(cb + 1) * 512], in_=psM[cb][:]), r=[("psM", cb), ("sqd", cb)], w=[("x1", i % 3)])

                def stage2(tt, i):
                    o8 = 8 * (i % 3)
                    x1_ = x1[i % 3]
                    S.add("dve", lambda h: h.tensor_reduce(out=sq4[:, o8 + 4:o8 + 5], in_=sq4[:, o8:o8 + 4], axis=AX.X, op=ALU.add), r=[("sq4", i % 3)], w=[("mss", i % 3)])
                    rstd_ops(sq4[:, o8 + 4:o8 + 5], sq4[:, o8 + 5:o8 + 6], 1.0 / D, [("mss", i % 3)], [("mrstd", i % 3)])
                    S.add("dve", lambda h: h.scalar_tensor_tensor(out=x1_[:], in0=x1_[:], scalar=sq4[:, o8 + 5:o8 + 6], in1=G1[:], op0=ALU.mult, op1=ALU.mult),
                          r=[("x1", i % 3), ("mrstd", i % 3), "oG"], w=[("x1", i % 3)])
                    S.add("dve", lambda h: h.tensor_tensor(out=x1_[:], in0=x1_[:], in1=xt[i % 3][:], op=ALU.add),
                          r=[("x1", i % 3), ("xt", i % 3)], w=[("x1", i % 3)])
                    S.dma("pool", lambda h: h.dma_start(out=x1s[tt * 128:(tt + 1) * 128, :], in_=x1_[:]), r=[("x1", i % 3)], w=[("x1s", tt)])

                    def dst(ptile, pkey, wkeys):
                        sg = stg[i % 2]
                        S.add("act", lambda h: h.activation(out=sg[:], in_=ptile[:], func=AF.Copy), r=[pkey], w=[("ostg", i % 2)])
                        S.dma("pool", lambda h: h.dma_start(out=h2T[:, tt * 128:(tt + 1) * 128].rearrange("(k p) t -> p k t", p=128), in_=sg[:]),
                              r=[("ostg", i % 2)], w=wkeys)
                    return norm_transpose_tile(l, tt, x1_[:], ("x1", i % 3), gm, sh, "o", tmp, hb[i % 2], junk, ssq, pt[i % 2], dst, [("h2T", tt)], i)

                def run_tiles(tts, i0):
                    pend = None
                    n_ = len(tts)
                    for j in range(min(2, n_)):
                        stage1(tts[j], i0 + j)
                    for j, tt in enumerate(tts):
                        if j + 2 < n_:
                            stage1(tts[j + 2], i0 + j + 2)
                        th = stage2(tt, i0 + j)
                        if pend is not None:
                            pend()
                        pend = th
                    pend()
                if not last:
                    load_mods(1)
                    run_tiles([0, 1], 0)
                load_mods(0)
                run_tiles(list(range(2, NT)), 2)
                S.phase_end("out")

        def phase_ffn(l):
            last = (l == nlayers - 1)
            if last:
                halves = [(256, 1024), (1280, 1024)]
            else:
                halves = [(0, 1152), (1152, 1152)]
            wg_v = I["w_gate"][l].rearrange("(k p) n -> p k n", p=128)
            wu_v = I["w_up"][l].rearrange("(k p) n -> p k n", p=128)
            wd_v = I["w_down"][l].rearrange("(k p) n -> p k n", p=128)
            NJ = DFF // 128
            for (t0, nt) in halves:
                with ExitStack() as st_o:
                    To, Po = mk_alloc(st_o)
                    aT = To("aT", [128, NJ, nt], BF16)
                    with ExitStack() as st:
                        Tl, Pl = mk_alloc(st)
                        hh_ = Tl("h2h", [128, 16, nt], BF16)
                        for k4 in range(4):
                            S.dma("sp", lambda h, k4=k4: h.dma_start(out=hh_[:, 4 * k4:4 * k4 + 4, :],
                                                                     in_=h2T[:, t0:t0 + nt].rearrange("(k p) t -> p k t", p=128)[:, 4 * k4:4 * k4 + 4, :]),
                                  w=[("h2h", k4)])
                        wg = [Tl("wg%d" % i, [128, 16, 256], BF16) for i in range(2)]
                        wu = [Tl("wu%d" % i, [128, 16, 256], BF16) for i in range(2)]
                        psg = [Pl("psg%d" % i, [128, 512]) for i in range(3)]
                        psu = [Pl("psu%d" % i, [128, 512]) for i in range(3)]
                        ee = [Tl("fe%d" % i, [128, 512]) for i in range(2)]
                        tg = [Tl("ftg%d" % i, [128, 512]) for i in range(2)]
                        tbs = [(a, min(512, nt - a)) for a in range(0, nt, 512)]
                        cnt = [0]

                        def wblock(jb):
                            b = jb % 2
                            for k4 in range(4):
                                S.dma("pool", lambda h, k4=k4: h.dma_start(out=wg[b][:, 4 * k4:4 * k4 + 4, :], in_=wg_v[:, 4 * k4:4 * k4 + 4, jb * 256:(jb + 1) * 256]),
                                      w=[("wg", b, k4)])
                                S.dma("pool", lambda h, k4=k4: h.dma_start(out=wu[b][:, 4 * k4:4 * k4 + 4, :], in_=wu_v[:, 4 * k4:4 * k4 + 4, jb * 256:(jb + 1) * 256]),
                                      w=[("wu", b, k4)])
                            for jj in range(2):
                                j = jb * 2 + jj
                                for (a, n) in tbs:
                                    i = cnt[0]
                                    cnt[0] += 1
                                    pg = psg[i % 3]
                                    pu = psu[i % 3]
                                    for k in range(16):
                                        S.add("pe", lambda h, k=k, pg=pg, a=a, n=n, jj=jj: h.matmul(pg[:, 0:n], lhsT=wg[b][:, k, jj * 128:(jj + 1) * 128],
                                                                                             rhs=hh_[:, k, a:a + n], start=(k == 0), stop=(k == 15)),
                                              r=[("wg", b, k // 4), ("h2h", k // 4)], w=[("psg", i % 3)])
                                    for k in range(16):
                                        S.add("pe", lambda h, k=k, pu=pu, a=a, n=n, jj=jj: h.matmul(pu[:, 0:n], lhsT=wu[b][:, k, jj * 128:(jj + 1) * 128],
                                                                                             rhs=hh_[:, k, a:a + n], start=(k == 0), stop=(k == 15)),
                                              r=[("wu", b, k // 4), ("h2h", k // 4)], w=[("psu", i % 3)])
                                    e_ = ee[i % 2]
                                    t_ = tg[i % 2]
                                    S.add("act", lambda h, pg=pg, e_=e_, n=n: h.activation(out=e_[:, 0:n], in_=pg[:, 0:n], func=AF.Tanh, scale=0.5),
                                          r=[("psg", i % 3)], w=[("fe", i % 2)])
                                    S.add("dve", lambda h, e_=e_, t_=t_, pg=pg, n=n: h.scalar_tensor_tensor(out=t_[:, 0:n], in0=e_[:, 0:n], scalar=1.0, in1=pg[:, 0:n],
                                                                                                     op0=ALU.add, op1=ALU.mult),
                                          r=[("fe", i % 2), ("psg", i % 3)], w=[("ftg", i % 2)])
                                    S.add("dve", lambda h, t_=t_, pu=pu, n=n, a=a, j=j: h.scalar_tensor_tensor(out=aT[:, j, a:a + n], in0=t_[:, 0:n], scalar=0.5, in1=pu[:, 0:n],
                                                                                                        op0=ALU.mult, op1=ALU.mult),
                                          r=[("ftg", i % 2), ("psu", i % 3)], w=[("aT", j, a)])
                        for jb in range(NJ // 2):
                            wblock(jb)
                        S.phase_end("ffnA")
                    with ExitStack() as st:
                        Tl, Pl = mk_alloc(st)
                        wd = [Tl("wd%d" % i, [128, NJ, 256], BF16) for i in range(2)]
                        psd = [Pl("psd%d" % i, [128, 512]) for i in range(4)]
                        stg = [Tl("fstg%d" % i, [128, 256]) for i in range(4)]
                        cnt = [0]

                        def dblock(cb):
                            b = cb % 2
                            for k4 in range(4):
                                S.dma("pool", lambda h, k4=k4: h.dma_start(out=wd[b][:, 11 * k4:11 * k4 + 11, :], in_=wd_v[:, 11 * k4:11 * k4 + 11, cb * 256:(cb + 1) * 256]),
                                      w=[("wd", b, k4)])
                            for a in range(0, nt, 128):
                                i = cnt[0]
                                cnt[0] += 1
                                p_ = psd[i % 4]
                                for k in range(NJ):
                                    S.add("pe", lambda h, k=k, p_=p_, a=a: h.matmul(p_[:, 0:256], lhsT=aT[:, k, a:a + 128], rhs=wd[b][:, k, :], start=(k == 0), stop=(k == NJ - 1)),
                                          r=[("wd", b, k // 11)], w=[("psd", i % 4)])
                                s_ = stg[i % 4]
                                if i % 2 == 0:
                                    S.add("act", lambda h, p_=p_, s_=s_: h.activation(out=s_[:], in_=p_[:, 0:256], func=AF.Copy), r=[("psd", i % 4)], w=[("fstg", i % 4)])
                                else:
                                    S.add("dve", lambda h, p_=p_, s_=s_: h.tensor_copy(out=s_[:], in_=p_[:, 0:256]), r=[("psd", i % 4)], w=[("fstg", i % 4)])
                                S.dma("sp", lambda h, s_=s_, a=a: h.dma_start(out=fsc[t0 + a:t0 + a + 128, cb * 256:(cb + 1) * 256], in_=s_[:]),
                                      r=[("fstg", i % 4)], w=[("fsc", i)])
                        for cb in range(8):
                            dblock(cb)
                        S.phase_end("ffnB")

        def phase_fin(l):
            last = (l == nlayers - 1)
            with ExitStack() as st:
                Tl, Pl = mk_alloc(st)
                G2 = Tl("G2", [128, D])
                gp = Tl("fgp", [128, D])
                xa = [Tl("fxa%d" % i, [128, D]) for i in range(3)]
                fa = [Tl("ffa%d" % i, [128, D]) for i in range(3)]
                xo = [Tl("fxo%d" % i, [128, D]) for i in range(3)]
                junk = Tl("fjunk", [128, D], BF16)
                ssq = Tl("fssq", [128, 2])

                def tile(tt, i):
                    S.dma("sp", lambda h: h.dma_start(out=xa[i % 3][:], in_=x1s[tt * 128:(tt + 1) * 128, :]), w=[("xa", i % 3)])
                    S.dma("sp", lambda h: h.dma_start(out=fa[i % 3][:], in_=fsc[tt * 128:(tt + 1) * 128, :]), w=[("fa", i % 3)])
                    S.add("act", lambda h: h.activation(out=junk[:], in_=fa[i % 3][:], func=AF.Square, accum_out=ssq[:, 0:1]), r=[("fa", i % 3)], w=["fjunk", "fss"])
                    rstd_ops(ssq[:, 0:1], ssq[:, 1:2], 1.0 / D, ["fss"], ["frstd"])
                    S.add("dve", lambda h: h.scalar_tensor_tensor(out=xo[i % 3][:], in0=fa[i % 3][:], scalar=ssq[:, 1:2], in1=G2[:], op0=ALU.mult, op1=ALU.mult),
                          r=[("fa", i % 3), "frstd", "fG"], w=[("xo", i % 3)])
                    S.add("dve", lambda h: h.tensor_tensor(out=xo[i % 3][:], in0=xo[i % 3][:], in1=xa[i % 3][:], op=ALU.add), r=[("xo", i % 3), ("xa", i % 3)], w=[("xo", i % 3)])
                    if last:
                        dst = out[(tt - 2) * 128:(tt - 1) * 128, :]
                    else:
                        dst = xnext[tt * 128:(tt + 1) * 128, :]
                    S.dma("pool", lambda h: h.dma_start(out=dst, in_=xo[i % 3][:]), r=[("xo", i % 3)], w=[("xout", tt)])
                i = 0
                if not last:
                    load_gate_mod(l, G2, gp, "f", "post_ffn_g", 5 * 2048, 1)
                    for tt in range(2):
                        tile(tt, i)
                        i += 1
                load_gate_mod(l, G2, gp, "f", "post_ffn_g", 5 * 2048, 0)
                for tt in range(2, NT):
                    tile(tt, i)
                    i += 1
                S.phase_end("fin")

        PH = {}
        PH["out"] = phase_out
        PH["ffn"] = phase_ffn
        PH["fin"] = phase_fin
        PH["ssd"] = phase_ssd
        PH["ret"] = phase_ret
        PH["att"] = phase_att

        PH["in"] = phase_in
        try:
            if phases is None or "mod" in phases:
                phase_mod(0)
            check_stop("mod")
            for l in (layers if layers is not None else range(nlayers)):
                for nm in ("in", "att", "ret", "ssd", "out", "ffn", "fin"):
                    if nm in PH and (phases is None or nm in phases):
                        PH[nm](l)
                        check_stop("%s%d" % (nm, l))
        except Stop:
            pass
        S.phase_end("final")
        stats = S.emit()
    return nc, stats, list(I.keys())


def make_in_maps(inputs):
    consts = make_consts()
    rope = make_rope()
    maps = []
    shared = {n: np.ascontiguousarray(inputs[n], dtype=np.float32) for n, _ in SMALL + BIG}
    for b in range(8):
        m = dict(shared)
        m["x"] = np.ascontiguousarray(inputs["x"][b])
        m["ctx"] = np.ascontiguousarray(inputs["ctx"][b])
        m["cvec"] = np.ascontiguousarray(np.stack([inputs["c"][b], inputs["c_ctx"]]))
        m["consts"] = consts
        m["rope"] = rope
        maps.append(m)
    return maps


def kernel(**inputs):
    nc, _, _ = build(nlayers=2, debug=False)
    maps = make_in_maps(inputs)
    res = run_bass_kernel_spmd(nc, maps, core_ids=list(range(8)))
    return np.stack([np.asarray(r["out"], dtype=np.float32) for r in res.results], axis=0)
```

```python
import math
import numpy as np
from contextlib import ExitStack
import concourse.bass as bass
import concourse.mybir as mybir
from concourse.bass_utils import run_bass_kernel_spmd

F32 = mybir.dt.float32
BF16 = mybir.dt.bfloat16
AF = mybir.ActivationFunctionType
ALU = mybir.AluOpType
AX = mybir.AxisListType

D = 2048
T = 2304
NT = 18
DIN = 5400
DFF = 5632
EPS = 1e-6
NCONST = 10 * 128 + 2
PW = 4120


class _Op:
    __slots__ = ("eng", "fn", "deps", "ch", "pos", "is_dma", "vc", "waits", "signal", "rank")


class Sched:
    ENGS = ("pe", "act", "dve", "pool", "sp")

    def __init__(self, nc, stack, n_dma_sems=12):
        self.nc = nc
        self.h = {"pe": nc.tensor, "act": nc.scalar, "dve": nc.vector, "pool": nc.gpsimd, "sp": nc.sync}
        self.ops = []
        self.n_emitted = 0
        self.eng_pos = {e: 0 for e in self.ENGS}
        self.last_w = {}
        self.readers = {}
        self.esem = {e: stack.enter_context(nc.semaphore("s_" + e)) for e in self.ENGS}
        self.dsems = {}
        self.dcount = {}
        self.drr = {}
        for q in ("sp", "pool"):
            self.dsems[q] = [stack.enter_context(nc.semaphore("d_%s%d" % (q, i))) for i in range(n_dma_sems)]
            self.drr[q] = 0
            for i in range(n_dma_sems):
                self.dcount[(q, i)] = 0
        self.by_chpos = {}
        self.last_on_ch = {}
        self.clock = {e: {} for e in self.ENGS}
        self.rk = {e: 0 for e in self.ENGS}
        self.nw = 0

    def _deps(self, r, w):
        deps = []
        for k in r:
            o = self.last_w.get(k)
            if o is not None:
                deps.append(o)
        for k in w:
            o = self.last_w.get(k)
            if o is not None:
                deps.append(o)
            deps.extend(self.readers.get(k, ()))
        return deps

    def _commit(self, op, r, w):
        for k in r:
            self.readers.setdefault(k, []).append(op)
        for k in w:
            self.last_w[k] = op
            self.readers[k] = []
        self.ops.append(op)
        self.by_chpos[(op.ch, op.pos)] = op
        self.last_on_ch[op.ch] = op

    def add(self, eng, fn, r=(), w=()):
        op = _Op()
        op.eng = eng
        op.fn = fn
        op.is_dma = False
        op.deps = self._deps(r, w)
        self.eng_pos[eng] += 1
        op.ch = eng
        op.pos = self.eng_pos[eng]
        op.signal = False
        self._commit(op, r, w)
        return op

    def dma(self, q, fn, r=(), w=()):
        op = _Op()
        op.eng = q
        op.fn = fn
        op.is_dma = True
        op.deps = self._deps(r, w)
        i = self.drr[q]
        self.drr[q] = (i + 1) % len(self.dsems[q])
        self.dcount[(q, i)] += 1
        op.ch = ("d", q, i)
        op.pos = self.dcount[(q, i)]
        op.signal = True
        self._commit(op, r, w)
        return op

    def barrier(self):
        lasts = [o for o in self.last_on_ch.values() if o.fn is not None or o.is_dma]
        for e in self.ENGS:
            op = _Op()
            op.eng = e
            op.fn = None
            op.is_dma = False
            op.deps = list(lasts)
            self.eng_pos[e] += 1
            op.ch = e
            op.pos = self.eng_pos[e]
            op.signal = False
            self.ops.append(op)
            self.by_chpos[(op.ch, op.pos)] = op
        self.last_w = {}
        self.readers = {}

    def emit(self):
        ops = self.ops[self.n_emitted:]
        clock = self.clock
        for op in ops:
            E = op.eng
            ck = clock[E]
            need = {}
            for d in op.deps:
                if (not d.is_dma) and d.eng == "pe" and E == "pe" and (not op.is_dma) and op.fn is not None:
                    continue
                if ck.get(d.ch, 0) < d.pos and need.get(d.ch, 0) < d.pos:
                    need[d.ch] = d.pos
            if op.is_dma and op.pos > 1:
                if ck.get(op.ch, 0) < op.pos - 1 and need.get(op.ch, 0) < op.pos - 1:
                    need[op.ch] = op.pos - 1
            op.waits = []
            for ch, pos in need.items():
                if ck.get(ch, 0) >= pos:
                    continue
                p = self.by_chpos[(ch, pos)]
                p.signal = True
                op.waits.append(p)
                for c, v in p.vc.items():
                    if ck.get(c, 0) < v:
                        ck[c] = v
            vc = dict(ck)
            vc[op.ch] = op.pos
            op.vc = vc
        for op in ops:
            if op.is_dma:
                op.rank = 16 * op.pos
            elif op.signal:
                self.rk[op.eng] += 1
                op.rank = self.rk[op.eng]
        for op in ops:
            h = self.h[op.eng]
            for p in op.waits:
                if p.is_dma:
                    sem = self.dsems[p.ch[1]][p.ch[2]]
                else:
                    sem = self.esem[p.eng]
                h.wait_ge(sem, p.rank)
                self.nw += 1
            if op.fn is None:
                continue
            inst = op.fn(h)
            if op.is_dma:
                inst.then_inc(self.dsems[op.ch[1]][op.ch[2]], 16)
            elif op.signal:
                inst.then_inc(self.esem[op.eng], 1)
            op.fn = None
        self.n_emitted = len(self.ops)
        return dict(n_ops=len(self.ops), n_waits=self.nw, ranks=dict(self.rk))

    def phase_end(self, name=""):
        npe = sum(1 for o in self.ops if o.eng == "pe" and not o.is_dma and (o.fn is not None))
        self.phase_log = getattr(self, "phase_log", [])
        self.pe_total = getattr(self, "pe_total", 0) + npe
        self.phase_log.append((name, self.pe_total))
        self.barrier()
        r = self.emit()
        self.ops = []
        self.n_emitted = 0
        return r


def make_consts():
    i = np.arange(128)
    J, I = np.meshgrid(i, i, indexing="ij")
    c = np.zeros((128, NCONST), np.float32)
    c[:, 0:128] = (J == I)
    c[:, 128:256] = (J <= I)
    c[:, 256:384] = (J >= I)
    c[:, 384:512] = 1.0
    c[:, 512:640] = np.where(I >= J, 0.0, -30000.0)
    c[:, 640:768] = np.where(I <= J, 0.0, -30000.0)
    c[:, 768:896] = np.maximum(I - J, 0)
    c[:, 896:1024] = np.maximum(J - I, 0)
    c[:, 1024:1152] = I + 1
    c[:, 1152:1280] = 128 - I
    c[:, 1280] = 127 - i
    c[:, 1281] = i
    return c


def make_rope():
    rows = 2048 // 64
    row = np.repeat(np.arange(rows, dtype=np.float32), 64)
    col = np.tile(np.arange(64, dtype=np.float32), rows)
    n_freq = 32
    inv = (np.float32(10000.0) ** (-np.arange(n_freq, dtype=np.float32) / n_freq)).astype(np.float32)
    ang = np.concatenate([row[:, None] * inv, col[:, None] * inv], axis=-1).astype(np.float32)
    return np.concatenate([np.cos(ang), np.sin(ang)], axis=-1).astype(np.float32)


SMALL = [("b_mod", [2, 12288]), ("pre_mix_g", [2, 2048]), ("post_mix_g", [2, 2048]), ("pre_ffn_g", [2, 2048]),
         ("post_ffn_g", [2, 2048]), ("q_norm_g", [2, 128]), ("k_norm_g", [2, 128]), ("ret_decay_f", [2, 4]),
         ("ret_decay_b", [2, 4]), ("conv_w", [2, 5, 1280]), ("conv_b", [2, 1280]), ("dt_bias_f", [2, 12]),
         ("dt_bias_b", [2, 12]), ("a_log_f", [2, 12]), ("a_log_b", [2, 12]), ("d_skip", [2, 12]),
         ("ssd_norm_g", [2, 768])]
BIG = [("w_mod", [2, 2048, 12288]), ("w_in", [2, 2048, DIN]), ("w_out", [2, 2048, 2048]),
       ("w_gate", [2, 2048, DFF]), ("w_up", [2, 2048, DFF]), ("w_down", [2, DFF, 2048])]


def build(nlayers=2, debug=False, stop=None, phases=None, feed=(), layers=None):
    nc = bass.Bass("TRN2", target_bir_lowering=False)
    SHAPES = dict(SMALL + BIG)
    SHAPES.update({"x": [2048, 2048], "ctx": [256, 2048], "cvec": [2, 2048], "consts": [128, NCONST], "rope": [2048, 128]})

    class LazyIn(dict):
        def __missing__(self, name):
            ap = nc.dram_tensor(name, SHAPES[name], F32, kind="ExternalInput").ap()
            self[name] = ap
            return ap

    I = LazyIn()
    if not debug:
        for n in ["x", "ctx", "cvec", "consts", "rope"] + [n for n, _ in SMALL + BIG]:
            I[n]
    out = nc.dram_tensor("out", [2048, 2048], F32, kind="ExternalOutput").ap()
    skind = "ExternalOutput" if debug else "Internal"

    def scr(name, shape, dt=F32):
        k = "ExternalInput" if name in feed else skind
        return nc.dram_tensor(name, shape, dt, kind=k).ap()

    modv = scr("modv", [2, 2, 12288])
    proj = scr("proj", [T, PW])
    xbcT = scr("xbcT", [1280, T])
    mixT = scr("mixT", [2048, T], BF16)
    x1s = scr("x1s", [T, D])
    h2T = scr("h2T", [2048, T], BF16)
    fsc = scr("fsc", [T, D])
    xnext = scr("xnext", [T, D])
    zsd = scr("zsd", [T, 768], BF16)

    class Stop(Exception):
        pass

    with ExitStack() as gst:
        S = Sched(nc, gst)

        uid = [0]

        def mk_alloc(st):
            def Tl(name, shape, dt=F32):
                uid[0] += 1
                return st.enter_context(nc.sbuf_tensor("%s_%d" % (name, uid[0]), shape, dt))

            def Pl(name, shape, dt=F32):
                uid[0] += 1
                return st.enter_context(nc.psum_tensor("%s_%d" % (name, uid[0]), shape, dt))
            return Tl, Pl

        GT, GP = mk_alloc(gst)
        cst = GT("cst", [128, NCONST])
        ident_b = GT("ident_b", [128, 128], BF16)
        ones_b = GT("ones_b", [128, 128], BF16)
        S.dma("sp", lambda h: h.dma_start(out=cst[:], in_=I["consts"][:, :]), w=["cst"])
        S.add("dve", lambda h: h.tensor_copy(out=ident_b[:], in_=cst[:, 0:128]), r=["cst"], w=["ident_b"])
        S.add("dve", lambda h: h.tensor_copy(out=ones_b[:], in_=cst[:, 384:512]), r=["cst"], w=["ones_b"])
        ident_f = cst[:, 0:128]
        tri_f = cst[:, 128:256]
        tri_b = cst[:, 256:384]
        ones_f = cst[:, 384:512]
        mneg = [cst[:, 512:640], cst[:, 640:768]]
        D1 = cst[:, 768:896]
        D2 = cst[:, 896:1024]
        rowidx = [cst[:, 1024:1152], cst[:, 1152:1280]]
        colexp = [cst[:, 1280:1281], cst[:, 1281:1282]]
        S.phase_end("init")

        def check_stop(tag):
            if stop == tag:
                raise Stop()

        def rstd_ops(ssq_ap, out_ap, inv_n, rk, wk):
            S.add("act", lambda h: h.activation(out=out_ap, in_=ssq_ap, func=AF.Ln, scale=inv_n, bias=EPS), r=rk, w=wk)
            S.add("act", lambda h: h.activation(out=out_ap, in_=out_ap, func=AF.Exp, scale=-0.5), r=wk, w=wk)

        def xsrc(l, tt):
            if l == 0:
                if tt < 2:
                    return I["ctx"][tt * 128:(tt + 1) * 128, :]
                return I["x"][(tt - 2) * 128:(tt - 1) * 128, :]
            return xnext[tt * 128:(tt + 1) * 128, :]

        def bcast_row(ap_row, n):
            return ap_row.to_broadcast([128, n])

        def mod_task(l, Tl, Pl, blocks=range(24)):
            cT = Tl("cT", [128, 16, 2])
            ce = Tl("ce", [128, 16, 2])
            sT = Tl("sT", [128, 16, 2], BF16)
            for r_ in range(2):
                S.dma("sp", lambda h, r_=r_: h.dma_start(out=cT[:, :, r_], in_=I["cvec"][r_].rearrange("(k p) -> p k", p=128),
                                                         allow_slow_non_contiguous=True), w=["mcT"])
            S.add("act", lambda h: h.activation(out=ce[:], in_=cT[:], func=AF.Exp, scale=-1.0), r=["mcT"], w=["mce"])
            S.add("dve", lambda h: h.tensor_scalar(out=ce[:], in0=ce[:], scalar1=1.0, scalar2=None, op0=ALU.add), r=["mce"], w=["mce"])
            S.add("dve", lambda h: h.reciprocal(out=ce[:], in_=ce[:]), r=["mce"], w=["mce"])
            S.add("dve", lambda h: h.tensor_tensor(out=sT[:], in0=cT[:], in1=ce[:], op=ALU.mult), r=["mce", "mcT"], w=["msT"])
            wb = [Tl("wmb%d" % i, [128, 16, 512], BF16) for i in range(2)]
            pm = Pl("pm", [128, 512])
            bsb = [Tl("bsb%d" % i, [2, 512]) for i in range(2)]
            msb = [Tl("msb%d" % i, [2, 512]) for i in range(2)]
            wv = I["w_mod"][l].rearrange("(k p) n -> p k n", p=128)
            yield

            def block(nb):
                b = nb % 2
                S.dma("sp", lambda h: h.dma_start(out=bsb[b][:], in_=I["b_mod"][l:l + 1, nb * 512:(nb + 1) * 512].to_broadcast([2, 512])), w=[("mbsb", b)])
                for k4 in range(4):
                    S.dma("pool", lambda h, k4=k4: h.dma_start(out=wb[b][:, 4 * k4:4 * k4 + 4, :], in_=wv[:, 4 * k4:4 * k4 + 4, nb * 512:(nb + 1) * 512]),
                          w=[("wmb", b, k4)])
                for k in range(16):
                    S.add("pe", lambda h, k=k: h.matmul(pm[0:2, :], lhsT=sT[:, k, :], rhs=wb[b][:, k, :], start=(k == 0), stop=(k == 15)),
                          r=["msT", ("wmb", b, k // 4)], w=["mpm"])
                S.add("dve", lambda h: h.tensor_tensor(out=msb[b][:], in0=pm[0:2, :], in1=bsb[b][:], op=ALU.add), r=["mpm", ("mbsb", b)], w=[("mmsb", b)])
                S.dma("sp", lambda h: h.dma_start(out=modv[l, :, nb * 512:(nb + 1) * 512], in_=msb[b][:]), r=[("mmsb", b)], w=[("modv", l, nb)])
            for nb in blocks:
                block(nb)
                yield

        def phase_mod(l):
            with ExitStack() as st:
                Tl, Pl = mk_alloc(st)
                for _ in mod_task(l, Tl, Pl, blocks=range(8)):
                    pass
                S.phase_end("mod")

        def norm_mod_tiles(l, Tl, tagp, gname, sc_off, sh_off, r):
            gm = Tl(tagp + "gm", [128, D])
            sh = Tl(tagp + "sh", [128, D])
            gp = Tl(tagp + "gp", [128, D])
            return gm, sh, gp

        def load_norm_mod(l, gm, sh, gp, key, gname, sc_off, sh_off, r):
            S.dma("sp", lambda h: h.dma_start(out=gp[:], in_=bcast_row(I[gname][l:l + 1, :], D)), w=[key + "gp"])
            S.dma("sp", lambda h: h.dma_start(out=gm[:], in_=bcast_row(modv[l, r:r + 1, sc_off:sc_off + D], D)), w=[key + "gm"])
            S.dma("sp", lambda h: h.dma_start(out=sh[:], in_=bcast_row(modv[l, r:r + 1, sh_off:sh_off + D], D)), w=[key + "sh"])
            S.add("dve", lambda h: h.scalar_tensor_tensor(out=gm[:], in0=gm[:], scalar=1.0, in1=gp[:], op0=ALU.add, op1=ALU.mult),
                  r=[key + "gp", key + "gm"], w=[key + "gm"])

        def load_gate_mod(l, G, gp, key, gname, g_off, r):
            S.dma("sp", lambda h: h.dma_start(out=gp[:], in_=bcast_row(I[gname][l:l + 1, :], D)), w=[key + "gp"])
            S.dma("sp", lambda h: h.dma_start(out=G[:], in_=bcast_row(modv[l, r:r + 1, g_off:g_off + D], D)), w=[key + "G"])
            S.add("dve", lambda h: h.tensor_tensor(out=G[:], in0=G[:], in1=gp[:], op=ALU.mult), r=[key + "gp", key + "G"], w=[key + "G"])

        def norm_transpose_tile(l, tt, xt_ap, xkey, gm, sh, mkey, tmp, hb, junk, ssq, pt, dst_fn, wkeys, i):
            sq = ssq[:, 2 * (i % 2):2 * (i % 2) + 1]
            rs = ssq[:, 2 * (i % 2) + 1:2 * (i % 2) + 2]
            S.add("act", lambda h: h.activation(out=junk[:], in_=xt_ap, func=AF.Square, accum_out=sq),
                  r=[xkey], w=["junk", ("ssq", i % 2)])
            rstd_ops(sq, rs, 1.0 / D, [("ssq", i % 2)], [("rstd", i % 2)])
            S.add("dve", lambda h: h.scalar_tensor_tensor(out=tmp[:], in0=xt_ap, scalar=rs, in1=gm[:], op0=ALU.mult, op1=ALU.mult),
                  r=[xkey, ("rstd", i % 2), mkey + "gm"], w=["tmp"])
            S.add("dve", lambda h: h.tensor_tensor(out=hb[:], in0=tmp[:], in1=sh[:], op=ALU.add), r=["tmp", mkey + "sh"], w=[("hb", i % 2)])
            def later():
                for k in range(16):
                    S.add("pe", lambda h, k=k: h.transpose(out=pt[:, k, :], in_=hb[:, k * 128:(k + 1) * 128], identity=ident_b[:]),
                          r=[("hb", i % 2)], w=[("pt", i % 2)])
                dst_fn(pt, ("pt", i % 2), wkeys)
            return later

        def phase_in(l):
            with ExitStack() as st_o:
                To, Po = mk_alloc(st_o)
                hT = To("hT", [128, 16, T], BF16)
                with ExitStack() as st:
                    Tl, Pl = mk_alloc(st)
                    mods = {}
                    for r, nm in ((1, "c"), (0, "l")):
                        gm = Tl("gm" + nm, [128, D])
                        sh = Tl("sh" + nm, [128, D])
                        gp = Tl("gp" + nm, [128, D])
                        load_norm_mod(l, gm, sh, gp, nm, "pre_mix_g", 2048, 0, r)
                        mods[r] = (gm, sh, nm)
                    xt = [Tl("xt%d" % i, [128, D]) for i in range(3)]
                    tmp = Tl("tmp", [128, D])
                    hb = [Tl("hb%d" % i, [128, D], BF16) for i in range(2)]
                    junk = Tl("junk", [128, D], BF16)
                    ssq = Tl("ssq", [128, 4])
                    pt = [Pl("pt%d" % i, [128, 16, 128], BF16) for i in range(2)]
                    pend = None
                    for tt in range(NT):
                        i = tt
                        gm, sh, nm = mods[1 if tt < 2 else 0]
                        S.dma("sp", lambda h, tt=tt, i=i: h.dma_start(out=xt[i % 3][:], in_=xsrc(l, tt)), w=[("xt", i % 3)])

                        def dst(ptile, pkey, wkeys, tt=tt):
                            S.add("act", lambda h: h.activation(out=hT[:, :, tt * 128:(tt + 1) * 128], in_=ptile[:], func=AF.Copy),
                                  r=[pkey], w=wkeys)
                        th = norm_transpose_tile(l, tt, xt[i % 3][:], ("xt", i % 3), gm, sh, nm, tmp, hb[i % 2], junk, ssq, pt[i % 2], dst,
                                                 [("hT", tt)], i)
                        if pend is not None:
                            pend()
                        pend = th
                    pend()
                    S.phase_end("P1")
                with ExitStack() as st:
                    Tl, Pl = mk_alloc(st)
                    wb = [Tl("wib%d" % i, [128, 16, 512], BF16) for i in range(2)]
                    stg = [Tl("stg%d" % i, [128, 512]) for i in range(4)]
                    pp = [Pl("pp%d" % i, [128, 512]) for i in range(4)]
                    wv = I["w_in"][l].rearrange("(k p) n -> p k n", p=128)
                    cnt = [0]

                    def evac(ps_ap, n, dram_ap, pkey):
                        i = cnt[0]
                        cnt[0] += 1
                        s = stg[i % 4]
                        if i % 2 == 0:
                            S.add("act", lambda h: h.activation(out=s[:, 0:n], in_=ps_ap, func=AF.Copy), r=[pkey], w=[("stg", i % 4)])
                        else:
                            S.add("dve", lambda h: h.tensor_copy(out=s[:, 0:n], in_=ps_ap), r=[pkey], w=[("stg", i % 4)])
                        S.dma("sp", lambda h: h.dma_start(out=dram_ap, in_=s[:, 0:n]), r=[("stg", i % 4)], w=[("dram", i)])

                    blocks = [(c0, 512) for c0 in range(0, 5120, 512)] + [(5120, 280)]
                    pi = [0]
                    bg = mod_task(0, Tl, Pl, blocks=range(8, 24)) if (l == 0 and (phases is None or "mod" in phases)) else iter(())
                    next(bg, None)
                    for bi, (c0, ncol) in enumerate(blocks):
                        next(bg, None)
                        next(bg, None)
                        b = bi % 2
                        for k4 in range(4):
                            S.dma("pool", lambda h, b=b, k4=k4, c0=c0, ncol=ncol: h.dma_start(
                                out=wb[b][:, 4 * k4:4 * k4 + 4, 0:ncol], in_=wv[:, 4 * k4:4 * k4 + 4, c0:c0 + ncol]), w=[("wib", b, k4)])
                        wkeys = [("wib", b, k4) for k4 in range(4)]
                        if c0 < 4096:
                            tm = (0, ncol, c0)
                            fm_chunks = []
                        elif c0 < 5120:
                            tm = None
                            fm_chunks = [(j, (c0 - 4096) // 128 + j) for j in range(4)]
                        else:
                            tm = (256, 24, 4096)
                            fm_chunks = [(0, 8), (1, 9)]
                        if tm is not None:
                            co, n, dc = tm
                            for tt in range(NT):
                                p = pi[0] % 4
                                pi[0] += 1
                                for k in range(16):
                                    S.add("pe", lambda h, p=p, k=k, tt=tt, co=co, n=n, b=b: h.matmul(
                                        pp[p][:, 0:n], lhsT=hT[:, k, tt * 128:(tt + 1) * 128], rhs=wb[b][:, k, co:co + n],
                                        start=(k == 0), stop=(k == 15)), r=wkeys, w=[("pp", p)])
                                evac(pp[p][:, 0:n], n, proj[tt * 128:(tt + 1) * 128, dc:dc + n], ("pp", p))
                        for (j, ch) in fm_chunks:
                            for t0 in range(0, T, 512):
                                n = min(512, T - t0)
                                p = pi[0] % 4
                                pi[0] += 1
                                for k in range(16):
                                    S.add("pe", lambda h, p=p, k=k, t0=t0, n=n, j=j, b=b: h.matmul(
                                        pp[p][:, 0:n], lhsT=wb[b][:, k, j * 128:(j + 1) * 128], rhs=hT[:, k, t0:t0 + n],
                                        start=(k == 0), stop=(k == 15)), r=wkeys, w=[("pp", p)])
                                evac(pp[p][:, 0:n], n, xbcT[ch * 128:(ch + 1) * 128, t0:t0 + n], ("pp", p))
                    for _ in bg:
                        pass
                    S.phase_end("P2")

        def rope_ops(src, dst, cs, nh, lat, skey, dkey, cskey, tmps):
            if not lat:
                S.add("dve", lambda h: h.tensor_copy(out=dst, in_=src), r=[skey], w=[dkey, (dkey, "b")])
                return
            t1, t2, t3, t4 = tmps
            s4 = src.rearrange("p (h i two) -> p h i two", h=nh, two=2)
            d4 = dst.rearrange("p (h i two) -> p h i two", h=nh, two=2)
            x1 = s4[:, :, :, 0]
            x2 = s4[:, :, :, 1]
            cosb = cs[:, 0:64].unsqueeze(1).to_broadcast([128, nh, 64])
            sinb = cs[:, 64:128].unsqueeze(1).to_broadcast([128, nh, 64])
            v = lambda t: t[:, 0:nh * 64].rearrange("p (h i) -> p h i", h=nh)
            S.add("dve", lambda h: h.tensor_tensor(out=v(t1), in0=x1, in1=cosb, op=ALU.mult), r=[skey, cskey], w=["rt1"])
            S.add("dve", lambda h: h.tensor_tensor(out=v(t2), in0=x2, in1=sinb, op=ALU.mult), r=[skey, cskey], w=["rt2"])
            S.add("dve", lambda h: h.tensor_tensor(out=d4[:, :, :, 0], in0=v(t1), in1=v(t2), op=ALU.subtract), r=["rt1", "rt2"], w=[dkey])
            S.add("dve", lambda h: h.tensor_tensor(out=v(t3), in0=x1, in1=sinb, op=ALU.mult), r=[skey, cskey], w=["rt3"])
            S.add("dve", lambda h: h.tensor_tensor(out=v(t4), in0=x2, in1=cosb, op=ALU.mult), r=[skey, cskey], w=["rt4"])
            S.add("dve", lambda h: h.tensor_tensor(out=d4[:, :, :, 1], in0=v(t3), in1=v(t4), op=ALU.add), r=["rt3", "rt4"], w=[(dkey, "b")])

        def phase_att(l):
            last = (l == nlayers - 1)
            with ExitStack() as st:
                Tl, Pl = mk_alloc(st)
                QKT = Tl("QKT", [128, 8, T], BF16)
                Vtm = Tl("Vtm", [128, NT, 256], BF16)
                gqk = Tl("gqk", [128, 8, 128])
                pr = [Tl("pr%d" % i, [128, 1280]) for i in range(2)]
                sq = Tl("sq", [128, 1024])
                qn = Tl("qn", [128, 1024])
                tmps = [Tl("rt%d" % i, [128, 512]) for i in range(4)]
                qr = [Tl("qr%d" % i, [128, 1024], BF16) for i in range(2)]
                cs = [Tl("cs%d" % i, [128, 128]) for i in range(2)]
                st8 = Tl("st8", [128, 16])
                ptq = Pl("ptq", [128, 8, 128], BF16)
                for hh in range(8):
                    src = I["q_norm_g"] if hh < 6 else I["k_norm_g"]
                    S.dma("sp", lambda h, hh=hh, src=src: h.dma_start(out=gqk[:, hh, :], in_=src[l:l + 1, :].to_broadcast([128, 128])), w=["gqk"])
                S.add("dve", lambda h: h.tensor_scalar(out=gqk[:, 0:6, :], in0=gqk[:, 0:6, :], scalar1=float(128 ** -0.5), scalar2=None, op0=ALU.mult),
                      r=["gqk"], w=["gqk"])
                def prep_tile(tt):
                    i = tt
                    lat = tt >= 2
                    p_ = pr[i % 2]
                    S.dma("sp", lambda h, tt=tt, p_=p_: h.dma_start(out=p_[:], in_=proj[tt * 128:(tt + 1) * 128, 0:1280]), w=[("pr", i % 2)])
                    if lat:
                        S.dma("sp", lambda h, tt=tt, i=i: h.dma_start(out=cs[i % 2][:], in_=I["rope"][(tt - 2) * 128:(tt - 1) * 128, :]), w=[("cs", i % 2)])
                    S.add("act", lambda h, p_=p_: h.activation(out=sq[:], in_=p_[:, 0:1024], func=AF.Square), r=[("pr", i % 2)], w=["sq"])
                    S.add("dve", lambda h: h.tensor_reduce(out=st8[:, 0:8], in_=sq[:].rearrange("p (h d) -> p h d", h=8), axis=AX.X, op=ALU.add),
                          r=["sq"], w=["ssq8"])
                    rstd_ops(st8[:, 0:8], st8[:, 8:16], 1.0 / 128, ["ssq8"], ["rstd8"])
                    S.add("dve", lambda h, p_=p_: h.tensor_tensor(out=qn[:].rearrange("p (h d) -> p h d", h=8),
                                                                 in0=p_[:, 0:1024].rearrange("p (h d) -> p h d", h=8),
                                                                 in1=st8[:, 8:16].unsqueeze(2).to_broadcast([128, 8, 128]), op=ALU.mult),
                          r=[("pr", i % 2), "rstd8"], w=["qn"])
                    S.add("dve", lambda h: h.tensor_tensor(out=qn[:], in0=qn[:], in1=gqk[:].rearrange("p h d -> p (h d)"), op=ALU.mult),
                          r=["qn", "gqk"], w=["qn"])
                    q_ = qr[i % 2]
                    rope_ops(qn[:], q_[:], cs[i % 2], 8, lat, "qn", ("qr", i % 2), ("cs", i % 2), tmps)
                    for hh in range(8):
                        S.add("pe", lambda h, hh=hh, q_=q_: h.transpose(out=ptq[:, hh, :], in_=q_[:, hh * 128:(hh + 1) * 128], identity=ident_b[:]),
                              r=[("qr", i % 2), (("qr", i % 2), "b")], w=["ptq"])
                    S.add("act", lambda h, tt=tt: h.activation(out=QKT[:, :, tt * 128:(tt + 1) * 128], in_=ptq[:], func=AF.Copy), r=["ptq"], w=[("QKT", tt)])
                    S.add("act", lambda h, tt=tt, p_=p_: h.activation(out=Vtm[:, tt, :], in_=p_[:, 1024:1280], func=AF.Copy), r=[("pr", i % 2)], w=[("Vtm", tt)])
                for tt in range(NT):
                    prep_tile(tt)
                ps_s = [Pl("ps_s%d" % i, [128, 512]) for i in range(2)]
                ps_o = [Pl("ps_o%d" % i, [128, 512]) for i in range(2)]
                ps_d = [Pl("ps_d%d" % i, [128, 512]) for i in range(2)]
                pT = [Tl("pT%d" % i, [128, 512], BF16) for i in range(3)]
                rden = Tl("rden", [128, 512])
                oT = [Tl("oT%d" % i, [128, 512], BF16) for i in range(2)]
                qblocks = [(256 + qb * 512, 512, list(range(NT))) for qb in range(4)]
                if not last:
                    qblocks = [(0, 256, [0, 1])] + qblocks
                gi = 0
                si = 0
                def att_head(q0, n, kts, hh, gi):
                        nonlocal si
                        g = hh // 3
                        o = gi % 2
                        nk = len(kts)
                        slots = []

                        def do_s(j):
                            nonlocal si
                            sidx = si
                            si += 1
                            kt = kts[j]
                            S.add("pe", lambda h, sidx=sidx, kt=kt: h.matmul(ps_s[sidx % 2][:, 0:n], lhsT=QKT[:, 6 + g, kt * 128:(kt + 1) * 128],
                                                                           rhs=QKT[:, hh, q0:q0 + n], start=True, stop=True),
                                  r=[("QKT", kt)] + [("QKT", q0 // 128 + a) for a in range(n // 128)], w=[("ps_s", sidx % 2)])
                            S.add("act", lambda h, sidx=sidx: h.activation(out=pT[sidx % 3][:, 0:n], in_=ps_s[sidx % 2][:, 0:n], func=AF.Exp),
                                  r=[("ps_s", sidx % 2)], w=[("pT", sidx % 3)])
                            slots.append(sidx)

                        def do_o(j):
                            sidx = slots[j]
                            kt = kts[j]
                            S.add("pe", lambda h, sidx=sidx, kt=kt: h.matmul(ps_o[o][:, 0:n], lhsT=Vtm[:, kt, g * 128:(g + 1) * 128], rhs=pT[sidx % 3][:, 0:n],
                                                                           start=(j == 0), stop=(j == nk - 1)),
                                  r=[("pT", sidx % 3), ("Vtm", kt)], w=[("ps_o", o)])
                            S.add("pe", lambda h, sidx=sidx: h.matmul(ps_d[o][:, 0:n], lhsT=ones_b[:], rhs=pT[sidx % 3][:, 0:n],
                                                                    start=(j == 0), stop=(j == nk - 1)),
                                  r=[("pT", sidx % 3)], w=[("ps_d", o)])
                        do_s(0)
                        for j in range(nk):
                            if j + 1 < nk:
                                do_s(j + 1)
                            do_o(j)
                        S.add("dve", lambda h, o=o: h.reciprocal(out=rden[:, 0:n], in_=ps_d[o][:, 0:n]), r=[("ps_d", o)], w=["rden"])
                        S.add("dve", lambda h, o=o: h.tensor_tensor(out=oT[o][:, 0:n], in0=ps_o[o][:, 0:n], in1=rden[:, 0:n], op=ALU.mult),
                              r=[("ps_o", o), "rden"], w=[("oT", o)])
                        S.dma("sp", lambda h, o=o, hh=hh, q0=q0, n=n: h.dma_start(out=mixT[hh * 128:(hh + 1) * 128, q0:q0 + n], in_=oT[o][:, 0:n]),
                              r=[("oT", o)], w=[("mixT", gi)])
                bg = mod_task(l + 1, Tl, Pl) if (l + 1 < nlayers and (phases is None or "mod" in phases)) else iter(())
                next(bg, None)
                for (q0, n, kts) in qblocks:
                    for hh in range(6):
                        att_head(q0, n, kts, hh, gi)
                        gi += 1
                        next(bg, None)
                for _ in bg:
                    pass
                S.phase_end("att")

        def silu2_ops(src, e, dst, skey, ekey, dkey, in_scale=0.5):
            S.add("act", lambda h: h.activation(out=e, in_=src, func=AF.Tanh, scale=in_scale), r=[skey], w=[ekey])
            S.add("dve", lambda h: h.scalar_tensor_tensor(out=dst, in0=e, scalar=1.0, in1=src, op0=ALU.add, op1=ALU.mult), r=[skey, ekey], w=[dkey])

        def phase_ret(l):
            last = (l == nlayers - 1)
            with ExitStack() as st:
                Tl, Pl = mk_alloc(st)
                RQK = Tl("RQK", [128, 8, T], BF16)
                RKtm = Tl("RKtm", [128, NT, 512], BF16)
                RVtm = Tl("RVtm", [128, NT, 512], BF16)
                rgs = Tl("rgs", [128, NT, 512], BF16)
                SAll = [Tl("SAll%d" % d_, [128, NT, 512], BF16) for d_ in range(2)]
                S32 = [Tl("S32%d" % d_, [128, 512]) for d_ in range(2)]
                MaskT = Tl("MaskT", [128, 4, 128])
                ERow = [Tl("ERow%d" % d_, [128, 4, 128], BF16) for d_ in range(2)]
                dec = Tl("dec", [128, 8])
                lg = Tl("lg", [128, 8])
                dend = Tl("dend", [128, 8])
                gam = Tl("gam", [128, 8])
                marg = Tl("marg", [128, 128])
                pr = [Tl("rpr%d" % i, [128, 2048]) for i in range(2)]
                tmps = [Tl("rrt%d" % i, [128, 512]) for i in range(4)]
                qr = [Tl("rqr%d" % i, [128, 1024], BF16) for i in range(2)]
                cs = [Tl("rcs%d" % i, [128, 128]) for i in range(2)]
                ee = Tl("ree", [128, 512])
                ptq = Pl("rptq", [128, 8, 128], BF16)
                S.dma("sp", lambda h: h.dma_start(out=dec[:, 0:4], in_=I["ret_decay_f"][l:l + 1, :].to_broadcast([128, 4])), w=["dec"])
                S.dma("sp", lambda h: h.dma_start(out=dec[:, 4:8], in_=I["ret_decay_b"][l:l + 1, :].to_broadcast([128, 4])), w=["dec"])
                S.add("act", lambda h: h.activation(out=lg[:], in_=dec[:], func=AF.Exp, scale=float(math.log(2.0))), r=["dec"], w=["lg"])
                S.add("act", lambda h: h.activation(out=lg[:], in_=lg[:], func=AF.Ln, scale=-1.0, bias=1.0), r=["lg"], w=["lg"])
                S.add("act", lambda h: h.activation(out=gam[:], in_=lg[:], func=AF.Exp, scale=128.0), r=["lg"], w=["gam"])

                def mk_head(hh):
                    S.add("dve", lambda h: h.tensor_scalar(out=marg[:], in0=D1, scalar1=lg[:, hh:hh + 1], scalar2=None, op0=ALU.mult), r=["lg"], w=["marg"])
                    S.add("dve", lambda h: h.scalar_tensor_tensor(out=marg[:], in0=D2, scalar=lg[:, 4 + hh:5 + hh], in1=marg[:], op0=ALU.mult, op1=ALU.add),
                          r=["lg", "marg"], w=["marg"])
                    S.add("act", lambda h: h.activation(out=marg[:], in_=marg[:], func=AF.Exp), r=["marg"], w=["marg"])
                    S.add("dve", lambda h: h.tensor_tensor(out=MaskT[:, hh, :], in0=marg[:], in1=ident_f, op=ALU.add), r=["marg"], w=["MaskT"])
                    for d_ in range(2):
                        S.add("act", lambda h, d_=d_: h.activation(out=ERow[d_][:, hh, :], in_=rowidx[d_], func=AF.Exp, scale=lg[:, 4 * d_ + hh:4 * d_ + hh + 1]),
                              r=["lg"], w=["ERow"])
                        S.add("act", lambda h, d_=d_: h.activation(out=dend[:, 4 * d_ + hh:4 * d_ + hh + 1], in_=colexp[d_], func=AF.Exp,
                                                                   scale=lg[:, 4 * d_ + hh:4 * d_ + hh + 1]), r=["lg"], w=["dend"])
                for hh in range(4):
                    mk_head(hh)

                def prep_tile(tt):
                    i = tt
                    lat = tt >= 2
                    p_ = pr[i % 2]
                    S.dma("sp", lambda h: h.dma_start(out=p_[:], in_=proj[tt * 128:(tt + 1) * 128, 1280:3328]), w=[("pr", i % 2)])
                    if lat:
                        S.dma("sp", lambda h: h.dma_start(out=cs[i % 2][:], in_=I["rope"][(tt - 2) * 128:(tt - 1) * 128, :]), w=[("cs", i % 2)])
                    S.add("act", lambda h: h.activation(out=p_[:, 512:1024], in_=p_[:, 512:1024], func=AF.Copy, scale=float(128 ** -0.5)),
                          r=[("pr", i % 2)], w=[("pr", i % 2)])
                    q_ = qr[i % 2]
                    rope_ops(p_[:, 0:1024], q_[:], cs[i % 2], 8, lat, ("pr", i % 2), ("qr", i % 2), ("cs", i % 2), tmps)
                    for hh in range(8):
                        S.add("pe", lambda h, hh=hh: h.transpose(out=ptq[:, hh, :], in_=q_[:, hh * 128:(hh + 1) * 128], identity=ident_b[:]),
                              r=[("qr", i % 2), (("qr", i % 2), "b")], w=["ptq"])
                    S.add("act", lambda h: h.activation(out=RQK[:, :, tt * 128:(tt + 1) * 128], in_=ptq[:], func=AF.Copy), r=["ptq"], w=[("RQK", tt)])
                    S.add("dve", lambda h: h.tensor_copy(out=RKtm[:, tt, :], in_=q_[:, 512:1024]), r=[("qr", i % 2), (("qr", i % 2), "b")], w=[("RKtm", tt)])
                    S.add("act", lambda h: h.activation(out=RVtm[:, tt, :], in_=p_[:, 1024:1536], func=AF.Copy), r=[("pr", i % 2)], w=[("RVtm", tt)])
                    silu2_ops(p_[:, 1536:2048], ee[:], rgs[:, tt, :], ("pr", i % 2), "ee", ("rgs", tt))
                for tt in range(NT):
                    prep_tile(tt)

                psA2 = [Pl("psA%d" % d_, [128, 512]) for d_ in range(2)]
                RVs2 = [[Tl("RVs%d_%d" % (d_, i), [128, 512], BF16) for i in range(2)] for d_ in range(2)]
                orders = [list(range(NT)), [1, 0] + list(range(NT - 1, 1, -1))]

                def state_step(d_, idx):
                    order = orders[d_]
                    c = order[idx]
                    if idx == 0:
                        S.add("pool", lambda h: h.memset(S32[d_][:], 0.0), w=[("S32", d_)])
                        S.add("pool", lambda h: h.memset(SAll[d_][:, c, :], 0.0), w=[("SAll", d_, c)])
                    if idx == NT - 1:
                        return
                    rv = RVs2[d_][idx % 2]
                    psA = psA2[d_]
                    S.add("dve", lambda h: h.tensor_tensor(out=rv[:].rearrange("p (h d) -> p h d", h=4),
                                                            in0=RVtm[:, c, :].rearrange("p (h d) -> p h d", h=4),
                                                            in1=dend[:, 4 * d_:4 * d_ + 4].unsqueeze(2).to_broadcast([128, 4, 128]), op=ALU.mult),
                          r=[("RVtm", c), "dend"], w=[("RVs", d_, idx % 2)])
                    for hh in range(4):
                        S.add("pe", lambda h, hh=hh: h.matmul(psA[:, hh * 128:(hh + 1) * 128], lhsT=RKtm[:, c, hh * 128:(hh + 1) * 128],
                                                             rhs=rv[:, hh * 128:(hh + 1) * 128], start=True, stop=True),
                              r=[("RKtm", c), ("RVs", d_, idx % 2)], w=[("psA", d_)])
                    S.add("dve", lambda h: h.tensor_tensor(out=S32[d_][:].rearrange("p (h d) -> p h d", h=4),
                                                           in0=S32[d_][:].rearrange("p (h d) -> p h d", h=4),
                                                           in1=gam[:, 4 * d_:4 * d_ + 4].unsqueeze(2).to_broadcast([128, 4, 128]), op=ALU.mult),
                          r=[("S32", d_), "gam"], w=[("S32", d_)])
                    S.add("dve", lambda h: h.tensor_tensor(out=S32[d_][:], in0=S32[d_][:], in1=psA[:], op=ALU.add), r=[("S32", d_), ("psA", d_)], w=[("S32", d_)])
                    cn = order[idx + 1]
                    S.add("act", lambda h: h.activation(out=SAll[d_][:, cn, :], in_=S32[d_][:], func=AF.Copy), r=[("S32", d_)], w=[("SAll", d_, cn)])
                for idx in range(NT):
                    for d_ in range(2):
                        state_step(d_, idx)

                psS = [Pl("psS%d" % i, [128, 4, 128]) for i in range(2)]
                psY = [Pl("psY%d" % i, [128, 512]) for i in range(2)]
                ptr = Pl("ptr", [128, 8, 128], BF16)
                SDT = [Tl("SDT%d" % i, [128, 4, 128], BF16) for i in range(2)]
                RQs = [[Tl("RQs%d_%d" % (d_, i), [128, 4, 128], BF16) for i in range(2)] for d_ in range(2)]
                sqy = Tl("sqy", [128, 512])
                yn = Tl("yn", [128, 512])
                yb = [Tl("yb%d" % i, [128, 512], BF16) for i in range(2)]
                stgr = [Tl("stgr%d" % i, [128, 4, 128], BF16) for i in range(2)]
                st4 = Tl("st4", [128, 8])

                def chunk(c, i):
                    y = psY[i % 2]
                    cs_ = slice(c * 128, (c + 1) * 128)
                    for hh in range(4):
                        S.add("pe", lambda h, hh=hh: h.matmul(psS[i % 2][:, hh, :], lhsT=RQK[:, 4 + hh, cs_], rhs=RQK[:, hh, cs_], start=(hh == 0), stop=(hh == 3)),
                              r=[("RQK", c)], w=[("psS", i % 2)])
                    S.add("dve", lambda h: h.tensor_tensor(out=SDT[i % 2][:], in0=psS[i % 2][:], in1=MaskT[:], op=ALU.mult),
                          r=[("psS", i % 2), "MaskT"], w=[("SDT", i % 2)])
                    for d_ in range(2):
                        S.add("dve", lambda h, d_=d_: h.tensor_tensor(out=RQs[d_][i % 2][:], in0=RQK[:, 0:4, cs_], in1=ERow[d_][:], op=ALU.mult),
                              r=[("RQK", c), "ERow"], w=[("RQs", d_, i % 2)])
                    for hh in range(4):
                        S.add("pe", lambda h, hh=hh: h.matmul(y[:, hh * 128:(hh + 1) * 128], lhsT=SDT[i % 2][:, hh, :], rhs=RVtm[:, c, hh * 128:(hh + 1) * 128],
                                                             start=(hh == 0), stop=False), r=[("SDT", i % 2), ("RVtm", c)], w=[("psY", i % 2)])
                        for d_ in range(2):
                            S.add("pe", lambda h, hh=hh, d_=d_: h.matmul(y[:, hh * 128:(hh + 1) * 128], lhsT=RQs[d_][i % 2][:, hh, :],
                                                                        rhs=SAll[d_][:, c, hh * 128:(hh + 1) * 128], start=False, stop=(d_ == 1 and hh == 3)),
                                  r=[("RQs", d_, i % 2), ("SAll", d_, c)], w=[("psY", i % 2)])
                def chunk_epi(c, i):
                    y = psY[i % 2]
                    S.add("act", lambda h: h.activation(out=sqy[:], in_=y[:], func=AF.Square), r=[("psY", i % 2)], w=["sqy"])
                    S.add("dve", lambda h: h.tensor_reduce(out=st4[:, 0:4], in_=sqy[:].rearrange("p (h d) -> p h d", h=4), axis=AX.X, op=ALU.add),
                          r=["sqy"], w=["ssq4"])
                    rstd_ops(st4[:, 0:4], st4[:, 4:8], 1.0 / 128, ["ssq4"], ["rstd4"])
                    S.add("dve", lambda h: h.tensor_tensor(out=yn[:].rearrange("p (h d) -> p h d", h=4), in0=y[:].rearrange("p (h d) -> p h d", h=4),
                                                           in1=st4[:, 4:8].unsqueeze(2).to_broadcast([128, 4, 128]), op=ALU.mult),
                          r=[("psY", i % 2), "rstd4"], w=["yn"])
                    yb_ = yb[i % 2]
                    S.add("dve", lambda h: h.scalar_tensor_tensor(out=yb_[:], in0=yn[:], scalar=0.5, in1=rgs[:, c, :], op0=ALU.mult, op1=ALU.mult),
                          r=["yn", ("rgs", c)], w=[("yb", i % 2)])
                    for hh in range(4):
                        S.add("pe", lambda h, hh=hh: h.transpose(out=ptr[:, hh, :], in_=yb_[:, hh * 128:(hh + 1) * 128], identity=ident_b[:]),
                              r=[("yb", i % 2)], w=["ptr"])
                    sg = stgr[i % 2]
                    S.add("act", lambda h: h.activation(out=sg[:], in_=ptr[:, 0:4, :], func=AF.Copy), r=["ptr"], w=[("stgr", i % 2)])
                    S.dma("sp", lambda h: h.dma_start(out=mixT[768:1280, c * 128:(c + 1) * 128].rearrange("(h d) t -> d h t", h=4), in_=sg[:]),
                          r=[("stgr", i % 2)], w=[("mixTr", c)])
                cl = list(range(2 if last else 0, NT))
                chunk(cl[0], 0)
                for i_, c in enumerate(cl):
                    if i_ + 1 < len(cl):
                        chunk(cl[i_ + 1], i_ + 1)
                    chunk_epi(c, i_)
                S.phase_end("ret")

        def phase_ssd(l):
            last = (l == nlayers - 1)
            NH = 12
            with ExitStack() as st_o:
                To, Po = mk_alloc(st_o)
                BT = To("BT", [128, 2, T], BF16)
                CT = To("CT", [128, 2, T], BF16)
                Btm = To("Btm", [128, NT, 256], BF16)
                xs = To("xs", [128, NT, 768], BF16)
                la = To("la", [128, 2, NT, NH])
                lndt = To("lndt", [128, 2, NT, NH])
                acs = To("acs", [128, 2, NT, NH])
                tot = To("tot", [128, 2, NT, NH])
                ein = To("ein", [128, 2, NT, NH])
                cdc = To("cdc", [128, 2, NT, NH])
                wend = To("wend", [128, 2, NT, NH])
                lb = To("lb", [128, 2, NT, NH])
                dsk = To("dsk", [128, NH])
                gssd = To("gssd", [128, 768])
                with ExitStack() as st:
                    Tl, Pl = mk_alloc(st)
                    cw = Tl("cw", [128, 10, 5])
                    cb = Tl("cb", [128, 10])
                    for k in range(5):
                        S.dma("sp", lambda h, k=k: h.dma_start(out=cw[:, :, k], in_=I["conv_w"][l, k].rearrange("(c p) -> p c", p=128),
                                                               allow_slow_non_contiguous=True), w=["cw"])
                    S.dma("sp", lambda h: h.dma_start(out=cb[:], in_=I["conv_b"][l].rearrange("(c p) -> p c", p=128), allow_slow_non_contiguous=True), w=["cb"])
                    S.dma("sp", lambda h: h.dma_start(out=dsk[:], in_=I["d_skip"][l:l + 1, :].to_broadcast([128, NH])), w=["dsk"])
                    S.dma("sp", lambda h: h.dma_start(out=gssd[:], in_=I["ssd_norm_g"][l:l + 1, :].to_broadcast([128, 768])), w=["gssd"])
                    UW = 2312
                    u = [Tl("u%d" % i, [128, UW]) for i in range(2)]
                    acc = Tl("acc", [128, UW])
                    ee = Tl("cee", [128, UW])
                    ob = [Tl("ob%d" % i, [128, UW], BF16) for i in range(2)]
                    ptx = Pl("ptx", [128, 8, 128], BF16)
                    for i in range(2):
                        S.add("pool", lambda h, i=i: h.memset(u[i][:], 0.0), w=[("u", i)])
                    S.add("dve", lambda h: h.tensor_scalar(out=cw[:], in0=cw[:], scalar1=0.5, scalar2=None, op0=ALU.mult), r=["cw"], w=["cw"])
                    S.add("dve", lambda h: h.tensor_scalar(out=cb[:], in0=cb[:], scalar1=0.5, scalar2=None, op0=ALU.mult), r=["cb"], w=["cb"])

                    def conv_chunk(cc):
                        i = cc
                        u_ = u[i % 2]
                        o_ = ob[i % 2]
                        S.dma("sp", lambda h: h.dma_start(out=u_[:, 2:258], in_=xbcT[cc * 128:(cc + 1) * 128, 0:256]), w=[("u", i % 2)])
                        S.dma("sp", lambda h: h.dma_start(out=u_[:, 262:2310], in_=xbcT[cc * 128:(cc + 1) * 128, 256:T]), w=[("u", i % 2)])
                        n = 2308
                        S.add("dve", lambda h: h.tensor_scalar(out=acc[:, 2:2 + n], in0=u_[:, 0:n], scalar1=cw[:, cc, 0:1], scalar2=cb[:, cc:cc + 1],
                                                               op0=ALU.mult, op1=ALU.add), r=[("u", i % 2), "cw", "cb"], w=["acc"])
                        for k in range(1, 5):
                            S.add("dve", lambda h, k=k: h.scalar_tensor_tensor(out=acc[:, 2:2 + n], in0=u_[:, k:k + n], scalar=cw[:, cc, k:k + 1],
                                                                               in1=acc[:, 2:2 + n], op0=ALU.mult, op1=ALU.add),
                                  r=[("u", i % 2), "cw", "acc"], w=["acc"])
                        silu2_ops(acc[:, 2:2 + n], ee[:, 2:2 + n], o_[:, 2:2 + n], "acc", "cee", ("ob", i % 2), in_scale=1.0)
                        def tok(tt):
                            return (2 + tt * 128) if tt < 2 else (262 + (tt - 2) * 128)
                        if cc < 6 or cc in (6, 7):
                            for t0 in range(0, NT, 8):
                                nt = min(8, NT - t0)
                                for a in range(nt):
                                    tt = t0 + a
                                    S.add("pe", lambda h, a=a, tt=tt: h.transpose(out=ptx[:, a, :], in_=o_[:, tok(tt):tok(tt) + 128], identity=ident_b[:]),
                                          r=[("ob", i % 2)], w=["ptx"])
                                if cc < 6:
                                    S.add("act", lambda h, t0=t0, nt=nt: h.activation(out=xs[:, t0:t0 + nt, cc * 128:(cc + 1) * 128], in_=ptx[:, 0:nt, :], func=AF.Copy),
                                          r=["ptx"], w=[("xs", cc, t0)])
                                else:
                                    g = cc - 6
                                    S.add("act", lambda h, t0=t0, nt=nt: h.activation(out=Btm[:, t0:t0 + nt, g * 128:(g + 1) * 128], in_=ptx[:, 0:nt, :], func=AF.Copy),
                                          r=["ptx"], w=[("Btm", g, t0)])
                        if cc >= 6:
                            dstT = BT if cc < 8 else CT
                            g = (cc - 6) % 2
                            S.add("act", lambda h: h.activation(out=dstT[:, g, 0:256], in_=o_[:, 2:258], func=AF.Copy), r=[("ob", i % 2)], w=[("BCT", cc, 0)])
                            S.add("act", lambda h: h.activation(out=dstT[:, g, 256:T], in_=o_[:, 262:2310], func=AF.Copy), r=[("ob", i % 2)], w=[("BCT", cc, 1)])
                    for cc in range(10):
                        conv_chunk(cc)
                    ztl = [Tl("ztl%d" % i, [128, 768]) for i in range(2)]
                    zth = [Tl("zth%d" % i, [128, 768]) for i in range(2)]
                    zso = [Tl("zso%d" % i, [128, 768], BF16) for i in range(2)]

                    def ztile(tt):
                        b = tt % 2
                        S.dma("sp", lambda h: h.dma_start(out=ztl[b][:], in_=proj[tt * 128:(tt + 1) * 128, 3328:4096]), w=[("ztl", b)])
                        S.add("act", lambda h: h.activation(out=zth[b][:], in_=ztl[b][:], func=AF.Tanh, scale=0.5), r=[("ztl", b)], w=[("zth", b)])
                        S.add("dve", lambda h: h.scalar_tensor_tensor(out=zso[b][:], in0=zth[b][:], scalar=1.0, in1=ztl[b][:], op0=ALU.add, op1=ALU.mult),
                              r=[("ztl", b), ("zth", b)], w=[("zso", b)])
                        S.dma("sp", lambda h: h.dma_start(out=zsd[tt * 128:(tt + 1) * 128, :], in_=zso[b][:]), r=[("zso", b)], w=[("zsd", tt)])
                    for tt in range(2 if last else 0, NT):
                        ztile(tt)
                    dtr = Tl("dtr", [128, NT, 24])
                    dtb = Tl("dtb", [128, 24])
                    alg = Tl("alg", [128, 24])
                    dtv = Tl("dtv", [128, 2, NT, NH])
                    tmpd = Tl("tmpd", [128, 2, NT, NH])
                    psc = Pl("psc", [128, 512])
                    pst = Pl("pst", [128, 512])
                    S.dma("sp", lambda h: h.dma_start(out=dtr[:], in_=proj[:, 4096:4120].rearrange("(t p) c -> p t c", p=128)), w=["dtr"])
                    S.dma("sp", lambda h: h.dma_start(out=dtb[:, 0:12], in_=I["dt_bias_f"][l:l + 1, :].to_broadcast([128, 12])), w=["dtb"])
                    S.dma("sp", lambda h: h.dma_start(out=dtb[:, 12:24], in_=I["dt_bias_b"][l:l + 1, :].to_broadcast([128, 12])), w=["dtb"])
                    S.dma("sp", lambda h: h.dma_start(out=alg[:, 0:12], in_=I["a_log_f"][l:l + 1, :].to_broadcast([128, 12])), w=["alg"])
                    S.dma("sp", lambda h: h.dma_start(out=alg[:, 12:24], in_=I["a_log_b"][l:l + 1, :].to_broadcast([128, 12])), w=["alg"])
                    S.add("act", lambda h: h.activation(out=alg[:], in_=alg[:], func=AF.Exp), r=["alg"], w=["alg"])
                    for d_ in range(2):
                        S.add("dve", lambda h, d_=d_: h.tensor_tensor(out=dtv[:, d_], in0=dtr[:, :, d_ * 12:(d_ + 1) * 12],
                                                                    in1=dtb[:, d_ * 12:(d_ + 1) * 12].unsqueeze(1).to_broadcast([128, NT, NH]), op=ALU.add),
                              r=["dtr", "dtb"], w=["dtv"])
                    F2 = lambda t: t[:].rearrange("p a b c -> p (a b c)")
                    S.add("act", lambda h: h.activation(out=F2(dtv), in_=F2(dtv), func=AF.Exp), r=["dtv"], w=["dtv"])
                    S.add("act", lambda h: h.activation(out=F2(dtv), in_=F2(dtv), func=AF.Ln, bias=1.0), r=["dtv"], w=["dtv"])
                    S.add("dve", lambda h: h.tensor_scalar(out=F2(dtv), in0=F2(dtv), scalar1=1e-30, scalar2=None, op0=ALU.max), r=["dtv"], w=["dtv"])
                    S.add("act", lambda h: h.activation(out=F2(lndt), in_=F2(dtv), func=AF.Ln), r=["dtv"], w=["lndt"])
                    for d_ in range(2):
                        S.add("dve", lambda h, d_=d_: h.tensor_tensor(out=la[:, d_], in0=dtv[:, d_],
                                                                    in1=alg[:, d_ * 12:(d_ + 1) * 12].unsqueeze(1).to_broadcast([128, NT, NH]), op=ALU.mult),
                              r=["dtv", "alg"], w=["la"])
                    S.add("dve", lambda h: h.tensor_scalar(out=F2(la), in0=F2(la), scalar1=-1.0, scalar2=None, op0=ALU.mult), r=["la"], w=["la"])
                    NQ = NT * NH
                    S.add("pe", lambda h: h.matmul(psc[:, 0:NQ], lhsT=tri_f, rhs=la[:, 0].rearrange("p a b -> p (a b)"), start=True, stop=True), r=["la"], w=["psc"])
                    S.add("pe", lambda h: h.matmul(psc[:, NQ:2 * NQ], lhsT=tri_b, rhs=la[:, 1].rearrange("p a b -> p (a b)"), start=True, stop=True), r=["la"], w=["psc"])
                    S.add("pe", lambda h: h.matmul(pst[:, 0:2 * NQ], lhsT=ones_f, rhs=F2(la), start=True, stop=True), r=["la"], w=["pst"])
                    S.add("dve", lambda h: h.tensor_copy(out=F2(acs), in_=psc[:, 0:2 * NQ]), r=["psc"], w=["acs"])
                    S.add("dve", lambda h: h.tensor_copy(out=F2(tot), in_=pst[:, 0:2 * NQ]), r=["pst"], w=["tot"])
                    S.add("act", lambda h: h.activation(out=F2(ein), in_=F2(acs), func=AF.Exp), r=["acs"], w=["ein"])
                    S.add("act", lambda h: h.activation(out=F2(cdc), in_=F2(tot), func=AF.Exp), r=["tot"], w=["cdc"])
                    S.add("dve", lambda h: h.tensor_tensor(out=F2(lb), in0=F2(lndt), in1=F2(acs), op=ALU.subtract), r=["lndt", "acs"], w=["lb"])
                    S.add("dve", lambda h: h.tensor_tensor(out=F2(tmpd), in0=F2(lb), in1=F2(tot), op=ALU.add), r=["lb", "tot"], w=["tmpd"])
                    S.add("act", lambda h: h.activation(out=F2(wend), in_=F2(tmpd), func=AF.Exp), r=["tmpd"], w=["wend"])
                    S.phase_end("ssdprep")
                with ExitStack() as st:
                    Tl, Pl = mk_alloc(st)
                    SAll = [Tl("sSAll%d" % d_, [128, NT, 768], BF16) for d_ in range(2)]
                    S32 = [Tl("sS32%d" % d_, [128, 768]) for d_ in range(2)]
                    xsw2 = [[Tl("xsw%d_%d" % (d_, i), [128, 768], BF16) for i in range(2)] for d_ in range(2)]
                    psA = Pl("spsA", [128, 2, 512])
                    psY = Pl("spsY", [128, 1024])
                    psAB = [psA, psY[:, :].rearrange("p (g x) -> p g x", g=2)]
                    psABk = ["psA", "psY"]
                    orders = [list(range(NT)), [1, 0] + list(range(NT - 1, 1, -1))]

                    def bc12(t2d):
                        return t2d.unsqueeze(2).to_broadcast([128, NH, 64])

                    def v12(ap):
                        return ap.rearrange("p (h d) -> p h d", h=NH)

                    def state_step(d_, idx):
                        order = orders[d_]
                        c = order[idx]
                        if idx == 0:
                            S.add("pool", lambda h: h.memset(S32[d_][:], 0.0), w=[("S32", d_)])
                            S.add("pool", lambda h: h.memset(SAll[d_][:, c, :], 0.0), w=[("SAll", d_, c)])
                        if idx == NT - 1:
                            return
                        xw = xsw2[d_][idx % 2]
                        pA = psAB[d_]
                        pk = psABk[d_]
                        S.add("dve", lambda h: h.tensor_tensor(out=v12(xw[:]), in0=v12(xs[:, c, :]), in1=bc12(wend[:, d_, c, :]), op=ALU.mult),
                              w=[("xsw", d_, idx % 2)])
                        for g in range(2):
                            S.add("pe", lambda h, g=g: h.matmul(pA[:, g, 0:384], lhsT=Btm[:, c, g * 128:(g + 1) * 128], rhs=xw[:, g * 384:(g + 1) * 384],
                                                               start=True, stop=True), r=[("xsw", d_, idx % 2)], w=[pk])
                        S.add("dve", lambda h: h.tensor_tensor(out=v12(S32[d_][:]), in0=v12(S32[d_][:]), in1=bc12(cdc[:, d_, c, :]), op=ALU.mult),
                              r=[("S32", d_)], w=[("S32", d_)])
                        S.add("dve", lambda h: h.tensor_tensor(out=S32[d_][:].rearrange("p (g x) -> p g x", g=2), in0=S32[d_][:].rearrange("p (g x) -> p g x", g=2),
                                                               in1=pA[:, :, 0:384], op=ALU.add), r=[("S32", d_), pk], w=[("S32", d_)])
                        cn = order[idx + 1]
                        S.add("act", lambda h: h.activation(out=SAll[d_][:, cn, :], in_=S32[d_][:], func=AF.Copy), r=[("S32", d_)], w=[("SAll", d_, cn)])
                    for idx in range(NT):
                        for d_ in range(2):
                            state_step(d_, idx)

                    Rt = [Tl("Rt%d" % d_, [128, NH, 128]) for d_ in range(2)]
                    mneg4 = [Tl("mneg4_%d" % d_, [128, 4, 128]) for d_ in range(2)]
                    Et = [Tl("Et%d" % i, [128, 4, 128], BF16) for i in range(2)]
                    Wt = [Tl("Wt%d" % i, [128, 4, 128], BF16) for i in range(4)]
                    psE = [Pl("psE%d" % i, [128, 4, 128]) for i in range(2)]
                    psG = Pl("psG", [128, 4, 128])
                    psFB = psA
                    ptr = Pl("sptr", [128, 8, 128], BF16)
                    y1 = [Tl("y1_%d" % i, [128, 768]) for i in range(2)]
                    y2 = [Tl("y2_%d" % i, [128, 768]) for i in range(2)]
                    zt = [Tl("zt%d" % i, [128, 768], BF16) for i in range(2)]
                    xsd = [Tl("xsd%d" % i, [128, 768], BF16) for i in range(2)]
                    junk = Tl("sjunk", [128, 768], BF16)
                    yo = [Tl("yo%d" % i, [128, 768], BF16) for i in range(2)]
                    stg = [Tl("sstg%d" % i, [128, 6, 128], BF16) for i in range(2)]
                    st2 = Tl("st2", [128, 2])
                    for d_ in range(2):
                        S.add("dve", lambda h, d_=d_: h.tensor_copy(out=mneg4[d_][:], in_=mneg[d_].unsqueeze(1).to_broadcast([128, 4, 128])), w=["mneg4"])
                    tris = [tri_f, tri_b]
                    ecnt = [0]
                    wcnt = [0]

                    def chunk(c, ci):
                        cs_ = slice(c * 128, (c + 1) * 128)
                        b = ci % 2
                        S.dma("sp", lambda h: h.dma_start(out=zt[b][:], in_=zsd[c * 128:(c + 1) * 128, :]), w=[("zt", b)])
                        for g in range(2):
                            S.add("pe", lambda h, g=g: h.matmul(psG[:, g, :], lhsT=BT[:, g, cs_], rhs=CT[:, g, cs_], start=(g == 0), stop=(g == 1)), w=["psG"])
                        for d_ in range(2):
                            S.add("dve", lambda h, d_=d_: h.tensor_tensor(out=Rt[d_][:], in0=la[:, d_, c, :].unsqueeze(2).to_broadcast([128, NH, 128]),
                                                                        in1=tris[d_].unsqueeze(1).to_broadcast([128, NH, 128]), op=ALU.mult), w=[("Rt", d_)])
                        S.add("dve", lambda h: h.tensor_tensor(out=v12(xsd[b][:]), in0=v12(xs[:, c, :]), in1=bc12(dsk[:]), op=ALU.mult), w=[("xsd", b)])
                        for q in range(3):
                            wl = {}
                            for d_ in range(2):
                                e = ecnt[0]
                                ecnt[0] += 1
                                pe_ = psE[e % 2]
                                et_ = Et[e % 2]
                                S.add("pe", lambda h, d_=d_, q=q, pe_=pe_: h.matmul(pe_[:].rearrange("p a b -> p (a b)"), lhsT=ones_f,
                                                                                 rhs=Rt[d_][:, 4 * q:4 * q + 4, :].rearrange("p a b -> p (a b)"), start=True, stop=False),
                                      r=[("Rt", d_)], w=[("psE", e % 2)])
                                S.add("pe", lambda h, d_=d_, pe_=pe_: h.matmul(pe_[:].rearrange("p a b -> p (a b)"), lhsT=ident_f,
                                                                            rhs=mneg4[d_][:].rearrange("p a b -> p (a b)"), start=False, stop=True),
                                      r=["mneg4"], w=[("psE", e % 2)])
                                for a in range(4):
                                    hh = 4 * q + a
                                    S.add("act", lambda h, a=a, hh=hh, d_=d_, pe_=pe_, et_=et_: h.activation(out=et_[:, a, :], in_=pe_[:, a, :], func=AF.Exp,
                                                                                                      bias=lb[:, d_, c, hh:hh + 1]),
                                          r=[("psE", e % 2)], w=[("Et", e % 2)])
                                w_ = wcnt[0]
                                wcnt[0] += 1
                                wt_ = Wt[w_ % 4]
                                wl[d_] = (w_, wt_)
                                if q == 1:
                                    for half in range(2):
                                        S.add("dve", lambda h, half=half, et_=et_, wt_=wt_: h.tensor_tensor(
                                            out=wt_[:, 2 * half:2 * half + 2, :], in0=et_[:, 2 * half:2 * half + 2, :],
                                            in1=psG[:, half, :].unsqueeze(1).to_broadcast([128, 2, 128]), op=ALU.mult),
                                            r=[("Et", e % 2), "psG"], w=[("Wt", w_ % 4)])
                                else:
                                    g = 0 if q == 0 else 1
                                    S.add("dve", lambda h, g=g, et_=et_, wt_=wt_: h.tensor_tensor(
                                        out=wt_[:], in0=et_[:], in1=psG[:, g, :].unsqueeze(1).to_broadcast([128, 4, 128]), op=ALU.mult),
                                        r=[("Et", e % 2), "psG"], w=[("Wt", w_ % 4)])
                            for a in range(4):
                                hh = 4 * q + a
                                for d_ in range(2):
                                    w_, wt_ = wl[d_]
                                    S.add("pe", lambda h, hh=hh, a=a, d_=d_, wt_=wt_: h.matmul(psY[:, hh * 64:(hh + 1) * 64], lhsT=wt_[:, a, :], rhs=xs[:, c, hh * 64:(hh + 1) * 64],
                                                                                          start=(d_ == 0 and hh in (0, 8)), stop=False),
                                          r=[("Wt", w_ % 4)], w=["psY"])
                        S.add("pe", lambda h: h.matmul(psY[:, 0:512], lhsT=ident_b[:], rhs=xsd[b][:, 0:512], start=False, stop=True), r=[("xsd", b)], w=["psY"])
                        S.add("pe", lambda h: h.matmul(psY[:, 512:768], lhsT=ident_b[:], rhs=xsd[b][:, 512:768], start=False, stop=True), r=[("xsd", b)], w=["psY"])
                        for d_ in range(2):
                            for g in range(2):
                                S.add("pe", lambda h, d_=d_, g=g: h.matmul(psFB[:, g, 0:384], lhsT=CT[:, g, cs_], rhs=SAll[d_][:, c, g * 384:(g + 1) * 384],
                                                                         start=True, stop=True), r=[("SAll", d_, c)], w=["psA"])
                            yd = y1[b] if d_ == 0 else y2[b]
                            S.add("dve", lambda h, d_=d_, yd=yd: h.tensor_tensor(
                                out=yd[:].rearrange("p (g h d) -> p g h d", g=2, h=6),
                                in0=psFB[:, :, 0:384].rearrange("p g (h d) -> p g h d", h=6),
                                in1=ein[:, d_, c, :].rearrange("p (g h) -> p g h", g=2).unsqueeze(3).to_broadcast([128, 2, 6, 64]), op=ALU.mult),
                                r=["psA"], w=[("y", d_, b)])
                        S.add("dve", lambda h: h.tensor_tensor(out=y1[b][:], in0=y1[b][:], in1=y2[b][:], op=ALU.add), r=[("y", 0, b), ("y", 1, b)], w=[("y", 0, b)])
                        S.add("dve", lambda h: h.tensor_tensor(out=y1[b][:], in0=y1[b][:], in1=psY[:, 0:768], op=ALU.add), r=[("y", 0, b), "psY"], w=[("y", 0, b)])
                    def chunk_epi(c, ci):
                        cs_ = slice(c * 128, (c + 1) * 128)
                        b = ci % 2
                        S.add("dve", lambda h: h.scalar_tensor_tensor(out=y1[b][:], in0=y1[b][:], scalar=0.5, in1=zt[b][:], op0=ALU.mult, op1=ALU.mult),
                              r=[("y", 0, b), ("zt", b)], w=[("y", 0, b)])
                        S.add("act", lambda h: h.activation(out=junk[:], in_=y1[b][:], func=AF.Square, accum_out=st2[:, 0:1]), r=[("y", 0, b)], w=["sjunk", "ssq"])
                        rstd_ops(st2[:, 0:1], st2[:, 1:2], 1.0 / 768, ["ssq"], ["rstd"])
                        S.add("dve", lambda h: h.scalar_tensor_tensor(out=yo[b][:], in0=y1[b][:], scalar=st2[:, 1:2], in1=gssd[:], op0=ALU.mult, op1=ALU.mult),
                              r=[("y", 0, b), "rstd"], w=[("yo", b)])
                        for a in range(6):
                            S.add("pe", lambda h, a=a: h.transpose(out=ptr[:, a, :], in_=yo[b][:, a * 128:(a + 1) * 128], identity=ident_b[:]), r=[("yo", b)], w=["ptr"])
                        sg = stg[b]
                        S.add("act", lambda h: h.activation(out=sg[:], in_=ptr[:, 0:6, :], func=AF.Copy), r=["ptr"], w=[("stg", b)])
                        S.dma("pool", lambda h: h.dma_start(out=mixT[1280:2048, cs_].rearrange("(a d) t -> d a t", a=6), in_=sg[:]),
                              r=[("stg", b)], w=[("mixTs", c)])
                    cl = list(range(2 if last else 0, NT))
                    chunk(cl[0], 0)
                    for ci, c in enumerate(cl):
                        if ci + 1 < len(cl):
                            chunk(cl[ci + 1], ci + 1)
                        chunk_epi(c, ci)
                    S.phase_end("ssdmain")

        def phase_out(l):
            last = (l == nlayers - 1)
            with ExitStack() as st:
                Tl, Pl = mk_alloc(st)
                wo = Tl("wo", [128, 16, D], BF16)
                wv = I["w_out"][l].rearrange("(k p) n -> p k n", p=128)
                for k4 in range(4):
                    for nb in range(2):
                        S.dma("pool", lambda h, k4=k4, nb=nb: h.dma_start(out=wo[:, 4 * k4:4 * k4 + 4, nb * 1024:(nb + 1) * 1024],
                                                                        in_=wv[:, 4 * k4:4 * k4 + 4, nb * 1024:(nb + 1) * 1024]), w=[("wo", k4, nb)])
                G1 = Tl("G1", [128, D])
                gm = Tl("ogm", [128, D])
                sh = Tl("osh", [128, D])
                gp = Tl("ogp", [128, D])
                mT = [Tl("mT%d" % i, [128, 16, 128], BF16) for i in range(3)]
                xt = [Tl("oxt%d" % i, [128, D]) for i in range(3)]
                x1 = [Tl("ox1%d" % i, [128, D]) for i in range(3)]
                tmp = Tl("otmp", [128, D])
                hb = [Tl("ohb%d" % i, [128, D], BF16) for i in range(2)]
                junk = Tl("ojunk", [128, D], BF16)
                ssq = Tl("ossq", [128, 4])
                sq4 = Tl("osq4", [128, 24])
                stg = [Tl("ostg%d" % i, [128, 16, 128], BF16) for i in range(2)]
                psM = [Pl("psM%d" % i, [128, 512]) for i in range(4)]
                pt = [Pl("opt%d" % i, [128, 16, 128], BF16) for i in range(2)]

                def load_mods(r):
                    load_gate_mod(l, G1, gp, "o", "post_mix_g", 4096, r)
                    load_norm_mod(l, gm, sh, gp, "o", "pre_ffn_g", 4 * 2048, 3 * 2048, r)

                def stage1(tt, i):
                    o8 = 8 * (i % 3)
                    S.dma("sp", lambda h: h.dma_start(out=mT[i % 3][:], in_=mixT[:, tt * 128:(tt + 1) * 128].rearrange("(k p) t -> p k t", p=128)),
                          w=[("mT", i % 3)])
                    S.dma("sp", lambda h: h.dma_start(out=xt[i % 3][:], in_=xsrc(l, tt)), w=[("xt", i % 3)])
                    x1_ = x1[i % 3]
                    for cb in range(4):
                        for k in range(16):
                            S.add("pe", lambda h, cb=cb, k=k: h.matmul(psM[cb][:], lhsT=mT[i % 3][:, k, :], rhs=wo[:, k, cb * 512:(cb + 1) * 512],
                                                                      start=(k == 0), stop=(k == 15)),
                                  r=[("mT", i % 3), ("wo", k // 4, cb // 2)], w=[("psM", cb)])
                        S.add("act", lambda h, cb=cb: h.activation(out=junk[:, cb * 512:(cb + 1) * 512], in_=psM[cb][:], func=AF.Square,
                                                                   accum_out=sq4[:, o8 + cb:o8 + cb + 1]), r=[("psM", cb)], w=["ojunk", ("sq4", i % 3), ("sqd", cb)])
                        S.add("dve", lambda h, cb=cb: h.tensor_copy(out=x1_[:, cb * 512:(cb + 1) * 512], in_=psM[cb][:]), r=[("psM", cb), ("sqd", cb)], w=[("x1", i % 3)])

                def stage2(tt, i):
                    o8 = 8 * (i % 3)
                    x1_ = x1[i % 3]
                    S.add("dve", lambda h: h.tensor_reduce(out=sq4[:, o8 + 4:o8 + 5], in_=sq4[:, o8:o8 + 4], axis=AX.X, op=ALU.add), r=[("sq4", i % 3)], w=[("mss", i % 3)])
                    rstd_ops(sq4[:, o8 + 4:o8 + 5], sq4[:, o8 + 5:o8 + 6], 1.0 / D, [("mss", i % 3)], [("mrstd", i % 3)])
                    S.add("dve", lambda h: h.scalar_tensor_tensor(out=x1_[:], in0=x1_[:], scalar=sq4[:, o8 + 5:o8 + 6], in1=G1[:], op0=ALU.mult, op1=ALU.mult),
                          r=[("x1", i % 3), ("mrstd", i % 3), "oG"], w=[("x1", i % 3)])
                    S.add("dve", lambda h: h.tensor_tensor(out=x1_[:], in0=x1_[:], in1=xt[i % 3][:], op=ALU.add),
                          r=[("x1", i % 3), ("xt", i % 3)], w=[("x1", i % 3)])
                    S.dma("pool", lambda h: h.dma_start(out=x1s[tt * 128:(tt + 1) * 128, :], in_=x1_[:]), r=[("x1", i % 3)], w=[("x1s", tt)])

                    def dst(ptile, pkey, wkeys):
                        sg = stg[i % 2]
                        S.add("act", lambda h: h.activation(out=sg[:], in_=ptile[:], func=AF.Copy), r=[pkey], w=[("ostg", i % 2)])
                        S.dma("pool", lambda h: h.dma_start(out=h2T[:, tt * 128:(tt + 1) * 128].rearrange("(k p) t -> p k t", p=128), in_=sg[:]),
                              r=[("ostg", i % 2)], w=wkeys)
                    return norm_transpose_tile(l, tt, x1_[:], ("x1", i % 3), gm, sh, "o", tmp, hb[i % 2], junk, ssq, pt[i % 2], dst, [("h2T", tt)], i)

                def run_tiles(tts, i0):
                    pend = None
                    n_ = len(tts)
                    for j in range(min(2, n_)):
                        stage1(tts[j], i0 + j)
                    for j, tt in enumerate(tts):
                        if j + 2 < n_:
                            stage1(tts[j + 2], i0 + j + 2)
                        th = stage2(tt, i0 + j)
                        if pend is not None:
                            pend()
                        pend = th
                    pend()
                if not last:
                    load_mods(1)
                    run_tiles([0, 1], 0)
                load_mods(0)
                run_tiles(list(range(2, NT)), 2)
                S.phase_end("out")

        def phase_ffn(l):
            last = (l == nlayers - 1)
            if last:
                halves = [(256, 1024), (1280, 1024)]
            else:
                halves = [(0, 1152), (1152, 1152)]
            wg_v = I["w_gate"][l].rearrange("(k p) n -> p k n", p=128)
            wu_v = I["w_up"][l].rearrange("(k p) n -> p k n", p=128)
            wd_v = I["w_down"][l].rearrange("(k p) n -> p k n", p=128)
            NJ = DFF // 128
            for (t0, nt) in halves:
                with ExitStack() as st_o:
                    To, Po = mk_alloc(st_o)
                    aT = To("aT", [128, NJ, nt], BF16)
                    with ExitStack() as st:
                        Tl, Pl = mk_alloc(st)
                        hh_ = Tl("h2h", [128, 16, nt], BF16)
                        for k4 in range(4):
                            S.dma("sp", lambda h, k4=k4: h.dma_start(out=hh_[:, 4 * k4:4 * k4 + 4, :],
                                                                     in_=h2T[:, t0:t0 + nt].rearrange("(k p) t -> p k t", p=128)[:, 4 * k4:4 * k4 + 4, :]),
                                  w=[("h2h", k4)])
                        wg = [Tl("wg%d" % i, [128, 16, 256], BF16) for i in range(2)]
                        wu = [Tl("wu%d" % i, [128, 16, 256], BF16) for i in range(2)]
                        psg = [Pl("psg%d" % i, [128, 512]) for i in range(3)]
                        psu = [Pl("psu%d" % i, [128, 512]) for i in range(3)]
                        ee = [Tl("fe%d" % i, [128, 512]) for i in range(2)]
                        tg = [Tl("ftg%d" % i, [128, 512]) for i in range(2)]
                        tbs = [(a, min(512, nt - a)) for a in range(0, nt, 512)]
                        cnt = [0]

                        def wblock(jb):
                            b = jb % 2
                            for k4 in range(4):
                                S.dma("pool", lambda h, k4=k4: h.dma_start(out=wg[b][:, 4 * k4:4 * k4 + 4, :], in_=wg_v[:, 4 * k4:4 * k4 + 4, jb * 256:(jb + 1) * 256]),
                                      w=[("wg", b, k4)])
                                S.dma("pool", lambda h, k4=k4: h.dma_start(out=wu[b][:, 4 * k4:4 * k4 + 4, :], in_=wu_v[:, 4 * k4:4 * k4 + 4, jb * 256:(jb + 1) * 256]),
                                      w=[("wu", b, k4)])
                            for jj in range(2):
                                j = jb * 2 + jj
                                for (a, n) in tbs:
                                    i = cnt[0]
                                    cnt[0] += 1
                                    pg = psg[i % 3]
                                    pu = psu[i % 3]
                                    for k in range(16):
                                        S.add("pe", lambda h, k=k, pg=pg, a=a, n=n, jj=jj: h.matmul(pg[:, 0:n], lhsT=wg[b][:, k, jj * 128:(jj + 1) * 128],
                                                                                             rhs=hh_[:, k, a:a + n], start=(k == 0), stop=(k == 15)),
                                              r=[("wg", b, k // 4), ("h2h", k // 4)], w=[("psg", i % 3)])
                                    for k in range(16):
                                        S.add("pe", lambda h, k=k, pu=pu, a=a, n=n, jj=jj: h.matmul(pu[:, 0:n], lhsT=wu[b][:, k, jj * 128:(jj + 1) * 128],
                                                                                             rhs=hh_[:, k, a:a + n], start=(k == 0), stop=(k == 15)),
                                              r=[("wu", b, k // 4), ("h2h", k // 4)], w=[("psu", i % 3)])
                                    e_ = ee[i % 2]
                                    t_ = tg[i % 2]
                                    S.add("act", lambda h, pg=pg, e_=e_, n=n: h.activation(out=e_[:, 0:n], in_=pg[:, 0:n], func=AF.Tanh, scale=0.5),
                                          r=[("psg", i % 3)], w=[("fe", i % 2)])
                                    S.add("dve", lambda h, e_=e_, t_=t_, pg=pg, n=n: h.scalar_tensor_tensor(out=t_[:, 0:n], in0=e_[:, 0:n], scalar=1.0, in1=pg[:, 0:n],
                                                                                                     op0=ALU.add, op1=ALU.mult),
                                          r=[("fe", i % 2), ("psg", i % 3)], w=[("ftg", i % 2)])
                                    S.add("dve", lambda h, t_=t_, pu=pu, n=n, a=a, j=j: h.scalar_tensor_tensor(out=aT[:, j, a:a + n], in0=t_[:, 0:n], scalar=0.5, in1=pu[:, 0:n],
                                                                                                        op0=ALU.mult, op1=ALU.mult),
                                          r=[("ftg", i % 2), ("psu", i % 3)], w=[("aT", j, a)])
                        for jb in range(NJ // 2):
                            wblock(jb)
                        S.phase_end("ffnA")
                    with ExitStack() as st:
                        Tl, Pl = mk_alloc(st)
                        wd = [Tl("wd%d" % i, [128, NJ, 256], BF16) for i in range(2)]
                        psd = [Pl("psd%d" % i, [128, 512]) for i in range(4)]
                        stg = [Tl("fstg%d" % i, [128, 256]) for i in range(4)]
                        cnt = [0]

                        def dblock(cb):
                            b = cb % 2
                            for k4 in range(4):
                                S.dma("pool", lambda h, k4=k4: h.dma_start(out=wd[b][:, 11 * k4:11 * k4 + 11, :], in_=wd_v[:, 11 * k4:11 * k4 + 11, cb * 256:(cb + 1) * 256]),
                                      w=[("wd", b, k4)])
                            for a in range(0, nt, 128):
                                i = cnt[0]
                                cnt[0] += 1
                                p_ = psd[i % 4]
                                for k in range(NJ):
                                    S.add("pe", lambda h, k=k, p_=p_, a=a: h.matmul(p_[:, 0:256], lhsT=aT[:, k, a:a + 128], rhs=wd[b][:, k, :], start=(k == 0), stop=(k == NJ - 1)),
                                          r=[("wd", b, k // 11)], w=[("psd", i % 4)])
                                s_ = stg[i % 4]
                                if i % 2 == 0:
                                    S.add("act", lambda h, p_=p_, s_=s_: h.activation(out=s_[:], in_=p_[:, 0:256], func=AF.Copy), r=[("psd", i % 4)], w=[("fstg", i % 4)])
                                else:
                                    S.add("dve", lambda h, p_=p_, s_=s_: h.tensor_copy(out=s_[:], in_=p_[:, 0:256]), r=[("psd", i % 4)], w=[("fstg", i % 4)])
                                S.dma("sp", lambda h, s_=s_, a=a: h.dma_start(out=fsc[t0 + a:t0 + a + 128, cb * 256:(cb + 1) * 256], in_=s_[:]),
                                      r=[("fstg", i % 4)], w=[("fsc", i)])
                        for cb in range(8):
                            dblock(cb)
                        S.phase_end("ffnB")

        def phase_fin(l):
            last = (l == nlayers - 1)
            with ExitStack() as st:
                Tl, Pl = mk_alloc(st)
                G2 = Tl("G2", [128, D])
                gp = Tl("fgp", [128, D])
                xa = [Tl("fxa%d" % i, [128, D]) for i in range(3)]
                fa = [Tl("ffa%d" % i, [128, D]) for i in range(3)]
                xo = [Tl("fxo%d" % i, [128, D]) for i in range(3)]
                junk = Tl("fjunk", [128, D], BF16)
                ssq = Tl("fssq", [128, 2])

                def tile(tt, i):
                    S.dma("sp", lambda h: h.dma_start(out=xa[i % 3][:], in_=x1s[tt * 128:(tt + 1) * 128, :]), w=[("xa", i % 3)])
                    S.dma("sp", lambda h: h.dma_start(out=fa[i % 3][:], in_=fsc[tt * 128:(tt + 1) * 128, :]), w=[("fa", i % 3)])
                    S.add("act", lambda h: h.activation(out=junk[:], in_=fa[i % 3][:], func=AF.Square, accum_out=ssq[:, 0:1]), r=[("fa", i % 3)], w=["fjunk", "fss"])
                    rstd_ops(ssq[:, 0:1], ssq[:, 1:2], 1.0 / D, ["fss"], ["frstd"])
                    S.add("dve", lambda h: h.scalar_tensor_tensor(out=xo[i % 3][:], in0=fa[i % 3][:], scalar=ssq[:, 1:2], in1=G2[:], op0=ALU.mult, op1=ALU.mult),
                          r=[("fa", i % 3), "frstd", "fG"], w=[("xo", i % 3)])
                    S.add("dve", lambda h: h.tensor_tensor(out=xo[i % 3][:], in0=xo[i % 3][:], in1=xa[i % 3][:], op=ALU.add), r=[("xo", i % 3), ("xa", i % 3)], w=[("xo", i % 3)])
                    if last:
                        dst = out[(tt - 2) * 128:(tt - 1) * 128, :]
                    else:
                        dst = xnext[tt * 128:(tt + 1) * 128, :]
                    S.dma("pool", lambda h: h.dma_start(out=dst, in_=xo[i % 3][:]), r=[("xo", i % 3)], w=[("xout", tt)])
                i = 0
                if not last:
                    load_gate_mod(l, G2, gp, "f", "post_ffn_g", 5 * 2048, 1)
                    for tt in range(2):
                        tile(tt, i)
                        i += 1
                load_gate_mod(l, G2, gp, "f", "post_ffn_g", 5 * 2048, 0)
                for tt in range(2, NT):
                    tile(tt, i)
                    i += 1
                S.phase_end("fin")

        PH = {}
        PH["out"] = phase_out
        PH["ffn"] = phase_ffn
        PH["fin"] = phase_fin
        PH["ssd"] = phase_ssd
        PH["ret"] = phase_ret
        PH["att"] = phase_att

        PH["in"] = phase_in
        try:
            if phases is None or "mod" in phases:
                phase_mod(0)
            check_stop("mod")
            for l in (layers if layers is not None else range(nlayers)):
                for nm in ("in", "att", "ret", "ssd", "out", "ffn", "fin"):
                    if nm in PH and (phases is None or nm in phases):
                        PH[nm](l)
                        check_stop("%s%d" % (nm, l))
        except Stop:
            pass
        S.phase_end("final")
        stats = S.emit()
    return nc, stats, list(I.keys())


def make_in_maps(inputs):
    consts = make_consts()
    rope = make_rope()
    maps = []
    shared = {n: np.ascontiguousarray(inputs[n], dtype=np.float32) for n, _ in SMALL + BIG}
    for b in range(8):
        m = dict(shared)
        m["x"] = np.ascontiguousarray(inputs["x"][b])
        m["ctx"] = np.ascontiguousarray(inputs["ctx"][b])
        m["cvec"] = np.ascontiguousarray(np.stack([inputs["c"][b], inputs["c_ctx"]]))
        m["consts"] = consts
        m["rope"] = rope
        maps.append(m)
    return maps


def kernel(**inputs):
    nc, _, _ = build(nlayers=2, debug=False)
    maps = make_in_maps(inputs)
    res = run_bass_kernel_spmd(nc, maps, core_ids=list(range(8)))
    return np.stack([np.asarray(r["out"], dtype=np.float32) for r in res.results], axis=0)
```

```python
import math
import numpy as np
from contextlib import ExitStack
import concourse.bass as bass
import concourse.mybir as mybir
from concourse.bass_utils import run_bass_kernel_spmd

F32 = mybir.dt.float32
BF16 = mybir.dt.bfloat16
AF = mybir.ActivationFunctionType
ALU = mybir.AluOpType
AX = mybir.AxisListType

D = 2048
T = 2304
NT = 18
DIN = 5400
DFF = 5632
EPS = 1e-6
NCONST = 10 * 128 + 2
PW = 4120


class _Op:
    __slots__ = ("eng", "fn", "deps", "ch", "pos", "is_dma", "vc", "waits", "signal", "rank")


class Sched:
    ENGS = ("pe", "act", "dve", "pool", "sp")

    def __init__(self, nc, stack, n_dma_sems=12):
        self.nc = nc
        self.h = {"pe": nc.tensor, "act": nc.scalar, "dve": nc.vector, "pool": nc.gpsimd, "sp": nc.sync}
        self.ops = []
        self.n_emitted = 0
        self.eng_pos = {e: 0 for e in self.ENGS}
        self.last_w = {}
        self.readers = {}
        self.esem = {e: stack.enter_context(nc.semaphore("s_" + e)) for e in self.ENGS}
        self.dsems = {}
        self.dcount = {}
        self.drr = {}
        for q in ("sp", "pool"):
            self.dsems[q] = [stack.enter_context(nc.semaphore("d_%s%d" % (q, i))) for i in range(n_dma_sems)]
            self.drr[q] = 0
            for i in range(n_dma_sems):
                self.dcount[(q, i)] = 0
        self.by_chpos = {}
        self.last_on_ch = {}
        self.clock = {e: {} for e in self.ENGS}
        self.rk = {e: 0 for e in self.ENGS}
        self.nw = 0

    def _deps(self, r, w):
        deps = []
        for k in r:
            o = self.last_w.get(k)
            if o is not None:
                deps.append(o)
        for k in w:
            o = self.last_w.get(k)
            if o is not None:
                deps.append(o)
            deps.extend(self.readers.get(k, ()))
        return deps

    def _commit(self, op, r, w):
        for k in r:
            self.readers.setdefault(k, []).append(op)
        for k in w:
            self.last_w[k] = op
            self.readers[k] = []
        self.ops.append(op)
        self.by_chpos[(op.ch, op.pos)] = op
        self.last_on_ch[op.ch] = op

    def add(self, eng, fn, r=(), w=()):
        op = _Op()
        op.eng = eng
        op.fn = fn
        op.is_dma = False
        op.deps = self._deps(r, w)
        self.eng_pos[eng] += 1
        op.ch = eng
        op.pos = self.eng_pos[eng]
        op.signal = False
        self._commit(op, r, w)
        return op

    def dma(self, q, fn, r=(), w=()):
        op = _Op()
        op.eng = q
        op.fn = fn
        op.is_dma = True
        op.deps = self._deps(r, w)
        i = self.drr[q]
        self.drr[q] = (i + 1) % len(self.dsems[q])
        self.dcount[(q, i)] += 1
        op.ch = ("d", q, i)
        op.pos = self.dcount[(q, i)]
        op.signal = True
        self._commit(op, r, w)
        return op

    def barrier(self):
        lasts = [o for o in self.last_on_ch.values() if o.fn is not None or o.is_dma]
        for e in self.ENGS:
            op = _Op()
            op.eng = e
            op.fn = None
            op.is_dma = False
            op.deps = list(lasts)
            self.eng_pos[e] += 1
            op.ch = e
            op.pos = self.eng_pos[e]
            op.signal = False
            self.ops.append(op)
            self.by_chpos[(op.ch, op.pos)] = op
        self.last_w = {}
        self.readers = {}

    def emit(self):
        ops = self.ops[self.n_emitted:]
        clock = self.clock
        for op in ops:
            E = op.eng
            ck = clock[E]
            need = {}
            for d in op.deps:
                if (not d.is_dma) and d.eng == "pe" and E == "pe" and (not op.is_dma) and op.fn is not None:
                    continue
                if ck.get(d.ch, 0) < d.pos and need.get(d.ch, 0) < d.pos:
                    need[d.ch] = d.pos
            if op.is_dma and op.pos > 1:
                if ck.get(op.ch, 0) < op.pos - 1 and need.get(op.ch, 0) < op.pos - 1:
                    need[op.ch] = op.pos - 1
            op.waits = []
            for ch, pos in need.items():
                if ck.get(ch, 0) >= pos:
                    continue
                p = self.by_chpos[(ch, pos)]
                p.signal = True
                op.waits.append(p)
                for c, v in p.vc.items():
                    if ck.get(c, 0) < v:
                        ck[c] = v
            vc = dict(ck)
            vc[op.ch] = op.pos
            op.vc = vc
        for op in ops:
            if op.is_dma:
                op.rank = 16 * op.pos
            elif op.signal:
                self.rk[op.eng] += 1
                op.rank = self.rk[op.eng]
        for op in ops:
            h = self.h[op.eng]
            for p in op.waits:
                if p.is_dma:
                    sem = self.dsems[p.ch[1]][p.ch[2]]
                else:
                    sem = self.esem[p.eng]
                h.wait_ge(sem, p.rank)
                self.nw += 1
            if op.fn is None:
                continue
            inst = op.fn(h)
            if op.is_dma:
                inst.then_inc(self.dsems[op.ch[1]][op.ch[2]], 16)
            elif op.signal:
                inst.then_inc(self.esem[op.eng], 1)
            op.fn = None
        self.n_emitted = len(self.ops)
        return dict(n_ops=len(self.ops), n_waits=self.nw, ranks=dict(self.rk))

    def phase_end(self, name=""):
        npe = sum(1 for o in self.ops if o.eng == "pe" and not o.is_dma and (o.fn is not None))
        self.phase_log = getattr(self, "phase_log", [])
        self.pe_total = getattr(self, "pe_total", 0) + npe
        self.phase_log.append((name, self.pe_total))
        self.barrier()
        r = self.emit()
        self.ops = []
        self.n_emitted = 0
        return r


def make_consts():
    i = np.arange(128)
    J, I = np.meshgrid(i, i, indexing="ij")
    c = np.zeros((128, NCONST), np.float32)
    c[:, 0:128] = (J == I)
    c[:, 128:256] = (J <= I)
    c[:, 256:384] = (J >= I)
    c[:, 384:512] = 1.0
    c[:, 512:640] = np.where(I >= J, 0.0, -30000.0)
    c[:, 640:768] = np.where(I <= J, 0.0, -30000.0)
    c[:, 768:896] = np.maximum(I - J, 0)
    c[:, 896:1024] = np.maximum(J - I, 0)
    c[:, 1024:1152] = I + 1
    c[:, 1152:1280] = 128 - I
    c[:, 1280] = 127 - i
    c[:, 1281] = i
    return c


def make_rope():
    rows = 2048 // 64
    row = np.repeat(np.arange(rows, dtype=np.float32), 64)
    col = np.tile(np.arange(64, dtype=np.float32), rows)
    n_freq = 32
    inv = (np.float32(10000.0) ** (-np.arange(n_freq, dtype=np.float32) / n_freq)).astype(np.float32)
    ang = np.concatenate([row[:, None] * inv, col[:, None] * inv], axis=-1).astype(np.float32)
    return np.concatenate([np.cos(ang), np.sin(ang)], axis=-1).astype(np.float32)


SMALL = [("b_mod", [2, 12288]), ("pre_mix_g", [2, 2048]), ("post_mix_g", [2, 2048]), ("pre_ffn_g", [2, 2048]),
         ("post_ffn_g", [2, 2048]), ("q_norm_g", [2, 128]), ("k_norm_g", [2, 128]), ("ret_decay_f", [2, 4]),
         ("ret_decay_b", [2, 4]), ("conv_w", [2, 5, 1280]), ("conv_b", [2, 1280]), ("dt_bias_f", [2, 12]),
         ("dt_bias_b", [2, 12]), ("a_log_f", [2, 12]), ("a_log_b", [2, 12]), ("d_skip", [2, 12]),
         ("ssd_norm_g", [2, 768])]
BIG = [("w_mod", [2, 2048, 12288]), ("w_in", [2, 2048, DIN]), ("w_out", [2, 2048, 2048]),
       ("w_gate", [2, 2048, DFF]), ("w_up", [2, 2048, DFF]), ("w_down", [2, DFF, 2048])]


def build(nlayers=2, debug=False, stop=None, phases=None, feed=(), layers=None):
    nc = bass.Bass("TRN2", target_bir_lowering=False)
    SHAPES = dict(SMALL + BIG)
    SHAPES.update({"x": [2048, 2048], "ctx": [256, 2048], "cvec": [2, 2048], "consts": [128, NCONST], "rope": [2048, 128]})

    class LazyIn(dict):
        def __missing__(self, name):
            ap = nc.dram_tensor(name, SHAPES[name], F32, kind="ExternalInput").ap()
            self[name] = ap
            return ap

    I = LazyIn()
    if not debug:
        for n in ["x", "ctx", "cvec", "consts", "rope"] + [n for n, _ in SMALL + BIG]:
            I[n]
    out = nc.dram_tensor("out", [2048, 2048], F32, kind="ExternalOutput").ap()
    skind = "ExternalOutput" if debug else "Internal"

    def scr(name, shape, dt=F32):
        k = "ExternalInput" if name in feed else skind
        return nc.dram_tensor(name, shape, dt, kind=k).ap()

    modv = scr("modv", [2, 2, 12288])
    proj = scr("proj", [T, PW])
    xbcT = scr("xbcT", [1280, T])
    mixT = scr("mixT", [2048, T], BF16)
    x1s = scr("x1s", [T, D])
    h2T = scr("h2T", [2048, T], BF16)
    fsc = scr("fsc", [T, D])
    xnext = scr("xnext", [T, D])
    zsd = scr("zsd", [T, 768], BF16)

    class Stop(Exception):
        pass

    with ExitStack() as gst:
        S = Sched(nc, gst)

        uid = [0]

        def mk_alloc(st):
            def Tl(name, shape, dt=F32):
                uid[0] += 1
                return st.enter_context(nc.sbuf_tensor("%s_%d" % (name, uid[0]), shape, dt))

            def Pl(name, shape, dt=F32):
                uid[0] += 1
                return st.enter_context(nc.psum_tensor("%s_%d" % (name, uid[0]), shape, dt))
            return Tl, Pl

        GT, GP = mk_alloc(gst)
        cst = GT("cst", [128, NCONST])
        ident_b = GT("ident_b", [128, 128], BF16)
        ones_b = GT("ones_b", [128, 128], BF16)
        S.dma("sp", lambda h: h.dma_start(out=cst[:], in_=I["consts"][:, :]), w=["cst"])
        S.add("dve", lambda h: h.tensor_copy(out=ident_b[:], in_=cst[:, 0:128]), r=["cst"], w=["ident_b"])
        S.add("dve", lambda h: h.tensor_copy(out=ones_b[:], in_=cst[:, 384:512]), r=["cst"], w=["ones_b"])
        ident_f = cst[:, 0:128]
        tri_f = cst[:, 128:256]
        tri_b = cst[:, 256:384]
        ones_f = cst[:, 384:512]
        mneg = [cst[:, 512:640], cst[:, 640:768]]
        D1 = cst[:, 768:896]
        D2 = cst[:, 896:1024]
        rowidx = [cst[:, 1024:1152], cst[:, 1152:1280]]
        colexp = [cst[:, 1280:1281], cst[:, 1281:1282]]
        S.phase_end("init")

        def check_stop(tag):
            if stop == tag:
                raise Stop()

        def rstd_ops(ssq_ap, out_ap, inv_n, rk, wk):
            S.add("act", lambda h: h.activation(out=out_ap, in_=ssq_ap, func=AF.Ln, scale=inv_n, bias=EPS), r=rk, w=wk)
            S.add("act", lambda h: h.activation(out=out_ap, in_=out_ap, func=AF.Exp, scale=-0.5), r=wk, w=wk)

        def xsrc(l, tt):
            if l == 0:
                if tt < 2:
                    return I["ctx"][tt * 128:(tt + 1) * 128, :]
                return I["x"][(tt - 2) * 128:(tt - 1) * 128, :]
            return xnext[tt * 128:(tt + 1) * 128, :]

        def bcast_row(ap_row, n):
            return ap_row.to_broadcast([128, n])

        def mod_task(l, Tl, Pl, blocks=range(24)):
            cT = Tl("cT", [128, 16, 2])
            ce = Tl("ce", [128, 16, 2])
            sT = Tl("sT", [128, 16, 2], BF16)
            for r_ in range(2):
                S.dma("sp", lambda h, r_=r_: h.dma_start(out=cT[:, :, r_], in_=I["cvec"][r_].rearrange("(k p) -> p k", p=128),
                                                         allow_slow_non_contiguous=True), w=["mcT"])
            S.add("act", lambda h: h.activation(out=ce[:], in_=cT[:], func=AF.Exp, scale=-1.0), r=["mcT"], w=["mce"])
            S.add("dve", lambda h: h.tensor_scalar(out=ce[:], in0=ce[:], scalar1=1.0, scalar2=None, op0=ALU.add), r=["mce"], w=["mce"])
            S.add("dve", lambda h: h.reciprocal(out=ce[:], in_=ce[:]), r=["mce"], w=["mce"])
            S.add("dve", lambda h: h.tensor_tensor(out=sT[:], in0=cT[:], in1=ce[:], op=ALU.mult), r=["mce", "mcT"], w=["msT"])
            wb = [Tl("wmb%d" % i, [128, 16, 512], BF16) for i in range(2)]
            pm = Pl("pm", [128, 512])
            bsb = [Tl("bsb%d" % i, [2, 512]) for i in range(2)]
            msb = [Tl("msb%d" % i, [2, 512]) for i in range(2)]
            wv = I["w_mod"][l].rearrange("(k p) n -> p k n", p=128)
            yield

            def block(nb):
                b = nb % 2
                S.dma("sp", lambda h: h.dma_start(out=bsb[b][:], in_=I["b_mod"][l:l + 1, nb * 512:(nb + 1) * 512].to_broadcast([2, 512])), w=[("mbsb", b)])
                for k4 in range(4):
                    S.dma("pool", lambda h, k4=k4: h.dma_start(out=wb[b][:, 4 * k4:4 * k4 + 4, :], in_=wv[:, 4 * k4:4 * k4 + 4, nb * 512:(nb + 1) * 512]),
                          w=[("wmb", b, k4)])
                for k in range(16):
                    S.add("pe", lambda h, k=k: h.matmul(pm[0:2, :], lhsT=sT[:, k, :], rhs=wb[b][:, k, :], start=(k == 0), stop=(k == 15)),
                          r=["msT", ("wmb", b, k // 4)], w=["mpm"])
                S.add("dve", lambda h: h.tensor_tensor(out=msb[b][:], in0=pm[0:2, :], in1=bsb[b][:], op=ALU.add), r=["mpm", ("mbsb", b)], w=[("mmsb", b)])
                S.dma("sp", lambda h: h.dma_start(out=modv[l, :, nb * 512:(nb + 1) * 512], in_=msb[b][:]), r=[("mmsb", b)], w=[("modv", l, nb)])
            for nb in blocks:
                block(nb)
                yield

        def phase_mod(l):
            with ExitStack() as st:
                Tl, Pl = mk_alloc(st)
                for _ in mod_task(l, Tl, Pl, blocks=range(8)):
                    pass
                S.phase_end("mod")

        def norm_mod_tiles(l, Tl, tagp, gname, sc_off, sh_off, r):
            gm = Tl(tagp + "gm", [128, D])
            sh = Tl(tagp + "sh", [128, D])
            gp = Tl(tagp + "gp", [128, D])
            return gm, sh, gp

        def load_norm_mod(l, gm, sh, gp, key, gname, sc_off, sh_off, r):
            S.dma("sp", lambda h: h.dma_start(out=gp[:], in_=bcast_row(I[gname][l:l + 1, :], D)), w=[key + "gp"])
            S.dma("sp", lambda h: h.dma_start(out=gm[:], in_=bcast_row(modv[l, r:r + 1, sc_off:sc_off + D], D)), w=[key + "gm"])
            S.dma("sp", lambda h: h.dma_start(out=sh[:], in_=bcast_row(modv[l, r:r + 1, sh_off:sh_off + D], D)), w=[key + "sh"])
            S.add("dve", lambda h: h.scalar_tensor_tensor(out=gm[:], in0=gm[:], scalar=1.0, in1=gp[:], op0=ALU.add, op1=ALU.mult),
                  r=[key + "gp", key + "gm"], w=[key + "gm"])

        def load_gate_mod(l, G, gp, key, gname, g_off, r):
            S.dma("sp", lambda h: h.dma_start(out=gp[:], in_=bcast_row(I[gname][l:l + 1, :], D)), w=[key + "gp"])
            S.dma("sp", lambda h: h.dma_start(out=G[:], in_=bcast_row(modv[l, r:r + 1, g_off:g_off + D], D)), w=[key + "G"])
            S.add("dve", lambda h: h.tensor_tensor(out=G[:], in0=G[:], in1=gp[:], op=ALU.mult), r=[key + "gp", key + "G"], w=[key + "G"])

        def norm_transpose_tile(l, tt, xt_ap, xkey, gm, sh, mkey, tmp, hb, junk, ssq, pt, dst_fn, wkeys, i):
            sq = ssq[:, 2 * (i % 2):2 * (i % 2) + 1]
            rs = ssq[:, 2 * (i % 2) + 1:2 * (i % 2) + 2]
            S.add("act", lambda h: h.activation(out=junk[:], in_=xt_ap, func=AF.Square, accum_out=sq),
                  r=[xkey], w=["junk", ("ssq", i % 2)])
            rstd_ops(sq, rs, 1.0 / D, [("ssq", i % 2)], [("rstd", i % 2)])
            S.add("dve", lambda h: h.scalar_tensor_tensor(out=tmp[:], in0=xt_ap, scalar=rs, in1=gm[:], op0=ALU.mult, op1=ALU.mult),
                  r=[xkey, ("rstd", i % 2), mkey + "gm"], w=["tmp"])
            S.add("dve", lambda h: h.tensor_tensor(out=hb[:], in0=tmp[:], in1=sh[:], op=ALU.add), r=["tmp", mkey + "sh"], w=[("hb", i % 2)])
            def later():
                for k in range(16):
                    S.add("pe", lambda h, k=k: h.transpose(out=pt[:, k, :], in_=hb[:, k * 128:(k + 1) * 128], identity=ident_b[:]),
                          r=[("hb", i % 2)], w=[("pt", i % 2)])
                dst_fn(pt, ("pt", i % 2), wkeys)
            return later

        def phase_in(l):
            with ExitStack() as st_o:
                To, Po = mk_alloc(st_o)
                hT = To("hT", [128, 16, T], BF16)
                with ExitStack() as st:
                    Tl, Pl = mk_alloc(st)
                    mods = {}
                    for r, nm in ((1, "c"), (0, "l")):
                        gm = Tl("gm" + nm, [128, D])
                        sh = Tl("sh" + nm, [128, D])
                        gp = Tl("gp" + nm, [128, D])
                        load_norm_mod(l, gm, sh, gp, nm, "pre_mix_g", 2048, 0, r)
                        mods[r] = (gm, sh, nm)
                    xt = [Tl("xt%d" % i, [128, D]) for i in range(3)]
                    tmp = Tl("tmp", [128, D])
                    hb = [Tl("hb%d" % i, [128, D], BF16) for i in range(2)]
                    junk = Tl("junk", [128, D], BF16)
                    ssq = Tl("ssq", [128, 4])
                    pt = [Pl("pt%d" % i, [128, 16, 128], BF16) for i in range(2)]
                    pend = None
                    for tt in range(NT):
                        i = tt
                        gm, sh, nm = mods[1 if tt < 2 else 0]
                        S.dma("sp", lambda h, tt=tt, i=i: h.dma_start(out=xt[i % 3][:], in_=xsrc(l, tt)), w=[("xt", i % 3)])

                        def dst(ptile, pkey, wkeys, tt=tt):
                            S.add("act", lambda h: h.activation(out=hT[:, :, tt * 128:(tt + 1) * 128], in_=ptile[:], func=AF.Copy),
                                  r=[pkey], w=wkeys)
                        th = norm_transpose_tile(l, tt, xt[i % 3][:], ("xt", i % 3), gm, sh, nm, tmp, hb[i % 2], junk, ssq, pt[i % 2], dst,
                                                 [("hT", tt)], i)
                        if pend is not None:
                            pend()
                        pend = th
                    pend()
                    S.phase_end("P1")
                with ExitStack() as st:
                    Tl, Pl = mk_alloc(st)
                    wb = [Tl("wib%d" % i, [128, 16, 512], BF16) for i in range(2)]
                    stg = [Tl("stg%d" % i, [128, 512]) for i in range(4)]
                    pp = [Pl("pp%d" % i, [128, 512]) for i in range(4)]
                    wv = I["w_in"][l].rearrange("(k p) n -> p k n", p=128)
                    cnt = [0]

                    def evac(ps_ap, n, dram_ap, pkey):
                        i = cnt[0]
                        cnt[0] += 1
                        s = stg[i % 4]
                        if i % 2 == 0:
                            S.add("act", lambda h: h.activation(out=s[:, 0:n], in_=ps_ap, func=AF.Copy), r=[pkey], w=[("stg", i % 4)])
                        else:
                            S.add("dve", lambda h: h.tensor_copy(out=s[:, 0:n], in_=ps_ap), r=[pkey], w=[("stg", i % 4)])
                        S.dma("sp", lambda h: h.dma_start(out=dram_ap, in_=s[:, 0:n]), r=[("stg", i % 4)], w=[("dram", i)])

                    blocks = [(c0, 512) for c0 in range(0, 5120, 512)] + [(5120, 280)]
                    pi = [0]
                    bg = mod_task(0, Tl, Pl, blocks=range(8, 24)) if (l == 0 and (phases is None or "mod" in phases)) else iter(())
                    next(bg, None)
                    for bi, (c0, ncol) in enumerate(blocks):
                        next(bg, None)
                        next(bg, None)
                        b = bi % 2
                        for k4 in range(4):
                            S.dma("pool", lambda h, b=b, k4=k4, c0=c0, ncol=ncol: h.dma_start(
                                out=wb[b][:, 4 * k4:4 * k4 + 4, 0:ncol], in_=wv[:, 4 * k4:4 * k4 + 4, c0:c0 + ncol]), w=[("wib", b, k4)])
                        wkeys = [("wib", b, k4) for k4 in range(4)]
                        if c0 < 4096:
                            tm = (0, ncol, c0)
                            fm_chunks = []
                        elif c0 < 5120:
                            tm = None
                            fm_chunks = [(j, (c0 - 4096) // 128 + j) for j in range(4)]
                        else:
                            tm = (256, 24, 4096)
                            fm_chunks = [(0, 8), (1, 9)]
                        if tm is not None:
                            co, n, dc = tm
                            for tt in range(NT):
                                p = pi[0] % 4
                                pi[0] += 1
                                for k in range(16):
                                    S.add("pe", lambda h, p=p, k=k, tt=tt, co=co, n=n, b=b: h.matmul(
                                        pp[p][:, 0:n], lhsT=hT[:, k, tt * 128:(tt + 1) * 128], rhs=wb[b][:, k, co:co + n],
                                        start=(k == 0), stop=(k == 15)), r=wkeys, w=[("pp", p)])
                                evac(pp[p][:, 0:n], n, proj[tt * 128:(tt + 1) * 128, dc:dc + n], ("pp", p))
                        for (j, ch) in fm_chunks:
                            for t0 in range(0, T, 512):
                                n = min(512, T - t0)
                                p = pi[0] % 4
                                pi[0] += 1
                                for k in range(16):
                                    S.add("pe", lambda h, p=p, k=k, t0=t0, n=n, j=j, b=b: h.matmul(
                                        pp[p][:, 0:n], lhsT=wb[b][:, k, j * 128:(j + 1) * 128], rhs=hT[:, k, t0:t0 + n],
                                        start=(k == 0), stop=(k == 15)), r=wkeys, w=[("pp", p)])
                                evac(pp[p][:, 0:n], n, xbcT[ch * 128:(ch + 1) * 128, t0:t0 + n], ("pp", p))
                    for _ in bg:
                        pass
                    S.phase_end("P2")

        def rope_ops(src, dst, cs, nh, lat, skey, dkey, cskey, tmps):
            if not lat:
                S.add("dve", lambda h: h.tensor_copy(out=dst, in_=src), r=[skey], w=[dkey, (dkey, "b")])
                return
            t1, t2, t3, t4 = tmps
            s4 = src.rearrange("p (h i two) -> p h i two", h=nh, two=2)
            d4 = dst.rearrange("p (h i two) -> p h i two", h=nh, two=2)
            x1 = s4[:, :, :, 0]
            x2 = s4[:, :, :, 1]
            cosb = cs[:, 0:64].unsqueeze(1).to_broadcast([128, nh, 64])
            sinb = cs[:, 64:128].unsqueeze(1).to_broadcast([128, nh, 64])
            v = lambda t: t[:, 0:nh * 64].rearrange("p (h i) -> p h i", h=nh)
            S.add("dve", lambda h: h.tensor_tensor(out=v(t1), in0=x1, in1=cosb, op=ALU.mult), r=[skey, cskey], w=["rt1"])
            S.add("dve", lambda h: h.tensor_tensor(out=v(t2), in0=x2, in1=sinb, op=ALU.mult), r=[skey, cskey], w=["rt2"])
            S.add("dve", lambda h: h.tensor_tensor(out=d4[:, :, :, 0], in0=v(t1), in1=v(t2), op=ALU.subtract), r=["rt1", "rt2"], w=[dkey])
            S.add("dve", lambda h: h.tensor_tensor(out=v(t3), in0=x1, in1=sinb, op=ALU.mult), r=[skey, cskey], w=["rt3"])
            S.add("dve", lambda h: h.tensor_tensor(out=v(t4), in0=x2, in1=cosb, op=ALU.mult), r=[skey, cskey], w=["rt4"])
            S.add("dve", lambda h: h.tensor_tensor(out=d4[:, :, :, 1], in0=v(t3), in1=v(t4), op=ALU.add), r=["rt3", "rt4"], w=[(dkey, "b")])

        def phase_att(l):
            last = (l == nlayers - 1)
            with ExitStack() as st:
                Tl, Pl = mk_alloc(st)
                QKT = Tl("QKT", [128, 8, T], BF16)
                Vtm = Tl("Vtm", [128, NT, 256], BF16)
                gqk = Tl("gqk", [128, 8, 128])
                pr = [Tl("pr%d" % i, [128, 1280]) for i in range(2)]
                sq = Tl("sq", [128, 1024])
                qn = Tl("qn", [128, 1024])
                tmps = [Tl("rt%d" % i, [128, 512]) for i in range(4)]
                qr = [Tl("qr%d" % i, [128, 1024], BF16) for i in range(2)]
                cs = [Tl("cs%d" % i, [128, 128]) for i in range(2)]
                st8 = Tl("st8", [128, 16])
                ptq = Pl("ptq", [128, 8, 128], BF16)
                for hh in range(8):
                    src = I["q_norm_g"] if hh < 6 else I["k_norm_g"]
                    S.dma("sp", lambda h, hh=hh, src=src: h.dma_start(out=gqk[:, hh, :], in_=src[l:l + 1, :].to_broadcast([128, 128])), w=["gqk"])
                S.add("dve", lambda h: h.tensor_scalar(out=gqk[:, 0:6, :], in0=gqk[:, 0:6, :], scalar1=float(128 ** -0.5), scalar2=None, op0=ALU.mult),
                      r=["gqk"], w=["gqk"])
                def prep_tile(tt):
                    i = tt
                    lat = tt >= 2
                    p_ = pr[i % 2]
                    S.dma("sp", lambda h, tt=tt, p_=p_: h.dma_start(out=p_[:], in_=proj[tt * 128:(tt + 1) * 128, 0:1280]), w=[("pr", i % 2)])
                    if lat:
                        S.dma("sp", lambda h, tt=tt, i=i: h.dma_start(out=cs[i % 2][:], in_=I["rope"][(tt - 2) * 128:(tt - 1) * 128, :]), w=[("cs", i % 2)])
                    S.add("act", lambda h, p_=p_: h.activation(out=sq[:], in_=p_[:, 0:1024], func=AF.Square), r=[("pr", i % 2)], w=["sq"])
                    S.add("dve", lambda h: h.tensor_reduce(out=st8[:, 0:8], in_=sq[:].rearrange("p (h d) -> p h d", h=8), axis=AX.X, op=ALU.add),
                          r=["sq"], w=["ssq8"])
                    rstd_ops(st8[:, 0:8], st8[:, 8:16], 1.0 / 128, ["ssq8"], ["rstd8"])
                    S.add("dve", lambda h, p_=p_: h.tensor_tensor(out=qn[:].rearrange("p (h d) -> p h d", h=8),
                                                                 in0=p_[:, 0:1024].rearrange("p (h d) -> p h d", h=8),
                                                                 in1=st8[:, 8:16].unsqueeze(2).to_broadcast([128, 8, 128]), op=ALU.mult),
                          r=[("pr", i % 2), "rstd8"], w=["qn"])
                    S.add("dve", lambda h: h.tensor_tensor(out=qn[:], in0=qn[:], in1=gqk[:].rearrange("p h d -> p (h d)"), op=ALU.mult),
                          r=["qn", "gqk"], w=["qn"])
                    q_ = qr[i % 2]
                    rope_ops(qn[:], q_[:], cs[i % 2], 8, lat, "qn", ("qr", i % 2), ("cs", i % 2), tmps)
                    for hh in range(8):
                        S.add("pe", lambda h, hh=hh, q_=q_: h.transpose(out=ptq[:, hh, :], in_=q_[:, hh * 128:(hh + 1) * 128], identity=ident_b[:]),
                              r=[("qr", i % 2), (("qr", i % 2), "b")], w=["ptq"])
                    S.add("act", lambda h, tt=tt: h.activation(out=QKT[:, :, tt * 128:(tt + 1) * 128], in_=ptq[:], func=AF.Copy), r=["ptq"], w=[("QKT", tt)])
                    S.add("act", lambda h, tt=tt, p_=p_: h.activation(out=Vtm[:, tt, :], in_=p_[:, 1024:1280], func=AF.Copy), r=[("pr", i % 2)], w=[("Vtm", tt)])
                for tt in range(NT):
                    prep_tile(tt)
                NS = 3 if last else 2
                ps_s = [Pl("ps_s%d" % i, [128, 512]) for i in range(NS)]
                ps_o = [Pl("ps_o%d" % i, [128, 512]) for i in range(2)]
                ps_d = [Pl("ps_d%d" % i, [128, 512]) for i in range(2)]
                pT = [Tl("pT%d" % i, [128, 512], BF16) for i in range(3)]
                rden = Tl("rden", [128, 512])
                oT = [Tl("oT%d" % i, [128, 512], BF16) for i in range(2)]
                qblocks = [(256 + qb * 512, 512, list(range(NT))) for qb in range(4)]
                if not last:
                    qblocks = [(0, 256, [0, 1])] + qblocks
                gi = 0
                si = 0
                def att_head(q0, n, kts, hh, gi):
                        nonlocal si
                        g = hh // 3
                        o = gi % 2
                        nk = len(kts)
                        slots = []

                        def do_s(j):
                            nonlocal si
                            sidx = si
                            si += 1
                            kt = kts[j]
                            S.add("pe", lambda h, sidx=sidx, kt=kt: h.matmul(ps_s[sidx % NS][:, 0:n], lhsT=QKT[:, 6 + g, kt * 128:(kt + 1) * 128],
                                                                           rhs=QKT[:, hh, q0:q0 + n], start=True, stop=True),
                                  r=[("QKT", kt)] + [("QKT", q0 // 128 + a) for a in range(n // 128)], w=[("ps_s", sidx % NS)])
                            S.add("act", lambda h, sidx=sidx: h.activation(out=pT[sidx % 3][:, 0:n], in_=ps_s[sidx % NS][:, 0:n], func=AF.Exp),
                                  r=[("ps_s", sidx % NS)], w=[("pT", sidx % 3)])
                            slots.append(sidx)

                        def do_o(j):
                            sidx = slots[j]
                            kt = kts[j]
                            S.add("pe", lambda h, sidx=sidx, kt=kt: h.matmul(ps_o[o][:, 0:n], lhsT=Vtm[:, kt, g * 128:(g + 1) * 128], rhs=pT[sidx % 3][:, 0:n],
                                                                           start=(j == 0), stop=(j == nk - 1)),
                                  r=[("pT", sidx % 3), ("Vtm", kt)], w=[("ps_o", o)])
                            S.add("pe", lambda h, sidx=sidx: h.matmul(ps_d[o][:, 0:n], lhsT=ones_b[:], rhs=pT[sidx % 3][:, 0:n],
                                                                    start=(j == 0), stop=(j == nk - 1)),
                                  r=[("pT", sidx % 3)], w=[("ps_d", o)])
                        la_ = NS - 1
                        for j in range(min(la_, nk)):
                            do_s(j)
                        for j in range(nk):
                            if j + la_ < nk:
                                do_s(j + la_)
                            do_o(j)
                        S.add("dve", lambda h, o=o: h.reciprocal(out=rden[:, 0:n], in_=ps_d[o][:, 0:n]), r=[("ps_d", o)], w=["rden"])
                        S.add("dve", lambda h, o=o: h.tensor_tensor(out=oT[o][:, 0:n], in0=ps_o[o][:, 0:n], in1=rden[:, 0:n], op=ALU.mult),
                              r=[("ps_o", o), "rden"], w=[("oT", o)])
                        S.dma("sp", lambda h, o=o, hh=hh, q0=q0, n=n: h.dma_start(out=mixT[hh * 128:(hh + 1) * 128, q0:q0 + n], in_=oT[o][:, 0:n]),
                              r=[("oT", o)], w=[("mixT", gi)])
                bg = mod_task(l + 1, Tl, Pl) if (l + 1 < nlayers and (phases is None or "mod" in phases)) else iter(())
                next(bg, None)
                for (q0, n, kts) in qblocks:
                    for hh in range(6):
                        att_head(q0, n, kts, hh, gi)
                        gi += 1
                        next(bg, None)
                for _ in bg:
                    pass
                S.phase_end("att")

        def silu2_ops(src, e, dst, skey, ekey, dkey, in_scale=0.5):
            S.add("act", lambda h: h.activation(out=e, in_=src, func=AF.Tanh, scale=in_scale), r=[skey], w=[ekey])
            S.add("dve", lambda h: h.scalar_tensor_tensor(out=dst, in0=e, scalar=1.0, in1=src, op0=ALU.add, op1=ALU.mult), r=[skey, ekey], w=[dkey])

        def phase_ret(l):
            last = (l == nlayers - 1)
            with ExitStack() as st:
                Tl, Pl = mk_alloc(st)
                RQK = Tl("RQK", [128, 8, T], BF16)
                RKtm = Tl("RKtm", [128, NT, 512], BF16)
                RVtm = Tl("RVtm", [128, NT, 512], BF16)
                rgs = Tl("rgs", [128, NT, 512], BF16)
                SAll = [Tl("SAll%d" % d_, [128, NT, 512], BF16) for d_ in range(2)]
                S32 = [Tl("S32%d" % d_, [128, 512]) for d_ in range(2)]
                MaskT = Tl("MaskT", [128, 4, 128])
                ERow = [Tl("ERow%d" % d_, [128, 4, 128], BF16) for d_ in range(2)]
                dec = Tl("dec", [128, 8])
                lg = Tl("lg", [128, 8])
                dend = Tl("dend", [128, 8])
                gam = Tl("gam", [128, 8])
                marg = Tl("marg", [128, 128])
                pr = [Tl("rpr%d" % i, [128, 2048]) for i in range(2)]
                tmps = [Tl("rrt%d" % i, [128, 512]) for i in range(4)]
                qr = [Tl("rqr%d" % i, [128, 1024], BF16) for i in range(2)]
                cs = [Tl("rcs%d" % i, [128, 128]) for i in range(2)]
                ee = Tl("ree", [128, 512])
                ptq = Pl("rptq", [128, 8, 128], BF16)
                S.dma("sp", lambda h: h.dma_start(out=dec[:, 0:4], in_=I["ret_decay_f"][l:l + 1, :].to_broadcast([128, 4])), w=["dec"])
                S.dma("sp", lambda h: h.dma_start(out=dec[:, 4:8], in_=I["ret_decay_b"][l:l + 1, :].to_broadcast([128, 4])), w=["dec"])
                S.add("act", lambda h: h.activation(out=lg[:], in_=dec[:], func=AF.Exp, scale=float(math.log(2.0))), r=["dec"], w=["lg"])
                S.add("act", lambda h: h.activation(out=lg[:], in_=lg[:], func=AF.Ln, scale=-1.0, bias=1.0), r=["lg"], w=["lg"])
                S.add("act", lambda h: h.activation(out=gam[:], in_=lg[:], func=AF.Exp, scale=128.0), r=["lg"], w=["gam"])

                def mk_head(hh):
                    S.add("dve", lambda h: h.tensor_scalar(out=marg[:], in0=D1, scalar1=lg[:, hh:hh + 1], scalar2=None, op0=ALU.mult), r=["lg"], w=["marg"])
                    S.add("dve", lambda h: h.scalar_tensor_tensor(out=marg[:], in0=D2, scalar=lg[:, 4 + hh:5 + hh], in1=marg[:], op0=ALU.mult, op1=ALU.add),
                          r=["lg", "marg"], w=["marg"])
                    S.add("act", lambda h: h.activation(out=marg[:], in_=marg[:], func=AF.Exp), r=["marg"], w=["marg"])
                    S.add("dve", lambda h: h.tensor_tensor(out=MaskT[:, hh, :], in0=marg[:], in1=ident_f, op=ALU.add), r=["marg"], w=["MaskT"])
                    for d_ in range(2):
                        S.add("act", lambda h, d_=d_: h.activation(out=ERow[d_][:, hh, :], in_=rowidx[d_], func=AF.Exp, scale=lg[:, 4 * d_ + hh:4 * d_ + hh + 1]),
                              r=["lg"], w=["ERow"])
                        S.add("act", lambda h, d_=d_: h.activation(out=dend[:, 4 * d_ + hh:4 * d_ + hh + 1], in_=colexp[d_], func=AF.Exp,
                                                                   scale=lg[:, 4 * d_ + hh:4 * d_ + hh + 1]), r=["lg"], w=["dend"])
                for hh in range(4):
                    mk_head(hh)

                def prep_tile(tt):
                    i = tt
                    lat = tt >= 2
                    p_ = pr[i % 2]
                    S.dma("sp", lambda h: h.dma_start(out=p_[:], in_=proj[tt * 128:(tt + 1) * 128, 1280:3328]), w=[("pr", i % 2)])
                    if lat:
                        S.dma("sp", lambda h: h.dma_start(out=cs[i % 2][:], in_=I["rope"][(tt - 2) * 128:(tt - 1) * 128, :]), w=[("cs", i % 2)])
                    S.add("act", lambda h: h.activation(out=p_[:, 512:1024], in_=p_[:, 512:1024], func=AF.Copy, scale=float(128 ** -0.5)),
                          r=[("pr", i % 2)], w=[("pr", i % 2)])
                    q_ = qr[i % 2]
                    rope_ops(p_[:, 0:1024], q_[:], cs[i % 2], 8, lat, ("pr", i % 2), ("qr", i % 2), ("cs", i % 2), tmps)
                    for hh in range(8):
                        S.add("pe", lambda h, hh=hh: h.transpose(out=ptq[:, hh, :], in_=q_[:, hh * 128:(hh + 1) * 128], identity=ident_b[:]),
                              r=[("qr", i % 2), (("qr", i % 2), "b")], w=["ptq"])
                    S.add("act", lambda h: h.activation(out=RQK[:, :, tt * 128:(tt + 1) * 128], in_=ptq[:], func=AF.Copy), r=["ptq"], w=[("RQK", tt)])
                    S.add("dve", lambda h: h.tensor_copy(out=RKtm[:, tt, :], in_=q_[:, 512:1024]), r=[("qr", i % 2), (("qr", i % 2), "b")], w=[("RKtm", tt)])
                    S.add("act", lambda h: h.activation(out=RVtm[:, tt, :], in_=p_[:, 1024:1536], func=AF.Copy), r=[("pr", i % 2)], w=[("RVtm", tt)])
                    silu2_ops(p_[:, 1536:2048], ee[:], rgs[:, tt, :], ("pr", i % 2), "ee", ("rgs", tt))
                for tt in range(NT):
                    prep_tile(tt)

                psA2 = [Pl("psA%d" % d_, [128, 512]) for d_ in range(2)]
                RVs2 = [[Tl("RVs%d_%d" % (d_, i), [128, 512], BF16) for i in range(2)] for d_ in range(2)]
                orders = [list(range(NT)), [1, 0] + list(range(NT - 1, 1, -1))]

                def state_step(d_, idx):
                    order = orders[d_]
                    c = order[idx]
                    if idx == 0:
                        S.add("pool", lambda h: h.memset(S32[d_][:], 0.0), w=[("S32", d_)])
                        S.add("pool", lambda h: h.memset(SAll[d_][:, c, :], 0.0), w=[("SAll", d_, c)])
                    if idx == NT - 1:
                        return
                    rv = RVs2[d_][idx % 2]
                    psA = psA2[d_]
                    S.add("dve", lambda h: h.tensor_tensor(out=rv[:].rearrange("p (h d) -> p h d", h=4),
                                                            in0=RVtm[:, c, :].rearrange("p (h d) -> p h d", h=4),
                                                            in1=dend[:, 4 * d_:4 * d_ + 4].unsqueeze(2).to_broadcast([128, 4, 128]), op=ALU.mult),
                          r=[("RVtm", c), "dend"], w=[("RVs", d_, idx % 2)])
                    for hh in range(4):
                        S.add("pe", lambda h, hh=hh: h.matmul(psA[:, hh * 128:(hh + 1) * 128], lhsT=RKtm[:, c, hh * 128:(hh + 1) * 128],
                                                             rhs=rv[:, hh * 128:(hh + 1) * 128], start=True, stop=True),
                              r=[("RKtm", c), ("RVs", d_, idx % 2)], w=[("psA", d_)])
                    S.add("dve", lambda h: h.tensor_tensor(out=S32[d_][:].rearrange("p (h d) -> p h d", h=4),
                                                           in0=S32[d_][:].rearrange("p (h d) -> p h d", h=4),
                                                           in1=gam[:, 4 * d_:4 * d_ + 4].unsqueeze(2).to_broadcast([128, 4, 128]), op=ALU.mult),
                          r=[("S32", d_), "gam"], w=[("S32", d_)])
                    S.add("dve", lambda h: h.tensor_tensor(out=S32[d_][:], in0=S32[d_][:], in1=psA[:], op=ALU.add), r=[("S32", d_), ("psA", d_)], w=[("S32", d_)])
                    cn = order[idx + 1]
                    S.add("act", lambda h: h.activation(out=SAll[d_][:, cn, :], in_=S32[d_][:], func=AF.Copy), r=[("S32", d_)], w=[("SAll", d_, cn)])
                for idx in range(NT):
                    for d_ in range(2):
                        state_step(d_, idx)

                psS = [Pl("psS%d" % i, [128, 4, 128]) for i in range(2)]
                psY = [Pl("psY%d" % i, [128, 512]) for i in range(2)]
                ptr = Pl("ptr", [128, 8, 128], BF16)
                SDT = [Tl("SDT%d" % i, [128, 4, 128], BF16) for i in range(2)]
                RQs = [[Tl("RQs%d_%d" % (d_, i), [128, 4, 128], BF16) for i in range(2)] for d_ in range(2)]
                sqy = Tl("sqy", [128, 512])
                yn = Tl("yn", [128, 512])
                yb = [Tl("yb%d" % i, [128, 512], BF16) for i in range(2)]
                stgr = [Tl("stgr%d" % i, [128, 4, 128], BF16) for i in range(2)]
                st4 = Tl("st4", [128, 8])

                def chunk(c, i):
                    y = psY[i % 2]
                    cs_ = slice(c * 128, (c + 1) * 128)
                    for hh in range(4):
                        S.add("pe", lambda h, hh=hh: h.matmul(psS[i % 2][:, hh, :], lhsT=RQK[:, 4 + hh, cs_], rhs=RQK[:, hh, cs_], start=(hh == 0), stop=(hh == 3)),
                              r=[("RQK", c)], w=[("psS", i % 2)])
                    S.add("dve", lambda h: h.tensor_tensor(out=SDT[i % 2][:], in0=psS[i % 2][:], in1=MaskT[:], op=ALU.mult),
                          r=[("psS", i % 2), "MaskT"], w=[("SDT", i % 2)])
                    for d_ in range(2):
                        S.add("dve", lambda h, d_=d_: h.tensor_tensor(out=RQs[d_][i % 2][:], in0=RQK[:, 0:4, cs_], in1=ERow[d_][:], op=ALU.mult),
                              r=[("RQK", c), "ERow"], w=[("RQs", d_, i % 2)])
                    for hh in range(4):
                        S.add("pe", lambda h, hh=hh: h.matmul(y[:, hh * 128:(hh + 1) * 128], lhsT=SDT[i % 2][:, hh, :], rhs=RVtm[:, c, hh * 128:(hh + 1) * 128],
                                                             start=(hh == 0), stop=False), r=[("SDT", i % 2), ("RVtm", c)], w=[("psY", i % 2)])
                        for d_ in range(2):
                            S.add("pe", lambda h, hh=hh, d_=d_: h.matmul(y[:, hh * 128:(hh + 1) * 128], lhsT=RQs[d_][i % 2][:, hh, :],
                                                                        rhs=SAll[d_][:, c, hh * 128:(hh + 1) * 128], start=False, stop=(d_ == 1 and hh == 3)),
                                  r=[("RQs", d_, i % 2), ("SAll", d_, c)], w=[("psY", i % 2)])
                def chunk_epi(c, i):
                    y = psY[i % 2]
                    S.add("act", lambda h: h.activation(out=sqy[:], in_=y[:], func=AF.Square), r=[("psY", i % 2)], w=["sqy"])
                    S.add("dve", lambda h: h.tensor_reduce(out=st4[:, 0:4], in_=sqy[:].rearrange("p (h d) -> p h d", h=4), axis=AX.X, op=ALU.add),
                          r=["sqy"], w=["ssq4"])
                    rstd_ops(st4[:, 0:4], st4[:, 4:8], 1.0 / 128, ["ssq4"], ["rstd4"])
                    S.add("dve", lambda h: h.tensor_tensor(out=yn[:].rearrange("p (h d) -> p h d", h=4), in0=y[:].rearrange("p (h d) -> p h d", h=4),
                                                           in1=st4[:, 4:8].unsqueeze(2).to_broadcast([128, 4, 128]), op=ALU.mult),
                          r=[("psY", i % 2), "rstd4"], w=["yn"])
                    yb_ = yb[i % 2]
                    S.add("dve", lambda h: h.scalar_tensor_tensor(out=yb_[:], in0=yn[:], scalar=0.5, in1=rgs[:, c, :], op0=ALU.mult, op1=ALU.mult),
                          r=["yn", ("rgs", c)], w=[("yb", i % 2)])
                    for hh in range(4):
                        S.add("pe", lambda h, hh=hh: h.transpose(out=ptr[:, hh, :], in_=yb_[:, hh * 128:(hh + 1) * 128], identity=ident_b[:]),
                              r=[("yb", i % 2)], w=["ptr"])
                    sg = stgr[i % 2]
                    S.add("act", lambda h: h.activation(out=sg[:], in_=ptr[:, 0:4, :], func=AF.Copy), r=["ptr"], w=[("stgr", i % 2)])
                    S.dma("sp", lambda h: h.dma_start(out=mixT[768:1280, c * 128:(c + 1) * 128].rearrange("(h d) t -> d h t", h=4), in_=sg[:]),
                          r=[("stgr", i % 2)], w=[("mixTr", c)])
                cl = list(range(2 if last else 0, NT))
                chunk(cl[0], 0)
                for i_, c in enumerate(cl):
                    if i_ + 1 < len(cl):
                        chunk(cl[i_ + 1], i_ + 1)
                    chunk_epi(c, i_)
                S.phase_end("ret")

        def phase_ssd(l):
            last = (l == nlayers - 1)
            NH = 12
            with ExitStack() as st_o:
                To, Po = mk_alloc(st_o)
                BT = To("BT", [128, 2, T], BF16)
                CT = To("CT", [128, 2, T], BF16)
                Btm = To("Btm", [128, NT, 256], BF16)
                xs = To("xs", [128, NT, 768], BF16)
                la = To("la", [128, 2, NT, NH])
                lndt = To("lndt", [128, 2, NT, NH])
                acs = To("acs", [128, 2, NT, NH])
                tot = To("tot", [128, 2, NT, NH])
                ein = To("ein", [128, 2, NT, NH])
                cdc = To("cdc", [128, 2, NT, NH])
                wend = To("wend", [128, 2, NT, NH])
                lb = To("lb", [128, 2, NT, NH])
                dsk = To("dsk", [128, NH])
                gssd = To("gssd", [128, 768])
                with ExitStack() as st:
                    Tl, Pl = mk_alloc(st)
                    cw = Tl("cw", [128, 10, 5])
                    cb = Tl("cb", [128, 10])
                    for k in range(5):
                        S.dma("sp", lambda h, k=k: h.dma_start(out=cw[:, :, k], in_=I["conv_w"][l, k].rearrange("(c p) -> p c", p=128),
                                                               allow_slow_non_contiguous=True), w=["cw"])
                    S.dma("sp", lambda h: h.dma_start(out=cb[:], in_=I["conv_b"][l].rearrange("(c p) -> p c", p=128), allow_slow_non_contiguous=True), w=["cb"])
                    S.dma("sp", lambda h: h.dma_start(out=dsk[:], in_=I["d_skip"][l:l + 1, :].to_broadcast([128, NH])), w=["dsk"])
                    S.dma("sp", lambda h: h.dma_start(out=gssd[:], in_=I["ssd_norm_g"][l:l + 1, :].to_broadcast([128, 768])), w=["gssd"])
                    UW = 2312
                    u = [Tl("u%d" % i, [128, UW]) for i in range(2)]
                    acc = Tl("acc", [128, UW])
                    ee = Tl("cee", [128, UW])
                    ob = [Tl("ob%d" % i, [128, UW], BF16) for i in range(2)]
                    ptx = Pl("ptx", [128, 8, 128], BF16)
                    for i in range(2):
                        S.add("pool", lambda h, i=i: h.memset(u[i][:], 0.0), w=[("u", i)])
                    S.add("dve", lambda h: h.tensor_scalar(out=cw[:], in0=cw[:], scalar1=0.5, scalar2=None, op0=ALU.mult), r=["cw"], w=["cw"])
                    S.add("dve", lambda h: h.tensor_scalar(out=cb[:], in0=cb[:], scalar1=0.5, scalar2=None, op0=ALU.mult), r=["cb"], w=["cb"])

                    def conv_chunk(cc):
                        i = cc
                        u_ = u[i % 2]
                        o_ = ob[i % 2]
                        S.dma("sp", lambda h: h.dma_start(out=u_[:, 2:258], in_=xbcT[cc * 128:(cc + 1) * 128, 0:256]), w=[("u", i % 2)])
                        S.dma("sp", lambda h: h.dma_start(out=u_[:, 262:2310], in_=xbcT[cc * 128:(cc + 1) * 128, 256:T]), w=[("u", i % 2)])
                        n = 2308
                        S.add("dve", lambda h: h.tensor_scalar(out=acc[:, 2:2 + n], in0=u_[:, 0:n], scalar1=cw[:, cc, 0:1], scalar2=cb[:, cc:cc + 1],
                                                               op0=ALU.mult, op1=ALU.add), r=[("u", i % 2), "cw", "cb"], w=["acc"])
                        for k in range(1, 5):
                            S.add("dve", lambda h, k=k: h.scalar_tensor_tensor(out=acc[:, 2:2 + n], in0=u_[:, k:k + n], scalar=cw[:, cc, k:k + 1],
                                                                               in1=acc[:, 2:2 + n], op0=ALU.mult, op1=ALU.add),
                                  r=[("u", i % 2), "cw", "acc"], w=["acc"])
                        silu2_ops(acc[:, 2:2 + n], ee[:, 2:2 + n], o_[:, 2:2 + n], "acc", "cee", ("ob", i % 2), in_scale=1.0)
                        def tok(tt):
                            return (2 + tt * 128) if tt < 2 else (262 + (tt - 2) * 128)
                        if cc < 6 or cc in (6, 7):
                            for t0 in range(0, NT, 8):
                                nt = min(8, NT - t0)
                                for a in range(nt):
                                    tt = t0 + a
                                    S.add("pe", lambda h, a=a, tt=tt: h.transpose(out=ptx[:, a, :], in_=o_[:, tok(tt):tok(tt) + 128], identity=ident_b[:]),
                                          r=[("ob", i % 2)], w=["ptx"])
                                if cc < 6:
                                    S.add("act", lambda h, t0=t0, nt=nt: h.activation(out=xs[:, t0:t0 + nt, cc * 128:(cc + 1) * 128], in_=ptx[:, 0:nt, :], func=AF.Copy),
                                          r=["ptx"], w=[("xs", cc, t0)])
                                else:
                                    g = cc - 6
                                    S.add("act", lambda h, t0=t0, nt=nt: h.activation(out=Btm[:, t0:t0 + nt, g * 128:(g + 1) * 128], in_=ptx[:, 0:nt, :], func=AF.Copy),
                                          r=["ptx"], w=[("Btm", g, t0)])
                        if cc >= 6:
                            dstT = BT if cc < 8 else CT
                            g = (cc - 6) % 2
                            S.add("act", lambda h: h.activation(out=dstT[:, g, 0:256], in_=o_[:, 2:258], func=AF.Copy), r=[("ob", i % 2)], w=[("BCT", cc, 0)])
                            S.add("act", lambda h: h.activation(out=dstT[:, g, 256:T], in_=o_[:, 262:2310], func=AF.Copy), r=[("ob", i % 2)], w=[("BCT", cc, 1)])
                    for cc in range(10):
                        conv_chunk(cc)
                    ztl = [Tl("ztl%d" % i, [128, 768]) for i in range(2)]
                    zth = [Tl("zth%d" % i, [128, 768]) for i in range(2)]
                    zso = [Tl("zso%d" % i, [128, 768], BF16) for i in range(2)]

                    def ztile(tt):
                        b = tt % 2
                        S.dma("sp", lambda h: h.dma_start(out=ztl[b][:], in_=proj[tt * 128:(tt + 1) * 128, 3328:4096]), w=[("ztl", b)])
                        S.add("act", lambda h: h.activation(out=zth[b][:], in_=ztl[b][:], func=AF.Tanh, scale=0.5), r=[("ztl", b)], w=[("zth", b)])
                        S.add("dve", lambda h: h.scalar_tensor_tensor(out=zso[b][:], in0=zth[b][:], scalar=1.0, in1=ztl[b][:], op0=ALU.add, op1=ALU.mult),
                              r=[("ztl", b), ("zth", b)], w=[("zso", b)])
                        S.dma("sp", lambda h: h.dma_start(out=zsd[tt * 128:(tt + 1) * 128, :], in_=zso[b][:]), r=[("zso", b)], w=[("zsd", tt)])
                    for tt in range(2 if last else 0, NT):
                        ztile(tt)
                    dtr = Tl("dtr", [128, NT, 24])
                    dtb = Tl("dtb", [128, 24])
                    alg = Tl("alg", [128, 24])
                    dtv = Tl("dtv", [128, 2, NT, NH])
                    tmpd = Tl("tmpd", [128, 2, NT, NH])
                    psc = Pl("psc", [128, 512])
                    pst = Pl("pst", [128, 512])
                    S.dma("sp", lambda h: h.dma_start(out=dtr[:], in_=proj[:, 4096:4120].rearrange("(t p) c -> p t c", p=128)), w=["dtr"])
                    S.dma("sp", lambda h: h.dma_start(out=dtb[:, 0:12], in_=I["dt_bias_f"][l:l + 1, :].to_broadcast([128, 12])), w=["dtb"])
                    S.dma("sp", lambda h: h.dma_start(out=dtb[:, 12:24], in_=I["dt_bias_b"][l:l + 1, :].to_broadcast([128, 12])), w=["dtb"])
                    S.dma("sp", lambda h: h.dma_start(out=alg[:, 0:12], in_=I["a_log_f"][l:l + 1, :].to_broadcast([128, 12])), w=["alg"])
                    S.dma("sp", lambda h: h.dma_start(out=alg[:, 12:24], in_=I["a_log_b"][l:l + 1, :].to_broadcast([128, 12])), w=["alg"])
                    S.add("act", lambda h: h.activation(out=alg[:], in_=alg[:], func=AF.Exp), r=["alg"], w=["alg"])
                    for d_ in range(2):
                        S.add("dve", lambda h, d_=d_: h.tensor_tensor(out=dtv[:, d_], in0=dtr[:, :, d_ * 12:(d_ + 1) * 12],
                                                                    in1=dtb[:, d_ * 12:(d_ + 1) * 12].unsqueeze(1).to_broadcast([128, NT, NH]), op=ALU.add),
                              r=["dtr", "dtb"], w=["dtv"])
                    F2 = lambda t: t[:].rearrange("p a b c -> p (a b c)")
                    S.add("act", lambda h: h.activation(out=F2(dtv), in_=F2(dtv), func=AF.Exp), r=["dtv"], w=["dtv"])
                    S.add("act", lambda h: h.activation(out=F2(dtv), in_=F2(dtv), func=AF.Ln, bias=1.0), r=["dtv"], w=["dtv"])
                    S.add("dve", lambda h: h.tensor_scalar(out=F2(dtv), in0=F2(dtv), scalar1=1e-30, scalar2=None, op0=ALU.max), r=["dtv"], w=["dtv"])
                    S.add("act", lambda h: h.activation(out=F2(lndt), in_=F2(dtv), func=AF.Ln), r=["dtv"], w=["lndt"])
                    for d_ in range(2):
                        S.add("dve", lambda h, d_=d_: h.tensor_tensor(out=la[:, d_], in0=dtv[:, d_],
                                                                    in1=alg[:, d_ * 12:(d_ + 1) * 12].unsqueeze(1).to_broadcast([128, NT, NH]), op=ALU.mult),
                              r=["dtv", "alg"], w=["la"])
                    S.add("dve", lambda h: h.tensor_scalar(out=F2(la), in0=F2(la), scalar1=-1.0, scalar2=None, op0=ALU.mult), r=["la"], w=["la"])
                    NQ = NT * NH
                    S.add("pe", lambda h: h.matmul(psc[:, 0:NQ], lhsT=tri_f, rhs=la[:, 0].rearrange("p a b -> p (a b)"), start=True, stop=True), r=["la"], w=["psc"])
                    S.add("pe", lambda h: h.matmul(psc[:, NQ:2 * NQ], lhsT=tri_b, rhs=la[:, 1].rearrange("p a b -> p (a b)"), start=True, stop=True), r=["la"], w=["psc"])
                    S.add("pe", lambda h: h.matmul(pst[:, 0:2 * NQ], lhsT=ones_f, rhs=F2(la), start=True, stop=True), r=["la"], w=["pst"])
                    S.add("dve", lambda h: h.tensor_copy(out=F2(acs), in_=psc[:, 0:2 * NQ]), r=["psc"], w=["acs"])
                    S.add("dve", lambda h: h.tensor_copy(out=F2(tot), in_=pst[:, 0:2 * NQ]), r=["pst"], w=["tot"])
                    S.add("act", lambda h: h.activation(out=F2(ein), in_=F2(acs), func=AF.Exp), r=["acs"], w=["ein"])
                    S.add("act", lambda h: h.activation(out=F2(cdc), in_=F2(tot), func=AF.Exp), r=["tot"], w=["cdc"])
                    S.add("dve", lambda h: h.tensor_tensor(out=F2(lb), in0=F2(lndt), in1=F2(acs), op=ALU.subtract), r=["lndt", "acs"], w=["lb"])
                    S.add("dve", lambda h: h.tensor_tensor(out=F2(tmpd), in0=F2(lb), in1=F2(tot), op=ALU.add), r=["lb", "tot"], w=["tmpd"])
                    S.add("act", lambda h: h.activation(out=F2(wend), in_=F2(tmpd), func=AF.Exp), r=["tmpd"], w=["wend"])
                    S.phase_end("ssdprep")
                with ExitStack() as st:
                    Tl, Pl = mk_alloc(st)
                    SAll = [Tl("sSAll%d" % d_, [128, NT, 768], BF16) for d_ in range(2)]
                    S32 = [Tl("sS32%d" % d_, [128, 768]) for d_ in range(2)]
                    xsw2 = [[Tl("xsw%d_%d" % (d_, i), [128, 768], BF16) for i in range(2)] for d_ in range(2)]
                    psA = Pl("spsA", [128, 2, 512])
                    psY = Pl("spsY", [128, 1024])
                    psAB = [psA, psY[:, :].rearrange("p (g x) -> p g x", g=2)]
                    psABk = ["psA", "psY"]
                    orders = [list(range(NT)), [1, 0] + list(range(NT - 1, 1, -1))]

                    def bc12(t2d):
                        return t2d.unsqueeze(2).to_broadcast([128, NH, 64])

                    def v12(ap):
                        return ap.rearrange("p (h d) -> p h d", h=NH)

                    def state_step(d_, idx):
                        order = orders[d_]
                        c = order[idx]
                        if idx == 0:
                            S.add("pool", lambda h: h.memset(S32[d_][:], 0.0), w=[("S32", d_)])
                            S.add("pool", lambda h: h.memset(SAll[d_][:, c, :], 0.0), w=[("SAll", d_, c)])
                        if idx == NT - 1:
                            return
                        xw = xsw2[d_][idx % 2]
                        pA = psAB[d_]
                        pk = psABk[d_]
                        S.add("dve", lambda h: h.tensor_tensor(out=v12(xw[:]), in0=v12(xs[:, c, :]), in1=bc12(wend[:, d_, c, :]), op=ALU.mult),
                              w=[("xsw", d_, idx % 2)])
                        for g in range(2):
                            S.add("pe", lambda h, g=g: h.matmul(pA[:, g, 0:384], lhsT=Btm[:, c, g * 128:(g + 1) * 128], rhs=xw[:, g * 384:(g + 1) * 384],
                                                               start=True, stop=True), r=[("xsw", d_, idx % 2)], w=[pk])
                        S.add("dve", lambda h: h.tensor_tensor(out=v12(S32[d_][:]), in0=v12(S32[d_][:]), in1=bc12(cdc[:, d_, c, :]), op=ALU.mult),
                              r=[("S32", d_)], w=[("S32", d_)])
                        S.add("dve", lambda h: h.tensor_tensor(out=S32[d_][:].rearrange("p (g x) -> p g x", g=2), in0=S32[d_][:].rearrange("p (g x) -> p g x", g=2),
                                                               in1=pA[:, :, 0:384], op=ALU.add), r=[("S32", d_), pk], w=[("S32", d_)])
                        cn = order[idx + 1]
                        S.add("act", lambda h: h.activation(out=SAll[d_][:, cn, :], in_=S32[d_][:], func=AF.Copy), r=[("S32", d_)], w=[("SAll", d_, cn)])
                    for idx in range(NT):
                        for d_ in range(2):
                            state_step(d_, idx)

                    Rt = [Tl("Rt%d" % d_, [128, NH, 128]) for d_ in range(2)]
                    mneg4 = [Tl("mneg4_%d" % d_, [128, 4, 128]) for d_ in range(2)]
                    Et = [Tl("Et%d" % i, [128, 4, 128], BF16) for i in range(2)]
                    Wt = [Tl("Wt%d" % i, [128, 4, 128], BF16) for i in range(4)]
                    psE = [Pl("psE%d" % i, [128, 4, 128]) for i in range(2)]
                    psG = Pl("psG", [128, 4, 128])
                    psFB = psA
                    ptr = Pl("sptr", [128, 8, 128], BF16)
                    y1 = [Tl("y1_%d" % i, [128, 768]) for i in range(2)]
                    y2 = [Tl("y2_%d" % i, [128, 768]) for i in range(2)]
                    zt = [Tl("zt%d" % i, [128, 768], BF16) for i in range(2)]
                    xsd = [Tl("xsd%d" % i, [128, 768], BF16) for i in range(2)]
                    junk = Tl("sjunk", [128, 768], BF16)
                    yo = [Tl("yo%d" % i, [128, 768], BF16) for i in range(2)]
                    stg = [Tl("sstg%d" % i, [128, 6, 128], BF16) for i in range(2)]
                    st2 = Tl("st2", [128, 2])
                    for d_ in range(2):
                        S.add("dve", lambda h, d_=d_: h.tensor_copy(out=mneg4[d_][:], in_=mneg[d_].unsqueeze(1).to_broadcast([128, 4, 128])), w=["mneg4"])
                    tris = [tri_f, tri_b]
                    ecnt = [0]
                    wcnt = [0]

                    def chunk(c, ci):
                        cs_ = slice(c * 128, (c + 1) * 128)
                        b = ci % 2
                        S.dma("sp", lambda h: h.dma_start(out=zt[b][:], in_=zsd[c * 128:(c + 1) * 128, :]), w=[("zt", b)])
                        for g in range(2):
                            S.add("pe", lambda h, g=g: h.matmul(psG[:, g, :], lhsT=BT[:, g, cs_], rhs=CT[:, g, cs_], start=(g == 0), stop=(g == 1)), w=["psG"])
                        for d_ in range(2):
                            S.add("dve", lambda h, d_=d_: h.tensor_tensor(out=Rt[d_][:], in0=la[:, d_, c, :].unsqueeze(2).to_broadcast([128, NH, 128]),
                                                                        in1=tris[d_].unsqueeze(1).to_broadcast([128, NH, 128]), op=ALU.mult), w=[("Rt", d_)])
                        S.add("dve", lambda h: h.tensor_tensor(out=v12(xsd[b][:]), in0=v12(xs[:, c, :]), in1=bc12(dsk[:]), op=ALU.mult), w=[("xsd", b)])
                        for q in range(3):
                            wl = {}
                            for d_ in range(2):
                                e = ecnt[0]
                                ecnt[0] += 1
                                pe_ = psE[e % 2]
                                et_ = Et[e % 2]
                                S.add("pe", lambda h, d_=d_, q=q, pe_=pe_: h.matmul(pe_[:].rearrange("p a b -> p (a b)"), lhsT=ones_f,
                                                                                 rhs=Rt[d_][:, 4 * q:4 * q + 4, :].rearrange("p a b -> p (a b)"), start=True, stop=False),
                                      r=[("Rt", d_)], w=[("psE", e % 2)])
                                S.add("pe", lambda h, d_=d_, pe_=pe_: h.matmul(pe_[:].rearrange("p a b -> p (a b)"), lhsT=ident_f,
                                                                            rhs=mneg4[d_][:].rearrange("p a b -> p (a b)"), start=False, stop=True),
                                      r=["mneg4"], w=[("psE", e % 2)])
                                for a in range(4):
                                    hh = 4 * q + a
                                    S.add("act", lambda h, a=a, hh=hh, d_=d_, pe_=pe_, et_=et_: h.activation(out=et_[:, a, :], in_=pe_[:, a, :], func=AF.Exp,
                                                                                                      bias=lb[:, d_, c, hh:hh + 1]),
                                          r=[("psE", e % 2)], w=[("Et", e % 2)])
                                w_ = wcnt[0]
                                wcnt[0] += 1
                                wt_ = Wt[w_ % 4]
                                wl[d_] = (w_, wt_)
                                if q == 1:
                                    for half in range(2):
                                        S.add("dve", lambda h, half=half, et_=et_, wt_=wt_: h.tensor_tensor(
                                            out=wt_[:, 2 * half:2 * half + 2, :], in0=et_[:, 2 * half:2 * half + 2, :],
                                            in1=psG[:, half, :].unsqueeze(1).to_broadcast([128, 2, 128]), op=ALU.mult),
                                            r=[("Et", e % 2), "psG"], w=[("Wt", w_ % 4)])
                                else:
                                    g = 0 if q == 0 else 1
                                    S.add("dve", lambda h, g=g, et_=et_, wt_=wt_: h.tensor_tensor(
                                        out=wt_[:], in0=et_[:], in1=psG[:, g, :].unsqueeze(1).to_broadcast([128, 4, 128]), op=ALU.mult),
                                        r=[("Et", e % 2), "psG"], w=[("Wt", w_ % 4)])
                            for a in range(4):
                                hh = 4 * q + a
                                for d_ in range(2):
                                    w_, wt_ = wl[d_]
                                    S.add("pe", lambda h, hh=hh, a=a, d_=d_, wt_=wt_: h.matmul(psY[:, hh * 64:(hh + 1) * 64], lhsT=wt_[:, a, :], rhs=xs[:, c, hh * 64:(hh + 1) * 64],
                                                                                          start=(d_ == 0 and hh in (0, 8)), stop=False),
                                          r=[("Wt", w_ % 4)], w=["psY"])
                        S.add("pe", lambda h: h.matmul(psY[:, 0:512], lhsT=ident_b[:], rhs=xsd[b][:, 0:512], start=False, stop=True), r=[("xsd", b)], w=["psY"])
                        S.add("pe", lambda h: h.matmul(psY[:, 512:768], lhsT=ident_b[:], rhs=xsd[b][:, 512:768], start=False, stop=True), r=[("xsd", b)], w=["psY"])
                        for d_ in range(2):
                            for g in range(2):
                                S.add("pe", lambda h, d_=d_, g=g: h.matmul(psFB[:, g, 0:384], lhsT=CT[:, g, cs_], rhs=SAll[d_][:, c, g * 384:(g + 1) * 384],
                                                                         start=True, stop=True), r=[("SAll", d_, c)], w=["psA"])
                            yd = y1[b] if d_ == 0 else y2[b]
                            S.add("dve", lambda h, d_=d_, yd=yd: h.tensor_tensor(
                                out=yd[:].rearrange("p (g h d) -> p g h d", g=2, h=6),
                                in0=psFB[:, :, 0:384].rearrange("p g (h d) -> p g h d", h=6),
                                in1=ein[:, d_, c, :].rearrange("p (g h) -> p g h", g=2).unsqueeze(3).to_broadcast([128, 2, 6, 64]), op=ALU.mult),
                                r=["psA"], w=[("y", d_, b)])
                        S.add("dve", lambda h: h.tensor_tensor(out=y1[b][:], in0=y1[b][:], in1=y2[b][:], op=ALU.add), r=[("y", 0, b), ("y", 1, b)], w=[("y", 0, b)])
                        S.add("dve", lambda h: h.tensor_tensor(out=y1[b][:], in0=y1[b][:], in1=psY[:, 0:768], op=ALU.add), r=[("y", 0, b), "psY"], w=[("y", 0, b)])
                    def chunk_epi(c, ci):
                        cs_ = slice(c * 128, (c + 1) * 128)
                        b = ci % 2
                        S.add("dve", lambda h: h.scalar_tensor_tensor(out=y1[b][:], in0=y1[b][:], scalar=0.5, in1=zt[b][:], op0=ALU.mult, op1=ALU.mult),
                              r=[("y", 0, b), ("zt", b)], w=[("y", 0, b)])
                        S.add("act", lambda h: h.activation(out=junk[:], in_=y1[b][:], func=AF.Square, accum_out=st2[:, 0:1]), r=[("y", 0, b)], w=["sjunk", "ssq"])
                        rstd_ops(st2[:, 0:1], st2[:, 1:2], 1.0 / 768, ["ssq"], ["rstd"])
                        S.add("dve", lambda h: h.scalar_tensor_tensor(out=yo[b][:], in0=y1[b][:], scalar=st2[:, 1:2], in1=gssd[:], op0=ALU.mult, op1=ALU.mult),
                              r=[("y", 0, b), "rstd"], w=[("yo", b)])
                        for a in range(6):
                            S.add("pe", lambda h, a=a: h.transpose(out=ptr[:, a, :], in_=yo[b][:, a * 128:(a + 1) * 128], identity=ident_b[:]), r=[("yo", b)], w=["ptr"])
                        sg = stg[b]
                        S.add("act", lambda h: h.activation(out=sg[:], in_=ptr[:, 0:6, :], func=AF.Copy), r=["ptr"], w=[("stg", b)])
                        S.dma("pool", lambda h: h.dma_start(out=mixT[1280:2048, cs_].rearrange("(a d) t -> d a t", a=6), in_=sg[:]),
                              r=[("stg", b)], w=[("mixTs", c)])
                    cl = list(range(2 if last else 0, NT))
                    chunk(cl[0], 0)
                    for ci, c in enumerate(cl):
                        if ci + 1 < len(cl):
                            chunk(cl[ci + 1], ci + 1)
                        chunk_epi(c, ci)
                    S.phase_end("ssdmain")

        def phase_out(l):
            last = (l == nlayers - 1)
            with ExitStack() as st:
                Tl, Pl = mk_alloc(st)
                wo = Tl("wo", [128, 16, D], BF16)
                wv = I["w_out"][l].rearrange("(k p) n -> p k n", p=128)
                for k4 in range(4):
                    for nb in range(2):
                        S.dma("pool", lambda h, k4=k4, nb=nb: h.dma_start(out=wo[:, 4 * k4:4 * k4 + 4, nb * 1024:(nb + 1) * 1024],
                                                                        in_=wv[:, 4 * k4:4 * k4 + 4, nb * 1024:(nb + 1) * 1024]), w=[("wo", k4, nb)])
                G1 = Tl("G1", [128, D])
                gm = Tl("ogm", [128, D])
                sh = Tl("osh", [128, D])
                gp = Tl("ogp", [128, D])
                mT = [Tl("mT%d" % i, [128, 16, 128], BF16) for i in range(3)]
                xt = [Tl("oxt%d" % i, [128, D]) for i in range(3)]
                x1 = [Tl("ox1%d" % i, [128, D]) for i in range(3)]
                tmp = Tl("otmp", [128, D])
                hb = [Tl("ohb%d" % i, [128, D], BF16) for i in range(2)]
                junk = Tl("ojunk", [128, D], BF16)
                ssq = Tl("ossq", [128, 4])
                sq4 = Tl("osq4", [128, 24])
                stg = [Tl("ostg%d" % i, [128, 16, 128], BF16) for i in range(2)]
                psM = [Pl("psM%d" % i, [128, 512]) for i in range(4)]
                pt = [Pl("opt%d" % i, [128, 16, 128], BF16) for i in range(2)]

                def load_mods(r):
                    load_gate_mod(l, G1, gp, "o", "post_mix_g", 4096, r)
                    load_norm_mod(l, gm, sh, gp, "o", "pre_ffn_g", 4 * 2048, 3 * 2048, r)

                def stage1(tt, i):
                    o8 = 8 * (i % 3)
                    S.dma("sp", lambda h: h.dma_start(out=mT[i % 3][:], in_=mixT[:, tt * 128:(tt + 1) * 128].rearrange("(k p) t -> p k t", p=128)),
                          w=[("mT", i % 3)])
                    S.dma("sp", lambda h: h.dma_start(out=xt[i % 3][:], in_=xsrc(l, tt)), w=[("xt", i % 3)])
                    x1_ = x1[i % 3]
                    for cb in range(4):
                        for k in range(16):
                            S.add("pe", lambda h, cb=cb, k=k: h.matmul(psM[cb][:], lhsT=mT[i % 3][:, k, :], rhs=wo[:, k, cb * 512:(cb + 1) * 512],
                                                                      start=(k == 0), stop=(k == 15)),
                                  r=[("mT", i % 3), ("wo", k // 4, cb // 2)], w=[("psM", cb)])
                        S.add("act", lambda h, cb=cb: h.activation(out=junk[:, cb * 512:(cb + 1) * 512], in_=psM[cb][:], func=AF.Square,
                                                                   accum_out=sq4[:, o8 + cb:o8 + cb + 1]), r=[("psM", cb)], w=["ojunk", ("sq4", i % 3), ("sqd", cb)])
                        S.add("dve", lambda h, cb=cb: h.tensor_copy(out=x1_[:, cb * 512:(cb + 1) * 512], in_=psM[cb][:]), r=[("psM", cb), ("sqd", cb)], w=[("x1", i % 3)])

                def stage2(tt, i):
                    o8 = 8 * (i % 3)
                    x1_ = x1[i % 3]
                    S.add("dve", lambda h: h.tensor_reduce(out=sq4[:, o8 + 4:o8 + 5], in_=sq4[:, o8:o8 + 4], axis=AX.X, op=ALU.add), r=[("sq4", i % 3)], w=[("mss", i % 3)])
                    rstd_ops(sq4[:, o8 + 4:o8 + 5], sq4[:, o8 + 5:o8 + 6], 1.0 / D, [("mss", i % 3)], [("mrstd", i % 3)])
                    S.add("dve", lambda h: h.scalar_tensor_tensor(out=x1_[:], in0=x1_[:], scalar=sq4[:, o8 + 5:o8 + 6], in1=G1[:], op0=ALU.mult, op1=ALU.mult),
                          r=[("x1", i % 3), ("mrstd", i % 3), "oG"], w=[("x1", i % 3)])
                    S.add("dve", lambda h: h.tensor_tensor(out=x1_[:], in0=x1_[:], in1=xt[i % 3][:], op=ALU.add),
                          r=[("x1", i % 3), ("xt", i % 3)], w=[("x1", i % 3)])
                    S.dma("pool", lambda h: h.dma_start(out=x1s[tt * 128:(tt + 1) * 128, :], in_=x1_[:]), r=[("x1", i % 3)], w=[("x1s", tt)])

                    def dst(ptile, pkey, wkeys):
                        sg = stg[i % 2]
                        S.add("act", lambda h: h.activation(out=sg[:], in_=ptile[:], func=AF.Copy), r=[pkey], w=[("ostg", i % 2)])
                        S.dma("pool", lambda h: h.dma_start(out=h2T[:, tt * 128:(tt + 1) * 128].rearrange("(k p) t -> p k t", p=128), in_=sg[:]),
                              r=[("ostg", i % 2)], w=wkeys)
                    return norm_transpose_tile(l, tt, x1_[:], ("x1", i % 3), gm, sh, "o", tmp, hb[i % 2], junk, ssq, pt[i % 2], dst, [("h2T", tt)], i)

                def run_tiles(tts, i0):
                    pend = None
                    n_ = len(tts)
                    for j in range(min(2, n_)):
                        stage1(tts[j], i0 + j)
                    for j, tt in enumerate(tts):
                        if j + 2 < n_:
                            stage1(tts[j + 2], i0 + j + 2)
                        th = stage2(tt, i0 + j)
                        if pend is not None:
                            pend()
                        pend = th
                    pend()
                if not last:
                    load_mods(1)
                    run_tiles([0, 1], 0)
                load_mods(0)
                run_tiles(list(range(2, NT)), 2)
                S.phase_end("out")

        def phase_ffn(l):
            last = (l == nlayers - 1)
            if last:
                halves = [(256, 1024), (1280, 1024)]
            else:
                halves = [(0, 1152), (1152, 1152)]
            wg_v = I["w_gate"][l].rearrange("(k p) n -> p k n", p=128)
            wu_v = I["w_up"][l].rearrange("(k p) n -> p k n", p=128)
            wd_v = I["w_down"][l].rearrange("(k p) n -> p k n", p=128)
            NJ = DFF // 128
            for (t0, nt) in halves:
                with ExitStack() as st_o:
                    To, Po = mk_alloc(st_o)
                    aT = To("aT", [128, NJ, nt], BF16)
                    with ExitStack() as st:
                        Tl, Pl = mk_alloc(st)
                        hh_ = Tl("h2h", [128, 16, nt], BF16)
                        for k4 in range(4):
                            S.dma("sp", lambda h, k4=k4: h.dma_start(out=hh_[:, 4 * k4:4 * k4 + 4, :],
                                                                     in_=h2T[:, t0:t0 + nt].rearrange("(k p) t -> p k t", p=128)[:, 4 * k4:4 * k4 + 4, :]),
                                  w=[("h2h", k4)])
                        wg = [Tl("wg%d" % i, [128, 16, 256], BF16) for i in range(2)]
                        wu = [Tl("wu%d" % i, [128, 16, 256], BF16) for i in range(2)]
                        psg = [Pl("psg%d" % i, [128, 512]) for i in range(3)]
                        psu = [Pl("psu%d" % i, [128, 512]) for i in range(3)]
                        ee = [Tl("fe%d" % i, [128, 512]) for i in range(2)]
                        tg = [Tl("ftg%d" % i, [128, 512]) for i in range(2)]
                        tbs = [(a, min(512, nt - a)) for a in range(0, nt, 512)]
                        cnt = [0]

                        def wblock(jb):
                            b = jb % 2
                            for k4 in range(4):
                                S.dma("pool", lambda h, k4=k4: h.dma_start(out=wg[b][:, 4 * k4:4 * k4 + 4, :], in_=wg_v[:, 4 * k4:4 * k4 + 4, jb * 256:(jb + 1) * 256]),
                                      w=[("wg", b, k4)])
                                S.dma("pool", lambda h, k4=k4: h.dma_start(out=wu[b][:, 4 * k4:4 * k4 + 4, :], in_=wu_v[:, 4 * k4:4 * k4 + 4, jb * 256:(jb + 1) * 256]),
                                      w=[("wu", b, k4)])
                            for jj in range(2):
                                j = jb * 2 + jj
                                for (a, n) in tbs:
                                    i = cnt[0]
                                    cnt[0] += 1
                                    pg = psg[i % 3]
                                    pu = psu[i % 3]
                                    for k in range(16):
                                        S.add("pe", lambda h, k=k, pg=pg, a=a, n=n, jj=jj: h.matmul(pg[:, 0:n], lhsT=wg[b][:, k, jj * 128:(jj + 1) * 128],
                                                                                             rhs=hh_[:, k, a:a + n], start=(k == 0), stop=(k == 15)),
                                              r=[("wg", b, k // 4), ("h2h", k // 4)], w=[("psg", i % 3)])
                                    for k in range(16):
                                        S.add("pe", lambda h, k=k, pu=pu, a=a, n=n, jj=jj: h.matmul(pu[:, 0:n], lhsT=wu[b][:, k, jj * 128:(jj + 1) * 128],
                                                                                             rhs=hh_[:, k, a:a + n], start=(k == 0), stop=(k == 15)),
                                              r=[("wu", b, k // 4), ("h2h", k // 4)], w=[("psu", i % 3)])
                                    e_ = ee[i % 2]
                                    t_ = tg[i % 2]
                                    S.add("act", lambda h, pg=pg, e_=e_, n=n: h.activation(out=e_[:, 0:n], in_=pg[:, 0:n], func=AF.Tanh, scale=0.5),
                                          r=[("psg", i % 3)], w=[("fe", i % 2)])
                                    S.add("dve", lambda h, e_=e_, t_=t_, pg=pg, n=n: h.scalar_tensor_tensor(out=t_[:, 0:n], in0=e_[:, 0:n], scalar=1.0, in1=pg[:, 0:n],
                                                                                                     op0=ALU.add, op1=ALU.mult),
                                          r=[("fe", i % 2), ("psg", i % 3)], w=[("ftg", i % 2)])
                                    S.add("dve", lambda h, t_=t_, pu=pu, n=n, a=a, j=j: h.scalar_tensor_tensor(out=aT[:, j, a:a + n], in0=t_[:, 0:n], scalar=0.5, in1=pu[:, 0:n],
                                                                                                        op0=ALU.mult, op1=ALU.mult),
                                          r=[("ftg", i % 2), ("psu", i % 3)], w=[("aT", j, a)])
                        for jb in range(NJ // 2):
                            wblock(jb)
                        S.phase_end("ffnA")
                    with ExitStack() as st:
                        Tl, Pl = mk_alloc(st)
                        wd = [Tl("wd%d" % i, [128, NJ, 256], BF16) for i in range(2)]
                        psd = [Pl("psd%d" % i, [128, 512]) for i in range(4)]
                        stg = [Tl("fstg%d" % i, [128, 256]) for i in range(4)]
                        cnt = [0]

                        def dblock(cb):
                            b = cb % 2
                            for k4 in range(4):
                                S.dma("pool", lambda h, k4=k4: h.dma_start(out=wd[b][:, 11 * k4:11 * k4 + 11, :], in_=wd_v[:, 11 * k4:11 * k4 + 11, cb * 256:(cb + 1) * 256]),
                                      w=[("wd", b, k4)])
                            for a in range(0, nt, 128):
                                i = cnt[0]
                                cnt[0] += 1
                                p_ = psd[i % 4]
                                for k in range(NJ):
                                    S.add("pe", lambda h, k=k, p_=p_, a=a: h.matmul(p_[:, 0:256], lhsT=aT[:, k, a:a + 128], rhs=wd[b][:, k, :], start=(k == 0), stop=(k == NJ - 1)),
                                          r=[("wd", b, k // 11)], w=[("psd", i % 4)])
                                s_ = stg[i % 4]
                                if i % 2 == 0:
                                    S.add("act", lambda h, p_=p_, s_=s_: h.activation(out=s_[:], in_=p_[:, 0:256], func=AF.Copy), r=[("psd", i % 4)], w=[("fstg", i % 4)])
                                else:
                                    S.add("dve", lambda h, p_=p_, s_=s_: h.tensor_copy(out=s_[:], in_=p_[:, 0:256]), r=[("psd", i % 4)], w=[("fstg", i % 4)])
                                S.dma("sp", lambda h, s_=s_, a=a: h.dma_start(out=fsc[t0 + a:t0 + a + 128, cb * 256:(cb + 1) * 256], in_=s_[:]),
                                      r=[("fstg", i % 4)], w=[("fsc", i)])
                        for cb in range(8):
                            dblock(cb)
                        S.phase_end("ffnB")

        def phase_fin(l):
            last = (l == nlayers - 1)
            with ExitStack() as st:
                Tl, Pl = mk_alloc(st)
                G2 = Tl("G2", [128, D])
                gp = Tl("fgp", [128, D])
                xa = [Tl("fxa%d" % i, [128, D]) for i in range(3)]
                fa = [Tl("ffa%d" % i, [128, D]) for i in range(3)]
                xo = [Tl("fxo%d" % i, [128, D]) for i in range(3)]
                junk = Tl("fjunk", [128, D], BF16)
                ssq = Tl("fssq", [128, 2])

                def tile(tt, i):
                    S.dma("sp", lambda h: h.dma_start(out=xa[i % 3][:], in_=x1s[tt * 128:(tt + 1) * 128, :]), w=[("xa", i % 3)])
                    S.dma("sp", lambda h: h.dma_start(out=fa[i % 3][:], in_=fsc[tt * 128:(tt + 1) * 128, :]), w=[("fa", i % 3)])
                    S.add("act", lambda h: h.activation(out=junk[:], in_=fa[i % 3][:], func=AF.Square, accum_out=ssq[:, 0:1]), r=[("fa", i % 3)], w=["fjunk", "fss"])
                    rstd_ops(ssq[:, 0:1], ssq[:, 1:2], 1.0 / D, ["fss"], ["frstd"])
                    S.add("dve", lambda h: h.scalar_tensor_tensor(out=xo[i % 3][:], in0=fa[i % 3][:], scalar=ssq[:, 1:2], in1=G2[:], op0=ALU.mult, op1=ALU.mult),
                          r=[("fa", i % 3), "frstd", "fG"], w=[("xo", i % 3)])
                    S.add("dve", lambda h: h.tensor_tensor(out=xo[i % 3][:], in0=xo[i % 3][:], in1=xa[i % 3][:], op=ALU.add), r=[("xo", i % 3), ("xa", i % 3)], w=[("xo", i % 3)])
                    if last:
                        dst = out[(tt - 2) * 128:(tt - 1) * 128, :]
                    else:
                        dst = xnext[tt * 128:(tt + 1) * 128, :]
                    S.dma("pool", lambda h: h.dma_start(out=dst, in_=xo[i % 3][:]), r=[("xo", i % 3)], w=[("xout", tt)])
                i = 0
                if not last:
                    load_gate_mod(l, G2, gp, "f", "post_ffn_g", 5 * 2048, 1)
                    for tt in range(2):
                        tile(tt, i)
                        i += 1
                load_gate_mod(l, G2, gp, "f", "post_ffn_g", 5 * 2048, 0)
                for tt in range(2, NT):
                    tile(tt, i)
                    i += 1
                S.phase_end("fin")

        PH = {}
        PH["out"] = phase_out
        PH["ffn"] = phase_ffn
        PH["fin"] = phase_fin
        PH["ssd"] = phase_ssd
        PH["ret"] = phase_ret
        PH["att"] = phase_att

        PH["in"] = phase_in
        try:
            if phases is None or "mod" in phases:
                phase_mod(0)
            check_stop("mod")
            for l in (layers if layers is not None else range(nlayers)):
                for nm in ("in", "att", "ret", "ssd", "out", "ffn", "fin"):
                    if nm in PH and (phases is None or nm in phases):
                        PH[nm](l)
                        check_stop("%s%d" % (nm, l))
        except Stop:
            pass
        S.phase_end("final")
        stats = S.emit()
    return nc, stats, list(I.keys())


def make_in_maps(inputs):
    consts = make_consts()
    rope = make_rope()
    maps = []
    shared = {n: np.ascontiguousarray(inputs[n], dtype=np.float32) for n, _ in SMALL + BIG}
    for b in range(8):
        m = dict(shared)
        m["x"] = np.ascontiguousarray(inputs["x"][b])
        m["ctx"] = np.ascontiguousarray(inputs["ctx"][b])
        m["cvec"] = np.ascontiguousarray(np.stack([inputs["c"][b], inputs["c_ctx"]]))
        m["consts"] = consts
        m["rope"] = rope
        maps.append(m)
    return maps


def kernel(**inputs):
    nc, _, _ = build(nlayers=2, debug=False)
    maps = make_in_maps(inputs)
    res = run_bass_kernel_spmd(nc, maps, core_ids=list(range(8)))
    return np.stack([np.asarray(r["out"], dtype=np.float32) for r in res.results], axis=0)
```
